# Optimizing a Trainium2 kernel written in Bass

```python
import math
import jax
import jax.numpy as jnp
from jax import lax
import numpy as np

D_MODEL = 1024
BATCH = 8
SEQ = 4096
DEPTH = 2

HEAD_DIM = 64
GROUP_HEADS = 4
GROUP_WIDTH = GROUP_HEADS * HEAD_DIM
N_GROUPS = 4
D_MIX = N_GROUPS * GROUP_WIDTH
NSA_HEADS = GROUP_HEADS
GDN_HEADS = GROUP_HEADS
MLSTM_HEADS = GROUP_HEADS
CMP_STRIDE = 16
CMP_BLOCK = 2 * CMP_STRIDE
CMP_HIDDEN = 256
SLC_BLOCK = 64
N_SLC = 16
N_LOCAL_SLC = 2
WINDOW = 512
Q_BLOCK = 128
FORCE = 1e6
N_BUCKETS = 32
MAX_DISTANCE = 128
SC_WIDTH = 3
GDN_CONV = 4
GDN_CHUNK = 64
MLSTM_CHUNK = 64
D_FF = 2816
N_SUBLAYERS = 3
EPS = 1e-6

IN_LAYOUT = (
    ("a_q", GROUP_WIDTH), ("a_k_cmp", HEAD_DIM), ("a_v_cmp", HEAD_DIM),
    ("a_k_slc", HEAD_DIM), ("a_v_slc", HEAD_DIM), ("a_k_win", HEAD_DIM), ("a_v_win", HEAD_DIM),
    ("a_gate", 3 * NSA_HEADS),
    ("b_b", GROUP_WIDTH), ("b_c", GROUP_WIDTH), ("b_x", GROUP_WIDTH),
    ("c_q", GROUP_WIDTH), ("c_k", GROUP_WIDTH), ("c_v", GROUP_WIDTH),
    ("c_beta", GDN_HEADS), ("c_alpha", GDN_HEADS), ("c_z", GROUP_WIDTH),
    ("d_q", GROUP_WIDTH), ("d_k", GROUP_WIDTH), ("d_v", GROUP_WIDTH),
    ("d_i", MLSTM_HEADS), ("d_f", MLSTM_HEADS), ("d_o", GROUP_WIDTH),
)
D_IN = sum(s for _, s in IN_LAYOUT)

kernel_name = "hymba_style_nsa_conv_gdn_mlstm_macaron"


def rms_norm(x, g):
    xf = x.astype(jnp.float32)
    y = xf * lax.rsqrt(jnp.mean(xf * xf, axis=-1, keepdims=True) + EPS)
    return (y * g).astype(x.dtype)


def l2_norm(x):
    xf = x.astype(jnp.float32)
    return (xf * lax.rsqrt(jnp.sum(xf * xf, axis=-1, keepdims=True) + EPS)).astype(x.dtype)


def adaln_pre(h, g, shift, scale):
    return rms_norm(h, g) * (1 + scale) + shift


def swiglu(h, w13, w2):
    a, b = jnp.split(h @ w13, 2, axis=-1)
    return (jax.nn.silu(a) * b) @ w2


def causal_dwconv(x, w):
    K, C = w.shape
    return lax.conv_general_dilated(
        x, w[:, None, :], window_strides=(1,), padding=[(K - 1, 0)],
        dimension_numbers=("NWC", "WIO", "NWC"), feature_group_count=C)


def masked_softmax(logits, mask):
    l = jnp.where(mask, logits.astype(jnp.float32), -1e30)
    m = jnp.max(l, axis=-1, keepdims=True)
    e = jnp.where(mask, jnp.exp(l - m), 0.0)
    return e / jnp.maximum(jnp.sum(e, axis=-1, keepdims=True), 1e-30)


def t5_bucket(dist):
    n = jnp.maximum(dist, 0)
    max_exact = N_BUCKETS // 2
    nf = jnp.maximum(n, 1).astype(jnp.float32)
    large = max_exact + (jnp.log(nf / max_exact) / math.log(MAX_DISTANCE / max_exact)
                         * (N_BUCKETS - max_exact)).astype(jnp.int32)
    large = jnp.minimum(large, N_BUCKETS - 1)
    return jnp.where(n < max_exact, n, large)


def nsa_mixer(q, k_cmp, v_cmp, k_slc, v_slc, k_win, v_win, gate_raw, q_norm_g, k_norm_g,
              cmp_pos, cmp_k_w1, cmp_k_w2, cmp_v_w1, cmp_v_w2, t5_table):
    B, T = q.shape[0], q.shape[1]
    dt = q.dtype
    H, dh = NSA_HEADS, HEAD_DIM
    scale = dh ** -0.5
    q = rms_norm(q.reshape(B, T, H, dh), q_norm_g) * scale
    k_slc = rms_norm(k_slc, k_norm_g)
    k_win = rms_norm(k_win, k_norm_g)
    t = jnp.arange(T)

    nc = T // CMP_STRIDE - 1

    def compress(kv, w1, w2):
        ch = kv.reshape(B, T // CMP_STRIDE, CMP_STRIDE, dh)
        blocks = jnp.concatenate([ch[:, :-1], ch[:, 1:]], axis=2) + cmp_pos
        hid = jax.nn.silu(blocks.reshape(B, nc, CMP_BLOCK * dh) @ w1)
        return hid @ w2

    kc = rms_norm(compress(k_cmp, cmp_k_w1, cmp_k_w2), k_norm_g)
    vc = compress(v_cmp, cmp_v_w1, cmp_v_w2)
    cmp_end = jnp.arange(nc) * CMP_STRIDE + CMP_BLOCK - 1
    cmp_dist = t[:, None] - cmp_end[None, :]
    cmp_bias = t5_table[t5_bucket(cmp_dist)].transpose(2, 0, 1)
    cmp_logits = jnp.einsum("bthd,bnd->bhtn", q, kc) + cmp_bias
    p_cmp = masked_softmax(cmp_logits, cmp_dist >= 0)
    o_cmp = jnp.einsum("bhtn,bnd->bthd", p_cmp.astype(dt), vc)

    nblk = T // SLC_BLOCK
    n_sel = min(N_SLC, nblk)
    ci = np.arange(nc)[:, None]
    bj = np.arange(nblk)[None, :]
    overlap = ((ci * CMP_STRIDE < (bj + 1) * SLC_BLOCK)
               & (ci * CMP_STRIDE + CMP_BLOCK > bj * SLC_BLOCK)).astype(np.float32)
    score = jnp.sum(p_cmp, axis=1) @ jnp.asarray(overlap)
    cur = (t // SLC_BLOCK)[:, None]
    blk = jnp.arange(nblk)[None, :]
    forced = (blk == 0) | ((cur - blk >= 0) & (cur - blk < N_LOCAL_SLC))
    score = jnp.where(forced, FORCE, score)
    score = jnp.where(blk <= cur, score, -FORCE)
    _, sel_idx = lax.top_k(score, n_sel)

    ks_blocks = k_slc.reshape(B, nblk, SLC_BLOCK, dh)
    vs_blocks = v_slc.reshape(B, nblk, SLC_BLOCK, dh)
    kw_p = jnp.pad(k_win, ((0, 0), (WINDOW, 0), (0, 0)))
    vw_p = jnp.pad(v_win, ((0, 0), (WINDOW, 0), (0, 0)))
    nq = T // Q_BLOCK
    J = n_sel * SLC_BLOCK

    def qblock(i):
        s = i * Q_BLOCK
        qb = lax.dynamic_slice_in_dim(q, s, Q_BLOCK, axis=1)
        tq = s + jnp.arange(Q_BLOCK)
        kwb = lax.dynamic_slice_in_dim(kw_p, s, WINDOW + Q_BLOCK, axis=1)
        vwb = lax.dynamic_slice_in_dim(vw_p, s, WINDOW + Q_BLOCK, axis=1)
        kpos = s - WINDOW + jnp.arange(WINDOW + Q_BLOCK)
        wd = tq[:, None] - kpos[None, :]
        wmask = (wd >= 0) & (wd < WINDOW) & (kpos[None, :] >= 0)
        wl = jnp.einsum("bqhd,bkd->bhqk", qb, kwb) + t5_table[t5_bucket(wd)].transpose(2, 0, 1)
        o_w = jnp.einsum("bhqk,bkd->bqhd", masked_softmax(wl, wmask).astype(dt), vwb)
        idx = lax.dynamic_slice_in_dim(sel_idx, s, Q_BLOCK, axis=1)
        ksel = jax.vmap(lambda kb, ib: kb[ib])(ks_blocks, idx)
        vsel = jax.vmap(lambda vb, ib: vb[ib])(vs_blocks, idx)
        spos = idx[..., None] * SLC_BLOCK + jnp.arange(SLC_BLOCK)
        sd = (tq[None, :, None, None] - spos).reshape(B, Q_BLOCK, J)
        sl = jnp.einsum("bqhd,bqnsd->bhqns", qb, ksel).reshape(B, H, Q_BLOCK, J)
        sl = sl + t5_table[t5_bucket(sd)].transpose(0, 3, 1, 2)
        ps = masked_softmax(sl, (sd >= 0)[:, None]).astype(dt)
        o_s = jnp.einsum("bhqj,bqjd->bqhd", ps, vsel.reshape(B, Q_BLOCK, J, dh))
        return o_w, o_s

    o_win, o_slc = lax.map(qblock, jnp.arange(nq))
    o_win = o_win.transpose(1, 0, 2, 3, 4).reshape(B, T, H, dh)
    o_slc = o_slc.transpose(1, 0, 2, 3, 4).reshape(B, T, H, dh)

    g = jax.nn.sigmoid(gate_raw.reshape(B, T, H, 3))
    o = g[..., 0:1] * o_cmp + g[..., 1:2] * o_slc + g[..., 2:3] * o_win
    return o.reshape(B, T, GROUP_WIDTH)


def short_conv_mixer(b_gate, c_gate, xs, conv_w):
    return b_gate * causal_dwconv(c_gate * xs, conv_w)


def to_chunks(a, L):
    B, T, H = a.shape[:3]
    a = a.reshape((B, T // L, L, H) + a.shape[3:])
    return a.transpose((1, 0, 3, 2) + tuple(range(4, a.ndim)))


def from_chunks(a):
    N, B, H, L, d = a.shape
    return a.transpose(1, 0, 3, 2, 4).reshape(B, N * L, H, d)


def chunk_gated_delta_rule(q, k, v, g, beta):
    f32 = jnp.float32
    L = GDN_CHUNK
    q, k, v = (to_chunks(a.astype(f32), L) for a in (q, k, v))
    g = to_chunks(g.astype(f32), L)
    beta = to_chunks(beta.astype(f32), L)
    gc = jnp.cumsum(g, axis=-1)
    incl = jnp.tril(jnp.ones((L, L), bool))
    strict = jnp.tril(jnp.ones((L, L), bool), -1)
    seg = jnp.exp(jnp.where(incl, gc[..., :, None] - gc[..., None, :], -jnp.inf))
    kb = k * beta[..., None]
    a_mat = jnp.where(strict, jnp.einsum("nbhid,nbhjd->nbhij", kb, k) * seg, 0.0) + jnp.eye(L, dtype=f32)
    rhs = jnp.concatenate([v * beta[..., None], kb * jnp.exp(gc)[..., None]], axis=-1)
    sol = lax.linalg.triangular_solve(a_mat, rhs, left_side=True, lower=True, unit_diagonal=True)
    u, w = sol[..., :HEAD_DIM], sol[..., HEAD_DIM:]
    qk = jnp.einsum("nbhid,nbhjd->nbhij", q, k) * seg
    q_dec = q * jnp.exp(gc)[..., None]
    k_dec = k * jnp.exp(gc[..., -1:] - gc)[..., None]
    g_tot = jnp.exp(gc[..., -1])

    def step(S, inp):
        u_c, w_c, qd_c, kd_c, qk_c, gt_c = inp
        v_new = u_c - jnp.einsum("bhlk,bhkv->bhlv", w_c, S)
        o = jnp.einsum("bhlk,bhkv->bhlv", qd_c, S) + jnp.einsum("bhij,bhjv->bhiv", qk_c, v_new)
        S = S * gt_c[..., None, None] + jnp.einsum("bhlk,bhlv->bhkv", kd_c, v_new)
        return S, o

    S0 = jnp.zeros(u.shape[1:3] + (HEAD_DIM, HEAD_DIM), f32)
    _, o = lax.scan(step, S0, (u, w, q_dec, k_dec, qk, g_tot))
    return from_chunks(o)


def gated_deltanet(q, k, v, beta_raw, alpha_raw, z, conv_w, A_log, dt_bias, norm_g):
    B, T = q.shape[0], q.shape[1]
    qkv = jax.nn.silu(causal_dwconv(jnp.concatenate([q, k, v], axis=-1), conv_w))
    q, k, v = (a.reshape(B, T, GDN_HEADS, HEAD_DIM) for a in jnp.split(qkv, 3, axis=-1))
    q = l2_norm(q) * HEAD_DIM ** -0.5
    k = l2_norm(k)
    beta = jax.nn.sigmoid(beta_raw)
    g = -jnp.exp(A_log) * jax.nn.softplus(alpha_raw + dt_bias)
    o = chunk_gated_delta_rule(q, k, v, g, beta).astype(z.dtype)
    o = rms_norm(o, norm_g) * jax.nn.silu(z.reshape(B, T, GDN_HEADS, HEAD_DIM))
    return o.reshape(B, T, GROUP_WIDTH)


def mlstm_chunkwise(q, k, v, i_pre, f_pre):
    f32 = jnp.float32
    L = MLSTM_CHUNK
    q, k, v = (to_chunks(a.astype(f32), L) for a in (q, k, v))
    log_i = to_chunks(i_pre.astype(f32), L)
    log_f = to_chunks(jax.nn.log_sigmoid(f_pre.astype(f32)), L)
    b = jnp.cumsum(log_f, axis=-1)
    incl = jnp.tril(jnp.ones((L, L), bool))
    log_w = jnp.where(incl, b[..., :, None] - b[..., None, :] + log_i[..., None, :], -jnp.inf)
    m_intra = jnp.max(log_w, axis=-1)
    log_w_end = b[..., -1:] - b + log_i
    m_end = jnp.max(log_w_end, axis=-1)
    qk = jnp.einsum("nbhid,nbhjd->nbhij", q, k)

    def step(carry, inp):
        C, n, m = carry
        q_c, k_c, v_c, b_c, lw_c, mi_c, lwe_c, me_c, qk_c = inp
        log_inter = b_c + m[..., None]
        m_t = jnp.maximum(log_inter, mi_c)
        w_inter = jnp.exp(log_inter - m_t)
        s = qk_c * jnp.exp(lw_c - m_t[..., None])
        num = (w_inter[..., None] * jnp.einsum("bhlk,bhkv->bhlv", q_c, C)
               + jnp.einsum("bhij,bhjv->bhiv", s, v_c))
        den = w_inter * jnp.einsum("bhlk,bhk->bhl", q_c, n) + jnp.sum(s, axis=-1)
        h = num / jnp.maximum(jnp.abs(den), jnp.exp(-m_t))[..., None]
        m_new = jnp.maximum(b_c[..., -1] + m, me_c)
        w_old = jnp.exp(b_c[..., -1] + m - m_new)
        w_new = jnp.exp(lwe_c - m_new[..., None])
        C = w_old[..., None, None] * C + jnp.einsum("bhl,bhlk,bhlv->bhkv", w_new, k_c, v_c)
        n = w_old[..., None] * n + jnp.einsum("bhl,bhlk->bhk", w_new, k_c)
        return (C, n, m_new), h

    Bsz, H = q.shape[1], q.shape[2]
    carry0 = (jnp.zeros((Bsz, H, HEAD_DIM, HEAD_DIM), f32), jnp.zeros((Bsz, H, HEAD_DIM), f32),
              jnp.zeros((Bsz, H), f32))
    _, h = lax.scan(step, carry0, (q, k, v, b, log_w, m_intra, log_w_end, m_end, qk))
    return from_chunks(h)


def mlstm_mixer(q, k, v, i_raw, f_raw, o_raw, f_bias, norm_g):
    B, T = q.shape[0], q.shape[1]
    heads = lambda a: a.reshape(B, T, MLSTM_HEADS, HEAD_DIM)
    h = mlstm_chunkwise(heads(q), heads(k) * HEAD_DIM ** -0.5, heads(v), i_raw, f_raw + f_bias)
    h = rms_norm(h.astype(q.dtype), norm_g) * jax.nn.sigmoid(heads(o_raw))
    return h.reshape(B, T, GROUP_WIDTH)


def setup_inputs(seed: int = 0) -> dict:
    key = jax.random.key(seed)
    ks = jax.random.split(key, 32)
    L, D, F = DEPTH, D_MODEL, D_FF
    nrm = lambda k, shape, s: jax.random.normal(k, shape, jnp.float32) * s
    dt = jnp.exp(jax.random.uniform(ks[22], (L, GDN_HEADS), jnp.float32, math.log(1e-3), math.log(1e-1)))
    return {
        "x": nrm(ks[0], (BATCH, SEQ, D), 1.0),
        "c": nrm(ks[1], (BATCH, D), 1.0),
        "ada_w": nrm(ks[2], (L, D, 3 * N_SUBLAYERS * D), 0.5 * D ** -0.5),
        "ada_b": nrm(ks[3], (L, 3 * N_SUBLAYERS * D), 0.02),
        "norm_g": 1.0 + nrm(ks[4], (L, N_SUBLAYERS, D), 0.02),
        "ffn1_w13": nrm(ks[5], (L, D, 2 * F), D ** -0.5),
        "ffn1_w2": nrm(ks[6], (L, F, D), F ** -0.5),
        "ffn2_w13": nrm(ks[7], (L, D, 2 * F), D ** -0.5),
        "ffn2_w2": nrm(ks[8], (L, F, D), F ** -0.5),
        "w_in": nrm(ks[9], (L, D, D_IN), D ** -0.5),
        "b_in": nrm(ks[10], (L, D_IN), 0.02),
        "q_norm_g": 1.0 + nrm(ks[11], (L, HEAD_DIM), 0.02),
        "k_norm_g": 1.0 + nrm(ks[12], (L, HEAD_DIM), 0.02),
        "cmp_pos": nrm(ks[13], (L, CMP_BLOCK, HEAD_DIM), 0.1),
        "cmp_k_w1": nrm(ks[14], (L, CMP_BLOCK * HEAD_DIM, CMP_HIDDEN), (CMP_BLOCK * HEAD_DIM) ** -0.5),
        "cmp_k_w2": nrm(ks[15], (L, CMP_HIDDEN, HEAD_DIM), CMP_HIDDEN ** -0.5),
        "cmp_v_w1": nrm(ks[16], (L, CMP_BLOCK * HEAD_DIM, CMP_HIDDEN), (CMP_BLOCK * HEAD_DIM) ** -0.5),
        "cmp_v_w2": nrm(ks[17], (L, CMP_HIDDEN, HEAD_DIM), CMP_HIDDEN ** -0.5),
        "t5_table": nrm(ks[18], (N_BUCKETS, NSA_HEADS), 0.5),
        "sc_conv_w": nrm(ks[19], (L, SC_WIDTH, GROUP_WIDTH), SC_WIDTH ** -0.5),
        "gdn_conv_w": nrm(ks[20], (L, GDN_CONV, 3 * GROUP_WIDTH), GDN_CONV ** -0.5),
        "gdn_A_log": jnp.log(jax.random.uniform(ks[21], (L, GDN_HEADS), jnp.float32, 1.0, 16.0)),
        "gdn_dt_bias": dt + jnp.log(-jnp.expm1(-dt)),
        "gdn_norm_g": 1.0 + nrm(ks[23], (L, HEAD_DIM), 0.02),
        "mlstm_f_bias": 3.0 + jax.random.uniform(ks[24], (L, MLSTM_HEADS), jnp.float32, 0.0, 3.0),
        "mlstm_norm_g": 1.0 + nrm(ks[25], (L, HEAD_DIM), 0.02),
        "mix_norm_g": 1.0 + nrm(ks[26], (L, 2, GROUP_WIDTH), 0.02),
        "w_out": nrm(ks[27], (L, D_MIX, D), D_MIX ** -0.5),
    }


def reference(x, c, ada_w, ada_b, norm_g, ffn1_w13, ffn1_w2, ffn2_w13, ffn2_w2, w_in, b_in,
              q_norm_g, k_norm_g, cmp_pos, cmp_k_w1, cmp_k_w2, cmp_v_w1, cmp_v_w2, t5_table,
              sc_conv_w, gdn_conv_w, gdn_A_log, gdn_dt_bias, gdn_norm_g, mlstm_f_bias,
              mlstm_norm_g, mix_norm_g, w_out):
    B, T, D = x.shape
    names = [n for n, _ in IN_LAYOUT]
    splits = np.cumsum([s for _, s in IN_LAYOUT])[:-1].tolist()
    cond = jax.nn.silu(c)
    for l in range(DEPTH):
        mod = (cond @ ada_w[l] + ada_b[l]).reshape(B, N_SUBLAYERS, 3, 1, D)
        h = adaln_pre(x, norm_g[l, 0], mod[:, 0, 0], mod[:, 0, 1])
        x = x + 0.5 * mod[:, 0, 2] * swiglu(h, ffn1_w13[l], ffn1_w2[l])
        h = adaln_pre(x, norm_g[l, 1], mod[:, 1, 0], mod[:, 1, 1])
        p = dict(zip(names, jnp.split(h @ w_in[l] + b_in[l], splits, axis=-1)))
        y_a = nsa_mixer(p["a_q"], p["a_k_cmp"], p["a_v_cmp"], p["a_k_slc"], p["a_v_slc"],
                        p["a_k_win"], p["a_v_win"], p["a_gate"], q_norm_g[l], k_norm_g[l],
                        cmp_pos[l], cmp_k_w1[l], cmp_k_w2[l], cmp_v_w1[l], cmp_v_w2[l], t5_table)
        y_b = short_conv_mixer(p["b_b"], p["b_c"], p["b_x"], sc_conv_w[l])
        y_c = gated_deltanet(p["c_q"], p["c_k"], p["c_v"], p["c_beta"], p["c_alpha"], p["c_z"],
                             gdn_conv_w[l], gdn_A_log[l], gdn_dt_bias[l], gdn_norm_g[l])
        y_d = mlstm_mixer(p["d_q"], p["d_k"], p["d_v"], p["d_i"], p["d_f"], p["d_o"],
                          mlstm_f_bias[l], mlstm_norm_g[l])
        y = jnp.concatenate([rms_norm(y_a, mix_norm_g[l, 0]), rms_norm(y_b, mix_norm_g[l, 1]), y_c, y_d],
                            axis=-1)
        x = x + mod[:, 1, 2] * (y @ w_out[l])
        h = adaln_pre(x, norm_g[l, 2], mod[:, 2, 0], mod[:, 2, 1])
        x = x + 0.5 * mod[:, 2, 2] * swiglu(h, ffn2_w13[l], ffn2_w2[l])
    return x
```

```python
import math
import os
GSTOP = float(os.environ.get('GSTOP', '99'))
from contextlib import ExitStack
import numpy as np
import concourse.bass as bass
import concourse.mybir as mybir
from concourse.bass_utils import run_bass_kernel_spmd

F32 = mybir.dt.float32
BF16 = mybir.dt.bfloat16
AF = mybir.ActivationFunctionType
ALU = mybir.AluOpType
AX = mybir.AxisListType

D = 1024
DC = 8
DFF = 2816
FC = 22
D_IN = 3484
EPS = 1e-6

ENGS = ("pe", "act", "dve", "pool", "sp")


class _Op:
    __slots__ = ("eng", "fn", "reads", "writes", "deps", "sig", "tok", "waits", "dma", "snap", "inc", "cost", "odeps", "idx")

    def __init__(self, eng, fn, reads, writes, dma):
        self.eng = eng
        self.fn = fn
        self.reads = reads
        self.writes = writes
        self.dma = dma
        self.deps = ()
        self.sig = False
        self.tok = None
        self.waits = ()
        self.snap = None
        self.inc = 1
        self.cost = 300.0
        self.odeps = ()
        self.idx = 0


class Prog:
    EPOCH = 20000
    NDMA = 12

    def __init__(self, nc):
        self.nc = nc
        self.ops = []
        self.last_w = {}
        self.readers = {}
        self.last_on = {}
        self.dma_ops = []

    EXCL = ("pj", "pw", "ptk", "pab", "po", "pst", "pmod", "ps", "poa", "pob", "pmx", "pd", "pc")

    def _excl(self, k):
        return (k[0] if isinstance(k, tuple) else k) in self.EXCL

    def add(self, eng, fn, R=(), W=(), dma=False, cost=300.0):
        xr = [k for k in R if self._excl(k)]
        if xr:
            R = [k for k in R if not self._excl(k)]
            W = list(W) + [k for k in xr if k not in W]
        op = _Op(eng, fn, tuple(R), tuple(W), dma)
        i = len(self.ops)
        deps = set()
        for k in op.reads:
            w = self.last_w.get(k)
            if w is not None:
                deps.add(w)
        for k in op.writes:
            w = self.last_w.get(k)
            if w is not None:
                deps.add(w)
            deps.update(self.readers.get(k, ()))
        for k in op.writes:
            self.last_w[k] = i
            self.readers[k] = []
        for k in op.reads:
            if k not in op.writes:
                self.readers.setdefault(k, []).append(i)
        op.deps = deps
        op.cost = cost
        self.ops.append(op)
        self.last_on[eng] = i
        if dma:
            self.dma_ops.append(i)
        return i

    def barrier(self):
        lasts = set(self.last_on.values()) | set(self.dma_ops)
        self.dma_ops = []
        for e in ENGS:
            op = _Op(e, None, (), (), False)
            op.deps = set(lasts)
            self.ops.append(op)
            self.last_on[e] = len(self.ops) - 1
        self.last_w = {}
        self.readers = {}

    def fence(self, eng, keys):
        self.add(eng, None, R=keys)

    def schedule(self):
        import heapq
        ops = self.ops
        n = len(ops)
        for i, op in enumerate(ops):
            op.idx = i
        order = []
        LAT = 250.0
        W = 24
        seg_start = 0
        i = 0
        segs = []
        while i < n:
            if ops[i].fn is None and not ops[i].dma and len(ops[i].reads) == 0 and len(ops[i].writes) == 0:
                j = i
                while j < n and ops[j].fn is None and not ops[j].dma:
                    j += 1
                segs.append((seg_start, i))
                segs.append((i, j))
                seg_start = j
                i = j
            else:
                i += 1
        segs.append((seg_start, n))
        for (a, b) in segs:
            if b <= a:
                continue
            if ops[a].fn is None and not ops[a].dma:
                order.extend(range(a, b))
                continue
            nrem = {}
            users = {}
            for k in range(a, b):
                dl = [d for d in ops[k].deps if d >= a]
                nrem[k] = len(dl)
                for d in dl:
                    users.setdefault(d, []).append(k)
            blev = {}
            for k in range(b - 1, a - 1, -1):
                m_ = 0.0
                for u in users.get(k, ()):
                    if blev[u] > m_:
                        m_ = blev[u]
                blev[k] = ops[k].cost + LAT + m_
            ready = {e: [] for e in ENGS}
            for k in range(a, b):
                if nrem[k] == 0:
                    heapq.heappush(ready[ops[k].eng], k)
            fin = {}
            free = {e: 0.0 for e in ENGS}
            left = b - a
            while left > 0:
                best = None
                for e in ENGS:
                    rl = ready[e]
                    if not rl:
                        continue
                    cands = heapq.nsmallest(W, rl)
                    for k in cands:
                        st = free[e]
                        for d in ops[k].deps:
                            if d >= a:
                                f = fin[d] + LAT
                                if f > st:
                                    st = f
                        key = (int(st / 250.0), -blev[k], k, st)
                        if best is None or key < best[0]:
                            best = (key, e, k)
                (_q, _b, k, st), e, _ = best
                ready[e].remove(k)
                heapq.heapify(ready[e])
                op = ops[k]
                if op.dma:
                    free[e] = st + 60.0
                    fin[k] = st + op.cost
                else:
                    free[e] = st + op.cost
                    fin[k] = st + op.cost
                order.append(k)
                left -= 1
                for u in users.get(k, ()):
                    nrem[u] -= 1
                    if nrem[u] == 0:
                        heapq.heappush(ready[ops[u].eng], u)
        return order

    def emit(self):
        nc = self.nc
        if os.environ.get("NOSCHED", "0") != "1":
            order = self.schedule()
            old = self.ops
            remap = {o: nidx for nidx, o in enumerate(order)}
            newops = [old[o] for o in order]
            for op in newops:
                op.deps = {remap[d] for d in op.deps}
            self.ops = newops
        ops = self.ops
        for i, op in enumerate(ops):
            if op.eng == "pe":
                op.deps = {d for d in op.deps if not (ops[d].eng == "pe" and not ops[d].dma)}
        ndma = 0
        slot_last = {}
        for i, op in enumerate(ops):
            if op.dma:
                slot = ndma % self.NDMA
                if slot in slot_last:
                    op.deps = set(op.deps) | {slot_last[slot]}
                slot_last[slot] = i
                op.tok = ("dma", slot, 16 * (ndma // self.NDMA + 1))
                ndma += 1
        for op in ops:
            for d in op.deps:
                ops[d].sig = True
        cnt = {e: 0 for e in ENGS}
        for op in ops:
            if op.sig and not op.dma:
                if op.fn is None:
                    continue
                cnt[op.eng] += 1
                c = cnt[op.eng]
                op.tok = (op.eng, (c - 1) // self.EPOCH, (c - 1) % self.EPOCH + 1)
        nep = {e: (cnt[e] + self.EPOCH - 1) // self.EPOCH for e in ENGS}
        sems = {}
        for e in ENGS:
            for k in range(max(nep[e], 0)):
                sems[(e, k)] = nc.alloc_semaphore("s_%s_%d" % (e, k))
        for s in range(min(self.NDMA, max(ndma, 1))):
            sems[("dma", s)] = nc.alloc_semaphore("s_dma_%d" % s)
        known = {e: {} for e in ENGS}
        for op in ops:
            kn = known[op.eng]
            waits = []
            stack = list(op.deps)
            seen = set()
            while stack:
                d = stack.pop()
                if d in seen:
                    continue
                seen.add(d)
                dop = ops[d]
                if dop.fn is None and not dop.dma:
                    if dop.eng == op.eng:
                        continue
                    stack.extend(dop.deps)
                    continue
                tk = dop.tok
                key = (tk[0], tk[1])
                if kn.get(key, 0) >= tk[2]:
                    continue
                waits.append((key, tk[2]))
                if dop.snap is not None:
                    for k2, v2 in dop.snap.items():
                        if kn.get(k2, 0) < v2:
                            kn[k2] = v2
                kn[key] = tk[2]
            wm = {}
            for key, v in waits:
                if wm.get(key, 0) < v:
                    wm[key] = v
            op.waits = tuple(wm.items())
            if op.tok is not None:
                op.snap = dict(kn)
        handles = {"pe": nc.tensor, "act": nc.scalar, "dve": nc.vector, "pool": nc.gpsimd, "sp": nc.sync}
        per = {e: [op for op in ops if op.eng == e] for e in ENGS}

        def run(e, eng):
            for op in per[e]:
                for key, v in op.waits:
                    eng.wait_ge(sems[key], v)
                if op.fn is None:
                    continue
                ins = op.fn(eng)
                if op.tok is not None:
                    tk = op.tok
                    ins.then_inc(sems[(tk[0], tk[1])], 16 if op.dma else 1)

        with nc.Block() as block:
            @block.tensor
            def _(eng):
                run("pe", eng)

            @block.scalar
            def _(eng):
                run("act", eng)

            @block.vector
            def _(eng):
                run("dve", eng)

            @block.gpsimd
            def _(eng):
                run("pool", eng)

            @block.sync
            def _(eng):
                run("sp", eng)
        self.stats = dict(n_ops=len(ops), cnt=cnt, ndma=ndma)


class KB:
    def __init__(self, nc):
        self.nc = nc
        self.P = Prog(nc)

    @staticmethod
    def _fs(ap):
        n = 1
        for d in ap.shape[1:]:
            n *= d
        return n

    def mm(self, out, lhsT, rhs, start=True, stop=True, R=(), W=(), **kw):
        passes = 2.0 if rhs.dtype == F32 else 1.0
        c = 40.0 + (self._fs(lhsT) * 0.85 + max(64, self._fs(rhs)) * 0.85) * passes
        return self.P.add("pe", lambda e: e.matmul(out, lhsT, rhs, start=start, stop=stop, **kw), R, W, cost=c)

    def tr(self, out, in_, ident, R=(), W=()):
        return self.P.add("pe", lambda e: e.transpose(out, in_, ident), R, W, cost=120.0)

    def act(self, out, in_, func, R=(), W=(), bias=None, scale=None):
        kw = {}
        if bias is not None:
            kw["bias"] = bias
        if scale is not None:
            kw["scale"] = scale
        return self.P.add("act", lambda e: e.activation(out, in_, func, **kw), R, W, cost=260.0 + 0.75 * self._fs(in_))

    def tt(self, out, in0, in1, op, R=(), W=(), eng="dve"):
        return self.P.add(eng, lambda e: e.tensor_tensor(out, in0, in1, op), R, W, cost=150.0 + 1.45 * self._fs(out))

    def ts(self, out, in0, s1, s2, op0, op1=None, R=(), W=(), eng="dve"):
        if op1 is None:
            return self.P.add(eng, lambda e: e.tensor_scalar(out, in0, s1, None, op0), R, W, cost=150.0 + 1.1 * self._fs(out))
        return self.P.add(eng, lambda e: e.tensor_scalar(out, in0, s1, s2, op0, op1), R, W, cost=150.0 + 1.1 * self._fs(out))

    def stt(self, out, in0, scalar, in1, op0, op1, R=(), W=()):
        return self.P.add("dve", lambda e: e.scalar_tensor_tensor(out, in0, scalar, in1, op0, op1), R, W,
                          cost=150.0 + 1.1 * self._fs(out))

    def copy(self, out, in_, R=(), W=(), eng="dve"):
        if eng == "act":
            return self.P.add("act", lambda e: e.copy(out, in_), R, W, cost=260.0 + 0.75 * self._fs(out))
        return self.P.add(eng, lambda e: e.tensor_copy(out, in_), R, W, cost=150.0 + 0.9 * self._fs(out))

    def recip(self, out, in_, R=(), W=()):
        return self.P.add("dve", lambda e: e.reciprocal(out, in_), R, W, cost=200.0 + 2.6 * self._fs(out))

    def memset(self, ap, val, W=(), eng="dve"):
        return self.P.add(eng, lambda e: e.memset(ap, val), (), W, cost=110.0 + 1.05 * self._fs(ap))

    def dma(self, out, in_, R=(), W=(), eng="sp", **kw):
        nbytes = out.shape[0] * self._fs(out) * (2 if out.dtype == BF16 else 4)
        return self.P.add(eng, lambda e: e.dma_start(out, in_, **kw), R, W, dma=True, cost=2200.0 + nbytes / 120.0)


def _sb(es, nc, name, shape, dt):
    return es.enter_context(nc.sbuf_tensor(name, shape, dt))


def _ps(es, nc, name, shape, dt=F32):
    return es.enter_context(nc.psum_tensor(name, shape, dt))


def mod_phase(K, es_glob, lay, ada_w_l, ada_bT_l, normgT_l, condT, out):
    nc = K.nc
    P = K.P
    with ExitStack() as es_own:
        es = es_glob if es_glob is not None else es_own
        wt = [_sb(es, nc, "adaw%d_%d" % (lay, i), [128, DC, 1024], F32) for i in range(2)]
        pm = _ps(es, nc, "pmod%d" % lay, [128, 72 * 2], F32)
        awv = ada_w_l.rearrange("(kc p) n -> p kc n", p=128)
        for g in range(9):
            b = wt[g % 2]
            for kc in range(DC):
                K.dma(b[:, kc, :], awv[:, kc, g * 1024:(g + 1) * 1024], W=[("adaw", g % 2, kc)])
            for jj in range(8):
                j = g * 8 + jj
                for kc in range(DC):
                    K.mm(pm[:, 2 * j:2 * j + 2], b[:, kc, jj * 128:(jj + 1) * 128], condT[:, kc, :],
                         start=(kc == 0), stop=(kc == DC - 1), R=[("adaw", g % 2, kc), "condT"], W=["pmod"])
        mod = out["mod"]
        pmv = pm[:].rearrange("p (j two) -> p j two", two=2)[:, :, 0]
        K.tt(mod[:], pmv, ada_bT_l, ALU.add, R=["pmod", "adab"], W=["mod"])
        for s in range(3):
            K.stt(out["gs"][:, s, :], mod[:, s * 24 + 8:s * 24 + 16], 1.0, normgT_l[:, s, :], ALU.add, ALU.mult,
                  R=["mod", "normg"], W=["gs"])
            K.ts(out["gate"][:, s, :], mod[:, s * 24 + 16:s * 24 + 24], 0.5 if s != 1 else 1.0, None, ALU.mult,
                 R=["mod"], W=["gate"])
    if es_glob is None:
        P.barrier()


def ffn_phase(K, lay, which, x_src, sname, x_dst, dname, w13, w2, mv, s, T, TT, consts):
    nc = K.nc
    P = K.P
    NT = T // TT
    tg = "f%d%d" % (lay, which)
    with ExitStack() as es:
        w13s = _sb(es, nc, "w13_" + tg, [128, DC, 2 * DFF], BF16)
        w2s = _sb(es, nc, "w2_" + tg, [128, FC, D], BF16)
        xt = _sb(es, nc, "xt_" + tg, [128, DC, TT], F32)
        sq = _sb(es, nc, "sq_" + tg, [128, DC, TT], BF16)
        hb = _sb(es, nc, "hb_" + tg, [128, DC, TT], BF16)
        gb = _sb(es, nc, "gb_" + tg, [128, FC, TT], BF16)
        rs = _sb(es, nc, "rs_" + tg, [128, TT], F32)
        sa = [_sb(es, nc, "sa%d_" % i + tg, [128, TT], F32) for i in range(2)]
        pst = _ps(es, nc, "pst_" + tg, [128, TT])
        pa = [_ps(es, nc, "pa%d_" % i + tg, [128, TT]) for i in range(2)]
        pb = [_ps(es, nc, "pb%d_" % i + tg, [128, TT]) for i in range(2)]
        po = [_ps(es, nc, "po%d_" % i + tg, [128, TT]) for i in range(2)]
        w13v = w13.rearrange("(kc p) f -> p kc f", p=128)
        w2v = w2.rearrange("(fc p) d -> p fc d", p=128)
        for kc in range(DC):
            K.dma(w13s[:, kc, :], w13v[:, kc, :], W=[("w13", kc)], eng="pool")
        for fc in range(FC):
            K.dma(w2s[:, fc, :], w2v[:, fc, :], W=[("w2", fc)], eng="pool")
        xsv = x_src.rearrange("(dc p) t -> p dc t", p=128)
        xdv = x_dst.rearrange("(dc p) t -> p dc t", p=128)
        gs, shift, gate = mv["gs"], mv["mod"], mv["gate"]
        for tt in range(NT):
            tsl = slice(tt * TT, (tt + 1) * TT)
            K.dma(xt[:], xsv[:, :, tsl], R=[("xd", sname, tt)], W=[("xt", dc) for dc in range(DC)])
            K.act(sq[:], xt[:], AF.Square, R=[("xt", dc) for dc in range(DC)], W=["sq"])
            for dc in range(DC):
                K.mm(pst[:], consts["ones_bf"][:], sq[:, dc, :], start=(dc == 0), stop=(dc == DC - 1),
                     R=["sq"], W=["pst"])
            K.act(rs[:], pst[:], AF.Sqrt, R=["pst", "epsc"], W=["rs"], bias=consts["eps"][:], scale=1.0 / D)
            K.recip(rs[:], rs[:], R=["rs"], W=["rs"])
            for dc in range(DC):
                k2 = dc % 2
                K.stt(sa[k2][:], xt[:, dc, :], gs[:, s, dc:dc + 1], rs[:], ALU.mult, ALU.mult,
                      R=[("xt", dc), "rs", "gs"], W=[("sa", k2)])
                K.act(hb[:, dc, :], sa[k2][:], AF.Identity, R=[("sa", k2), "mod"], W=[("hb", dc)],
                      bias=shift[:, s * 24 + dc:s * 24 + dc + 1], scale=1.0)
            for fc in range(FC):
                k2 = fc % 2
                for half, pp in ((0, pa), (1, pb)):
                    for kc in range(DC):
                        K.mm(pp[k2][:], w13s[:, kc, half * DFF + fc * 128:half * DFF + (fc + 1) * 128],
                             hb[:, kc, :], start=(kc == 0), stop=(kc == DC - 1),
                             R=[("w13", kc), ("hb", kc)], W=[("pab", half, k2)])
                K.act(sa[k2][:], pa[k2][:], AF.Silu, R=[("pab", 0, k2)], W=[("sa", k2)])
                K.tt(gb[:, fc, :], sa[k2][:], pb[k2][:], ALU.mult, R=[("sa", k2), ("pab", 1, k2)], W=[("gb", fc)])
            for dc in range(DC):
                k2 = dc % 2
                for fc in range(FC):
                    K.mm(po[k2][:], w2s[:, fc, dc * 128:(dc + 1) * 128], gb[:, fc, :],
                         start=(fc == 0), stop=(fc == FC - 1), R=[("w2", fc), ("gb", fc)], W=[("po", k2)])
                K.stt(xt[:, dc, :], po[k2][:], gate[:, s, dc:dc + 1], xt[:, dc, :], ALU.mult, ALU.add,
                      R=[("po", k2), "gate", ("xt", dc)], W=[("xt", dc)])
            K.dma(xdv[:, :, tsl], xt[:], R=[("xt", dc) for dc in range(DC)], W=[("xd", dname, tt)])
    P.barrier()


OFF = {}
_o = 0
for _n, _w in (("a_q", 256), ("a_k_cmp", 64), ("a_v_cmp", 64), ("a_k_slc", 64), ("a_v_slc", 64), ("a_k_win", 64),
               ("a_v_win", 64), ("a_gate", 12), ("b_b", 256), ("b_c", 256), ("b_x", 256), ("c_q", 256), ("c_k", 256),
               ("c_v", 256), ("c_beta", 4), ("c_alpha", 4), ("c_z", 256), ("d_q", 256), ("d_k", 256), ("d_v", 256),
               ("d_i", 4), ("d_f", 4), ("d_o", 256)):
    OFF[_n] = _o
    _o += _w
assert _o == D_IN
FM = ([(OFF["a_q"] + 64 * h, 64) for h in range(4)] + [(OFF["a_k_cmp"], 128), (OFF["a_k_slc"], 64), (OFF["a_k_win"], 64)]
      + [(OFF["b_b"] + 128 * i, 128) for i in range(2)] + [(OFF["b_c"] + 128 * i, 128) for i in range(2)]
      + [(OFF["b_x"] + 128 * i, 128) for i in range(2)] + [(OFF["c_q"] + 64 * i, 64) for i in range(12)]
      + [(OFF["d_q"] + 64 * i, 64) for i in range(4)] + [(OFF["d_k"] + 64 * i, 64) for i in range(4)])
G_AQ, G_KVC, G_KSLC, G_KWIN, G_BB, G_BC, G_BX, G_CQKV, G_DQ, G_DK = 0, 4, 5, 6, 7, 9, 11, 13, 25, 29
NFM = len(FM)
CM_ID, CM_TRIU, CM_SU, CM_SL, CM_LI, CM_N = 0, 128, 192, 256, 320, 384


def bc(ap, shape):
    return ap.to_broadcast(list(shape))


def mixer_phase(K, lay, x_src, sname, x_dst, dname, Wd, mv, T, TT, consts, yscr, mode, enable="ABCD"):
    nc = K.nc
    P = K.P
    NT = T // TT
    NCH = TT // 64
    tg = "m%d%d" % (lay, mode)
    if mode == 1:
        enable = "".join(c for c in enable if c in "BCD")
        WOFF, WN = OFF["b_b"], D_IN - OFF["b_b"]
    else:
        enable = "".join(c for c in enable if c in "A")
        WOFF, WN = 0, OFF["b_b"]
    SUM = cm_su = None
    cm = consts["cmask"]
    SUM = cm[0:64, CM_SU:CM_SU + 64]
    TRIU = cm[0:64, CM_TRIU:CM_TRIU + 64]
    SLM = cm[0:64, CM_SL:CM_SL + 64]
    IDENT = cm[:, CM_ID:CM_ID + 128]
    onesf = consts["ones_f"]
    with ExitStack() as es:
        sb = lambda name, shape, d=F32: _sb(es, nc, name + "_" + tg, shape, d)
        wins = sb("win", [128, DC, WN], BF16)
        wouts = sb("wout", [128, DC, D if mode == 2 else 2], BF16)
        bfm = sb("bfm", [128, NFM], F32)
        bfm8 = sb("bfm8", [128, 4], F32)
        bfmv = sb("bfmv", [64, 1], F32)
        K.dma(bfmv[:], Wd["b_fm"][64:128, G_KVC:G_KVC + 1], W=["a_bfmv"], allow_slow_non_contiguous=True)
        xt = sb("xt", [128, DC, TT])
        sq = sb("sq", [128, DC, TT], BF16)
        hb = sb("hb", [128, DC, TT], BF16)
        yt = sb("yt", [128, DC, TT], BF16)
        rs = sb("rs", [128, TT])
        sa = [sb("sa%d" % i, [128, TT]) for i in range(2)]
        scw = sb("scw", [128, 2, 3])
        mixg = sb("mixg", [128, 2, 2])
        DC_ = DC
        PJ = [_ps(es, nc, "pj0_" + tg, [128, 512])]
        if mode == 1:
            PST = PJ[0]
            PSTK = ("pj", 0)
            PD = [_ps(es, nc, "pd%d_" % i + tg, [128, 512]) for i in range(3)]
            PC = [_ps(es, nc, "pc%d_" % i + tg, [128, 512]) for i in range(4)]
        else:
            PST = _ps(es, nc, "pst_" + tg, [128, 512])
            PSTK = "pst"
            PTK = [_ps(es, nc, "ptk0_" + tg, [128, 512])]
            PW = [_ps(es, nc, "pw0_" + tg, [128, 512])]
            PS2 = [_ps(es, nc, "ps2%d_" % i + tg, [128, 512]) for i in range(2)]
            POA = _ps(es, nc, "poa_" + tg, [128, 512])
            PMX = _ps(es, nc, "pmx_" + tg, [128, 512])
            POB = PST

        winv = Wd["w_in"].rearrange("(kc p) n -> p kc n", p=128)
        for kc in range(DC):
            K.dma(wins[:, kc, :], winv[:, kc, WOFF:WOFF + WN], W=[("win", kc)], eng="pool")
        woutv = Wd["w_out"].rearrange("(kc p) n -> p kc n", p=128)
        if mode == 2:
            for kc in range(DC):
                K.dma(wouts[:, kc, :], woutv[:, kc, :], W=[("wout", kc)], eng="pool")
        K.dma(bfm[:], Wd["b_fm"], W=["bfm"])
        K.dma(scw[:], Wd["sc_conv_wT"], W=["scw"])
        K.dma(mixg[:], Wd["mixgT"], W=["mixg"])
        K.ts(bfm8[:], bfm[:, G_DK:G_DK + 4], 0.125, None, ALU.mult, R=["bfm"], W=["bfm8"])

        xsv = x_src.rearrange("(dc p) t -> p dc t", p=128)
        xdv = x_dst.rearrange("(dc p) t -> p dc t", p=128)
        gs, shift, gate = mv["gs"], mv["mod"], mv["gate"]
        s = 1
        pjn = [0]

        def proj_fm(g, dst, dkeys, scale=1.0, bias=None, func=AF.Identity):
            c0, ncol = FM[g]
            pj = PJ[pjn[0] % len(PJ)]
            kk = ("pj", pjn[0] % len(PJ))
            pjn[0] += 1
            for kc in range(DC):
                K.mm(pj[0:ncol, 0:TT], wins[:, kc, c0 - WOFF:c0 - WOFF + ncol], hb[:, kc, :], start=(kc == 0), stop=(kc == DC - 1),
                     R=[("win", kc), ("hb", kc)], W=[kk])
            b = bias if bias is not None else bfm[0:ncol, g:g + 1]
            K.act(dst, pj[0:ncol, 0:TT], func, R=[kk, "bfm", "bfm8"], W=dkeys, bias=b, scale=scale)

        if "B" in enable:
            ub = [sb("ub%d" % ch, [128, 2 + TT]) for ch in range(2)]
            bbt = sb("bbt", [128, TT])
            cct = sb("cct", [128, TT])
            cvt = sb("cvt", [128, TT])
            ybt = [sb("ybt%d" % ch, [128, TT]) for ch in range(2)]
            for ch in range(2):
                K.memset(ub[ch][:, 0:2], 0.0, W=[("ub", ch)])
        if "D" in enable:
            QmT = sb("QmT", [64, 4, TT], BF16)
            KmT = sb("KmT", [64, 4, TT], BF16)
            Smb = sb("Smb", [64, 4, 65], BF16)
            K.memset(Smb[:], 0.0, W=["Smb"])
            Sm = sb("Sm", [64, 4, 65])
            K.memset(Sm[:], 0.0, W=["Sm"])
            btm_d = sb("btm_d", [64, 776])
            fbb = sb("fbb", [64, 4])
            gnd = sb("gnd", [64, 64])
            K.dma(btm_d[:], Wd["b_in"][OFF["d_k"]:OFF["d_k"] + 776].partition_broadcast(64), W=["btm_d"])
            K.dma(fbb[:], Wd["mlstm_f_bias"].partition_broadcast(64), W=["fbb"])
            K.dma(gnd[:], Wd["mlstm_norm_g"].partition_broadcast(64), W=["gnd"])
            K.tt(btm_d[:, 516:520], btm_d[:, 516:520], fbb[:], ALU.add, R=["btm_d", "fbb"], W=["btm_d"])
            DT = {}
            for nm_, *shp in (("ktok", [64, 4, 64]), ("vext", [64, 4, 65], BF16), ("ifo", [64, 264]), ("lf", [64, 4]),
                             ("gtot", [64, 4]), ("bsb", [64, 4]), ("ddd", [64, 4]), ("edec", [64, 4]), ("eb", [64, 4]),
                             ("slg", [64, 4, 64]), ("et", [64, 4, 64]), ("pt", [64, 4, 64], BF16), ("nd", [64, 4, 65]),
                             ("den", [64, 4]), ("hm", [64, 4, 64]), ("hsq", [64, 4, 64]), ("ss", [64, 4]),
                             ("so", [64, 4, 64]), ("kd", [64, 4, 64], BF16), ("stmp", [64, 4, 65])):
                DT[nm_] = [sb("d_%s%d" % (nm_, z), *shp) if False else sb("d_%s%d" % (nm_, z), shp[0], shp[1] if len(shp) > 1 else F32)
                           for z in range(2)]
            for z in range(2):
                K.memset(DT["vext"][z][:], 1.0, W=["d_vext%d" % z])

        def mlstm_chunk(c):
            cs = slice(c * 64, (c + 1) * 64)
            z = c % 2
            t_ = {k_: v_[z] for k_, v_ in DT.items()}
            kk_ = lambda n_: "d_%s%d" % (n_, z)
            ktok, vext, ifo, lf, gtot, bsb, ddd, edec, eb = (t_[x] for x in ("ktok", "vext", "ifo", "lf", "gtot", "bsb", "ddd", "edec", "eb"))
            slg, et, pt, nd, den, hm, hsq, ss, so, kd, stmp = (t_[x] for x in ("slg", "et", "pt", "nd", "den", "hm", "hsq", "ss", "so", "kd", "stmp"))
            DA, DB, DC = PD
            kA, kB, kC = ("pd", 0), ("pd", 1), ("pd", 2)
            for kc in range(DC_):
                K.mm(DA[0:64, 0:512], hb[:, kc, cs], wins[:, kc, OFF["d_k"] - WOFF:OFF["d_k"] - WOFF + 512],
                     start=(kc == 0), stop=(kc == DC_ - 1), R=[("win", kc), ("hb", kc)], W=[kA])
            for kc in range(DC_):
                K.mm(DB[0:64, 0:264], hb[:, kc, cs], wins[:, kc, OFF["d_i"] - WOFF:OFF["d_i"] - WOFF + 264],
                     start=(kc == 0), stop=(kc == DC_ - 1), R=[("win", kc), ("hb", kc)], W=[kB])
            K.tt(ktok[:].rearrange("p a b -> p (a b)"), DA[0:64, 0:256], btm_d[:, 0:256], ALU.add,
                 R=[kA, "btm_d"], W=[kk_("ktok")])
            K.tt(vext[:, :, 0:64], v4(DA[0:64, 256:512]), v4(btm_d[:, 256:512]), ALU.add, R=[kA, "btm_d"], W=[kk_("vext")])
            K.tt(ifo[:], DB[0:64, 0:264], btm_d[:, 512:776], ALU.add, R=[kB, "btm_d"], W=[kk_("ifo")])
            K.act(lf[:], ifo[:, 4:8], AF.Exp, R=[kk_("ifo")], W=[kk_("lf")], scale=-1.0)
            K.act(lf[:], lf[:], AF.Ln, R=[kk_("lf")], W=[kk_("lf")], bias=consts["one"][0:64, :], scale=1.0)
            K.ts(lf[:], lf[:], -1.0, None, ALU.mult, R=[kk_("lf")], W=[kk_("lf")])
            K.mm(DB[0:64, 264:268], TRIU, lf[:], R=[kk_("lf"), "cmask"], W=[kB])
            K.mm(DB[0:64, 268:272], onesf[0:64, 0:64], lf[:], R=[kk_("lf"), "ones_f"], W=[kB])
            K.act(gtot[:], DB[0:64, 268:272], AF.Exp, R=[kB], W=[kk_("gtot")])
            K.copy(bsb[:], DB[0:64, 264:268], R=[kB], W=[kk_("bsb")])
            K.tt(ddd[:], DB[0:64, 268:272], bsb[:], ALU.subtract, R=[kB, kk_("bsb")], W=[kk_("ddd")])
            K.tt(ddd[:], ddd[:], ifo[:, 0:4], ALU.add, R=[kk_("ddd"), kk_("ifo")], W=[kk_("ddd")])
            K.act(edec[:], ddd[:], AF.Exp, R=[kk_("ddd")], W=[kk_("edec")])
            K.act(eb[:], bsb[:], AF.Exp, R=[kk_("bsb")], W=[kk_("eb")])
            K.tt(slg[:], bc(TRIU.unsqueeze(1), [64, 4, 64]), bc(lf[:, :].unsqueeze(2), [64, 4, 64]), ALU.mult,
                 R=[kk_("lf"), "cmask"], W=[kk_("slg")])
            K.mm(DC[0:64, 0:256], SLM, slg[:].rearrange("p a b -> p (a b)"), R=[kk_("slg"), "cmask"], W=[kC])
            for h in range(4):
                K.mm(DC[0:64, 256 + h * 64:256 + (h + 1) * 64], KmT[:, h, cs], QmT[:, h, cs], R=["QmT", "KmT"], W=[kC])
            K.tt(et[:], v4(DC[0:64, 0:256]), bc(ifo[:, 0:4].unsqueeze(2), [64, 4, 64]), ALU.add, R=[kC, kk_("ifo")],
                 W=[kk_("et")])
            K.act(et[:], et[:], AF.Exp, R=[kk_("et")], W=[kk_("et")])
            K.tt(et[:], et[:], bc(TRIU.unsqueeze(1), [64, 4, 64]), ALU.mult, R=[kk_("et"), "cmask"], W=[kk_("et")])
            K.tt(pt[:], et[:], v4(DC[0:64, 256:512]), ALU.mult, R=[kk_("et"), kC], W=[kk_("pt")])
            for h in range(4):
                K.mm(DA[0:64, h * 65:(h + 1) * 65], QmT[:, h, cs], Smb[:, h, :], R=["QmT", "Smb"], W=[kA])
            for h in range(4):
                K.mm(DC[0:64, h * 65:(h + 1) * 65], pt[:, h, :], vext[:, h, :], R=[kk_("pt"), kk_("vext")], W=[kC])
            qs = DA[0:64, 0:260].rearrange("p (a b) -> p a b", a=4)
            K.tt(nd[:], qs, bc(eb[:, :].unsqueeze(2), [64, 4, 65]), ALU.mult, R=[kA, kk_("eb")], W=[kk_("nd")])
            K.tt(nd[:], nd[:], DC[0:64, 0:260].rearrange("p (a b) -> p a b", a=4), ALU.add, R=[kk_("nd"), kC], W=[kk_("nd")])
            K.stt(kd[:], ktok[:], 0.125, bc(edec[:, :].unsqueeze(2), [64, 4, 64]), ALU.mult, ALU.mult,
                  R=[kk_("ktok"), kk_("edec")], W=[kk_("kd")])
            for h in range(4):
                K.mm(DA[0:64, h * 65:(h + 1) * 65], kd[:, h, :], vext[:, h, :], R=[kk_("kd"), kk_("vext")], W=[kA])
            K.tt(stmp[:], Sm[:], bc(gtot[:, :].unsqueeze(2), [64, 4, 65]), ALU.mult, R=["Sm", kk_("gtot")], W=[kk_("stmp")])
            K.tt(Sm[:], stmp[:], DA[0:64, 0:260].rearrange("p (a b) -> p a b", a=4), ALU.add, R=[kk_("stmp"), kA], W=["Sm"])
            K.copy(Smb[:], Sm[:], R=["Sm"], W=["Smb"], eng="act")
            K.act(den[:], nd[:, :, 64], AF.Abs, R=[kk_("nd")], W=[kk_("den")])
            K.ts(den[:], den[:], 1.0, None, ALU.max, R=[kk_("den")], W=[kk_("den")])
            K.recip(den[:], den[:], R=[kk_("den")], W=[kk_("den")])
            K.tt(hm[:], nd[:, :, 0:64], bc(den[:, :].unsqueeze(2), [64, 4, 64]), ALU.mult, R=[kk_("nd"), kk_("den")], W=[kk_("hm")])
            K.tt(hsq[:], hm[:], hm[:], ALU.mult, R=[kk_("hm")], W=[kk_("hsq")])
            K.P.add("dve", lambda e: e.tensor_reduce(ss[:], hsq[:], AX.X, ALU.add), R=[kk_("hsq")], W=[kk_("ss")], cost=400.0)
            K.act(ss[:], ss[:], AF.Sqrt, R=[kk_("ss"), "epsc"], W=[kk_("ss")], bias=consts["eps"][0:64, :], scale=1.0 / 64)
            K.recip(ss[:], ss[:], R=[kk_("ss")], W=[kk_("ss")])
            K.act(so[:].rearrange("p a b -> p (a b)"), ifo[:, 8:264], AF.Sigmoid, R=[kk_("ifo")], W=[kk_("so")])
            K.tt(hm[:], hm[:], bc(ss[:, :].unsqueeze(2), [64, 4, 64]), ALU.mult, R=[kk_("hm"), kk_("ss")], W=[kk_("hm")])
            K.tt(hm[:], hm[:], bc(gnd[:, :].unsqueeze(1), [64, 4, 64]), ALU.mult, R=[kk_("hm"), "gnd"], W=[kk_("hm")])
            K.tt(hm[:], hm[:], so[:], ALU.mult, R=[kk_("hm"), kk_("so")], W=[kk_("hm")])
            hmf = hm[:].rearrange("p a b -> p (a b)")
            for pr in range(2):
                K.tr(DB[:, 272 + pr * 64:272 + (pr + 1) * 64], hmf[:, pr * 128:(pr + 1) * 128], IDENT[0:64, 0:64],
                     R=[kk_("hm"), "cmask"], W=[kB])
                K.copy(yt[:, 6 + pr, cs], DB[:, 272 + pr * 64:272 + (pr + 1) * 64], R=[kB], W=[("yt", 6 + pr)], eng="act")

        GDT = F32
        if "C" in enable:
            cin = sb("c_cin", [64, 12, 3 + TT])
            qkv = sb("c_qkv", [64, 12, TT])
            qkb = sb("c_qkb", [64, 12, TT], GDT)
            Sgb = sb("c_Sb", [64, 4, 64], GDT)
            K.memset(Sgb[:], 0.0, W=["c_Sb"])
            identb1 = sb("c_identb", [64, 64], GDT)
            K.copy(identb1[:], IDENT[0:64, 0:64], R=["cmask"], W=["c_identb"])
            gcw = sb("c_gcw", [64, 12, 4])
            Sg = sb("c_S", [64, 4, 64])
            K.memset(Sg[:], 0.0, W=["c_S"])
            K.memset(cin[:, :, 0:3], 0.0, W=[("c_cin", g) for g in range(12)])
            K.dma(gcw[:], Wd["gdn_conv_wT"], W=["c_gcw"])
            btm_c = sb("c_btm", [64, 264])
            dtb = sb("c_dtb", [64, 4])
            nea = sb("c_nea", [64, 4])
            gng = sb("c_gng", [64, 64])
            K.dma(btm_c[:], Wd["b_in"][OFF["c_beta"]:OFF["c_beta"] + 264].partition_broadcast(64), W=["c_btm"])
            K.dma(dtb[:], Wd["gdn_dt_bias"].partition_broadcast(64), W=["c_dtb"])
            K.dma(nea[:], Wd["gdn_A_log"].partition_broadcast(64), W=["c_nea"])
            K.dma(gng[:], Wd["gdn_norm_g"].partition_broadcast(64), W=["c_gng"])
            K.tt(btm_c[:, 4:8], btm_c[:, 4:8], dtb[:], ALU.add, R=["c_btm", "c_dtb"], W=["c_btm"])
            K.act(nea[:], nea[:], AF.Exp, R=["c_nea"], W=["c_nea"])
            K.ts(nea[:], nea[:], -1.0, None, ALU.mult, R=["c_nea"], W=["c_nea"])
            csq = sb("c_sq", [64, TT])
            crs = sb("c_rs", [64, TT])
            CT = {}
            for nm_, *shp in (("baz", [64, 264]), ("beta", [64, 4]), ("gg", [64, 4]), ("gcs", [64, 4]), ("egc", [64, 4]),
                             ("bgc", [64, 4]), ("edl", [64, 4]), ("gto", [64, 4]), ("ktk", [64, 4, 64]), ("vb", [64, 4, 64], GDT),
                             ("kbe", [64, 4, 64], GDT), ("kdc", [64, 4, 64], GDT), ("ug", [64, 4, 64]), ("slgc", [64, 4, 64]),
                             ("dgb", [64, 4, 64]), ("seg", [64, 4, 64]), ("segT", [64, 4, 64]), ("t1", [64, 4, 64]),
                             ("t2", [64, 4, 64]), ("NN", [64, 2, 4, 64], GDT), ("NN2", [64, 2, 4, 64], GDT), ("XX", [64, 4, 64]), ("XB", [64, 4, 64], GDT),
                             ("ptc", [64, 4, 64], GDT), ("uu", [64, 4, 64]), ("wt", [64, 4, 64], GDT), ("vn", [64, 4, 64]), ("vnb", [64, 4, 64], GDT),
                             ("oo", [64, 4, 64]), ("osq", [64, 4, 64]), ("oss", [64, 4]), ("sz", [64, 4, 64]),
                             ("stg", [64, 4, 64])):
                CT[nm_] = [sb("c_%s%d" % (nm_, z), shp[0], shp[1] if len(shp) > 1 else F32) for z in range(2)]

        def v4(ap):
            return ap.rearrange("p (a b) -> p a b", a=4)

        def gdn_tile():
            for g in range(12):
                proj_fm(G_CQKV + g, cin[:, g, 3:3 + TT], [("c_cin", g)])
                K.ts(qkv[:, g, :], cin[:, g, 3:3 + TT], gcw[:, g, 3:4], None, ALU.mult, R=[("c_cin", g), "c_gcw"],
                     W=[("c_qkv", g)])
                for k in range(3):
                    K.stt(qkv[:, g, :], cin[:, g, k:k + TT], gcw[:, g, k:k + 1], qkv[:, g, :], ALU.mult, ALU.add,
                          R=[("c_cin", g), "c_gcw", ("c_qkv", g)], W=[("c_qkv", g)])
                K.copy(cin[:, g, 0:3], cin[:, g, TT:TT + 3], R=[("c_cin", g)], W=[("c_cin", g)])
                K.act(qkv[:, g, :], qkv[:, g, :], AF.Silu, R=[("c_qkv", g)], W=[("c_qkv", g)])
                if g < 8:
                    K.tt(csq[:], qkv[:, g, :], qkv[:, g, :], ALU.mult, R=[("c_qkv", g)], W=["c_sq"])
                    K.mm(PST[0:64, 0:TT], onesf[0:64, 0:64], csq[:], R=["c_sq", "ones_f"], W=[PSTK])
                    K.act(crs[:], PST[0:64, 0:TT], AF.Sqrt, R=[PSTK, "epsc"], W=["c_rs"], bias=consts["eps"][0:64, :],
                          scale=1.0)
                    K.recip(crs[:], crs[:], R=["c_rs"], W=["c_rs"])
                    K.stt(qkb[:, g, :], qkv[:, g, :], 0.125 if g < 4 else 1.0, crs[:], ALU.mult, ALU.mult,
                          R=[("c_qkv", g), "c_rs"], W=[("c_qkb", g)])
                else:
                    K.copy(qkb[:, g, :], qkv[:, g, :], R=[("c_qkv", g)], W=[("c_qkb", g)])

        def gdn_chunk(c):
            cs = slice(c * 64, (c + 1) * 64)
            z = c % 2
            t_ = {k_: v_[z] for k_, v_ in CT.items()}
            kk_ = lambda n_: "c_%s%d" % (n_, z)
            baz, beta, gg, gcs, egc, bgc, edl, gto = (t_[x] for x in ("baz", "beta", "gg", "gcs", "egc", "bgc", "edl", "gto"))
            ktk, vb, kbe, kdc, ug, slgc, dgb, seg, segT = (t_[x] for x in ("ktk", "vb", "kbe", "kdc", "ug", "slgc", "dgb", "seg", "segT"))
            t1, t2, NN, NN2, XX, ptc, uu, wt, vn, oo, osq, oss, sz, stg = (t_[x] for x in ("t1", "t2", "NN", "NN2", "XX", "ptc", "uu", "wt", "vn", "oo", "osq", "oss", "sz", "stg"))
            XB, vnb = t_["XB"], t_["vnb"]
            CA, CB, CC, CD = PC
            kA, kB, kC, kD = ("pc", 0), ("pc", 1), ("pc", 2), ("pc", 3)
            QK = [("c_qkb", g) for g in range(12)]
            for kc in range(DC_):
                K.mm(CA[0:64, 0:264], hb[:, kc, cs], wins[:, kc, OFF["c_beta"] - WOFF:OFF["c_beta"] - WOFF + 264],
                     start=(kc == 0), stop=(kc == DC_ - 1), R=[("win", kc), ("hb", kc)], W=[kA])
            K.tt(baz[:], CA[0:64, 0:264], btm_c[:], ALU.add, R=[kA, "c_btm"], W=[kk_("baz")])
            K.act(beta[:], baz[:, 0:4], AF.Sigmoid, R=[kk_("baz")], W=[kk_("beta")])
            K.act(gg[:], baz[:, 4:8], AF.Exp, R=[kk_("baz")], W=[kk_("gg")])
            K.act(gg[:], gg[:], AF.Ln, R=[kk_("gg")], W=[kk_("gg")], bias=consts["one"][0:64, :], scale=1.0)
            K.tt(gg[:], gg[:], nea[:], ALU.mult, R=[kk_("gg"), "c_nea"], W=[kk_("gg")])
            K.mm(CA[0:64, 264:268], TRIU, gg[:], R=[kk_("gg"), "cmask"], W=[kA])
            K.mm(CA[0:64, 268:272], onesf[0:64, 0:64], gg[:], R=[kk_("gg"), "ones_f"], W=[kA])
            K.copy(gcs[:], CA[0:64, 264:268], R=[kA], W=[kk_("gcs")])
            K.act(egc[:], CA[0:64, 264:268], AF.Exp, R=[kA], W=[kk_("egc")])
            K.act(gto[:], CA[0:64, 268:272], AF.Exp, R=[kA], W=[kk_("gto")])
            K.tt(edl[:], CA[0:64, 268:272], gcs[:], ALU.subtract, R=[kA, kk_("gcs")], W=[kk_("edl")])
            K.act(edl[:], edl[:], AF.Exp, R=[kk_("edl")], W=[kk_("edl")])
            K.tt(bgc[:], beta[:], egc[:], ALU.mult, R=[kk_("beta"), kk_("egc")], W=[kk_("bgc")])
            for h in range(4):
                K.mm(CB[0:64, h * 64:(h + 1) * 64], qkb[:, 4 + h, cs], identb1[:], R=QK + ["c_identb"], W=[kB])
                K.mm(CB[0:64, 256 + h * 64:256 + (h + 1) * 64], qkb[:, 8 + h, cs], identb1[:], R=QK + ["c_identb"], W=[kB])
            K.copy(ktk[:], v4(CB[0:64, 0:256]), R=[kB], W=[kk_("ktk")], eng="act")
            K.tt(vb[:], v4(CB[0:64, 256:512]), bc(beta[:, :].unsqueeze(2), [64, 4, 64]), ALU.mult, R=[kB, kk_("beta")], W=[kk_("vb")])
            K.tt(kbe[:], ktk[:], bc(bgc[:, :].unsqueeze(2), [64, 4, 64]), ALU.mult, R=[kk_("ktk"), kk_("bgc")], W=[kk_("kbe")])
            K.tt(kdc[:], ktk[:], bc(edl[:, :].unsqueeze(2), [64, 4, 64]), ALU.mult, R=[kk_("ktk"), kk_("edl")], W=[kk_("kdc")])
            K.tt(ug[:], bc(TRIU.unsqueeze(1), [64, 4, 64]), bc(gg[:, :].unsqueeze(2), [64, 4, 64]), ALU.mult,
                 R=[kk_("gg"), "cmask"], W=[kk_("ug")])
            K.tt(slgc[:], bc(SLM.unsqueeze(1), [64, 4, 64]), bc(gg[:, :].unsqueeze(2), [64, 4, 64]), ALU.mult,
                 R=[kk_("gg"), "cmask"], W=[kk_("slgc")])
            K.mm(CC[0:64, 0:256], TRIU, slgc[:].rearrange("p a b -> p (a b)"), R=[kk_("slgc"), "cmask"], W=[kC])
            K.mm(CC[0:64, 256:512], SLM, ug[:].rearrange("p a b -> p (a b)"), R=[kk_("ug"), "cmask"], W=[kC])
            K.tt(dgb[:], bc(IDENT[0:64, 0:64].unsqueeze(1), [64, 4, 64]), bc(beta[:, :].unsqueeze(2), [64, 4, 64]), ALU.mult,
                 R=[kk_("beta"), "cmask"], W=[kk_("dgb")])
            K.act(seg[:], v4(CC[0:64, 0:256]), AF.Exp, R=[kC], W=[kk_("seg")])
            K.act(segT[:], v4(CC[0:64, 256:512]), AF.Exp, R=[kC], W=[kk_("segT")])
            for h in range(4):
                K.mm(CD[0:64, h * 64:(h + 1) * 64], qkb[:, 4 + h, cs], qkb[:, 4 + h, cs], R=QK, W=[kD])
                K.mm(CD[0:64, 256 + h * 64:256 + (h + 1) * 64], qkb[:, 4 + h, cs], qkb[:, h, cs], R=QK, W=[kD])
            K.mm(CB[0:64, 0:256], onesf[0:64, 0:64], dgb[:].rearrange("p a b -> p (a b)"), R=[kk_("dgb"), "ones_f"], W=[kB])
            K.tt(t1[:], seg[:], bc(SLM.unsqueeze(1), [64, 4, 64]), ALU.mult, R=[kk_("seg"), "cmask"], W=[kk_("t1")])
            K.tt(t1[:], t1[:], v4(CD[0:64, 0:256]), ALU.mult, R=[kk_("t1"), kD], W=[kk_("t1")])
            K.tt(NN[:, 0], t1[:], bc(beta[:, :].unsqueeze(2), [64, 4, 64]), ALU.mult, R=[kk_("t1"), kk_("beta")], W=[kk_("NN")])
            K.tt(t2[:], segT[:], bc(SUM.unsqueeze(1), [64, 4, 64]), ALU.mult, R=[kk_("segT"), "cmask"], W=[kk_("t2")])
            K.tt(t2[:], t2[:], v4(CD[0:64, 0:256]), ALU.mult, R=[kk_("t2"), kD], W=[kk_("t2")])
            K.tt(NN[:, 1], t2[:], v4(CB[0:64, 0:256]), ALU.mult, R=[kk_("t2"), kB], W=[kk_("NN")])
            K.tt(ptc[:], segT[:], bc(TRIU.unsqueeze(1), [64, 4, 64]), ALU.mult, R=[kk_("segT"), "cmask"], W=[kk_("ptc")])
            K.tt(ptc[:], ptc[:], v4(CD[0:64, 256:512]), ALU.mult, R=[kk_("ptc"), kD], W=[kk_("ptc")])
            K.tt(XX[:], bc(IDENT[0:64, 0:64].unsqueeze(1), [64, 4, 64]), NN[:, 1], ALU.subtract, R=[kk_("NN"), "cmask"],
                 W=[kk_("XX")])
            K.copy(XB[:], XX[:], R=[kk_("XX")], W=[kk_("XB")], eng="act")
            cur, nxt, ck, nk = NN, NN2, kk_("NN"), kk_("NN2")
            for lvl in range(5):
                last = (lvl == 4)
                for h in range(4):
                    K.mm(CC[0:64, h * 64:(h + 1) * 64], cur[:, 1, h, :], cur[:, 0, h, :], R=[ck], W=[kC])
                    if not last:
                        K.mm(CC[0:64, 256 + h * 64:256 + (h + 1) * 64], cur[:, 0, h, :], cur[:, 1, h, :], R=[ck], W=[kC])
                if last:
                    K.copy(nxt[:, 0], v4(CC[0:64, 0:256]), R=[kC], W=[nk], eng="act")
                else:
                    K.copy(nxt[:].rearrange("p t a b -> p (t a b)"), CC[0:64, 0:512], R=[kC], W=[nk], eng="act")
                for h in range(4):
                    K.mm(CB[0:64, h * 64:(h + 1) * 64], nxt[:, 0, h, :], XB[:, h, :], R=[nk, kk_("XB")], W=[kB])
                K.tt(XX[:], XX[:], v4(CB[0:64, 0:256]), ALU.add, R=[kk_("XX"), kB], W=[kk_("XX")])
                K.copy(XB[:], XX[:], R=[kk_("XX")], W=[kk_("XB")], eng="act")
                cur, nxt, ck, nk = nxt, cur, nk, ck
            for h in range(4):
                K.mm(CC[0:64, h * 64:(h + 1) * 64], XB[:, h, :], vb[:, h, :], R=[kk_("XB"), kk_("vb")], W=[kC])
                K.mm(CC[0:64, 256 + h * 64:256 + (h + 1) * 64], kbe[:, h, :], XB[:, h, :], R=[kk_("XB"), kk_("kbe")], W=[kC])
            K.copy(uu[:], v4(CC[0:64, 0:256]), R=[kC], W=[kk_("uu")], eng="act")
            K.copy(wt[:], v4(CC[0:64, 256:512]), R=[kC], W=[kk_("wt")])
            for h in range(4):
                K.mm(CD[0:64, h * 64:(h + 1) * 64], wt[:, h, :], Sgb[:, h, :], R=[kk_("wt"), "c_Sb"], W=[kD])
                K.mm(CD[0:64, 256 + h * 64:256 + (h + 1) * 64], qkb[:, h, cs], Sgb[:, h, :], R=QK + ["c_Sb"], W=[kD])
            K.tt(vnb[:], uu[:], v4(CD[0:64, 0:256]), ALU.subtract, R=[kk_("uu"), kD], W=[kk_("vnb")])
            K.tt(oo[:], v4(CD[0:64, 256:512]), bc(egc[:, :].unsqueeze(2), [64, 4, 64]), ALU.mult, R=[kD, kk_("egc")], W=[kk_("oo")])
            for h in range(4):
                K.mm(CD[0:64, h * 64:(h + 1) * 64], kdc[:, h, :], vnb[:, h, :], R=[kk_("kdc"), kk_("vnb")], W=[kD])
            K.tt(stg[:], Sg[:], bc(gto[:, :].unsqueeze(2), [64, 4, 64]), ALU.mult, R=["c_S", kk_("gto")], W=[kk_("stg")])
            K.tt(Sg[:], stg[:], v4(CD[0:64, 0:256]), ALU.add, R=[kk_("stg"), kD], W=["c_S"])
            K.copy(Sgb[:], Sg[:], R=["c_S"], W=["c_Sb"], eng="act")
            for h in range(4):
                K.mm(CB[0:64, 256 + h * 64:256 + (h + 1) * 64], ptc[:, h, :], vnb[:, h, :], R=[kk_("ptc"), kk_("vnb")], W=[kB])
            K.tt(oo[:], oo[:], v4(CB[0:64, 256:512]), ALU.add, R=[kk_("oo"), kB], W=[kk_("oo")])
            K.tt(osq[:], oo[:], oo[:], ALU.mult, R=[kk_("oo")], W=[kk_("osq")])
            K.P.add("dve", lambda e: e.tensor_reduce(oss[:], osq[:], AX.X, ALU.add), R=[kk_("osq")], W=[kk_("oss")], cost=400.0)
            K.act(oss[:], oss[:], AF.Sqrt, R=[kk_("oss"), "epsc"], W=[kk_("oss")], bias=consts["eps"][0:64, :], scale=1.0 / 64)
            K.recip(oss[:], oss[:], R=[kk_("oss")], W=[kk_("oss")])
            K.act(sz[:].rearrange("p a b -> p (a b)"), baz[:, 8:264], AF.Silu, R=[kk_("baz")], W=[kk_("sz")])
            K.tt(oo[:], oo[:], bc(oss[:, :].unsqueeze(2), [64, 4, 64]), ALU.mult, R=[kk_("oo"), kk_("oss")], W=[kk_("oo")])
            K.tt(oo[:], oo[:], bc(gng[:, :].unsqueeze(1), [64, 4, 64]), ALU.mult, R=[kk_("oo"), "c_gng"], W=[kk_("oo")])
            K.tt(oo[:], oo[:], sz[:], ALU.mult, R=[kk_("oo"), kk_("sz")], W=[kk_("oo")])
            oof = oo[:].rearrange("p a b -> p (a b)")
            for pr in range(2):
                K.tr(CA[:, 272 + pr * 64:272 + (pr + 1) * 64], oof[:, pr * 128:(pr + 1) * 128], IDENT[0:64, 0:64],
                     R=[kk_("oo"), "cmask"], W=[kA])
                K.copy(yt[:, 4 + pr, cs], CA[:, 272 + pr * 64:272 + (pr + 1) * 64], R=[kA], W=[("yt", 4 + pr)], eng="act")

        if "A" in enable:
            NB = T // 64
            NCB = T // 16
            NM = (NCB + 127) // 128
            NQ = T // 128
            w1k = sb("a_w1k", [64, 32, 256], BF16)
            w1v = sb("a_w1v", [64, 32, 256], BF16)
            w2k = sb("a_w2k", [128, 2, 64], BF16)
            w2v = sb("a_w2v", [128, 2, 64], BF16)
            K.dma(w1k[:], Wd["cmp_k_w1"].rearrange("(s d) n -> d s n", d=64), W=["a_w1k"], eng="pool")
            K.dma(w1v[:], Wd["cmp_v_w1"].rearrange("(s d) n -> d s n", d=64), W=["a_w1v"], eng="pool")
            K.dma(w2k[:], Wd["cmp_k_w2"].rearrange("(c p) n -> p c n", p=128), W=["a_w2k"], eng="pool")
            K.dma(w2v[:], Wd["cmp_v_w2"].rearrange("(c p) n -> p c n", p=128), W=["a_w2v"], eng="pool")
            posT = sb("a_posT", [64, 32], BF16)
            K.dma(posT[:], Wd["cmp_posT"], W=["a_posT"], eng="pool")
            hbias = sb("a_hbias", [128, 4])
            expc = sb("a_expc", [64, T], BF16)
            K.dma(expc[0:NB, :], Wd["expc"], W=["a_expc"], eng="pool")
            keepc = sb("a_keepc", [128, 2, 2 * NB])
            K.dma(keepc[:], Wd["keepadd"], W=["a_keepc"])
            identb = sb("a_identb", [128, 128], BF16)
            K.copy(identb[:], IDENT, R=["cmask"], W=["a_identb"])
            biasT = sb("a_bias", [128, 19, 512])
            for q in range(19):
                K.dma(biasT[:, q, :], Wd["bias_scr"][q], W=[("a_bias", q)])
            bw4 = sb("a_bw4", [128, 128])
            K.dma(bw4[:], Wd["bw4"], W=["a_bw4"])
            qng = sb("a_qng", [64, 2])
            K.dma(qng[:], Wd["qkng"], W=["a_qng"])
            K.ts(qng[:, 0:1], qng[:, 0:1], 0.125, None, ALU.mult, R=["a_qng"], W=["a_qng"])
            tabrow = sb("a_tabrow", [65, 4])
            K.dma(tabrow[64:65, :], Wd["t5_table"][31:32, :], W=["a_tabrow"])
            mg0 = sb("a_mg0", [128, 256])
            K.dma(mg0[:], Wd["mix_norm_g0"].partition_broadcast(128), W=["a_mg0"])
            btm_a = sb("a_btm", [128, 204])
            K.dma(btm_a[:], Wd["b_in"][OFF["a_v_slc"]:OFF["a_v_slc"] + 204].partition_broadcast(128), W=["a_btm"])
            ovl = sb("a_ovl", [128, NM, 64])
            K.dma(ovl[:], Wd["ovl"], W=["a_ovl"])
            kcmpT = sb("a_kcmpT", [64, T], BF16)
            vcmpT = sb("a_vcmpT", [64, T], BF16)
            KsT = sb("a_KsT", [128, T], BF16)
            KwT = sb("a_KwT", [128, T], BF16)
            kcT = sb("a_kcT", [128, NM * 128])
            vcT = sb("a_vcT", [64, NM * 128])
            vcx = sb("a_vcx", [128, NM, 65])
            Vs = sb("a_Vs", [128, NQ, 65], BF16)
            Vw = sb("a_Vw", [128, NQ, 65], BF16)
            NQT = TT // 128
            QaT = sb("a_QaT", [128, NQT, 4, 128], BF16)
            QaF = sb("a_QaF", [128, NQT, 4, 128])
            gsb = sb("a_gsb", [128, TT // 128, 12])
            K.memset(KsT[64:128, :], 0.0, W=["a_KsT"])
            K.memset(KwT[64:128, :], 0.0, W=["a_KwT"])
            K.memset(KsT[64:65, :], 1.0, W=["a_KsT"])
            K.memset(KwT[64:65, :], 1.0, W=["a_KwT"])
            K.memset(kcT[:, :], 0.0, W=["a_kcT"])
            K.memset(kcT[64:65, :], 1.0, W=["a_kcT"])
            K.memset(QaT[64:128], 0.0, W=["a_QaT"])
            K.memset(QaF[64:128], 0.0, W=["a_QaF"])
            K.memset(vcT[:], 0.0, W=["a_vcT"])
            K.memset(vcx[:], 1.0, W=["a_vcx"])
            K.memset(Vs[:], 1.0, W=["a_Vs"])
            K.memset(Vw[:], 1.0, W=["a_Vw"])
            for q_ in range(NQT):
                K.copy(QaT[64:65, q_], bc(tabrow[64:65, :].unsqueeze(2), [1, 4, 128]), R=["a_tabrow"], W=["a_QaT"])
                K.copy(QaF[64:65, q_], bc(tabrow[64:65, :].unsqueeze(2), [1, 4, 128]), R=["a_tabrow"], W=["a_QaF"])
            for kv, w1 in enumerate((w1k, w1v)):
                for hc in range(2):
                    for s_ in range(32):
                        K.mm(PW[0][:, (kv * 2 + hc) * 2:(kv * 2 + hc) * 2 + 1], w1[:, s_, hc * 128:(hc + 1) * 128],
                             posT[:, s_:s_ + 1], start=(s_ == 0), stop=(s_ == 31), R=["a_w1k", "a_w1v", "a_posT"],
                             W=[("pw", 0)])
            K.copy(hbias[:], PW[0][:, 0:8].rearrange("p (a b) -> p a b", b=2)[:, :, 0], R=[("pw", 0)], W=["a_hbias"])
            a_raw = sb("a_raw", [64, TT])
            a_sq = sb("a_sq", [64, TT])
            a_rs = sb("a_rs", [64, TT])
            hact = sb("a_hact", [128, 2, 2, 32], BF16)
            cst = sb("a_cst", [64, 32])
            csq2 = sb("a_csq2", [64, 32])
            crs2 = sb("a_crs2", [64, 32])
            vtm = sb("a_vtm", [128, 204])
            Eb = [sb("a_E%d" % i, [128, 4, 128]) for i in range(2)]
            Tb = [sb("a_T%d" % i, [128, 4, 128]) for i in range(2)]
            Pb = [sb("a_P%d" % i, [128, 4, 128], BF16) for i in range(2)]
            Pc = [sb("a_Pc%d" % i, [128, 4, 128]) for i in range(2)]
            scr = sb("a_scr", [128, 64])
            sc2 = sb("a_sc2", [128, 64])
            v8 = sb("a_v8", [128, 8])
            mskb = sb("a_mskb", [128, 64], BF16)
            mT = sb("a_mT", [64, 128], BF16)
            rden = sb("a_rden", [128, 4])
            coef = sb("a_coef", [128, 4])
            ya = sb("a_ya", [128, 4, 64])
            ytmp = sb("a_ytmp", [128, 4, 64])
            yss = sb("a_yss", [128, 1])

        def nsa_tile(tt):
            t0 = tt * TT
            tsl = slice(t0, t0 + TT)
            c0 = OFF["a_k_cmp"]
            for nm_, col, dst in (("k", OFF["a_k_cmp"], kcmpT), ("v", OFF["a_v_cmp"], vcmpT)):
                pj = PJ[pjn[0] % len(PJ)]
                kk = ("pj", pjn[0] % len(PJ))
                pjn[0] += 1
                for kc in range(DC):
                    K.mm(pj[0:64, 0:TT], wins[:, kc, col:col + 64], hb[:, kc, :], start=(kc == 0), stop=(kc == DC - 1),
                         R=[("win", kc), ("hb", kc)], W=[kk])
                g_ = G_KVC
                bcol = bfm[0:64, G_KVC:G_KVC + 1] if nm_ == "k" else bfmv[:, 0:1]
                K.act(dst[:, tsl], pj[0:64, 0:TT], AF.Identity, R=[kk, "bfm", "a_bfmv"], W=["a_" + nm_ + "cmpT"], bias=bcol,
                      scale=1.0)

            def normed(g, dst, dkey, gcol, split=False):
                proj_fm(g, a_raw[:], ["a_raw"])
                K.tt(a_sq[:], a_raw[:], a_raw[:], ALU.mult, R=["a_raw"], W=["a_sq"])
                K.mm(PST[0:64, 0:TT], onesf[0:64, 0:64], a_sq[:], R=["a_sq", "ones_f"], W=[PSTK])
                K.act(a_rs[:], PST[0:64, 0:TT], AF.Sqrt, R=[PSTK, "epsc"], W=["a_rs"], bias=consts["eps"][0:64, :],
                      scale=1.0 / 64)
                K.recip(a_rs[:], a_rs[:], R=["a_rs"], W=["a_rs"])
                for d_, dk_ in zip(dst, dkey):
                    if split:
                        K.stt(d_, a_raw[:].rearrange("p (a b) -> p a b", b=128), gcol,
                              a_rs[:].rearrange("p (a b) -> p a b", b=128), ALU.mult, ALU.mult,
                              R=["a_raw", "a_rs", "a_qng"], W=[dk_])
                    else:
                        K.stt(d_, a_raw[:], gcol, a_rs[:], ALU.mult, ALU.mult, R=["a_raw", "a_rs", "a_qng"], W=[dk_])

            normed(G_KSLC, [KsT[0:64, tsl]], ["a_KsT"], qng[:, 1:2])
            normed(G_KWIN, [KwT[0:64, tsl]], ["a_KwT"], qng[:, 1:2])
            for h in range(4):
                normed(G_AQ + h, [QaT[0:64, :, h, :], QaF[0:64, :, h, :]], ["a_QaT", "a_QaF"], qng[:, 0:1], split=True)
            for q in range(TT // 128):
                qg = t0 // 128 + q
                for kc in range(DC):
                    K.mm(PTK[0][:, 0:204], hb[:, kc, q * 128:(q + 1) * 128], wins[:, kc, OFF["a_v_slc"]:OFF["a_v_slc"] + 204],
                         start=(kc == 0), stop=(kc == DC - 1), R=[("win", kc), ("hb", kc)], W=[("ptk", 0)])
                K.tt(vtm[:], PTK[0][:, 0:204], btm_a[:], ALU.add, R=[("ptk", 0), "a_btm"], W=["a_vtm"])
                K.copy(Vs[:, qg, 0:64], vtm[:, 0:64], R=["a_vtm"], W=["a_Vs"])
                K.copy(Vw[:, qg, 0:64], vtm[:, 128:192], R=["a_vtm"], W=["a_Vw"])
                K.act(gsb[:, q, :], vtm[:, 192:204], AF.Sigmoid, R=["a_vtm"], W=["a_gsb"])
            nb0 = 0 if t0 == 0 else t0 // 16 - 1
            nb1 = (t0 + TT - 32) // 16 + 1
            nn = nb1 - nb0
            for kv, (w1, src, w2) in enumerate(((w1k, kcmpT, w2k), (w1v, vcmpT, w2v))):
                for hc in range(2):
                    for s_ in range(32):
                        K.mm(PW[0][:, 0:nn], w1[:, s_, hc * 128:(hc + 1) * 128],
                             src[:, 16 * nb0 + s_:16 * nb0 + s_ + 16 * (nn - 1) + 1:16], start=(s_ == 0), stop=(s_ == 31),
                             R=["a_w1k", "a_w1v", "a_kcmpT", "a_vcmpT"], W=[("pw", 0)])
                    K.act(hact[:, kv, hc, 0:nn], PW[0][:, 0:nn], AF.Silu, R=[("pw", 0), "a_hbias"], W=["a_hact"],
                          bias=hbias[:, kv * 2 + hc:kv * 2 + hc + 1], scale=1.0)
                for hc in range(2):
                    K.mm(PW[0][0:64, 64:64 + nn], w2[:, hc, :], hact[:, kv, hc, 0:nn], start=(hc == 0), stop=(hc == 1),
                         R=["a_w2k", "a_w2v", "a_hact"], W=[("pw", 0)])
                if kv == 0:
                    K.copy(cst[:, 0:nn], PW[0][0:64, 64:64 + nn], R=[("pw", 0)], W=["a_cst"])
                    K.tt(csq2[:, 0:nn], cst[:, 0:nn], cst[:, 0:nn], ALU.mult, R=["a_cst"], W=["a_csq2"])
                    K.mm(PW[0][0:64, 128:128 + nn], onesf[0:64, 0:64], csq2[:, 0:nn], R=["a_csq2", "ones_f"], W=[("pw", 0)])
                    K.act(crs2[:, 0:nn], PW[0][0:64, 128:128 + nn], AF.Sqrt, R=[("pw", 0), "epsc"], W=["a_crs2"],
                          bias=consts["eps"][0:64, :], scale=1.0 / 64)
                    K.recip(crs2[:, 0:nn], crs2[:, 0:nn], R=["a_crs2"], W=["a_crs2"])
                    K.stt(kcT[0:64, nb0:nb0 + nn], cst[:, 0:nn], qng[:, 1:2], crs2[:, 0:nn], ALU.mult, ALU.mult,
                          R=["a_cst", "a_crs2", "a_qng"], W=["a_kcT"])
                else:
                    K.copy(vcT[:, nb0:nb0 + nn], PW[0][0:64, 64:64 + nn], R=[("pw", 0)], W=["a_vcT"])
            for m in range(NM):
                K.tr(PW[0][:, 256 + m * 64:256 + (m + 1) * 64], vcT[:, m * 128:(m + 1) * 128], IDENT[0:64, 0:64],
                     R=["a_vcT", "cmask"], W=[("pw", 0)])
                K.copy(vcx[:, m, 0:64], PW[0][:, 256 + m * 64:256 + (m + 1) * 64], R=[("pw", 0)], W=["a_vcx"])
            for q in range(TT // 128):
                nsa_qtile(t0 // 128 + q, q)

        def nsa_qtile(i, q):
            qs = slice(q * 128, (q + 1) * 128)
            qT = QaT[:, q]
            qF = QaF[:, q]
            ek = [0]

            def scores(lhsT, rhs, bias_ap, dst, dkey, rkeys):
                k2 = ek[0] % 2
                ek[0] += 1
                ps = PS2[k2]
                K.mm(ps[:, 0:512], lhsT, rhs, R=rkeys, W=[("ps", k2)])
                if bias_ap is not None:
                    K.tt(Tb[k2][:], v4(ps[:, 0:512]), bias_ap, ALU.add, R=[("ps", k2), "a_bw4"] + [("a_bias", x) for x in range(19)],
                         W=[("a_T", k2)])
                    K.act(dst, Tb[k2][:], AF.Exp, R=[("a_T", k2)], W=[dkey])
                else:
                    K.act(dst, v4(ps[:, 0:512]), AF.Exp, R=[("ps", k2)], W=[dkey])

            first = True
            mlist = [m for m in range(NM) if i - 16 * m >= 0]
            for mi, m in enumerate(mlist):
                ip = i - 16 * m
                k2 = mi % 2
                b_ap = v4(biasT[:, ip, :]) if ip <= 16 else None
                scores(kcT[:, m * 128:(m + 1) * 128], qF.rearrange("p a b -> p (a b)"), b_ap, Pc[k2][:], ("a_Pc", k2),
                       ["a_kcT", "a_QaF"])
                for h in range(4):
                    K.mm(POA[:, h * 65:(h + 1) * 65], Pc[k2][:, h, :], vcx[:, m, :], start=(first and h == 0),
                         stop=(mi == len(mlist) - 1 and h == 3), R=[("a_Pc", k2), "a_vcx"], W=["poa"], skip_group_check=True)
                for h in range(4):
                    K.mm(POB[:, h * 64:(h + 1) * 64], Pc[k2][:, h, :], ovl[:, m, :], start=(first and h == 0),
                         stop=(mi == len(mlist) - 1 and h == 3), R=[("a_Pc", k2), "a_ovl"], W=[PSTK], skip_group_check=True)
                first = False
            oa = POA[:, 0:260].rearrange("p (a b) -> p a b", a=4)
            K.ts(rden[:], oa[:, :, 64], 1e-30, None, ALU.max, R=["poa"], W=["a_rden"])
            K.recip(rden[:], rden[:], R=["a_rden"], W=["a_rden"])
            K.tt(coef[:], rden[:], gsb[:, q, 0:12:3], ALU.mult, R=["a_rden", "a_gsb"], W=["a_coef"])
            K.tt(ya[:], oa[:, :, 0:64], bc(coef[:, :].unsqueeze(2), [128, 4, 64]), ALU.mult, R=["poa", "a_coef"], W=["a_ya"])
            for h in range(4):
                if h == 0:
                    K.ts(scr[:, 0:NB], POB[:, 0:NB], rden[:, 0:1], None, ALU.mult, R=[PSTK, "a_rden"], W=["a_scr"])
                else:
                    K.stt(scr[:, 0:NB], POB[:, h * 64:h * 64 + NB], rden[:, h:h + 1], scr[:, 0:NB], ALU.mult, ALU.add,
                          R=[PSTK, "a_rden", "a_scr"], W=["a_scr"])
            K.tt(scr[:, 0:NB], scr[:, 0:NB], keepc[:, 0, NB - 2 * i:2 * NB - 2 * i], ALU.mult, R=["a_scr", "a_keepc"], W=["a_scr"])
            K.tt(scr[:, 0:NB], scr[:, 0:NB], keepc[:, 1, NB - 2 * i:2 * NB - 2 * i], ALU.add, R=["a_scr", "a_keepc"], W=["a_scr"])
            K.memset(scr[:, 0:1], 1e6, W=["a_scr"])
            K.P.add("dve", lambda e: e.max(v8[:], scr[:, 0:NB]), R=["a_scr"], W=["a_v8"])
            K.P.add("dve", lambda e: e.match_replace(sc2[:, 0:NB], v8[:], scr[:, 0:NB], -3e6), R=["a_scr", "a_v8"], W=["a_sc2"])
            K.P.add("dve", lambda e: e.max(v8[:], sc2[:, 0:NB]), R=["a_sc2"], W=["a_v8"])
            K.ts(mskb[:, 0:NB], scr[:, 0:NB], v8[:, 7:8], None, ALU.is_ge, R=["a_scr", "a_v8"], W=["a_mskb"])
            K.mm(PMX[0:NB, 128:256], mskb[:, 0:NB], identb[:], R=["a_mskb", "a_identb"], W=["pmx"])
            K.copy(mT[0:NB, :], PMX[0:NB, 128:256], R=["pmx"], W=["a_mT"])
            for j in range(i + 1):
                k2 = j % 2
                if j == i:
                    b_ap = v4(biasT[:, 17, :])
                elif j == i - 1:
                    b_ap = v4(biasT[:, 18, :])
                else:
                    b_ap = None
                scores(KsT[:, j * 128:(j + 1) * 128], qT.rearrange("p a b -> p (a b)"), b_ap, Eb[k2][:], ("a_E", k2),
                       ["a_KsT", "a_QaT"])
                K.mm(PMX[:, 0:128], expc[0:NB, j * 128:(j + 1) * 128], mT[0:NB, :], R=["a_expc", "a_mT"], W=["pmx"])
                K.tt(Pb[k2][:], Eb[k2][:], bc(PMX[:, 0:128].unsqueeze(1), [128, 4, 128]), ALU.mult, R=[("a_E", k2), "pmx"],
                     W=[("a_P", k2)])
                for h in range(4):
                    K.mm(POA[:, h * 65:(h + 1) * 65], Pb[k2][:, h, :], Vs[:, j, :], start=(j == 0 and h == 0),
                         stop=(j == i and h == 3), R=[("a_P", k2), "a_Vs"], W=["poa"], skip_group_check=True)
            K.ts(rden[:], oa[:, :, 64], 1e-30, None, ALU.max, R=["poa"], W=["a_rden"])
            K.recip(rden[:], rden[:], R=["a_rden"], W=["a_rden"])
            K.tt(coef[:], rden[:], gsb[:, q, 1:12:3], ALU.mult, R=["a_rden", "a_gsb"], W=["a_coef"])
            K.tt(ytmp[:], oa[:, :, 0:64], bc(coef[:, :].unsqueeze(2), [128, 4, 64]), ALU.mult, R=["poa", "a_coef"], W=["a_ytmp"])
            K.tt(ya[:], ya[:], ytmp[:], ALU.add, R=["a_ya", "a_ytmp"], W=["a_ya"])
            jl = [j for j in range(i - 4, i + 1) if j >= 0]
            for ji, j in enumerate(jl):
                k2 = j % 2
                if j == i:
                    b_ap = v4(biasT[:, 17, :])
                elif j == i - 1:
                    b_ap = v4(biasT[:, 18, :])
                elif j == i - 4:
                    b_ap = bc(bw4[:, :].unsqueeze(1), [128, 4, 128])
                else:
                    b_ap = None
                scores(KwT[:, j * 128:(j + 1) * 128], qT.rearrange("p a b -> p (a b)"), b_ap, Pb[k2][:], ("a_P", k2),
                       ["a_KwT", "a_QaT"])
                for h in range(4):
                    K.mm(POA[:, h * 65:(h + 1) * 65], Pb[k2][:, h, :], Vw[:, j, :], start=(ji == 0 and h == 0),
                         stop=(j == i and h == 3), R=[("a_P", k2), "a_Vw"], W=["poa"], skip_group_check=True)
            K.ts(rden[:], oa[:, :, 64], 1e-30, None, ALU.max, R=["poa"], W=["a_rden"])
            K.recip(rden[:], rden[:], R=["a_rden"], W=["a_rden"])
            K.tt(coef[:], rden[:], gsb[:, q, 2:12:3], ALU.mult, R=["a_rden", "a_gsb"], W=["a_coef"])
            K.tt(ytmp[:], oa[:, :, 0:64], bc(coef[:, :].unsqueeze(2), [128, 4, 64]), ALU.mult, R=["poa", "a_coef"], W=["a_ytmp"])
            K.tt(ya[:], ya[:], ytmp[:], ALU.add, R=["a_ya", "a_ytmp"], W=["a_ya"])
            yaf = ya[:].rearrange("p a b -> p (a b)")
            K.tt(ytmp[:], ya[:], ya[:], ALU.mult, R=["a_ya"], W=["a_ytmp"])
            K.P.add("dve", lambda e: e.tensor_reduce(yss[:], ytmp[:].rearrange("p a b -> p (a b)"), AX.X, ALU.add),
                    R=["a_ytmp"], W=["a_yss"])
            K.act(yss[:], yss[:], AF.Sqrt, R=["a_yss", "epsc"], W=["a_yss"], bias=consts["eps"][:], scale=1.0 / 256)
            K.recip(yss[:], yss[:], R=["a_yss"], W=["a_yss"])
            K.stt(yaf, yaf, yss[:, 0:1], mg0[:], ALU.mult, ALU.mult, R=["a_ya", "a_yss", "a_mg0"], W=["a_ya"])
            for pr in range(2):
                K.tr(PMX[:, 256 + pr * 128:256 + (pr + 1) * 128], yaf[:, pr * 128:(pr + 1) * 128], IDENT, R=["a_ya", "cmask"],
                     W=["pmx"])
                K.copy(yt[:, pr, qs], PMX[:, 256 + pr * 128:256 + (pr + 1) * 128], R=["pmx"], W=[("yt", pr)], eng="act")


        for tt in range(NT):
            tsl = slice(tt * TT, (tt + 1) * TT)
            K.dma(xt[:], xsv[:, :, tsl], R=[("xd", sname, tt)], W=[("xt", dc) for dc in range(DC)])
            K.act(sq[:], xt[:], AF.Square, R=[("xt", dc) for dc in range(DC)], W=["sq"])
            for dc in range(DC):
                K.mm(PST[:, 0:TT], consts["ones_bf"][:], sq[:, dc, :], start=(dc == 0), stop=(dc == DC - 1),
                     R=["sq"], W=[PSTK])
            K.act(rs[:], PST[:, 0:TT], AF.Sqrt, R=[PSTK, "epsc"], W=["rs"], bias=consts["eps"][:], scale=1.0 / D)
            K.recip(rs[:], rs[:], R=["rs"], W=["rs"])
            for dc in range(DC):
                k2 = dc % 2
                K.stt(sa[k2][:], xt[:, dc, :], gs[:, s, dc:dc + 1], rs[:], ALU.mult, ALU.mult,
                      R=[("xt", dc), "rs", "gs"], W=[("sa", k2)])
                K.act(hb[:, dc, :], sa[k2][:], AF.Identity, R=[("sa", k2), "mod"], W=[("hb", dc)],
                      bias=shift[:, s * 24 + dc:s * 24 + dc + 1], scale=1.0)
            if mode == 1:
                for r in range(2, DC):
                    if not (("B" in enable and r in (2, 3)) or ("C" in enable and r in (4, 5)) or ("D" in enable and r in (6, 7))):
                        K.memset(yt[:, r, :], 0.0, W=[("yt", r)], eng="pool")
            else:
                K.dma(yt[:, 2:DC, :], yscr.rearrange("(dc p) t -> p dc t", p=128)[:, 2:DC, tsl],
                      R=[("yscr", tt * TT // 256 + q) for q in range(TT // 256)], W=[("yt", r) for r in range(2, DC)])
                if "A" not in enable:
                    for r in range(2):
                        K.memset(yt[:, r, :], 0.0, W=[("yt", r)], eng="pool")
            if "B" in enable:
                for ch in range(2):
                    proj_fm(G_BB + ch, bbt[:], ["bbt"])
                    proj_fm(G_BC + ch, cct[:], ["cct"])
                    proj_fm(G_BX + ch, cvt[:], ["cvt"])
                    K.tt(ub[ch][:, 2:2 + TT], cct[:], cvt[:], ALU.mult, R=["cct", "cvt"], W=[("ub", ch)])
                    K.ts(cvt[:], ub[ch][:, 2:2 + TT], scw[:, ch, 2:3], None, ALU.mult, R=[("ub", ch), "scw"], W=["cvt"])
                    K.stt(cvt[:], ub[ch][:, 1:1 + TT], scw[:, ch, 1:2], cvt[:], ALU.mult, ALU.add,
                          R=[("ub", ch), "scw", "cvt"], W=["cvt"])
                    K.stt(cvt[:], ub[ch][:, 0:TT], scw[:, ch, 0:1], cvt[:], ALU.mult, ALU.add,
                          R=[("ub", ch), "scw", "cvt"], W=["cvt"])
                    K.tt(ybt[ch][:], bbt[:], cvt[:], ALU.mult, R=["bbt", "cvt"], W=[("ybt", ch)])
                    K.copy(ub[ch][:, 0:2], ub[ch][:, TT:TT + 2], R=[("ub", ch)], W=[("ub", ch)])
                    K.act(sq[:, ch, :], ybt[ch][:], AF.Square, R=[("ybt", ch)], W=["sq"])
                for ch in range(2):
                    K.mm(PST[:, 0:TT], consts["ones_bf"][:], sq[:, ch, :], start=(ch == 0), stop=(ch == 1),
                         R=["sq"], W=[PSTK])
                K.act(sa[0][:], PST[:, 0:TT], AF.Sqrt, R=[PSTK, "epsc"], W=[("sa", 0)], bias=consts["eps"][:],
                      scale=1.0 / 256)
                K.recip(sa[0][:], sa[0][:], R=[("sa", 0)], W=[("sa", 0)])
                for ch in range(2):
                    K.stt(yt[:, 2 + ch, :], ybt[ch][:], mixg[:, 1, ch:ch + 1], sa[0][:], ALU.mult, ALU.mult,
                          R=[("ybt", ch), "mixg", ("sa", 0)], W=[("yt", 2 + ch)])
            if "D" in enable:
                for h in range(4):
                    proj_fm(G_DQ + h, QmT[:, h, :], ["QmT"])
                    proj_fm(G_DK + h, KmT[:, h, :], ["KmT"], scale=0.125, bias=bfm8[0:64, h:h + 1])
            if "C" in enable:
                gdn_tile()
            for c in range(NCH):
                if "D" in enable:
                    mlstm_chunk(c)
                if "C" in enable:
                    gdn_chunk(c)
            if mode == 1:
                K.dma(yscr.rearrange("(dc p) t -> p dc t", p=128)[:, 2:DC, tsl], yt[:, 2:DC, :],
                      R=[("yt", r) for r in range(2, DC)], W=[("yscr", tt * TT // 256 + q) for q in range(TT // 256)])
                continue
            if "A" in enable:
                nsa_tile(tt)
                K.dma(yscr.rearrange("(dc p) t -> p dc t", p=128)[:, 0:2, tsl], yt[:, 0:2, :],
                      R=[("yt", r) for r in range(2)], W=[("yscrA", tt)])
            for dc in range(DC):
                pj = PJ[pjn[0] % len(PJ)]
                kk = ("pj", pjn[0] % len(PJ))
                pjn[0] += 1
                for fc in range(DC):
                    K.mm(pj[:, 0:TT], wouts[:, fc, dc * 128:(dc + 1) * 128], yt[:, fc, :], start=(fc == 0),
                         stop=(fc == DC - 1), R=[("wout", fc), ("yt", fc)], W=[kk])
                K.stt(xt[:, dc, :], pj[:, 0:TT], gate[:, s, dc:dc + 1], xt[:, dc, :], ALU.mult, ALU.add,
                      R=[kk, "gate", ("xt", dc)], W=[("xt", dc)])
            K.dma(xdv[:, :, tsl], xt[:], R=[("xt", dc) for dc in range(DC)], W=[("xd", dname, tt)])
    P.barrier()


def t5_thresholds():
    def bucket(n):
        if n < 16:
            return n
        nf = np.float32(n)
        v = np.log(nf / np.float32(16)) / np.float32(math.log(128 / 16)) * np.float32(16)
        return min(16 + int(np.float32(v)), 31)
    bs = [bucket(n) for n in range(0, 400)]
    return [min(n for n in range(400) if bs[n] >= b) for b in range(32)]


def bias_build(K, t5_table, dist_d, bias_scr, consts, es_ext=None):
    nc = K.nc
    lo = t5_thresholds()
    with ExitStack() as es_own:
        es = es_ext if es_ext is not None else es_own
        tb = _sb(es, nc, "bb_tb", [128, 32, 4], F32)
        ndl = _sb(es, nc, "bb_ndl", [128, 31, 4], F32)
        dtl = [_sb(es, nc, "bb_dt%d" % i, [128, 128], F32) for i in range(2)]
        acc = [_sb(es, nc, "bb_acc%d" % i, [128, 4, 128], F32) for i in range(2)]
        tmp = [_sb(es, nc, "bb_tmp%d" % i, [128, 4, 128], F32) for i in range(2)]
        K.dma(tb[:].rearrange("p b h -> p (b h)"), t5_table.rearrange("b h -> (b h)").partition_broadcast(128), W=["bb_tb"])
        K.tt(ndl[:], tb[:, 0:31, :], tb[:, 1:32, :], ALU.subtract, R=["bb_tb"], W=["bb_ndl"])
        for q in range(19):
            k2 = q % 2
            K.dma(dtl[k2][:], dist_d[q], W=[("bb_dt", k2)])
            dbc = bc(dtl[k2][:, :].unsqueeze(1), [128, 4, 128])
            for b in range(1, 32):
                dst = acc[k2] if b == 1 else tmp[b % 2]
                dk = ("bb_acc", k2) if b == 1 else ("bb_tmp", b % 2)
                K.stt(dst[:], dbc, float(lo[b]), bc(ndl[:, b - 1, :].unsqueeze(2), [128, 4, 128]), ALU.is_lt, ALU.mult,
                      R=[("bb_dt", k2), "bb_ndl"], W=[dk])
                if b > 1:
                    K.tt(acc[k2][:], acc[k2][:], dst[:], ALU.add, R=[("bb_acc", k2), dk], W=[("bb_acc", k2)])
            K.ts(tmp[0][:], dbc, 0.0, -30000.0, ALU.is_lt, ALU.mult, R=[("bb_dt", k2)], W=[("bb_tmp", 0)])
            K.tt(acc[k2][:], acc[k2][:], tmp[0][:], ALU.add, R=[("bb_acc", k2), ("bb_tmp", 0)], W=[("bb_acc", k2)])
            K.dma(bias_scr[q], acc[k2][:].rearrange("p a b -> p (a b)"), R=[("bb_acc", k2)], W=[("bias_scr", q)])
    if es_ext is None:
        K.P.barrier()


def build(T=4096, TT=512, layers=2, debug_y=False, enable="ABCD"):
    nc = bass.Bass("TRN2", target_bir_lowering=False)
    K = KB(nc)
    P = K.P
    dt = lambda name, shape, kind="ExternalInput", d=F32: nc.dram_tensor(name, list(shape), d, kind=kind).ap()
    xT = dt("xT", [D, T])
    cT = dt("cT", [128, DC])
    ada_w = dt("ada_w", [2, D, 9 * D])
    ada_bT = dt("ada_bT", [2, 128, 72])
    normgT = dt("normgT", [2, 128, 3, DC])
    ffn_w13 = [dt("ffn1_w13", [2, D, 2 * DFF]), dt("ffn2_w13", [2, D, 2 * DFF])]
    ffn_w2 = [dt("ffn1_w2", [2, DFF, D]), dt("ffn2_w2", [2, DFF, D])]
    outT = dt("outT", [D, T], kind="ExternalOutput")
    Win = {
        "w_in": dt("w_in", [2, D, D_IN]), "b_in": dt("b_in", [2, D_IN]), "b_fm": dt("b_fm", [2, 128, NFM]),
        "w_out": dt("w_out", [2, D, D]), "sc_conv_wT": dt("sc_conv_wT", [2, 128, 2, 3]),
        "mixgT": dt("mixgT", [2, 128, 2, 2]), "mlstm_f_bias": dt("mlstm_f_bias", [2, 4]),
        "mlstm_norm_g": dt("mlstm_norm_g", [2, 64]),
        "gdn_conv_wT": dt("gdn_conv_wT", [2, 64, 12, 4]), "gdn_A_log": dt("gdn_A_log", [2, 4]),
        "gdn_dt_bias": dt("gdn_dt_bias", [2, 4]), "gdn_norm_g": dt("gdn_norm_g", [2, 64]),
        "cmp_k_w1": dt("cmp_k_w1", [2, 2048, 256]), "cmp_v_w1": dt("cmp_v_w1", [2, 2048, 256]),
        "cmp_k_w2": dt("cmp_k_w2", [2, 256, 64]), "cmp_v_w2": dt("cmp_v_w2", [2, 256, 64]),
        "cmp_posT": dt("cmp_posT", [2, 64, 32]), "qkng": dt("qkng", [2, 64, 2]), "mix_norm_g0": dt("mix_norm_g0", [2, 256]),
    }
    NB_ = T // 64
    NM_ = (T // 16 + 127) // 128
    Wsh = {
        "t5_table": dt("t5_table", [32, 4]), "expc": dt("expc", [NB_, T]), "keepadd": dt("keepadd", [128, 2, 2 * NB_]),
        "ovl": dt("ovl", [128, NM_, 64]), "bw4": dt("bw4", [128, 128]),
        "bias_scr": dt("bias_scr", [19, 128, 512], kind="Internal"),
    }
    dist_d = dt("dist_tiles", [19, 128, 128])
    cmask_d = dt("cmask", [128, CM_N])
    yscr = dt("ydbg", [D, T], kind="ExternalOutput" if debug_y else "Internal", d=BF16)
    xa = dt("xa_scr", [D, T], kind="Internal")
    xb = dt("xb_scr", [D, T], kind="Internal")

    with ExitStack() as es:
        consts = {
            "ones_bf": _sb(es, nc, "ones_bf", [128, 128], BF16),
            "eps": _sb(es, nc, "epsc", [128, 1], F32),
        }
        K.memset(consts["ones_bf"][:], 1.0, W=["ones_bf"])
        consts["ones_f"] = _sb(es, nc, "ones_f", [128, 128], F32)
        consts["one"] = _sb(es, nc, "onec", [128, 1], F32)
        consts["cmask"] = _sb(es, nc, "cmask_sb", [128, CM_N], F32)
        K.memset(consts["ones_f"][:], 1.0, W=["ones_f"])
        K.memset(consts["one"][:], 1.0, W=["onec"])
        K.dma(consts["cmask"][:], cmask_d, W=["cmask"])
        K.memset(consts["eps"][:], EPS, W=["epsc"])
        condT = _sb(es, nc, "condT", [128, DC, 2], F32)
        ctmp = _sb(es, nc, "ctmp", [128, DC], F32)
        K.memset(condT[:], 0.0, W=["condT"])
        K.dma(ctmp[:], cT, W=["ctmp"])
        K.act(condT[:, :, 0], ctmp[:], AF.Silu, R=["ctmp"], W=["condT"])
        mv = []
        for l in range(2):
            mv.append({
                "mod": _sb(es, nc, "mod%d" % l, [128, 72], F32),
                "gs": _sb(es, nc, "gs%d" % l, [128, 3, DC], F32),
                "gate": _sb(es, nc, "gate%d" % l, [128, 3, DC], F32),
            })
        adab = _sb(es, nc, "adab", [128, 2, 72], F32)
        normg = _sb(es, nc, "normg", [128, 2, 3, DC], F32)
        K.dma(adab[:], ada_bT.rearrange("l p j -> p l j"), W=["adab"])
        K.dma(normg[:], normgT.rearrange("l p s d -> p l s d"), W=["normg"])
        P.barrier()

        with ExitStack() as es0:
            for l in range(layers):
                mod_phase(K, es0, l, ada_w[l], adab[:, l, :], normg[:, l], condT, mv[l])
            if "A" in enable:
                bias_build(K, Wsh["t5_table"], dist_d, Wsh["bias_scr"], consts, es_ext=es0)
        P.barrier()
        cur = xT
        curname = "xT"
        for l in range(layers):
            last = (l == layers - 1)
            ffn_phase(K, l, 0, cur, curname, xa, "xa", ffn_w13[0][l], ffn_w2[0][l], mv[l], 0, T, TT, consts)
            Wd = {k: v[l] for k, v in Win.items()}
            Wd.update(Wsh)
            mixer_phase(K, l, xa, "xa", xb, "xb", Wd, mv[l], T, 256, consts, yscr, 1, enable=enable)
            mixer_phase(K, l, xa, "xa", xb, "xb", Wd, mv[l], T, 256, consts, yscr, 2, enable=enable)
            ffn_phase(K, l, 1, xb, "xb", outT if last else xa, "outT" if last else "xa", ffn_w13[1][l], ffn_w2[1][l], mv[l], 2, T, TT, consts)
            cur = xa
            curname = "xa"
        P.fence("sp", [("xd", "outT", tt) for tt in range(T // TT)])
        P.emit()
    return nc


def _cmask():
    m = np.zeros((128, CM_N), np.float32)
    m[:, CM_ID:CM_ID + 128] = np.eye(128, dtype=np.float32)
    k = np.arange(64)[:, None]
    i = np.arange(64)[None, :]
    m[0:64, CM_TRIU:CM_TRIU + 64] = (k <= i)
    m[0:64, CM_SU:CM_SU + 64] = (k < i)
    m[0:64, CM_SL:CM_SL + 64] = (k > i)
    m[0:64, CM_LI:CM_LI + 64] = (k >= i)
    return m


def prep_shared(inp, T=4096):
    f = lambda a: np.ascontiguousarray(np.asarray(a, dtype=np.float32))
    b_in = f(inp["b_in"])
    b_fm = np.zeros((2, 128, NFM), np.float32)
    for g, (c0, n) in enumerate(FM):
        b_fm[:, 0:n, g] = b_in[:, c0:c0 + n]
    sh = {
        "ada_w": f(inp["ada_w"]),
        "ada_bT": f(np.asarray(inp["ada_b"]).reshape(2, 72, 128).transpose(0, 2, 1)),
        "normgT": f(np.asarray(inp["norm_g"]).reshape(2, 3, 8, 128).transpose(0, 3, 1, 2)),
        "ffn1_w13": f(inp["ffn1_w13"]), "ffn2_w13": f(inp["ffn2_w13"]),
        "ffn1_w2": f(inp["ffn1_w2"]), "ffn2_w2": f(inp["ffn2_w2"]),
        "w_in": f(inp["w_in"]), "b_in": b_in, "b_fm": b_fm, "w_out": f(inp["w_out"]),
        "sc_conv_wT": f(np.asarray(inp["sc_conv_w"]).reshape(2, 3, 2, 128).transpose(0, 3, 2, 1)),
        "mixgT": f(np.asarray(inp["mix_norm_g"]).reshape(2, 2, 2, 128).transpose(0, 3, 1, 2)),
        "mlstm_f_bias": f(inp["mlstm_f_bias"]), "mlstm_norm_g": f(inp["mlstm_norm_g"]),
        "gdn_conv_wT": f(np.asarray(inp["gdn_conv_w"]).reshape(2, 4, 12, 64).transpose(0, 3, 2, 1)),
        "gdn_A_log": f(inp["gdn_A_log"]), "gdn_dt_bias": f(inp["gdn_dt_bias"]), "gdn_norm_g": f(inp["gdn_norm_g"]),
        "cmask": _cmask(),
        "cmp_k_w1": f(inp["cmp_k_w1"]), "cmp_v_w1": f(inp["cmp_v_w1"]), "cmp_k_w2": f(inp["cmp_k_w2"]),
        "cmp_v_w2": f(inp["cmp_v_w2"]), "cmp_posT": f(np.asarray(inp["cmp_pos"]).transpose(0, 2, 1)),
        "qkng": f(np.stack([np.asarray(inp["q_norm_g"]), np.asarray(inp["k_norm_g"])], axis=-1)),
        "mix_norm_g0": f(np.asarray(inp["mix_norm_g"])[:, 0]), "t5_table": f(inp["t5_table"]),
    }
    sh.update(_nsa_consts(T))
    return sh


def _nsa_consts(T):
    NB = T // 64
    NM = (T // 16 + 127) // 128
    c = np.arange(128)[:, None]
    r = np.arange(128)[None, :]
    dist = np.zeros((19, 128, 128), np.float32)
    for ip in range(17):
        dist[ip] = r - 16 * c + 128 * ip - 31
    dist[17] = r - c
    dist[18] = 128 + r - c
    expc = (np.arange(T)[None, :] // 64 == np.arange(NB)[:, None]).astype(np.float32)
    keep = np.ones((128, 2 * NB), np.float32)
    add = np.zeros((128, 2 * NB), np.float32)
    for rr in range(128):
        for x in range(2 * NB):
            rb = x - NB
            if rr < 64:
                forced, invalid = rb in (-1, 0), rb > 0
            else:
                forced, invalid = rb in (0, 1), rb > 1
            if invalid:
                keep[rr, x], add[rr, x] = 0.0, -1e6
            elif forced:
                keep[rr, x], add[rr, x] = 0.0, 1e6
    ovl = np.zeros((128, NM, 64), np.float32)
    for m in range(NM):
        ci = 128 * m + np.arange(128)[:, None]
        bj = np.arange(64)[None, :]
        ovl[:, m, :] = ((ci * 16 < (bj + 1) * 64) & (ci * 16 + 32 > bj * 64) & (bj < NB) & (ci < T // 16 - 1))
    bw4 = np.where(c > r, 0.0, -30000.0).astype(np.float32)
    return {"dist_tiles": dist, "expc": expc, "keepadd": np.ascontiguousarray(np.stack([keep, add], axis=1)),
            "ovl": ovl, "bw4": bw4}


def prep_core(inp, b):
    x = np.asarray(inp["x"], dtype=np.float32)
    c = np.asarray(inp["c"], dtype=np.float32)
    return {"xT": np.ascontiguousarray(x[b].T), "cT": np.ascontiguousarray(c[b].reshape(8, 128).T)}


_NC_CACHE = {}


def kernel(**inputs):
    x = np.asarray(inputs["x"])
    B, T, _ = x.shape
    if T not in _NC_CACHE:
        _NC_CACHE[T] = build(T=T)
    nc = _NC_CACHE[T]
    sh = prep_shared(inputs, T)
    in_maps = []
    for b in range(B):
        m = dict(sh)
        m.update(prep_core(inputs, b))
        in_maps.append(m)
    res = run_bass_kernel_spmd(nc, in_maps, core_ids=list(range(B)))
    out = np.stack([np.asarray(r["outT"]).T for r in res.results], axis=0)
    return np.ascontiguousarray(out.astype(np.float32))
```

```python
import math
import os
GSTOP = float(os.environ.get('GSTOP', '99'))
from contextlib import ExitStack
import numpy as np
import concourse.bass as bass
import concourse.mybir as mybir
from concourse.bass_utils import run_bass_kernel_spmd

F32 = mybir.dt.float32
BF16 = mybir.dt.bfloat16
AF = mybir.ActivationFunctionType
ALU = mybir.AluOpType
AX = mybir.AxisListType

D = 1024
DC = 8
DFF = 2816
FC = 22
D_IN = 3484
EPS = 1e-6

ENGS = ("pe", "act", "dve", "pool", "sp")


class _Op:
    __slots__ = ("eng", "fn", "reads", "writes", "deps", "sig", "tok", "waits", "dma", "snap", "inc", "cost", "odeps", "idx")

    def __init__(self, eng, fn, reads, writes, dma):
        self.eng = eng
        self.fn = fn
        self.reads = reads
        self.writes = writes
        self.dma = dma
        self.deps = ()
        self.sig = False
        self.tok = None
        self.waits = ()
        self.snap = None
        self.inc = 1
        self.cost = 300.0
        self.odeps = ()
        self.idx = 0


class Prog:
    EPOCH = 20000
    NDMA = 12

    def __init__(self, nc):
        self.nc = nc
        self.ops = []
        self.last_w = {}
        self.readers = {}
        self.last_on = {}
        self.dma_ops = []

    EXCL = ("pj", "pw", "ptk", "pab", "po", "pst", "pmod", "ps", "poa", "pob", "pmx", "pd", "pc")

    def _excl(self, k):
        return (k[0] if isinstance(k, tuple) else k) in self.EXCL

    def add(self, eng, fn, R=(), W=(), dma=False, cost=300.0):
        xr = [k for k in R if self._excl(k)]
        if xr:
            R = [k for k in R if not self._excl(k)]
            W = list(W) + [k for k in xr if k not in W]
        op = _Op(eng, fn, tuple(R), tuple(W), dma)
        i = len(self.ops)
        deps = set()
        for k in op.reads:
            w = self.last_w.get(k)
            if w is not None:
                deps.add(w)
        for k in op.writes:
            w = self.last_w.get(k)
            if w is not None:
                deps.add(w)
            deps.update(self.readers.get(k, ()))
        for k in op.writes:
            self.last_w[k] = i
            self.readers[k] = []
        for k in op.reads:
            if k not in op.writes:
                self.readers.setdefault(k, []).append(i)
        op.deps = deps
        op.cost = cost
        self.ops.append(op)
        self.last_on[eng] = i
        if dma:
            self.dma_ops.append(i)
        return i

    def barrier(self):
        lasts = set(self.last_on.values()) | set(self.dma_ops)
        self.dma_ops = []
        for e in ENGS:
            op = _Op(e, None, (), (), False)
            op.deps = set(lasts)
            self.ops.append(op)
            self.last_on[e] = len(self.ops) - 1
        self.last_w = {}
        self.readers = {}

    def fence(self, eng, keys):
        self.add(eng, None, R=keys)

    def schedule(self):
        import heapq
        ops = self.ops
        n = len(ops)
        for i, op in enumerate(ops):
            op.idx = i
        order = []
        LAT = 250.0
        W = 24
        seg_start = 0
        i = 0
        segs = []
        while i < n:
            if ops[i].fn is None and not ops[i].dma and len(ops[i].reads) == 0 and len(ops[i].writes) == 0:
                j = i
                while j < n and ops[j].fn is None and not ops[j].dma:
                    j += 1
                segs.append((seg_start, i))
                segs.append((i, j))
                seg_start = j
                i = j
            else:
                i += 1
        segs.append((seg_start, n))
        for (a, b) in segs:
            if b <= a:
                continue
            if ops[a].fn is None and not ops[a].dma:
                order.extend(range(a, b))
                continue
            nrem = {}
            users = {}
            for k in range(a, b):
                dl = [d for d in ops[k].deps if d >= a]
                nrem[k] = len(dl)
                for d in dl:
                    users.setdefault(d, []).append(k)
            blev = {}
            for k in range(b - 1, a - 1, -1):
                m_ = 0.0
                for u in users.get(k, ()):
                    if blev[u] > m_:
                        m_ = blev[u]
                blev[k] = ops[k].cost + LAT + m_
            ready = {e: [] for e in ENGS}
            for k in range(a, b):
                if nrem[k] == 0:
                    heapq.heappush(ready[ops[k].eng], k)
            fin = {}
            free = {e: 0.0 for e in ENGS}
            left = b - a
            while left > 0:
                best = None
                for e in ENGS:
                    rl = ready[e]
                    if not rl:
                        continue
                    cands = heapq.nsmallest(W, rl)
                    for k in cands:
                        st = free[e]
                        for d in ops[k].deps:
                            if d >= a:
                                f = fin[d] + LAT
                                if f > st:
                                    st = f
                        key = (int(st / 250.0), -blev[k], k, st)
                        if best is None or key < best[0]:
                            best = (key, e, k)
                (_q, _b, k, st), e, _ = best
                ready[e].remove(k)
                heapq.heapify(ready[e])
                op = ops[k]
                if op.dma:
                    free[e] = st + 60.0
                    fin[k] = st + op.cost
                else:
                    free[e] = st + op.cost
                    fin[k] = st + op.cost
                order.append(k)
                left -= 1
                for u in users.get(k, ()):
                    nrem[u] -= 1
                    if nrem[u] == 0:
                        heapq.heappush(ready[ops[u].eng], u)
        return order

    def emit(self):
        nc = self.nc
        if os.environ.get("NOSCHED", "0") != "1":
            order = self.schedule()
            old = self.ops
            remap = {o: nidx for nidx, o in enumerate(order)}
            newops = [old[o] for o in order]
            for op in newops:
                op.deps = {remap[d] for d in op.deps}
            self.ops = newops
        ops = self.ops
        for i, op in enumerate(ops):
            if op.eng == "pe":
                op.deps = {d for d in op.deps if not (ops[d].eng == "pe" and not ops[d].dma)}
        ndma = 0
        slot_last = {}
        for i, op in enumerate(ops):
            if op.dma:
                slot = ndma % self.NDMA
                if slot in slot_last:
                    op.deps = set(op.deps) | {slot_last[slot]}
                slot_last[slot] = i
                op.tok = ("dma", slot, 16 * (ndma // self.NDMA + 1))
                ndma += 1
        for op in ops:
            for d in op.deps:
                ops[d].sig = True
        cnt = {e: 0 for e in ENGS}
        for op in ops:
            if op.sig and not op.dma:
                if op.fn is None:
                    continue
                cnt[op.eng] += 1
                c = cnt[op.eng]
                op.tok = (op.eng, (c - 1) // self.EPOCH, (c - 1) % self.EPOCH + 1)
        nep = {e: (cnt[e] + self.EPOCH - 1) // self.EPOCH for e in ENGS}
        sems = {}
        for e in ENGS:
            for k in range(max(nep[e], 0)):
                sems[(e, k)] = nc.alloc_semaphore("s_%s_%d" % (e, k))
        for s in range(min(self.NDMA, max(ndma, 1))):
            sems[("dma", s)] = nc.alloc_semaphore("s_dma_%d" % s)
        known = {e: {} for e in ENGS}
        for op in ops:
            kn = known[op.eng]
            waits = []
            stack = list(op.deps)
            seen = set()
            while stack:
                d = stack.pop()
                if d in seen:
                    continue
                seen.add(d)
                dop = ops[d]
                if dop.fn is None and not dop.dma:
                    if dop.eng == op.eng:
                        continue
                    stack.extend(dop.deps)
                    continue
                tk = dop.tok
                key = (tk[0], tk[1])
                if kn.get(key, 0) >= tk[2]:
                    continue
                waits.append((key, tk[2]))
                if dop.snap is not None:
                    for k2, v2 in dop.snap.items():
                        if kn.get(k2, 0) < v2:
                            kn[k2] = v2
                kn[key] = tk[2]
            wm = {}
            for key, v in waits:
                if wm.get(key, 0) < v:
                    wm[key] = v
            op.waits = tuple(wm.items())
            if op.tok is not None:
                op.snap = dict(kn)
        handles = {"pe": nc.tensor, "act": nc.scalar, "dve": nc.vector, "pool": nc.gpsimd, "sp": nc.sync}
        per = {e: [op for op in ops if op.eng == e] for e in ENGS}

        def run(e, eng):
            for op in per[e]:
                for key, v in op.waits:
                    eng.wait_ge(sems[key], v)
                if op.fn is None:
                    continue
                ins = op.fn(eng)
                if op.tok is not None:
                    tk = op.tok
                    ins.then_inc(sems[(tk[0], tk[1])], 16 if op.dma else 1)

        with nc.Block() as block:
            @block.tensor
            def _(eng):
                run("pe", eng)

            @block.scalar
            def _(eng):
                run("act", eng)

            @block.vector
            def _(eng):
                run("dve", eng)

            @block.gpsimd
            def _(eng):
                run("pool", eng)

            @block.sync
            def _(eng):
                run("sp", eng)
        self.stats = dict(n_ops=len(ops), cnt=cnt, ndma=ndma)


class KB:
    def __init__(self, nc):
        self.nc = nc
        self.P = Prog(nc)

    @staticmethod
    def _fs(ap):
        n = 1
        for d in ap.shape[1:]:
            n *= d
        return n

    def mm(self, out, lhsT, rhs, start=True, stop=True, R=(), W=(), **kw):
        passes = 2.0 if rhs.dtype == F32 else 1.0
        c = 40.0 + (self._fs(lhsT) * 0.85 + max(64, self._fs(rhs)) * 0.85) * passes
        return self.P.add("pe", lambda e: e.matmul(out, lhsT, rhs, start=start, stop=stop, **kw), R, W, cost=c)

    def tr(self, out, in_, ident, R=(), W=()):
        return self.P.add("pe", lambda e: e.transpose(out, in_, ident), R, W, cost=120.0)

    def act(self, out, in_, func, R=(), W=(), bias=None, scale=None):
        kw = {}
        if bias is not None:
            kw["bias"] = bias
        if scale is not None:
            kw["scale"] = scale
        return self.P.add("act", lambda e: e.activation(out, in_, func, **kw), R, W, cost=260.0 + 0.75 * self._fs(in_))

    def tt(self, out, in0, in1, op, R=(), W=(), eng="dve"):
        return self.P.add(eng, lambda e: e.tensor_tensor(out, in0, in1, op), R, W, cost=150.0 + 1.45 * self._fs(out))

    def ts(self, out, in0, s1, s2, op0, op1=None, R=(), W=(), eng="dve"):
        if op1 is None:
            return self.P.add(eng, lambda e: e.tensor_scalar(out, in0, s1, None, op0), R, W, cost=150.0 + 1.1 * self._fs(out))
        return self.P.add(eng, lambda e: e.tensor_scalar(out, in0, s1, s2, op0, op1), R, W, cost=150.0 + 1.1 * self._fs(out))

    def stt(self, out, in0, scalar, in1, op0, op1, R=(), W=()):
        return self.P.add("dve", lambda e: e.scalar_tensor_tensor(out, in0, scalar, in1, op0, op1), R, W,
                          cost=150.0 + 1.1 * self._fs(out))

    def copy(self, out, in_, R=(), W=(), eng="dve"):
        if eng == "act":
            return self.P.add("act", lambda e: e.copy(out, in_), R, W, cost=260.0 + 0.75 * self._fs(out))
        return self.P.add(eng, lambda e: e.tensor_copy(out, in_), R, W, cost=150.0 + 0.9 * self._fs(out))

    def recip(self, out, in_, R=(), W=()):
        return self.P.add("dve", lambda e: e.reciprocal(out, in_), R, W, cost=200.0 + 2.6 * self._fs(out))

    def memset(self, ap, val, W=(), eng="dve"):
        return self.P.add(eng, lambda e: e.memset(ap, val), (), W, cost=110.0 + 1.05 * self._fs(ap))

    def dma(self, out, in_, R=(), W=(), eng="sp", **kw):
        nbytes = out.shape[0] * self._fs(out) * (2 if out.dtype == BF16 else 4)
        return self.P.add(eng, lambda e: e.dma_start(out, in_, **kw), R, W, dma=True, cost=2200.0 + nbytes / 120.0)


def _sb(es, nc, name, shape, dt):
    return es.enter_context(nc.sbuf_tensor(name, shape, dt))


def _ps(es, nc, name, shape, dt=F32):
    return es.enter_context(nc.psum_tensor(name, shape, dt))


def mod_phase(K, es_glob, lay, ada_w_l, ada_bT_l, normgT_l, condT, out):
    nc = K.nc
    P = K.P
    with ExitStack() as es_own:
        es = es_glob if es_glob is not None else es_own
        wt = [_sb(es, nc, "adaw%d_%d" % (lay, i), [128, DC, 1024], F32) for i in range(2)]
        pm = _ps(es, nc, "pmod%d" % lay, [128, 72 * 2], F32)
        awv = ada_w_l.rearrange("(kc p) n -> p kc n", p=128)
        for g in range(9):
            b = wt[g % 2]
            for kc in range(DC):
                K.dma(b[:, kc, :], awv[:, kc, g * 1024:(g + 1) * 1024], W=[("adaw", g % 2, kc)])
            for jj in range(8):
                j = g * 8 + jj
                for kc in range(DC):
                    K.mm(pm[:, 2 * j:2 * j + 2], b[:, kc, jj * 128:(jj + 1) * 128], condT[:, kc, :],
                         start=(kc == 0), stop=(kc == DC - 1), R=[("adaw", g % 2, kc), "condT"], W=["pmod"])
        mod = out["mod"]
        pmv = pm[:].rearrange("p (j two) -> p j two", two=2)[:, :, 0]
        K.tt(mod[:], pmv, ada_bT_l, ALU.add, R=["pmod", "adab"], W=["mod"])
        for s in range(3):
            K.stt(out["gs"][:, s, :], mod[:, s * 24 + 8:s * 24 + 16], 1.0, normgT_l[:, s, :], ALU.add, ALU.mult,
                  R=["mod", "normg"], W=["gs"])
            K.ts(out["gate"][:, s, :], mod[:, s * 24 + 16:s * 24 + 24], 0.5 if s != 1 else 1.0, None, ALU.mult,
                 R=["mod"], W=["gate"])
    if es_glob is None:
        P.barrier()


def ffn_phase(K, lay, which, x_src, sname, x_dst, dname, w13, w2, mv, s, T, TT, consts):
    nc = K.nc
    P = K.P
    NT = T // TT
    tg = "f%d%d" % (lay, which)
    with ExitStack() as es:
        w13s = _sb(es, nc, "w13_" + tg, [128, DC, 2 * DFF], BF16)
        w2s = _sb(es, nc, "w2_" + tg, [128, FC, D], BF16)
        xt = _sb(es, nc, "xt_" + tg, [128, DC, TT], F32)
        sq = _sb(es, nc, "sq_" + tg, [128, DC, TT], BF16)
        hb = _sb(es, nc, "hb_" + tg, [128, DC, TT], BF16)
        gb = _sb(es, nc, "gb_" + tg, [128, FC, TT], BF16)
        rs = _sb(es, nc, "rs_" + tg, [128, TT], F32)
        sa = [_sb(es, nc, "sa%d_" % i + tg, [128, TT], F32) for i in range(2)]
        pst = _ps(es, nc, "pst_" + tg, [128, TT])
        pa = [_ps(es, nc, "pa%d_" % i + tg, [128, TT]) for i in range(2)]
        pb = [_ps(es, nc, "pb%d_" % i + tg, [128, TT]) for i in range(2)]
        po = [_ps(es, nc, "po%d_" % i + tg, [128, TT]) for i in range(2)]
        w13v = w13.rearrange("(kc p) f -> p kc f", p=128)
        w2v = w2.rearrange("(fc p) d -> p fc d", p=128)
        for kc in range(DC):
            K.dma(w13s[:, kc, :], w13v[:, kc, :], W=[("w13", kc)], eng="pool")
        for fc in range(FC):
            K.dma(w2s[:, fc, :], w2v[:, fc, :], W=[("w2", fc)], eng="pool")
        xsv = x_src.rearrange("(dc p) t -> p dc t", p=128)
        xdv = x_dst.rearrange("(dc p) t -> p dc t", p=128)
        gs, shift, gate = mv["gs"], mv["mod"], mv["gate"]
        for tt in range(NT):
            tsl = slice(tt * TT, (tt + 1) * TT)
            K.dma(xt[:], xsv[:, :, tsl], R=[("xd", sname, tt)], W=[("xt", dc) for dc in range(DC)])
            K.act(sq[:], xt[:], AF.Square, R=[("xt", dc) for dc in range(DC)], W=["sq"])
            for dc in range(DC):
                K.mm(pst[:], consts["ones_bf"][:], sq[:, dc, :], start=(dc == 0), stop=(dc == DC - 1),
                     R=["sq"], W=["pst"])
            K.act(rs[:], pst[:], AF.Sqrt, R=["pst", "epsc"], W=["rs"], bias=consts["eps"][:], scale=1.0 / D)
            K.recip(rs[:], rs[:], R=["rs"], W=["rs"])
            for dc in range(DC):
                k2 = dc % 2
                K.stt(sa[k2][:], xt[:, dc, :], gs[:, s, dc:dc + 1], rs[:], ALU.mult, ALU.mult,
                      R=[("xt", dc), "rs", "gs"], W=[("sa", k2)])
                K.act(hb[:, dc, :], sa[k2][:], AF.Identity, R=[("sa", k2), "mod"], W=[("hb", dc)],
                      bias=shift[:, s * 24 + dc:s * 24 + dc + 1], scale=1.0)
            for fc in range(FC):
                k2 = fc % 2
                for half, pp in ((0, pa), (1, pb)):
                    for kc in range(DC):
                        K.mm(pp[k2][:], w13s[:, kc, half * DFF + fc * 128:half * DFF + (fc + 1) * 128],
                             hb[:, kc, :], start=(kc == 0), stop=(kc == DC - 1),
                             R=[("w13", kc), ("hb", kc)], W=[("pab", half, k2)])
                K.act(sa[k2][:], pa[k2][:], AF.Silu, R=[("pab", 0, k2)], W=[("sa", k2)])
                K.tt(gb[:, fc, :], sa[k2][:], pb[k2][:], ALU.mult, R=[("sa", k2), ("pab", 1, k2)], W=[("gb", fc)])
            for dc in range(DC):
                k2 = dc % 2
                for fc in range(FC):
                    K.mm(po[k2][:], w2s[:, fc, dc * 128:(dc + 1) * 128], gb[:, fc, :],
                         start=(fc == 0), stop=(fc == FC - 1), R=[("w2", fc), ("gb", fc)], W=[("po", k2)])
                K.stt(xt[:, dc, :], po[k2][:], gate[:, s, dc:dc + 1], xt[:, dc, :], ALU.mult, ALU.add,
                      R=[("po", k2), "gate", ("xt", dc)], W=[("xt", dc)])
            K.dma(xdv[:, :, tsl], xt[:], R=[("xt", dc) for dc in range(DC)], W=[("xd", dname, tt)])
    P.barrier()


OFF = {}
_o = 0
for _n, _w in (("a_q", 256), ("a_k_cmp", 64), ("a_v_cmp", 64), ("a_k_slc", 64), ("a_v_slc", 64), ("a_k_win", 64),
               ("a_v_win", 64), ("a_gate", 12), ("b_b", 256), ("b_c", 256), ("b_x", 256), ("c_q", 256), ("c_k", 256),
               ("c_v", 256), ("c_beta", 4), ("c_alpha", 4), ("c_z", 256), ("d_q", 256), ("d_k", 256), ("d_v", 256),
               ("d_i", 4), ("d_f", 4), ("d_o", 256)):
    OFF[_n] = _o
    _o += _w
assert _o == D_IN
FM = ([(OFF["a_q"] + 64 * h, 64) for h in range(4)] + [(OFF["a_k_cmp"], 128), (OFF["a_k_slc"], 64), (OFF["a_k_win"], 64)]
      + [(OFF["b_b"] + 128 * i, 128) for i in range(2)] + [(OFF["b_c"] + 128 * i, 128) for i in range(2)]
      + [(OFF["b_x"] + 128 * i, 128) for i in range(2)] + [(OFF["c_q"] + 64 * i, 64) for i in range(12)]
      + [(OFF["d_q"] + 64 * i, 64) for i in range(4)] + [(OFF["d_k"] + 64 * i, 64) for i in range(4)])
G_AQ, G_KVC, G_KSLC, G_KWIN, G_BB, G_BC, G_BX, G_CQKV, G_DQ, G_DK = 0, 4, 5, 6, 7, 9, 11, 13, 25, 29
NFM = len(FM)
CM_ID, CM_TRIU, CM_SU, CM_SL, CM_LI, CM_N = 0, 128, 192, 256, 320, 384


def bc(ap, shape):
    return ap.to_broadcast(list(shape))


def mixer_phase(K, lay, x_src, sname, x_dst, dname, Wd, mv, T, TT, consts, yscr, mode, enable="ABCD"):
    nc = K.nc
    P = K.P
    NT = T // TT
    NCH = TT // 64
    tg = "m%d%d" % (lay, mode)
    if mode == 1:
        enable = "".join(c for c in enable if c in "BCD")
        WOFF, WN = OFF["b_b"], D_IN - OFF["b_b"]
    else:
        enable = "".join(c for c in enable if c in "A")
        WOFF, WN = 0, OFF["b_b"]
    SUM = cm_su = None
    cm = consts["cmask"]
    SUM = cm[0:64, CM_SU:CM_SU + 64]
    TRIU = cm[0:64, CM_TRIU:CM_TRIU + 64]
    SLM = cm[0:64, CM_SL:CM_SL + 64]
    IDENT = cm[:, CM_ID:CM_ID + 128]
    onesf = consts["ones_f"]
    with ExitStack() as es:
        sb = lambda name, shape, d=F32: _sb(es, nc, name + "_" + tg, shape, d)
        wins = sb("win", [128, DC, WN], BF16)
        wouts = sb("wout", [128, DC, D if mode == 2 else 2], BF16)
        bfm = sb("bfm", [128, NFM], F32)
        bfm8 = sb("bfm8", [128, 4], F32)
        bfmv = sb("bfmv", [64, 1], F32)
        K.dma(bfmv[:], Wd["b_fm"][64:128, G_KVC:G_KVC + 1], W=["a_bfmv"], allow_slow_non_contiguous=True)
        xt = sb("xt", [128, DC, TT])
        sq = sb("sq", [128, DC, TT], BF16)
        hb = sb("hb", [128, DC, TT], BF16)
        yt = sb("yt", [128, DC, TT], BF16)
        rs = sb("rs", [128, TT])
        sa = [sb("sa%d" % i, [128, TT]) for i in range(2)]
        scw = sb("scw", [128, 2, 3])
        mixg = sb("mixg", [128, 2, 2])
        DC_ = DC
        PJ = [_ps(es, nc, "pj0_" + tg, [128, 512])]
        if mode == 1:
            PST = PJ[0]
            PSTK = ("pj", 0)
            PD = [_ps(es, nc, "pd%d_" % i + tg, [128, 512]) for i in range(3)]
            PC = [_ps(es, nc, "pc%d_" % i + tg, [128, 512]) for i in range(4)]
        else:
            PST = _ps(es, nc, "pst_" + tg, [128, 512])
            PSTK = "pst"
            PTK = [_ps(es, nc, "ptk0_" + tg, [128, 512])]
            PW = [_ps(es, nc, "pw0_" + tg, [128, 512])]
            PS2 = [_ps(es, nc, "ps2%d_" % i + tg, [128, 512]) for i in range(2)]
            POA = _ps(es, nc, "poa_" + tg, [128, 512])
            PMX = _ps(es, nc, "pmx_" + tg, [128, 512])
            POB = PST

        winv = Wd["w_in"].rearrange("(kc p) n -> p kc n", p=128)
        for kc in range(DC):
            K.dma(wins[:, kc, :], winv[:, kc, WOFF:WOFF + WN], W=[("win", kc)], eng="pool")
        woutv = Wd["w_out"].rearrange("(kc p) n -> p kc n", p=128)
        if mode == 2:
            for kc in range(DC):
                K.dma(wouts[:, kc, :], woutv[:, kc, :], W=[("wout", kc)], eng="pool")
        K.dma(bfm[:], Wd["b_fm"], W=["bfm"])
        K.dma(scw[:], Wd["sc_conv_wT"], W=["scw"])
        K.dma(mixg[:], Wd["mixgT"], W=["mixg"])
        K.ts(bfm8[:], bfm[:, G_DK:G_DK + 4], 0.125, None, ALU.mult, R=["bfm"], W=["bfm8"])

        xsv = x_src.rearrange("(dc p) t -> p dc t", p=128)
        xdv = x_dst.rearrange("(dc p) t -> p dc t", p=128)
        gs, shift, gate = mv["gs"], mv["mod"], mv["gate"]
        s = 1
        pjn = [0]

        def proj_fm(g, dst, dkeys, scale=1.0, bias=None, func=AF.Identity):
            c0, ncol = FM[g]
            pj = PJ[pjn[0] % len(PJ)]
            kk = ("pj", pjn[0] % len(PJ))
            pjn[0] += 1
            for kc in range(DC):
                K.mm(pj[0:ncol, 0:TT], wins[:, kc, c0 - WOFF:c0 - WOFF + ncol], hb[:, kc, :], start=(kc == 0), stop=(kc == DC - 1),
                     R=[("win", kc), ("hb", kc)], W=[kk])
            b = bias if bias is not None else bfm[0:ncol, g:g + 1]
            K.act(dst, pj[0:ncol, 0:TT], func, R=[kk, "bfm", "bfm8"], W=dkeys, bias=b, scale=scale)

        if "B" in enable:
            ub = [sb("ub%d" % ch, [128, 2 + TT]) for ch in range(2)]
            bbt = sb("bbt", [128, TT])
            cct = sb("cct", [128, TT])
            cvt = sb("cvt", [128, TT])
            ybt = [sb("ybt%d" % ch, [128, TT]) for ch in range(2)]
            for ch in range(2):
                K.memset(ub[ch][:, 0:2], 0.0, W=[("ub", ch)])
        if "D" in enable:
            QmT = sb("QmT", [64, 4, TT], BF16)
            KmT = sb("KmT", [64, 4, TT], BF16)
            Smb = sb("Smb", [64, 4, 65], BF16)
            K.memset(Smb[:], 0.0, W=["Smb"])
            Sm = sb("Sm", [64, 4, 65])
            K.memset(Sm[:], 0.0, W=["Sm"])
            btm_d = sb("btm_d", [64, 776])
            fbb = sb("fbb", [64, 4])
            gnd = sb("gnd", [64, 64])
            K.dma(btm_d[:], Wd["b_in"][OFF["d_k"]:OFF["d_k"] + 776].partition_broadcast(64), W=["btm_d"])
            K.dma(fbb[:], Wd["mlstm_f_bias"].partition_broadcast(64), W=["fbb"])
            K.dma(gnd[:], Wd["mlstm_norm_g"].partition_broadcast(64), W=["gnd"])
            K.tt(btm_d[:, 516:520], btm_d[:, 516:520], fbb[:], ALU.add, R=["btm_d", "fbb"], W=["btm_d"])
            DT = {}
            for nm_, *shp in (("ktok", [64, 4, 64]), ("vext", [64, 4, 65], BF16), ("ifo", [64, 264]), ("lf", [64, 4]),
                             ("gtot", [64, 4]), ("bsb", [64, 4]), ("ddd", [64, 4]), ("edec", [64, 4]), ("eb", [64, 4]),
                             ("slg", [64, 4, 64]), ("et", [64, 4, 64]), ("pt", [64, 4, 64], BF16), ("nd", [64, 4, 65]),
                             ("den", [64, 4]), ("hm", [64, 4, 64]), ("hsq", [64, 4, 64]), ("ss", [64, 4]),
                             ("so", [64, 4, 64]), ("kd", [64, 4, 64], BF16), ("stmp", [64, 4, 65])):
                DT[nm_] = [sb("d_%s%d" % (nm_, z), *shp) if False else sb("d_%s%d" % (nm_, z), shp[0], shp[1] if len(shp) > 1 else F32)
                           for z in range(2)]
            for z in range(2):
                K.memset(DT["vext"][z][:], 1.0, W=["d_vext%d" % z])

        def mlstm_chunk(c):
            cs = slice(c * 64, (c + 1) * 64)
            z = c % 2
            t_ = {k_: v_[z] for k_, v_ in DT.items()}
            kk_ = lambda n_: "d_%s%d" % (n_, z)
            ktok, vext, ifo, lf, gtot, bsb, ddd, edec, eb = (t_[x] for x in ("ktok", "vext", "ifo", "lf", "gtot", "bsb", "ddd", "edec", "eb"))
            slg, et, pt, nd, den, hm, hsq, ss, so, kd, stmp = (t_[x] for x in ("slg", "et", "pt", "nd", "den", "hm", "hsq", "ss", "so", "kd", "stmp"))
            DA, DB, DC = PD
            kA, kB, kC = ("pd", 0), ("pd", 1), ("pd", 2)
            for kc in range(DC_):
                K.mm(DC[0:64, 0:512], hb[:, kc, cs], wins[:, kc, OFF["d_k"] - WOFF:OFF["d_k"] - WOFF + 512],
                     start=(kc == 0), stop=(kc == DC_ - 1), R=[("win", kc), ("hb", kc)], W=[kC])
            for kc in range(DC_):
                K.mm(DB[0:64, 0:264], hb[:, kc, cs], wins[:, kc, OFF["d_i"] - WOFF:OFF["d_i"] - WOFF + 264],
                     start=(kc == 0), stop=(kc == DC_ - 1), R=[("win", kc), ("hb", kc)], W=[kB])
            K.tt(ktok[:].rearrange("p a b -> p (a b)"), DC[0:64, 0:256], btm_d[:, 0:256], ALU.add,
                 R=[kC, "btm_d"], W=[kk_("ktok")])
            K.tt(vext[:, :, 0:64], v4(DC[0:64, 256:512]), v4(btm_d[:, 256:512]), ALU.add, R=[kC, "btm_d"], W=[kk_("vext")])
            K.tt(ifo[:], DB[0:64, 0:264], btm_d[:, 512:776], ALU.add, R=[kB, "btm_d"], W=[kk_("ifo")])
            K.act(lf[:], ifo[:, 4:8], AF.Exp, R=[kk_("ifo")], W=[kk_("lf")], scale=-1.0)
            K.act(lf[:], lf[:], AF.Ln, R=[kk_("lf")], W=[kk_("lf")], bias=consts["one"][0:64, :], scale=1.0)
            K.ts(lf[:], lf[:], -1.0, None, ALU.mult, R=[kk_("lf")], W=[kk_("lf")])
            K.mm(DB[0:64, 264:268], TRIU, lf[:], R=[kk_("lf"), "cmask"], W=[kB])
            K.mm(DB[0:64, 268:272], onesf[0:64, 0:64], lf[:], R=[kk_("lf"), "ones_f"], W=[kB])
            K.act(gtot[:], DB[0:64, 268:272], AF.Exp, R=[kB], W=[kk_("gtot")])
            K.copy(bsb[:], DB[0:64, 264:268], R=[kB], W=[kk_("bsb")])
            K.tt(ddd[:], DB[0:64, 268:272], bsb[:], ALU.subtract, R=[kB, kk_("bsb")], W=[kk_("ddd")])
            K.tt(ddd[:], ddd[:], ifo[:, 0:4], ALU.add, R=[kk_("ddd"), kk_("ifo")], W=[kk_("ddd")])
            K.act(edec[:], ddd[:], AF.Exp, R=[kk_("ddd")], W=[kk_("edec")])
            K.act(eb[:], bsb[:], AF.Exp, R=[kk_("bsb")], W=[kk_("eb")])
            K.tt(slg[:], bc(TRIU.unsqueeze(1), [64, 4, 64]), bc(lf[:, :].unsqueeze(2), [64, 4, 64]), ALU.mult,
                 R=[kk_("lf"), "cmask"], W=[kk_("slg")])
            K.mm(DC[0:64, 0:256], SLM, slg[:].rearrange("p a b -> p (a b)"), R=[kk_("slg"), "cmask"], W=[kC])
            for h in range(4):
                K.mm(DC[0:64, 256 + h * 64:256 + (h + 1) * 64], KmT[:, h, cs], QmT[:, h, cs], R=["QmT", "KmT"], W=[kC])
            K.tt(et[:], v4(DC[0:64, 0:256]), bc(ifo[:, 0:4].unsqueeze(2), [64, 4, 64]), ALU.add, R=[kC, kk_("ifo")],
                 W=[kk_("et")])
            K.act(et[:], et[:], AF.Exp, R=[kk_("et")], W=[kk_("et")])
            K.tt(et[:], et[:], bc(TRIU.unsqueeze(1), [64, 4, 64]), ALU.mult, R=[kk_("et"), "cmask"], W=[kk_("et")])
            K.tt(pt[:], et[:], v4(DC[0:64, 256:512]), ALU.mult, R=[kk_("et"), kC], W=[kk_("pt")])
            for h in range(4):
                K.mm(DA[0:64, h * 65:(h + 1) * 65], QmT[:, h, cs], Smb[:, h, :], R=["QmT", "Smb"], W=[kA])
            qs = DA[0:64, 0:260].rearrange("p (a b) -> p a b", a=4)
            K.tt(nd[:], qs, bc(eb[:, :].unsqueeze(2), [64, 4, 65]), ALU.mult, R=[kA, kk_("eb")], W=[kk_("nd")])
            for h in range(4):
                K.mm(DA[0:64, h * 65:(h + 1) * 65], pt[:, h, :], vext[:, h, :], R=[kk_("pt"), kk_("vext")], W=[kA])
            K.tt(nd[:], nd[:], DA[0:64, 0:260].rearrange("p (a b) -> p a b", a=4), ALU.add, R=[kk_("nd"), kA], W=[kk_("nd")])
            K.stt(kd[:], ktok[:], 0.125, bc(edec[:, :].unsqueeze(2), [64, 4, 64]), ALU.mult, ALU.mult,
                  R=[kk_("ktok"), kk_("edec")], W=[kk_("kd")])
            for h in range(4):
                K.mm(DA[0:64, h * 65:(h + 1) * 65], kd[:, h, :], vext[:, h, :], R=[kk_("kd"), kk_("vext")], W=[kA])
            K.tt(stmp[:], Sm[:], bc(gtot[:, :].unsqueeze(2), [64, 4, 65]), ALU.mult, R=["Sm", kk_("gtot")], W=[kk_("stmp")])
            K.tt(Sm[:], stmp[:], DA[0:64, 0:260].rearrange("p (a b) -> p a b", a=4), ALU.add, R=[kk_("stmp"), kA], W=["Sm"])
            K.copy(Smb[:], Sm[:], R=["Sm"], W=["Smb"], eng="act")
            K.act(den[:], nd[:, :, 64], AF.Abs, R=[kk_("nd")], W=[kk_("den")])
            K.ts(den[:], den[:], 1.0, None, ALU.max, R=[kk_("den")], W=[kk_("den")])
            K.recip(den[:], den[:], R=[kk_("den")], W=[kk_("den")])
            K.tt(hm[:], nd[:, :, 0:64], bc(den[:, :].unsqueeze(2), [64, 4, 64]), ALU.mult, R=[kk_("nd"), kk_("den")], W=[kk_("hm")])
            K.tt(hsq[:], hm[:], hm[:], ALU.mult, R=[kk_("hm")], W=[kk_("hsq")])
            K.P.add("dve", lambda e: e.tensor_reduce(ss[:], hsq[:], AX.X, ALU.add), R=[kk_("hsq")], W=[kk_("ss")], cost=400.0)
            K.act(ss[:], ss[:], AF.Sqrt, R=[kk_("ss"), "epsc"], W=[kk_("ss")], bias=consts["eps"][0:64, :], scale=1.0 / 64)
            K.recip(ss[:], ss[:], R=[kk_("ss")], W=[kk_("ss")])
            K.act(so[:].rearrange("p a b -> p (a b)"), ifo[:, 8:264], AF.Sigmoid, R=[kk_("ifo")], W=[kk_("so")])
            K.tt(hm[:], hm[:], bc(ss[:, :].unsqueeze(2), [64, 4, 64]), ALU.mult, R=[kk_("hm"), kk_("ss")], W=[kk_("hm")])
            K.tt(hm[:], hm[:], bc(gnd[:, :].unsqueeze(1), [64, 4, 64]), ALU.mult, R=[kk_("hm"), "gnd"], W=[kk_("hm")])
            K.tt(hm[:], hm[:], so[:], ALU.mult, R=[kk_("hm"), kk_("so")], W=[kk_("hm")])
            hmf = hm[:].rearrange("p a b -> p (a b)")
            for pr in range(2):
                K.tr(DA[:, 272 + pr * 64:272 + (pr + 1) * 64], hmf[:, pr * 128:(pr + 1) * 128], IDENT[0:64, 0:64],
                     R=[kk_("hm"), "cmask"], W=[kA])
                K.copy(yt[:, 6 + pr, cs], DA[:, 272 + pr * 64:272 + (pr + 1) * 64], R=[kA], W=[("yt", 6 + pr)], eng="act")

        GDT = F32
        if "C" in enable:
            cin = sb("c_cin", [64, 12, 3 + TT])
            qkv = sb("c_qkv", [64, 12, TT])
            qkb = sb("c_qkb", [64, 12, TT], GDT)
            Sgb = sb("c_Sb", [64, 4, 64], GDT)
            K.memset(Sgb[:], 0.0, W=["c_Sb"])
            identb1 = sb("c_identb", [64, 64], GDT)
            K.copy(identb1[:], IDENT[0:64, 0:64], R=["cmask"], W=["c_identb"])
            gcw = sb("c_gcw", [64, 12, 4])
            Sg = sb("c_S", [64, 4, 64])
            K.memset(Sg[:], 0.0, W=["c_S"])
            K.memset(cin[:, :, 0:3], 0.0, W=[("c_cin", g) for g in range(12)])
            K.dma(gcw[:], Wd["gdn_conv_wT"], W=["c_gcw"])
            btm_c = sb("c_btm", [64, 264])
            dtb = sb("c_dtb", [64, 4])
            nea = sb("c_nea", [64, 4])
            gng = sb("c_gng", [64, 64])
            K.dma(btm_c[:], Wd["b_in"][OFF["c_beta"]:OFF["c_beta"] + 264].partition_broadcast(64), W=["c_btm"])
            K.dma(dtb[:], Wd["gdn_dt_bias"].partition_broadcast(64), W=["c_dtb"])
            K.dma(nea[:], Wd["gdn_A_log"].partition_broadcast(64), W=["c_nea"])
            K.dma(gng[:], Wd["gdn_norm_g"].partition_broadcast(64), W=["c_gng"])
            K.tt(btm_c[:, 4:8], btm_c[:, 4:8], dtb[:], ALU.add, R=["c_btm", "c_dtb"], W=["c_btm"])
            K.act(nea[:], nea[:], AF.Exp, R=["c_nea"], W=["c_nea"])
            K.ts(nea[:], nea[:], -1.0, None, ALU.mult, R=["c_nea"], W=["c_nea"])
            csq = sb("c_sq", [64, TT])
            crs = sb("c_rs", [64, TT])
            CT = {}
            for nm_, *shp in (("baz", [64, 264]), ("beta", [64, 4]), ("gg", [64, 4]), ("gcs", [64, 4]), ("egc", [64, 4]),
                             ("bgc", [64, 4]), ("edl", [64, 4]), ("gto", [64, 4]), ("ktk", [64, 4, 64]), ("vb", [64, 4, 64], GDT),
                             ("kbe", [64, 4, 64], GDT), ("kdc", [64, 4, 64], GDT), ("ug", [64, 4, 64]), ("slgc", [64, 4, 64]),
                             ("dgb", [64, 4, 64]), ("seg", [64, 4, 64]), ("segT", [64, 4, 64]), ("t1", [64, 4, 64]),
                             ("t2", [64, 4, 64]), ("NN", [64, 2, 4, 64], GDT), ("NN2", [64, 2, 4, 64], GDT), ("XX", [64, 4, 64]), ("XB", [64, 4, 64], GDT),
                             ("ptc", [64, 4, 64], GDT), ("uu", [64, 4, 64]), ("wt", [64, 4, 64], GDT), ("vn", [64, 4, 64]), ("vnb", [64, 4, 64], GDT),
                             ("oo", [64, 4, 64]), ("osq", [64, 4, 64]), ("oss", [64, 4]), ("sz", [64, 4, 64]),
                             ("stg", [64, 4, 64])):
                CT[nm_] = [sb("c_%s%d" % (nm_, z), shp[0], shp[1] if len(shp) > 1 else F32) for z in range(2)]

        def v4(ap):
            return ap.rearrange("p (a b) -> p a b", a=4)

        def gdn_tile():
            for g in range(12):
                proj_fm(G_CQKV + g, cin[:, g, 3:3 + TT], [("c_cin", g)])
                K.ts(qkv[:, g, :], cin[:, g, 3:3 + TT], gcw[:, g, 3:4], None, ALU.mult, R=[("c_cin", g), "c_gcw"],
                     W=[("c_qkv", g)])
                for k in range(3):
                    K.stt(qkv[:, g, :], cin[:, g, k:k + TT], gcw[:, g, k:k + 1], qkv[:, g, :], ALU.mult, ALU.add,
                          R=[("c_cin", g), "c_gcw", ("c_qkv", g)], W=[("c_qkv", g)])
                K.copy(cin[:, g, 0:3], cin[:, g, TT:TT + 3], R=[("c_cin", g)], W=[("c_cin", g)])
                K.act(qkv[:, g, :], qkv[:, g, :], AF.Silu, R=[("c_qkv", g)], W=[("c_qkv", g)])
                if g < 8:
                    K.tt(csq[:], qkv[:, g, :], qkv[:, g, :], ALU.mult, R=[("c_qkv", g)], W=["c_sq"])
                    K.mm(PST[0:64, 0:TT], onesf[0:64, 0:64], csq[:], R=["c_sq", "ones_f"], W=[PSTK])
                    K.act(crs[:], PST[0:64, 0:TT], AF.Sqrt, R=[PSTK, "epsc"], W=["c_rs"], bias=consts["eps"][0:64, :],
                          scale=1.0)
                    K.recip(crs[:], crs[:], R=["c_rs"], W=["c_rs"])
                    K.stt(qkb[:, g, :], qkv[:, g, :], 0.125 if g < 4 else 1.0, crs[:], ALU.mult, ALU.mult,
                          R=[("c_qkv", g), "c_rs"], W=[("c_qkb", g)])
                else:
                    K.copy(qkb[:, g, :], qkv[:, g, :], R=[("c_qkv", g)], W=[("c_qkb", g)])

        def gdn_chunk(c):
            cs = slice(c * 64, (c + 1) * 64)
            z = c % 2
            t_ = {k_: v_[z] for k_, v_ in CT.items()}
            kk_ = lambda n_: "c_%s%d" % (n_, z)
            baz, beta, gg, gcs, egc, bgc, edl, gto = (t_[x] for x in ("baz", "beta", "gg", "gcs", "egc", "bgc", "edl", "gto"))
            ktk, vb, kbe, kdc, ug, slgc, dgb, seg, segT = (t_[x] for x in ("ktk", "vb", "kbe", "kdc", "ug", "slgc", "dgb", "seg", "segT"))
            t1, t2, NN, NN2, XX, ptc, uu, wt, vn, oo, osq, oss, sz, stg = (t_[x] for x in ("t1", "t2", "NN", "NN2", "XX", "ptc", "uu", "wt", "vn", "oo", "osq", "oss", "sz", "stg"))
            XB, vnb = t_["XB"], t_["vnb"]
            CA, CB, CC, CD = PC
            kA, kB, kC, kD = ("pc", 0), ("pc", 1), ("pc", 2), ("pc", 3)
            QK = [("c_qkb", g) for g in range(12)]
            for kc in range(DC_):
                K.mm(CA[0:64, 0:264], hb[:, kc, cs], wins[:, kc, OFF["c_beta"] - WOFF:OFF["c_beta"] - WOFF + 264],
                     start=(kc == 0), stop=(kc == DC_ - 1), R=[("win", kc), ("hb", kc)], W=[kA])
            K.tt(baz[:], CA[0:64, 0:264], btm_c[:], ALU.add, R=[kA, "c_btm"], W=[kk_("baz")])
            K.act(beta[:], baz[:, 0:4], AF.Sigmoid, R=[kk_("baz")], W=[kk_("beta")])
            K.act(gg[:], baz[:, 4:8], AF.Exp, R=[kk_("baz")], W=[kk_("gg")])
            K.act(gg[:], gg[:], AF.Ln, R=[kk_("gg")], W=[kk_("gg")], bias=consts["one"][0:64, :], scale=1.0)
            K.tt(gg[:], gg[:], nea[:], ALU.mult, R=[kk_("gg"), "c_nea"], W=[kk_("gg")])
            K.mm(CA[0:64, 264:268], TRIU, gg[:], R=[kk_("gg"), "cmask"], W=[kA])
            K.mm(CA[0:64, 268:272], onesf[0:64, 0:64], gg[:], R=[kk_("gg"), "ones_f"], W=[kA])
            K.copy(gcs[:], CA[0:64, 264:268], R=[kA], W=[kk_("gcs")])
            K.act(egc[:], CA[0:64, 264:268], AF.Exp, R=[kA], W=[kk_("egc")])
            K.act(gto[:], CA[0:64, 268:272], AF.Exp, R=[kA], W=[kk_("gto")])
            K.tt(edl[:], CA[0:64, 268:272], gcs[:], ALU.subtract, R=[kA, kk_("gcs")], W=[kk_("edl")])
            K.act(edl[:], edl[:], AF.Exp, R=[kk_("edl")], W=[kk_("edl")])
            K.tt(bgc[:], beta[:], egc[:], ALU.mult, R=[kk_("beta"), kk_("egc")], W=[kk_("bgc")])
            for h in range(4):
                K.mm(CB[0:64, h * 64:(h + 1) * 64], qkb[:, 4 + h, cs], identb1[:], R=QK + ["c_identb"], W=[kB])
                K.mm(CB[0:64, 256 + h * 64:256 + (h + 1) * 64], qkb[:, 8 + h, cs], identb1[:], R=QK + ["c_identb"], W=[kB])
            K.copy(ktk[:], v4(CB[0:64, 0:256]), R=[kB], W=[kk_("ktk")], eng="act")
            K.tt(vb[:], v4(CB[0:64, 256:512]), bc(beta[:, :].unsqueeze(2), [64, 4, 64]), ALU.mult, R=[kB, kk_("beta")], W=[kk_("vb")])
            K.tt(kbe[:], ktk[:], bc(bgc[:, :].unsqueeze(2), [64, 4, 64]), ALU.mult, R=[kk_("ktk"), kk_("bgc")], W=[kk_("kbe")])
            K.tt(kdc[:], ktk[:], bc(edl[:, :].unsqueeze(2), [64, 4, 64]), ALU.mult, R=[kk_("ktk"), kk_("edl")], W=[kk_("kdc")])
            K.tt(ug[:], bc(TRIU.unsqueeze(1), [64, 4, 64]), bc(gg[:, :].unsqueeze(2), [64, 4, 64]), ALU.mult,
                 R=[kk_("gg"), "cmask"], W=[kk_("ug")])
            K.tt(slgc[:], bc(SLM.unsqueeze(1), [64, 4, 64]), bc(gg[:, :].unsqueeze(2), [64, 4, 64]), ALU.mult,
                 R=[kk_("gg"), "cmask"], W=[kk_("slgc")])
            K.mm(CC[0:64, 0:256], TRIU, slgc[:].rearrange("p a b -> p (a b)"), R=[kk_("slgc"), "cmask"], W=[kC])
            K.mm(CC[0:64, 256:512], SLM, ug[:].rearrange("p a b -> p (a b)"), R=[kk_("ug"), "cmask"], W=[kC])
            K.tt(dgb[:], bc(IDENT[0:64, 0:64].unsqueeze(1), [64, 4, 64]), bc(beta[:, :].unsqueeze(2), [64, 4, 64]), ALU.mult,
                 R=[kk_("beta"), "cmask"], W=[kk_("dgb")])
            K.act(seg[:], v4(CC[0:64, 0:256]), AF.Exp, R=[kC], W=[kk_("seg")])
            K.act(segT[:], v4(CC[0:64, 256:512]), AF.Exp, R=[kC], W=[kk_("segT")])
            for h in range(4):
                K.mm(CB[0:64, h * 64:(h + 1) * 64], qkb[:, 4 + h, cs], qkb[:, 4 + h, cs], R=QK, W=[kB])
                K.mm(CB[0:64, 256 + h * 64:256 + (h + 1) * 64], qkb[:, 4 + h, cs], qkb[:, h, cs], R=QK, W=[kB])
            K.mm(CC[0:64, 0:256], onesf[0:64, 0:64], dgb[:].rearrange("p a b -> p (a b)"), R=[kk_("dgb"), "ones_f"], W=[kC])
            K.tt(t1[:], seg[:], bc(SLM.unsqueeze(1), [64, 4, 64]), ALU.mult, R=[kk_("seg"), "cmask"], W=[kk_("t1")])
            K.tt(t1[:], t1[:], v4(CB[0:64, 0:256]), ALU.mult, R=[kk_("t1"), kB], W=[kk_("t1")])
            K.tt(NN[:, 0], t1[:], bc(beta[:, :].unsqueeze(2), [64, 4, 64]), ALU.mult, R=[kk_("t1"), kk_("beta")], W=[kk_("NN")])
            K.tt(t2[:], segT[:], bc(SUM.unsqueeze(1), [64, 4, 64]), ALU.mult, R=[kk_("segT"), "cmask"], W=[kk_("t2")])
            K.tt(t2[:], t2[:], v4(CB[0:64, 0:256]), ALU.mult, R=[kk_("t2"), kB], W=[kk_("t2")])
            K.tt(NN[:, 1], t2[:], v4(CC[0:64, 0:256]), ALU.mult, R=[kk_("t2"), kC], W=[kk_("NN")])
            K.tt(ptc[:], segT[:], bc(TRIU.unsqueeze(1), [64, 4, 64]), ALU.mult, R=[kk_("segT"), "cmask"], W=[kk_("ptc")])
            K.tt(ptc[:], ptc[:], v4(CB[0:64, 256:512]), ALU.mult, R=[kk_("ptc"), kB], W=[kk_("ptc")])
            K.tt(XX[:], bc(IDENT[0:64, 0:64].unsqueeze(1), [64, 4, 64]), NN[:, 1], ALU.subtract, R=[kk_("NN"), "cmask"],
                 W=[kk_("XX")])
            K.copy(XB[:], XX[:], R=[kk_("XX")], W=[kk_("XB")], eng="act")
            cur, nxt, ck, nk = NN, NN2, kk_("NN"), kk_("NN2")
            for lvl in range(5):
                last = (lvl == 4)
                for h in range(4):
                    K.mm(CC[0:64, h * 64:(h + 1) * 64], cur[:, 1, h, :], cur[:, 0, h, :], R=[ck], W=[kC])
                    if not last:
                        K.mm(CC[0:64, 256 + h * 64:256 + (h + 1) * 64], cur[:, 0, h, :], cur[:, 1, h, :], R=[ck], W=[kC])
                if last:
                    K.copy(nxt[:, 0], v4(CC[0:64, 0:256]), R=[kC], W=[nk], eng="act")
                else:
                    K.copy(nxt[:].rearrange("p t a b -> p (t a b)"), CC[0:64, 0:512], R=[kC], W=[nk], eng="act")
                for h in range(4):
                    K.mm(CB[0:64, h * 64:(h + 1) * 64], nxt[:, 0, h, :], XB[:, h, :], R=[nk, kk_("XB")], W=[kB])
                K.tt(XX[:], XX[:], v4(CB[0:64, 0:256]), ALU.add, R=[kk_("XX"), kB], W=[kk_("XX")])
                K.copy(XB[:], XX[:], R=[kk_("XX")], W=[kk_("XB")], eng="act")
                cur, nxt, ck, nk = nxt, cur, nk, ck
            for h in range(4):
                K.mm(CC[0:64, h * 64:(h + 1) * 64], XB[:, h, :], vb[:, h, :], R=[kk_("XB"), kk_("vb")], W=[kC])
                K.mm(CC[0:64, 256 + h * 64:256 + (h + 1) * 64], kbe[:, h, :], XB[:, h, :], R=[kk_("XB"), kk_("kbe")], W=[kC])
            K.copy(uu[:], v4(CC[0:64, 0:256]), R=[kC], W=[kk_("uu")], eng="act")
            K.copy(wt[:], v4(CC[0:64, 256:512]), R=[kC], W=[kk_("wt")])
            for h in range(4):
                K.mm(CD[0:64, h * 64:(h + 1) * 64], wt[:, h, :], Sgb[:, h, :], R=[kk_("wt"), "c_Sb"], W=[kD])
                K.mm(CD[0:64, 256 + h * 64:256 + (h + 1) * 64], qkb[:, h, cs], Sgb[:, h, :], R=QK + ["c_Sb"], W=[kD])
            K.tt(vnb[:], uu[:], v4(CD[0:64, 0:256]), ALU.subtract, R=[kk_("uu"), kD], W=[kk_("vnb")])
            K.tt(oo[:], v4(CD[0:64, 256:512]), bc(egc[:, :].unsqueeze(2), [64, 4, 64]), ALU.mult, R=[kD, kk_("egc")], W=[kk_("oo")])
            for h in range(4):
                K.mm(CD[0:64, h * 64:(h + 1) * 64], kdc[:, h, :], vnb[:, h, :], R=[kk_("kdc"), kk_("vnb")], W=[kD])
            K.tt(stg[:], Sg[:], bc(gto[:, :].unsqueeze(2), [64, 4, 64]), ALU.mult, R=["c_S", kk_("gto")], W=[kk_("stg")])
            K.tt(Sg[:], stg[:], v4(CD[0:64, 0:256]), ALU.add, R=[kk_("stg"), kD], W=["c_S"])
            K.copy(Sgb[:], Sg[:], R=["c_S"], W=["c_Sb"], eng="act")
            for h in range(4):
                K.mm(CD[0:64, 256 + h * 64:256 + (h + 1) * 64], ptc[:, h, :], vnb[:, h, :], R=[kk_("ptc"), kk_("vnb")], W=[kD])
            K.tt(oo[:], oo[:], v4(CD[0:64, 256:512]), ALU.add, R=[kk_("oo"), kD], W=[kk_("oo")])
            K.tt(osq[:], oo[:], oo[:], ALU.mult, R=[kk_("oo")], W=[kk_("osq")])
            K.P.add("dve", lambda e: e.tensor_reduce(oss[:], osq[:], AX.X, ALU.add), R=[kk_("osq")], W=[kk_("oss")], cost=400.0)
            K.act(oss[:], oss[:], AF.Sqrt, R=[kk_("oss"), "epsc"], W=[kk_("oss")], bias=consts["eps"][0:64, :], scale=1.0 / 64)
            K.recip(oss[:], oss[:], R=[kk_("oss")], W=[kk_("oss")])
            K.act(sz[:].rearrange("p a b -> p (a b)"), baz[:, 8:264], AF.Silu, R=[kk_("baz")], W=[kk_("sz")])
            K.tt(oo[:], oo[:], bc(oss[:, :].unsqueeze(2), [64, 4, 64]), ALU.mult, R=[kk_("oo"), kk_("oss")], W=[kk_("oo")])
            K.tt(oo[:], oo[:], bc(gng[:, :].unsqueeze(1), [64, 4, 64]), ALU.mult, R=[kk_("oo"), "c_gng"], W=[kk_("oo")])
            K.tt(oo[:], oo[:], sz[:], ALU.mult, R=[kk_("oo"), kk_("sz")], W=[kk_("oo")])
            oof = oo[:].rearrange("p a b -> p (a b)")
            for pr in range(2):
                K.tr(CD[:, pr * 64:(pr + 1) * 64], oof[:, pr * 128:(pr + 1) * 128], IDENT[0:64, 0:64],
                     R=[kk_("oo"), "cmask"], W=[kD])
                K.copy(yt[:, 4 + pr, cs], CD[:, pr * 64:(pr + 1) * 64], R=[kD], W=[("yt", 4 + pr)], eng="act")

        if "A" in enable:
            NB = T // 64
            NCB = T // 16
            NM = (NCB + 127) // 128
            NQ = T // 128
            w1k = sb("a_w1k", [64, 32, 256], BF16)
            w1v = sb("a_w1v", [64, 32, 256], BF16)
            w2k = sb("a_w2k", [128, 2, 64], BF16)
            w2v = sb("a_w2v", [128, 2, 64], BF16)
            K.dma(w1k[:], Wd["cmp_k_w1"].rearrange("(s d) n -> d s n", d=64), W=["a_w1k"], eng="pool")
            K.dma(w1v[:], Wd["cmp_v_w1"].rearrange("(s d) n -> d s n", d=64), W=["a_w1v"], eng="pool")
            K.dma(w2k[:], Wd["cmp_k_w2"].rearrange("(c p) n -> p c n", p=128), W=["a_w2k"], eng="pool")
            K.dma(w2v[:], Wd["cmp_v_w2"].rearrange("(c p) n -> p c n", p=128), W=["a_w2v"], eng="pool")
            posT = sb("a_posT", [64, 32], BF16)
            K.dma(posT[:], Wd["cmp_posT"], W=["a_posT"], eng="pool")
            hbias = sb("a_hbias", [128, 4])
            expc = sb("a_expc", [64, T], BF16)
            K.dma(expc[0:NB, :], Wd["expc"], W=["a_expc"], eng="pool")
            keepc = sb("a_keepc", [128, 2, 2 * NB])
            K.dma(keepc[:], Wd["keepadd"], W=["a_keepc"])
            identb = sb("a_identb", [128, 128], BF16)
            K.copy(identb[:], IDENT, R=["cmask"], W=["a_identb"])
            biasT = sb("a_bias", [128, 19, 512])
            for q in range(19):
                K.dma(biasT[:, q, :], Wd["bias_scr"][q], W=[("a_bias", q)])
            bw4 = sb("a_bw4", [128, 128])
            K.dma(bw4[:], Wd["bw4"], W=["a_bw4"])
            qng = sb("a_qng", [64, 2])
            K.dma(qng[:], Wd["qkng"], W=["a_qng"])
            K.ts(qng[:, 0:1], qng[:, 0:1], 0.125, None, ALU.mult, R=["a_qng"], W=["a_qng"])
            tabrow = sb("a_tabrow", [65, 4])
            K.dma(tabrow[64:65, :], Wd["t5_table"][31:32, :], W=["a_tabrow"])
            mg0 = sb("a_mg0", [128, 256])
            K.dma(mg0[:], Wd["mix_norm_g0"].partition_broadcast(128), W=["a_mg0"])
            btm_a = sb("a_btm", [128, 204])
            K.dma(btm_a[:], Wd["b_in"][OFF["a_v_slc"]:OFF["a_v_slc"] + 204].partition_broadcast(128), W=["a_btm"])
            ovl = sb("a_ovl", [128, NM, 64])
            K.dma(ovl[:], Wd["ovl"], W=["a_ovl"])
            kcmpT = sb("a_kcmpT", [64, T], BF16)
            vcmpT = sb("a_vcmpT", [64, T], BF16)
            KsT = sb("a_KsT", [128, T], BF16)
            KwT = sb("a_KwT", [128, T], BF16)
            kcT = sb("a_kcT", [128, NM * 128])
            vcT = sb("a_vcT", [64, NM * 128])
            vcx = sb("a_vcx", [128, NM, 65])
            Vs = sb("a_Vs", [128, NQ, 65], BF16)
            Vw = sb("a_Vw", [128, NQ, 65], BF16)
            NQT = TT // 128
            QaT = sb("a_QaT", [128, NQT, 4, 128], BF16)
            QaF = sb("a_QaF", [128, NQT, 4, 128])
            gsb = sb("a_gsb", [128, TT // 128, 12])
            K.memset(KsT[64:128, :], 0.0, W=["a_KsT"])
            K.memset(KwT[64:128, :], 0.0, W=["a_KwT"])
            K.memset(KsT[64:65, :], 1.0, W=["a_KsT"])
            K.memset(KwT[64:65, :], 1.0, W=["a_KwT"])
            K.memset(kcT[:, :], 0.0, W=["a_kcT"])
            K.memset(kcT[64:65, :], 1.0, W=["a_kcT"])
            K.memset(QaT[64:128], 0.0, W=["a_QaT"])
            K.memset(QaF[64:128], 0.0, W=["a_QaF"])
            K.memset(vcT[:], 0.0, W=["a_vcT"])
            K.memset(vcx[:], 1.0, W=["a_vcx"])
            K.memset(Vs[:], 1.0, W=["a_Vs"])
            K.memset(Vw[:], 1.0, W=["a_Vw"])
            for q_ in range(NQT):
                K.copy(QaT[64:65, q_], bc(tabrow[64:65, :].unsqueeze(2), [1, 4, 128]), R=["a_tabrow"], W=["a_QaT"])
                K.copy(QaF[64:65, q_], bc(tabrow[64:65, :].unsqueeze(2), [1, 4, 128]), R=["a_tabrow"], W=["a_QaF"])
            for kv, w1 in enumerate((w1k, w1v)):
                for hc in range(2):
                    for s_ in range(32):
                        K.mm(PW[0][:, (kv * 2 + hc) * 2:(kv * 2 + hc) * 2 + 1], w1[:, s_, hc * 128:(hc + 1) * 128],
                             posT[:, s_:s_ + 1], start=(s_ == 0), stop=(s_ == 31), R=["a_w1k", "a_w1v", "a_posT"],
                             W=[("pw", 0)])
            K.copy(hbias[:], PW[0][:, 0:8].rearrange("p (a b) -> p a b", b=2)[:, :, 0], R=[("pw", 0)], W=["a_hbias"])
            a_raw = sb("a_raw", [64, TT])
            a_sq = sb("a_sq", [64, TT])
            a_rs = sb("a_rs", [64, TT])
            hact = sb("a_hact", [128, 2, 2, 32], BF16)
            cst = sb("a_cst", [64, 32])
            csq2 = sb("a_csq2", [64, 32])
            crs2 = sb("a_crs2", [64, 32])
            vtm = sb("a_vtm", [128, 204])
            Eb = [sb("a_E%d" % i, [128, 4, 128]) for i in range(2)]
            Tb = [sb("a_T%d" % i, [128, 4, 128]) for i in range(2)]
            Pb = [sb("a_P%d" % i, [128, 4, 128], BF16) for i in range(2)]
            Pc = [sb("a_Pc%d" % i, [128, 4, 128]) for i in range(2)]
            scr = sb("a_scr", [128, 64])
            sc2 = sb("a_sc2", [128, 64])
            v8 = sb("a_v8", [128, 8])
            mskb = sb("a_mskb", [128, 64], BF16)
            mT = sb("a_mT", [64, 128], BF16)
            rden = sb("a_rden", [128, 4])
            coef = sb("a_coef", [128, 4])
            ya = sb("a_ya", [128, 4, 64])
            ytmp = sb("a_ytmp", [128, 4, 64])
            yss = sb("a_yss", [128, 1])

        def nsa_tile(tt):
            t0 = tt * TT
            tsl = slice(t0, t0 + TT)
            c0 = OFF["a_k_cmp"]
            for nm_, col, dst in (("k", OFF["a_k_cmp"], kcmpT), ("v", OFF["a_v_cmp"], vcmpT)):
                pj = PJ[pjn[0] % len(PJ)]
                kk = ("pj", pjn[0] % len(PJ))
                pjn[0] += 1
                for kc in range(DC):
                    K.mm(pj[0:64, 0:TT], wins[:, kc, col:col + 64], hb[:, kc, :], start=(kc == 0), stop=(kc == DC - 1),
                         R=[("win", kc), ("hb", kc)], W=[kk])
                g_ = G_KVC
                bcol = bfm[0:64, G_KVC:G_KVC + 1] if nm_ == "k" else bfmv[:, 0:1]
                K.act(dst[:, tsl], pj[0:64, 0:TT], AF.Identity, R=[kk, "bfm", "a_bfmv"], W=["a_" + nm_ + "cmpT"], bias=bcol,
                      scale=1.0)

            def normed(g, dst, dkey, gcol, split=False):
                proj_fm(g, a_raw[:], ["a_raw"])
                K.tt(a_sq[:], a_raw[:], a_raw[:], ALU.mult, R=["a_raw"], W=["a_sq"])
                K.mm(PST[0:64, 0:TT], onesf[0:64, 0:64], a_sq[:], R=["a_sq", "ones_f"], W=[PSTK])
                K.act(a_rs[:], PST[0:64, 0:TT], AF.Sqrt, R=[PSTK, "epsc"], W=["a_rs"], bias=consts["eps"][0:64, :],
                      scale=1.0 / 64)
                K.recip(a_rs[:], a_rs[:], R=["a_rs"], W=["a_rs"])
                for d_, dk_ in zip(dst, dkey):
                    if split:
                        K.stt(d_, a_raw[:].rearrange("p (a b) -> p a b", b=128), gcol,
                              a_rs[:].rearrange("p (a b) -> p a b", b=128), ALU.mult, ALU.mult,
                              R=["a_raw", "a_rs", "a_qng"], W=[dk_])
                    else:
                        K.stt(d_, a_raw[:], gcol, a_rs[:], ALU.mult, ALU.mult, R=["a_raw", "a_rs", "a_qng"], W=[dk_])

            normed(G_KSLC, [KsT[0:64, tsl]], ["a_KsT"], qng[:, 1:2])
            normed(G_KWIN, [KwT[0:64, tsl]], ["a_KwT"], qng[:, 1:2])
            for h in range(4):
                normed(G_AQ + h, [QaT[0:64, :, h, :], QaF[0:64, :, h, :]], ["a_QaT", "a_QaF"], qng[:, 0:1], split=True)
            for q in range(TT // 128):
                qg = t0 // 128 + q
                for kc in range(DC):
                    K.mm(PTK[0][:, 0:204], hb[:, kc, q * 128:(q + 1) * 128], wins[:, kc, OFF["a_v_slc"]:OFF["a_v_slc"] + 204],
                         start=(kc == 0), stop=(kc == DC - 1), R=[("win", kc), ("hb", kc)], W=[("ptk", 0)])
                K.tt(vtm[:], PTK[0][:, 0:204], btm_a[:], ALU.add, R=[("ptk", 0), "a_btm"], W=["a_vtm"])
                K.copy(Vs[:, qg, 0:64], vtm[:, 0:64], R=["a_vtm"], W=["a_Vs"])
                K.copy(Vw[:, qg, 0:64], vtm[:, 128:192], R=["a_vtm"], W=["a_Vw"])
                K.act(gsb[:, q, :], vtm[:, 192:204], AF.Sigmoid, R=["a_vtm"], W=["a_gsb"])
            nb0 = 0 if t0 == 0 else t0 // 16 - 1
            nb1 = (t0 + TT - 32) // 16 + 1
            nn = nb1 - nb0
            for kv, (w1, src, w2) in enumerate(((w1k, kcmpT, w2k), (w1v, vcmpT, w2v))):
                for hc in range(2):
                    for s_ in range(32):
                        K.mm(PW[0][:, 0:nn], w1[:, s_, hc * 128:(hc + 1) * 128],
                             src[:, 16 * nb0 + s_:16 * nb0 + s_ + 16 * (nn - 1) + 1:16], start=(s_ == 0), stop=(s_ == 31),
                             R=["a_w1k", "a_w1v", "a_kcmpT", "a_vcmpT"], W=[("pw", 0)])
                    K.act(hact[:, kv, hc, 0:nn], PW[0][:, 0:nn], AF.Silu, R=[("pw", 0), "a_hbias"], W=["a_hact"],
                          bias=hbias[:, kv * 2 + hc:kv * 2 + hc + 1], scale=1.0)
                for hc in range(2):
                    K.mm(PW[0][0:64, 64:64 + nn], w2[:, hc, :], hact[:, kv, hc, 0:nn], start=(hc == 0), stop=(hc == 1),
                         R=["a_w2k", "a_w2v", "a_hact"], W=[("pw", 0)])
                if kv == 0:
                    K.copy(cst[:, 0:nn], PW[0][0:64, 64:64 + nn], R=[("pw", 0)], W=["a_cst"])
                    K.tt(csq2[:, 0:nn], cst[:, 0:nn], cst[:, 0:nn], ALU.mult, R=["a_cst"], W=["a_csq2"])
                    K.mm(PW[0][0:64, 128:128 + nn], onesf[0:64, 0:64], csq2[:, 0:nn], R=["a_csq2", "ones_f"], W=[("pw", 0)])
                    K.act(crs2[:, 0:nn], PW[0][0:64, 128:128 + nn], AF.Sqrt, R=[("pw", 0), "epsc"], W=["a_crs2"],
                          bias=consts["eps"][0:64, :], scale=1.0 / 64)
                    K.recip(crs2[:, 0:nn], crs2[:, 0:nn], R=["a_crs2"], W=["a_crs2"])
                    K.stt(kcT[0:64, nb0:nb0 + nn], cst[:, 0:nn], qng[:, 1:2], crs2[:, 0:nn], ALU.mult, ALU.mult,
                          R=["a_cst", "a_crs2", "a_qng"], W=["a_kcT"])
                else:
                    K.copy(vcT[:, nb0:nb0 + nn], PW[0][0:64, 64:64 + nn], R=[("pw", 0)], W=["a_vcT"])
            for m in range(NM):
                K.tr(PW[0][:, 256 + m * 64:256 + (m + 1) * 64], vcT[:, m * 128:(m + 1) * 128], IDENT[0:64, 0:64],
                     R=["a_vcT", "cmask"], W=[("pw", 0)])
                K.copy(vcx[:, m, 0:64], PW[0][:, 256 + m * 64:256 + (m + 1) * 64], R=[("pw", 0)], W=["a_vcx"])
            for q in range(TT // 128):
                nsa_qtile(t0 // 128 + q, q)

        def nsa_qtile(i, q):
            qs = slice(q * 128, (q + 1) * 128)
            qT = QaT[:, q]
            qF = QaF[:, q]
            ek = [0]

            def scores(lhsT, rhs, bias_ap, dst, dkey, rkeys):
                k2 = ek[0] % 2
                ek[0] += 1
                ps = PS2[k2]
                K.mm(ps[:, 0:512], lhsT, rhs, R=rkeys, W=[("ps", k2)])
                if bias_ap is not None:
                    K.tt(Tb[k2][:], v4(ps[:, 0:512]), bias_ap, ALU.add, R=[("ps", k2), "a_bw4"] + [("a_bias", x) for x in range(19)],
                         W=[("a_T", k2)])
                    K.act(dst, Tb[k2][:], AF.Exp, R=[("a_T", k2)], W=[dkey])
                else:
                    K.act(dst, v4(ps[:, 0:512]), AF.Exp, R=[("ps", k2)], W=[dkey])

            first = True
            mlist = [m for m in range(NM) if i - 16 * m >= 0]
            for mi, m in enumerate(mlist):
                ip = i - 16 * m
                k2 = mi % 2
                b_ap = v4(biasT[:, ip, :]) if ip <= 16 else None
                scores(kcT[:, m * 128:(m + 1) * 128], qF.rearrange("p a b -> p (a b)"), b_ap, Pc[k2][:], ("a_Pc", k2),
                       ["a_kcT", "a_QaF"])
                for h in range(4):
                    K.mm(POA[:, h * 65:(h + 1) * 65], Pc[k2][:, h, :], vcx[:, m, :], start=(first and h == 0),
                         stop=(mi == len(mlist) - 1 and h == 3), R=[("a_Pc", k2), "a_vcx"], W=["poa"], skip_group_check=True)
                for h in range(4):
                    K.mm(POB[:, h * 64:(h + 1) * 64], Pc[k2][:, h, :], ovl[:, m, :], start=(first and h == 0),
                         stop=(mi == len(mlist) - 1 and h == 3), R=[("a_Pc", k2), "a_ovl"], W=[PSTK], skip_group_check=True)
                first = False
            oa = POA[:, 0:260].rearrange("p (a b) -> p a b", a=4)
            K.ts(rden[:], oa[:, :, 64], 1e-30, None, ALU.max, R=["poa"], W=["a_rden"])
            K.recip(rden[:], rden[:], R=["a_rden"], W=["a_rden"])
            K.tt(coef[:], rden[:], gsb[:, q, 0:12:3], ALU.mult, R=["a_rden", "a_gsb"], W=["a_coef"])
            K.tt(ya[:], oa[:, :, 0:64], bc(coef[:, :].unsqueeze(2), [128, 4, 64]), ALU.mult, R=["poa", "a_coef"], W=["a_ya"])
            for h in range(4):
                if h == 0:
                    K.ts(scr[:, 0:NB], POB[:, 0:NB], rden[:, 0:1], None, ALU.mult, R=[PSTK, "a_rden"], W=["a_scr"])
                else:
                    K.stt(scr[:, 0:NB], POB[:, h * 64:h * 64 + NB], rden[:, h:h + 1], scr[:, 0:NB], ALU.mult, ALU.add,
                          R=[PSTK, "a_rden", "a_scr"], W=["a_scr"])
            K.tt(scr[:, 0:NB], scr[:, 0:NB], keepc[:, 0, NB - 2 * i:2 * NB - 2 * i], ALU.mult, R=["a_scr", "a_keepc"], W=["a_scr"])
            K.tt(scr[:, 0:NB], scr[:, 0:NB], keepc[:, 1, NB - 2 * i:2 * NB - 2 * i], ALU.add, R=["a_scr", "a_keepc"], W=["a_scr"])
            K.memset(scr[:, 0:1], 1e6, W=["a_scr"])
            K.P.add("dve", lambda e: e.max(v8[:], scr[:, 0:NB]), R=["a_scr"], W=["a_v8"])
            K.P.add("dve", lambda e: e.match_replace(sc2[:, 0:NB], v8[:], scr[:, 0:NB], -3e6), R=["a_scr", "a_v8"], W=["a_sc2"])
            K.P.add("dve", lambda e: e.max(v8[:], sc2[:, 0:NB]), R=["a_sc2"], W=["a_v8"])
            K.ts(mskb[:, 0:NB], scr[:, 0:NB], v8[:, 7:8], None, ALU.is_ge, R=["a_scr", "a_v8"], W=["a_mskb"])
            K.mm(PMX[0:NB, 128:256], mskb[:, 0:NB], identb[:], R=["a_mskb", "a_identb"], W=["pmx"])
            K.copy(mT[0:NB, :], PMX[0:NB, 128:256], R=["pmx"], W=["a_mT"])
            for j in range(i + 1):
                k2 = j % 2
                if j == i:
                    b_ap = v4(biasT[:, 17, :])
                elif j == i - 1:
                    b_ap = v4(biasT[:, 18, :])
                else:
                    b_ap = None
                scores(KsT[:, j * 128:(j + 1) * 128], qT.rearrange("p a b -> p (a b)"), b_ap, Eb[k2][:], ("a_E", k2),
                       ["a_KsT", "a_QaT"])
                K.mm(PMX[:, 0:128], expc[0:NB, j * 128:(j + 1) * 128], mT[0:NB, :], R=["a_expc", "a_mT"], W=["pmx"])
                K.tt(Pb[k2][:], Eb[k2][:], bc(PMX[:, 0:128].unsqueeze(1), [128, 4, 128]), ALU.mult, R=[("a_E", k2), "pmx"],
                     W=[("a_P", k2)])
                for h in range(4):
                    K.mm(POA[:, h * 65:(h + 1) * 65], Pb[k2][:, h, :], Vs[:, j, :], start=(j == 0 and h == 0),
                         stop=(j == i and h == 3), R=[("a_P", k2), "a_Vs"], W=["poa"], skip_group_check=True)
            K.ts(rden[:], oa[:, :, 64], 1e-30, None, ALU.max, R=["poa"], W=["a_rden"])
            K.recip(rden[:], rden[:], R=["a_rden"], W=["a_rden"])
            K.tt(coef[:], rden[:], gsb[:, q, 1:12:3], ALU.mult, R=["a_rden", "a_gsb"], W=["a_coef"])
            K.tt(ytmp[:], oa[:, :, 0:64], bc(coef[:, :].unsqueeze(2), [128, 4, 64]), ALU.mult, R=["poa", "a_coef"], W=["a_ytmp"])
            K.tt(ya[:], ya[:], ytmp[:], ALU.add, R=["a_ya", "a_ytmp"], W=["a_ya"])
            jl = [j for j in range(i - 4, i + 1) if j >= 0]
            for ji, j in enumerate(jl):
                k2 = j % 2
                if j == i:
                    b_ap = v4(biasT[:, 17, :])
                elif j == i - 1:
                    b_ap = v4(biasT[:, 18, :])
                elif j == i - 4:
                    b_ap = bc(bw4[:, :].unsqueeze(1), [128, 4, 128])
                else:
                    b_ap = None
                scores(KwT[:, j * 128:(j + 1) * 128], qT.rearrange("p a b -> p (a b)"), b_ap, Pb[k2][:], ("a_P", k2),
                       ["a_KwT", "a_QaT"])
                for h in range(4):
                    K.mm(POA[:, h * 65:(h + 1) * 65], Pb[k2][:, h, :], Vw[:, j, :], start=(ji == 0 and h == 0),
                         stop=(j == i and h == 3), R=[("a_P", k2), "a_Vw"], W=["poa"], skip_group_check=True)
            K.ts(rden[:], oa[:, :, 64], 1e-30, None, ALU.max, R=["poa"], W=["a_rden"])
            K.recip(rden[:], rden[:], R=["a_rden"], W=["a_rden"])
            K.tt(coef[:], rden[:], gsb[:, q, 2:12:3], ALU.mult, R=["a_rden", "a_gsb"], W=["a_coef"])
            K.tt(ytmp[:], oa[:, :, 0:64], bc(coef[:, :].unsqueeze(2), [128, 4, 64]), ALU.mult, R=["poa", "a_coef"], W=["a_ytmp"])
            K.tt(ya[:], ya[:], ytmp[:], ALU.add, R=["a_ya", "a_ytmp"], W=["a_ya"])
            yaf = ya[:].rearrange("p a b -> p (a b)")
            K.tt(ytmp[:], ya[:], ya[:], ALU.mult, R=["a_ya"], W=["a_ytmp"])
            K.P.add("dve", lambda e: e.tensor_reduce(yss[:], ytmp[:].rearrange("p a b -> p (a b)"), AX.X, ALU.add),
                    R=["a_ytmp"], W=["a_yss"])
            K.act(yss[:], yss[:], AF.Sqrt, R=["a_yss", "epsc"], W=["a_yss"], bias=consts["eps"][:], scale=1.0 / 256)
            K.recip(yss[:], yss[:], R=["a_yss"], W=["a_yss"])
            K.stt(yaf, yaf, yss[:, 0:1], mg0[:], ALU.mult, ALU.mult, R=["a_ya", "a_yss", "a_mg0"], W=["a_ya"])
            for pr in range(2):
                K.tr(PMX[:, 256 + pr * 128:256 + (pr + 1) * 128], yaf[:, pr * 128:(pr + 1) * 128], IDENT, R=["a_ya", "cmask"],
                     W=["pmx"])
                K.copy(yt[:, pr, qs], PMX[:, 256 + pr * 128:256 + (pr + 1) * 128], R=["pmx"], W=[("yt", pr)], eng="act")


        for tt in range(NT):
            tsl = slice(tt * TT, (tt + 1) * TT)
            K.dma(xt[:], xsv[:, :, tsl], R=[("xd", sname, tt)], W=[("xt", dc) for dc in range(DC)])
            K.act(sq[:], xt[:], AF.Square, R=[("xt", dc) for dc in range(DC)], W=["sq"])
            for dc in range(DC):
                K.mm(PST[:, 0:TT], consts["ones_bf"][:], sq[:, dc, :], start=(dc == 0), stop=(dc == DC - 1),
                     R=["sq"], W=[PSTK])
            K.act(rs[:], PST[:, 0:TT], AF.Sqrt, R=[PSTK, "epsc"], W=["rs"], bias=consts["eps"][:], scale=1.0 / D)
            K.recip(rs[:], rs[:], R=["rs"], W=["rs"])
            for dc in range(DC):
                k2 = dc % 2
                K.stt(sa[k2][:], xt[:, dc, :], gs[:, s, dc:dc + 1], rs[:], ALU.mult, ALU.mult,
                      R=[("xt", dc), "rs", "gs"], W=[("sa", k2)])
                K.act(hb[:, dc, :], sa[k2][:], AF.Identity, R=[("sa", k2), "mod"], W=[("hb", dc)],
                      bias=shift[:, s * 24 + dc:s * 24 + dc + 1], scale=1.0)
            if mode == 1:
                for r in range(2, DC):
                    if not (("B" in enable and r in (2, 3)) or ("C" in enable and r in (4, 5)) or ("D" in enable and r in (6, 7))):
                        K.memset(yt[:, r, :], 0.0, W=[("yt", r)], eng="pool")
            else:
                K.dma(yt[:, 2:DC, :], yscr.rearrange("(dc p) t -> p dc t", p=128)[:, 2:DC, tsl],
                      R=[("yscr", tt * TT // 256 + q) for q in range(TT // 256)], W=[("yt", r) for r in range(2, DC)])
                if "A" not in enable:
                    for r in range(2):
                        K.memset(yt[:, r, :], 0.0, W=[("yt", r)], eng="pool")
            if "B" in enable:
                for ch in range(2):
                    proj_fm(G_BB + ch, bbt[:], ["bbt"])
                    proj_fm(G_BC + ch, cct[:], ["cct"])
                    proj_fm(G_BX + ch, cvt[:], ["cvt"])
                    K.tt(ub[ch][:, 2:2 + TT], cct[:], cvt[:], ALU.mult, R=["cct", "cvt"], W=[("ub", ch)])
                    K.ts(cvt[:], ub[ch][:, 2:2 + TT], scw[:, ch, 2:3], None, ALU.mult, R=[("ub", ch), "scw"], W=["cvt"])
                    K.stt(cvt[:], ub[ch][:, 1:1 + TT], scw[:, ch, 1:2], cvt[:], ALU.mult, ALU.add,
                          R=[("ub", ch), "scw", "cvt"], W=["cvt"])
                    K.stt(cvt[:], ub[ch][:, 0:TT], scw[:, ch, 0:1], cvt[:], ALU.mult, ALU.add,
                          R=[("ub", ch), "scw", "cvt"], W=["cvt"])
                    K.tt(ybt[ch][:], bbt[:], cvt[:], ALU.mult, R=["bbt", "cvt"], W=[("ybt", ch)])
                    K.copy(ub[ch][:, 0:2], ub[ch][:, TT:TT + 2], R=[("ub", ch)], W=[("ub", ch)])
                    K.act(sq[:, ch, :], ybt[ch][:], AF.Square, R=[("ybt", ch)], W=["sq"])
                for ch in range(2):
                    K.mm(PST[:, 0:TT], consts["ones_bf"][:], sq[:, ch, :], start=(ch == 0), stop=(ch == 1),
                         R=["sq"], W=[PSTK])
                K.act(sa[0][:], PST[:, 0:TT], AF.Sqrt, R=[PSTK, "epsc"], W=[("sa", 0)], bias=consts["eps"][:],
                      scale=1.0 / 256)
                K.recip(sa[0][:], sa[0][:], R=[("sa", 0)], W=[("sa", 0)])
                for ch in range(2):
                    K.stt(yt[:, 2 + ch, :], ybt[ch][:], mixg[:, 1, ch:ch + 1], sa[0][:], ALU.mult, ALU.mult,
                          R=[("ybt", ch), "mixg", ("sa", 0)], W=[("yt", 2 + ch)])
            if "D" in enable:
                for h in range(4):
                    proj_fm(G_DQ + h, QmT[:, h, :], ["QmT"])
                    proj_fm(G_DK + h, KmT[:, h, :], ["KmT"], scale=0.125, bias=bfm8[0:64, h:h + 1])
            if "C" in enable:
                gdn_tile()
            for c in range(NCH):
                if "D" in enable:
                    mlstm_chunk(c)
                if "C" in enable:
                    gdn_chunk(c)
            if mode == 1:
                K.dma(yscr.rearrange("(dc p) t -> p dc t", p=128)[:, 2:DC, tsl], yt[:, 2:DC, :],
                      R=[("yt", r) for r in range(2, DC)], W=[("yscr", tt * TT // 256 + q) for q in range(TT // 256)])
                continue
            if "A" in enable:
                nsa_tile(tt)
                K.dma(yscr.rearrange("(dc p) t -> p dc t", p=128)[:, 0:2, tsl], yt[:, 0:2, :],
                      R=[("yt", r) for r in range(2)], W=[("yscrA", tt)])
            for dc in range(DC):
                pj = PJ[pjn[0] % len(PJ)]
                kk = ("pj", pjn[0] % len(PJ))
                pjn[0] += 1
                for fc in range(DC):
                    K.mm(pj[:, 0:TT], wouts[:, fc, dc * 128:(dc + 1) * 128], yt[:, fc, :], start=(fc == 0),
                         stop=(fc == DC - 1), R=[("wout", fc), ("yt", fc)], W=[kk])
                K.stt(xt[:, dc, :], pj[:, 0:TT], gate[:, s, dc:dc + 1], xt[:, dc, :], ALU.mult, ALU.add,
                      R=[kk, "gate", ("xt", dc)], W=[("xt", dc)])
            K.dma(xdv[:, :, tsl], xt[:], R=[("xt", dc) for dc in range(DC)], W=[("xd", dname, tt)])
    P.barrier()


def t5_thresholds():
    def bucket(n):
        if n < 16:
            return n
        nf = np.float32(n)
        v = np.log(nf / np.float32(16)) / np.float32(math.log(128 / 16)) * np.float32(16)
        return min(16 + int(np.float32(v)), 31)
    bs = [bucket(n) for n in range(0, 400)]
    return [min(n for n in range(400) if bs[n] >= b) for b in range(32)]


def bias_build(K, t5_table, dist_d, bias_scr, consts, es_ext=None):
    nc = K.nc
    lo = t5_thresholds()
    with ExitStack() as es_own:
        es = es_ext if es_ext is not None else es_own
        tb = _sb(es, nc, "bb_tb", [128, 32, 4], F32)
        ndl = _sb(es, nc, "bb_ndl", [128, 31, 4], F32)
        dtl = [_sb(es, nc, "bb_dt%d" % i, [128, 128], F32) for i in range(2)]
        acc = [_sb(es, nc, "bb_acc%d" % i, [128, 4, 128], F32) for i in range(2)]
        tmp = [_sb(es, nc, "bb_tmp%d" % i, [128, 4, 128], F32) for i in range(2)]
        K.dma(tb[:].rearrange("p b h -> p (b h)"), t5_table.rearrange("b h -> (b h)").partition_broadcast(128), W=["bb_tb"])
        K.tt(ndl[:], tb[:, 0:31, :], tb[:, 1:32, :], ALU.subtract, R=["bb_tb"], W=["bb_ndl"])
        for q in range(19):
            k2 = q % 2
            K.dma(dtl[k2][:], dist_d[q], W=[("bb_dt", k2)])
            dbc = bc(dtl[k2][:, :].unsqueeze(1), [128, 4, 128])
            for b in range(1, 32):
                dst = acc[k2] if b == 1 else tmp[b % 2]
                dk = ("bb_acc", k2) if b == 1 else ("bb_tmp", b % 2)
                K.stt(dst[:], dbc, float(lo[b]), bc(ndl[:, b - 1, :].unsqueeze(2), [128, 4, 128]), ALU.is_lt, ALU.mult,
                      R=[("bb_dt", k2), "bb_ndl"], W=[dk])
                if b > 1:
                    K.tt(acc[k2][:], acc[k2][:], dst[:], ALU.add, R=[("bb_acc", k2), dk], W=[("bb_acc", k2)])
            K.ts(tmp[0][:], dbc, 0.0, -30000.0, ALU.is_lt, ALU.mult, R=[("bb_dt", k2)], W=[("bb_tmp", 0)])
            K.tt(acc[k2][:], acc[k2][:], tmp[0][:], ALU.add, R=[("bb_acc", k2), ("bb_tmp", 0)], W=[("bb_acc", k2)])
            K.dma(bias_scr[q], acc[k2][:].rearrange("p a b -> p (a b)"), R=[("bb_acc", k2)], W=[("bias_scr", q)])
    if es_ext is None:
        K.P.barrier()


def build(T=4096, TT=512, layers=2, debug_y=False, enable="ABCD"):
    nc = bass.Bass("TRN2", target_bir_lowering=False)
    K = KB(nc)
    P = K.P
    dt = lambda name, shape, kind="ExternalInput", d=F32: nc.dram_tensor(name, list(shape), d, kind=kind).ap()
    xT = dt("xT", [D, T])
    cT = dt("cT", [128, DC])
    ada_w = dt("ada_w", [2, D, 9 * D])
    ada_bT = dt("ada_bT", [2, 128, 72])
    normgT = dt("normgT", [2, 128, 3, DC])
    ffn_w13 = [dt("ffn1_w13", [2, D, 2 * DFF]), dt("ffn2_w13", [2, D, 2 * DFF])]
    ffn_w2 = [dt("ffn1_w2", [2, DFF, D]), dt("ffn2_w2", [2, DFF, D])]
    outT = dt("outT", [D, T], kind="ExternalOutput")
    Win = {
        "w_in": dt("w_in", [2, D, D_IN]), "b_in": dt("b_in", [2, D_IN]), "b_fm": dt("b_fm", [2, 128, NFM]),
        "w_out": dt("w_out", [2, D, D]), "sc_conv_wT": dt("sc_conv_wT", [2, 128, 2, 3]),
        "mixgT": dt("mixgT", [2, 128, 2, 2]), "mlstm_f_bias": dt("mlstm_f_bias", [2, 4]),
        "mlstm_norm_g": dt("mlstm_norm_g", [2, 64]),
        "gdn_conv_wT": dt("gdn_conv_wT", [2, 64, 12, 4]), "gdn_A_log": dt("gdn_A_log", [2, 4]),
        "gdn_dt_bias": dt("gdn_dt_bias", [2, 4]), "gdn_norm_g": dt("gdn_norm_g", [2, 64]),
        "cmp_k_w1": dt("cmp_k_w1", [2, 2048, 256]), "cmp_v_w1": dt("cmp_v_w1", [2, 2048, 256]),
        "cmp_k_w2": dt("cmp_k_w2", [2, 256, 64]), "cmp_v_w2": dt("cmp_v_w2", [2, 256, 64]),
        "cmp_posT": dt("cmp_posT", [2, 64, 32]), "qkng": dt("qkng", [2, 64, 2]), "mix_norm_g0": dt("mix_norm_g0", [2, 256]),
    }
    NB_ = T // 64
    NM_ = (T // 16 + 127) // 128
    Wsh = {
        "t5_table": dt("t5_table", [32, 4]), "expc": dt("expc", [NB_, T]), "keepadd": dt("keepadd", [128, 2, 2 * NB_]),
        "ovl": dt("ovl", [128, NM_, 64]), "bw4": dt("bw4", [128, 128]),
        "bias_scr": dt("bias_scr", [19, 128, 512], kind="Internal"),
    }
    dist_d = dt("dist_tiles", [19, 128, 128])
    cmask_d = dt("cmask", [128, CM_N])
    yscr = dt("ydbg", [D, T], kind="ExternalOutput" if debug_y else "Internal", d=BF16)
    xa = dt("xa_scr", [D, T], kind="Internal")
    xb = dt("xb_scr", [D, T], kind="Internal")

    with ExitStack() as es:
        consts = {
            "ones_bf": _sb(es, nc, "ones_bf", [128, 128], BF16),
            "eps": _sb(es, nc, "epsc", [128, 1], F32),
        }
        K.memset(consts["ones_bf"][:], 1.0, W=["ones_bf"])
        consts["ones_f"] = _sb(es, nc, "ones_f", [128, 128], F32)
        consts["one"] = _sb(es, nc, "onec", [128, 1], F32)
        consts["cmask"] = _sb(es, nc, "cmask_sb", [128, CM_N], F32)
        K.memset(consts["ones_f"][:], 1.0, W=["ones_f"])
        K.memset(consts["one"][:], 1.0, W=["onec"])
        K.dma(consts["cmask"][:], cmask_d, W=["cmask"])
        K.memset(consts["eps"][:], EPS, W=["epsc"])
        condT = _sb(es, nc, "condT", [128, DC, 2], F32)
        ctmp = _sb(es, nc, "ctmp", [128, DC], F32)
        K.memset(condT[:], 0.0, W=["condT"])
        K.dma(ctmp[:], cT, W=["ctmp"])
        K.act(condT[:, :, 0], ctmp[:], AF.Silu, R=["ctmp"], W=["condT"])
        mv = []
        for l in range(2):
            mv.append({
                "mod": _sb(es, nc, "mod%d" % l, [128, 72], F32),
                "gs": _sb(es, nc, "gs%d" % l, [128, 3, DC], F32),
                "gate": _sb(es, nc, "gate%d" % l, [128, 3, DC], F32),
            })
        adab = _sb(es, nc, "adab", [128, 2, 72], F32)
        normg = _sb(es, nc, "normg", [128, 2, 3, DC], F32)
        K.dma(adab[:], ada_bT.rearrange("l p j -> p l j"), W=["adab"])
        K.dma(normg[:], normgT.rearrange("l p s d -> p l s d"), W=["normg"])
        P.barrier()

        with ExitStack() as es0:
            for l in range(layers):
                mod_phase(K, es0, l, ada_w[l], adab[:, l, :], normg[:, l], condT, mv[l])
            if "A" in enable:
                bias_build(K, Wsh["t5_table"], dist_d, Wsh["bias_scr"], consts, es_ext=es0)
        P.barrier()
        cur = xT
        curname = "xT"
        for l in range(layers):
            last = (l == layers - 1)
            ffn_phase(K, l, 0, cur, curname, xa, "xa", ffn_w13[0][l], ffn_w2[0][l], mv[l], 0, T, TT, consts)
            Wd = {k: v[l] for k, v in Win.items()}
            Wd.update(Wsh)
            mixer_phase(K, l, xa, "xa", xb, "xb", Wd, mv[l], T, 256, consts, yscr, 1, enable=enable)
            mixer_phase(K, l, xa, "xa", xb, "xb", Wd, mv[l], T, 256, consts, yscr, 2, enable=enable)
            ffn_phase(K, l, 1, xb, "xb", outT if last else xa, "outT" if last else "xa", ffn_w13[1][l], ffn_w2[1][l], mv[l], 2, T, TT, consts)
            cur = xa
            curname = "xa"
        P.fence("sp", [("xd", "outT", tt) for tt in range(T // TT)])
        P.emit()
    return nc


def _cmask():
    m = np.zeros((128, CM_N), np.float32)
    m[:, CM_ID:CM_ID + 128] = np.eye(128, dtype=np.float32)
    k = np.arange(64)[:, None]
    i = np.arange(64)[None, :]
    m[0:64, CM_TRIU:CM_TRIU + 64] = (k <= i)
    m[0:64, CM_SU:CM_SU + 64] = (k < i)
    m[0:64, CM_SL:CM_SL + 64] = (k > i)
    m[0:64, CM_LI:CM_LI + 64] = (k >= i)
    return m


def prep_shared(inp, T=4096):
    f = lambda a: np.ascontiguousarray(np.asarray(a, dtype=np.float32))
    b_in = f(inp["b_in"])
    b_fm = np.zeros((2, 128, NFM), np.float32)
    for g, (c0, n) in enumerate(FM):
        b_fm[:, 0:n, g] = b_in[:, c0:c0 + n]
    sh = {
        "ada_w": f(inp["ada_w"]),
        "ada_bT": f(np.asarray(inp["ada_b"]).reshape(2, 72, 128).transpose(0, 2, 1)),
        "normgT": f(np.asarray(inp["norm_g"]).reshape(2, 3, 8, 128).transpose(0, 3, 1, 2)),
        "ffn1_w13": f(inp["ffn1_w13"]), "ffn2_w13": f(inp["ffn2_w13"]),
        "ffn1_w2": f(inp["ffn1_w2"]), "ffn2_w2": f(inp["ffn2_w2"]),
        "w_in": f(inp["w_in"]), "b_in": b_in, "b_fm": b_fm, "w_out": f(inp["w_out"]),
        "sc_conv_wT": f(np.asarray(inp["sc_conv_w"]).reshape(2, 3, 2, 128).transpose(0, 3, 2, 1)),
        "mixgT": f(np.asarray(inp["mix_norm_g"]).reshape(2, 2, 2, 128).transpose(0, 3, 1, 2)),
        "mlstm_f_bias": f(inp["mlstm_f_bias"]), "mlstm_norm_g": f(inp["mlstm_norm_g"]),
        "gdn_conv_wT": f(np.asarray(inp["gdn_conv_w"]).reshape(2, 4, 12, 64).transpose(0, 3, 2, 1)),
        "gdn_A_log": f(inp["gdn_A_log"]), "gdn_dt_bias": f(inp["gdn_dt_bias"]), "gdn_norm_g": f(inp["gdn_norm_g"]),
        "cmask": _cmask(),
        "cmp_k_w1": f(inp["cmp_k_w1"]), "cmp_v_w1": f(inp["cmp_v_w1"]), "cmp_k_w2": f(inp["cmp_k_w2"]),
        "cmp_v_w2": f(inp["cmp_v_w2"]), "cmp_posT": f(np.asarray(inp["cmp_pos"]).transpose(0, 2, 1)),
        "qkng": f(np.stack([np.asarray(inp["q_norm_g"]), np.asarray(inp["k_norm_g"])], axis=-1)),
        "mix_norm_g0": f(np.asarray(inp["mix_norm_g"])[:, 0]), "t5_table": f(inp["t5_table"]),
    }
    sh.update(_nsa_consts(T))
    return sh


def _nsa_consts(T):
    NB = T // 64
    NM = (T // 16 + 127) // 128
    c = np.arange(128)[:, None]
    r = np.arange(128)[None, :]
    dist = np.zeros((19, 128, 128), np.float32)
    for ip in range(17):
        dist[ip] = r - 16 * c + 128 * ip - 31
    dist[17] = r - c
    dist[18] = 128 + r - c
    expc = (np.arange(T)[None, :] // 64 == np.arange(NB)[:, None]).astype(np.float32)
    keep = np.ones((128, 2 * NB), np.float32)
    add = np.zeros((128, 2 * NB), np.float32)
    for rr in range(128):
        for x in range(2 * NB):
            rb = x - NB
            if rr < 64:
                forced, invalid = rb in (-1, 0), rb > 0
            else:
                forced, invalid = rb in (0, 1), rb > 1
            if invalid:
                keep[rr, x], add[rr, x] = 0.0, -1e6
            elif forced:
                keep[rr, x], add[rr, x] = 0.0, 1e6
    ovl = np.zeros((128, NM, 64), np.float32)
    for m in range(NM):
        ci = 128 * m + np.arange(128)[:, None]
        bj = np.arange(64)[None, :]
        ovl[:, m, :] = ((ci * 16 < (bj + 1) * 64) & (ci * 16 + 32 > bj * 64) & (bj < NB) & (ci < T // 16 - 1))
    bw4 = np.where(c > r, 0.0, -30000.0).astype(np.float32)
    return {"dist_tiles": dist, "expc": expc, "keepadd": np.ascontiguousarray(np.stack([keep, add], axis=1)),
            "ovl": ovl, "bw4": bw4}


def prep_core(inp, b):
    x = np.asarray(inp["x"], dtype=np.float32)
    c = np.asarray(inp["c"], dtype=np.float32)
    return {"xT": np.ascontiguousarray(x[b].T), "cT": np.ascontiguousarray(c[b].reshape(8, 128).T)}


_NC_CACHE = {}


def kernel(**inputs):
    x = np.asarray(inputs["x"])
    B, T, _ = x.shape
    if T not in _NC_CACHE:
        _NC_CACHE[T] = build(T=T)
    nc = _NC_CACHE[T]
    sh = prep_shared(inputs, T)
    in_maps = []
    for b in range(B):
        m = dict(sh)
        m.update(prep_core(inputs, b))
        in_maps.append(m)
    res = run_bass_kernel_spmd(nc, in_maps, core_ids=list(range(B)))
    out = np.stack([np.asarray(r["outT"]).T for r in res.results], axis=0)
    return np.ascontiguousarray(out.astype(np.float32))
```

```python
import math
import os
GSTOP = float(os.environ.get('GSTOP', '99'))
from contextlib import ExitStack
import numpy as np
import concourse.bass as bass
import concourse.mybir as mybir
from concourse.bass_utils import run_bass_kernel_spmd

F32 = mybir.dt.float32
BF16 = mybir.dt.bfloat16
AF = mybir.ActivationFunctionType
ALU = mybir.AluOpType
AX = mybir.AxisListType

D = 1024
DC = 8
DFF = 2816
FC = 22
D_IN = 3484
EPS = 1e-6

ENGS = ("pe", "act", "dve", "pool", "sp")


class _Op:
    __slots__ = ("eng", "fn", "reads", "writes", "deps", "sig", "tok", "waits", "dma", "snap", "inc", "cost", "odeps", "idx")

    def __init__(self, eng, fn, reads, writes, dma):
        self.eng = eng
        self.fn = fn
        self.reads = reads
        self.writes = writes
        self.dma = dma
        self.deps = ()
        self.sig = False
        self.tok = None
        self.waits = ()
        self.snap = None
        self.inc = 1
        self.cost = 300.0
        self.odeps = ()
        self.idx = 0


class Prog:
    EPOCH = 20000
    NDMA = 12

    def __init__(self, nc):
        self.nc = nc
        self.ops = []
        self.last_w = {}
        self.readers = {}
        self.last_on = {}
        self.dma_ops = []

    EXCL = ("pj", "pw", "ptk", "pab", "po", "pst", "pmod", "ps", "poa", "pob", "pmx", "pd", "pc", "pow")

    def _excl(self, k):
        return (k[0] if isinstance(k, tuple) else k) in self.EXCL

    def add(self, eng, fn, R=(), W=(), dma=False, cost=300.0):
        xr = [k for k in R if self._excl(k)]
        if xr:
            R = [k for k in R if not self._excl(k)]
            W = list(W) + [k for k in xr if k not in W]
        op = _Op(eng, fn, tuple(R), tuple(W), dma)
        i = len(self.ops)
        deps = set()
        for k in op.reads:
            w = self.last_w.get(k)
            if w is not None:
                deps.add(w)
        for k in op.writes:
            w = self.last_w.get(k)
            if w is not None:
                deps.add(w)
            deps.update(self.readers.get(k, ()))
        for k in op.writes:
            self.last_w[k] = i
            self.readers[k] = []
        for k in op.reads:
            if k not in op.writes:
                self.readers.setdefault(k, []).append(i)
        op.deps = deps
        op.cost = cost
        self.ops.append(op)
        self.last_on[eng] = i
        if dma:
            self.dma_ops.append(i)
        return i

    def barrier(self):
        lasts = set(self.last_on.values()) | set(self.dma_ops)
        self.dma_ops = []
        for e in ENGS:
            op = _Op(e, None, (), (), False)
            op.deps = set(lasts)
            self.ops.append(op)
            self.last_on[e] = len(self.ops) - 1
        self.last_w = {}
        self.readers = {}

    def fence(self, eng, keys):
        self.add(eng, None, R=keys)

    def schedule(self):
        import heapq
        ops = self.ops
        n = len(ops)
        for i, op in enumerate(ops):
            op.idx = i
        order = []
        LAT = 250.0
        W = 24
        seg_start = 0
        i = 0
        segs = []
        while i < n:
            if ops[i].fn is None and not ops[i].dma and len(ops[i].reads) == 0 and len(ops[i].writes) == 0:
                j = i
                while j < n and ops[j].fn is None and not ops[j].dma:
                    j += 1
                segs.append((seg_start, i))
                segs.append((i, j))
                seg_start = j
                i = j
            else:
                i += 1
        segs.append((seg_start, n))
        for (a, b) in segs:
            if b <= a:
                continue
            if ops[a].fn is None and not ops[a].dma:
                order.extend(range(a, b))
                continue
            nrem = {}
            users = {}
            for k in range(a, b):
                dl = [d for d in ops[k].deps if d >= a]
                nrem[k] = len(dl)
                for d in dl:
                    users.setdefault(d, []).append(k)
            blev = {}
            for k in range(b - 1, a - 1, -1):
                m_ = 0.0
                for u in users.get(k, ()):
                    if blev[u] > m_:
                        m_ = blev[u]
                blev[k] = ops[k].cost + LAT + m_
            ready = {e: [] for e in ENGS}
            for k in range(a, b):
                if nrem[k] == 0:
                    heapq.heappush(ready[ops[k].eng], k)
            fin = {}
            free = {e: 0.0 for e in ENGS}
            left = b - a
            while left > 0:
                best = None
                for e in ENGS:
                    rl = ready[e]
                    if not rl:
                        continue
                    cands = heapq.nsmallest(W, rl)
                    for k in cands:
                        st = free[e]
                        for d in ops[k].deps:
                            if d >= a:
                                f = fin[d] + LAT
                                if f > st:
                                    st = f
                        key = (int(st / 250.0), -blev[k], k, st)
                        if best is None or key < best[0]:
                            best = (key, e, k)
                (_q, _b, k, st), e, _ = best
                ready[e].remove(k)
                heapq.heapify(ready[e])
                op = ops[k]
                if op.dma:
                    free[e] = st + 60.0
                    fin[k] = st + op.cost
                else:
                    free[e] = st + op.cost
                    fin[k] = st + op.cost
                order.append(k)
                left -= 1
                for u in users.get(k, ()):
                    nrem[u] -= 1
                    if nrem[u] == 0:
                        heapq.heappush(ready[ops[u].eng], u)
        return order

    def emit(self):
        nc = self.nc
        if os.environ.get("NOSCHED", "0") != "1":
            order = self.schedule()
            old = self.ops
            remap = {o: nidx for nidx, o in enumerate(order)}
            newops = [old[o] for o in order]
            for op in newops:
                op.deps = {remap[d] for d in op.deps}
            self.ops = newops
        ops = self.ops
        for i, op in enumerate(ops):
            if op.eng == "pe":
                op.deps = {d for d in op.deps if not (ops[d].eng == "pe" and not ops[d].dma)}
        ndma = 0
        slot_last = {}
        for i, op in enumerate(ops):
            if op.dma:
                slot = ndma % self.NDMA
                if slot in slot_last:
                    op.deps = set(op.deps) | {slot_last[slot]}
                slot_last[slot] = i
                op.tok = ("dma", slot, 16 * (ndma // self.NDMA + 1))
                ndma += 1
        for op in ops:
            for d in op.deps:
                ops[d].sig = True
        cnt = {e: 0 for e in ENGS}
        for op in ops:
            if op.sig and not op.dma:
                if op.fn is None:
                    continue
                cnt[op.eng] += 1
                c = cnt[op.eng]
                op.tok = (op.eng, (c - 1) // self.EPOCH, (c - 1) % self.EPOCH + 1)
        nep = {e: (cnt[e] + self.EPOCH - 1) // self.EPOCH for e in ENGS}
        sems = {}
        for e in ENGS:
            for k in range(max(nep[e], 0)):
                sems[(e, k)] = nc.alloc_semaphore("s_%s_%d" % (e, k))
        for s in range(min(self.NDMA, max(ndma, 1))):
            sems[("dma", s)] = nc.alloc_semaphore("s_dma_%d" % s)
        known = {e: {} for e in ENGS}
        for op in ops:
            kn = known[op.eng]
            waits = []
            stack = list(op.deps)
            seen = set()
            while stack:
                d = stack.pop()
                if d in seen:
                    continue
                seen.add(d)
                dop = ops[d]
                if dop.fn is None and not dop.dma:
                    if dop.eng == op.eng:
                        continue
                    stack.extend(dop.deps)
                    continue
                tk = dop.tok
                key = (tk[0], tk[1])
                if kn.get(key, 0) >= tk[2]:
                    continue
                waits.append((key, tk[2]))
                if dop.snap is not None:
                    for k2, v2 in dop.snap.items():
                        if kn.get(k2, 0) < v2:
                            kn[k2] = v2
                kn[key] = tk[2]
            wm = {}
            for key, v in waits:
                if wm.get(key, 0) < v:
                    wm[key] = v
            op.waits = tuple(wm.items())
            if op.tok is not None:
                op.snap = dict(kn)
        handles = {"pe": nc.tensor, "act": nc.scalar, "dve": nc.vector, "pool": nc.gpsimd, "sp": nc.sync}
        per = {e: [op for op in ops if op.eng == e] for e in ENGS}

        def run(e, eng):
            for op in per[e]:
                for key, v in op.waits:
                    eng.wait_ge(sems[key], v)
                if op.fn is None:
                    continue
                ins = op.fn(eng)
                if op.tok is not None:
                    tk = op.tok
                    ins.then_inc(sems[(tk[0], tk[1])], 16 if op.dma else 1)

        with nc.Block() as block:
            @block.tensor
            def _(eng):
                run("pe", eng)

            @block.scalar
            def _(eng):
                run("act", eng)

            @block.vector
            def _(eng):
                run("dve", eng)

            @block.gpsimd
            def _(eng):
                run("pool", eng)

            @block.sync
            def _(eng):
                run("sp", eng)
        self.stats = dict(n_ops=len(ops), cnt=cnt, ndma=ndma)


class KB:
    def __init__(self, nc):
        self.nc = nc
        self.P = Prog(nc)

    @staticmethod
    def _fs(ap):
        n = 1
        for d in ap.shape[1:]:
            n *= d
        return n

    def mm(self, out, lhsT, rhs, start=True, stop=True, R=(), W=(), **kw):
        passes = 2.0 if rhs.dtype == F32 else 1.0
        c = 40.0 + (self._fs(lhsT) * 0.85 + max(64, self._fs(rhs)) * 0.85) * passes
        return self.P.add("pe", lambda e: e.matmul(out, lhsT, rhs, start=start, stop=stop, **kw), R, W, cost=c)

    def tr(self, out, in_, ident, R=(), W=()):
        return self.P.add("pe", lambda e: e.transpose(out, in_, ident), R, W, cost=120.0)

    def act(self, out, in_, func, R=(), W=(), bias=None, scale=None):
        kw = {}
        if bias is not None:
            kw["bias"] = bias
        if scale is not None:
            kw["scale"] = scale
        return self.P.add("act", lambda e: e.activation(out, in_, func, **kw), R, W, cost=260.0 + 0.75 * self._fs(in_))

    def tt(self, out, in0, in1, op, R=(), W=(), eng="dve"):
        return self.P.add(eng, lambda e: e.tensor_tensor(out, in0, in1, op), R, W, cost=150.0 + 1.45 * self._fs(out))

    def ts(self, out, in0, s1, s2, op0, op1=None, R=(), W=(), eng="dve"):
        if op1 is None:
            return self.P.add(eng, lambda e: e.tensor_scalar(out, in0, s1, None, op0), R, W, cost=150.0 + 1.1 * self._fs(out))
        return self.P.add(eng, lambda e: e.tensor_scalar(out, in0, s1, s2, op0, op1), R, W, cost=150.0 + 1.1 * self._fs(out))

    def stt(self, out, in0, scalar, in1, op0, op1, R=(), W=()):
        return self.P.add("dve", lambda e: e.scalar_tensor_tensor(out, in0, scalar, in1, op0, op1), R, W,
                          cost=150.0 + 1.1 * self._fs(out))

    def copy(self, out, in_, R=(), W=(), eng="dve"):
        if eng == "act":
            return self.P.add("act", lambda e: e.copy(out, in_), R, W, cost=260.0 + 0.75 * self._fs(out))
        return self.P.add(eng, lambda e: e.tensor_copy(out, in_), R, W, cost=150.0 + 0.9 * self._fs(out))

    def recip(self, out, in_, R=(), W=()):
        return self.P.add("dve", lambda e: e.reciprocal(out, in_), R, W, cost=200.0 + 2.6 * self._fs(out))

    def memset(self, ap, val, W=(), eng="dve"):
        return self.P.add(eng, lambda e: e.memset(ap, val), (), W, cost=110.0 + 1.05 * self._fs(ap))

    def dma(self, out, in_, R=(), W=(), eng="sp", **kw):
        nbytes = out.shape[0] * self._fs(out) * (2 if out.dtype == BF16 else 4)
        return self.P.add(eng, lambda e: e.dma_start(out, in_, **kw), R, W, dma=True, cost=2200.0 + nbytes / 120.0)


def _sb(es, nc, name, shape, dt):
    return es.enter_context(nc.sbuf_tensor(name, shape, dt))


def _ps(es, nc, name, shape, dt=F32):
    return es.enter_context(nc.psum_tensor(name, shape, dt))


def mod_phase(K, es_glob, lay, ada_w_l, ada_bT_l, normgT_l, condT, out):
    nc = K.nc
    P = K.P
    with ExitStack() as es_own:
        es = es_glob if es_glob is not None else es_own
        wt = [_sb(es, nc, "adaw%d_%d" % (lay, i), [128, DC, 1024], F32) for i in range(2)]
        pm = _ps(es, nc, "pmod%d" % lay, [128, 72 * 2], F32)
        awv = ada_w_l.rearrange("(kc p) n -> p kc n", p=128)
        for g in range(9):
            b = wt[g % 2]
            for kc in range(DC):
                K.dma(b[:, kc, :], awv[:, kc, g * 1024:(g + 1) * 1024], W=[("adaw", g % 2, kc)])
            for jj in range(8):
                j = g * 8 + jj
                for kc in range(DC):
                    K.mm(pm[:, 2 * j:2 * j + 2], b[:, kc, jj * 128:(jj + 1) * 128], condT[:, kc, :],
                         start=(kc == 0), stop=(kc == DC - 1), R=[("adaw", g % 2, kc), "condT"], W=["pmod"])
        mod = out["mod"]
        pmv = pm[:].rearrange("p (j two) -> p j two", two=2)[:, :, 0]
        K.tt(mod[:], pmv, ada_bT_l, ALU.add, R=["pmod", "adab"], W=["mod"])
        for s in range(3):
            K.stt(out["gs"][:, s, :], mod[:, s * 24 + 8:s * 24 + 16], 1.0, normgT_l[:, s, :], ALU.add, ALU.mult,
                  R=["mod", "normg"], W=["gs"])
            K.ts(out["gate"][:, s, :], mod[:, s * 24 + 16:s * 24 + 24], 0.5 if s != 1 else 1.0, None, ALU.mult,
                 R=["mod"], W=["gate"])
    if es_glob is None:
        P.barrier()


def ffn_phase(K, lay, which, x_src, sname, x_dst, dname, w13, w2, mv, s, T, TT, consts):
    nc = K.nc
    P = K.P
    NT = T // TT
    tg = "f%d%d" % (lay, which)
    with ExitStack() as es:
        w13s = _sb(es, nc, "w13_" + tg, [128, DC, 2 * DFF], BF16)
        w2s = _sb(es, nc, "w2_" + tg, [128, FC, D], BF16)
        xt = _sb(es, nc, "xt_" + tg, [128, DC, TT], F32)
        sq = _sb(es, nc, "sq_" + tg, [128, DC, TT], BF16)
        hb = _sb(es, nc, "hb_" + tg, [128, DC, TT], BF16)
        gb = _sb(es, nc, "gb_" + tg, [128, FC, TT], BF16)
        rs = _sb(es, nc, "rs_" + tg, [128, TT], F32)
        sa = [_sb(es, nc, "sa%d_" % i + tg, [128, TT], F32) for i in range(2)]
        pst = _ps(es, nc, "pst_" + tg, [128, TT])
        pa = [_ps(es, nc, "pa%d_" % i + tg, [128, TT]) for i in range(2)]
        pb = [_ps(es, nc, "pb%d_" % i + tg, [128, TT]) for i in range(2)]
        po = [_ps(es, nc, "po%d_" % i + tg, [128, TT]) for i in range(2)]
        w13v = w13.rearrange("(kc p) f -> p kc f", p=128)
        w2v = w2.rearrange("(fc p) d -> p fc d", p=128)
        for kc in range(DC):
            K.dma(w13s[:, kc, :], w13v[:, kc, :], W=[("w13", kc)], eng="pool")
        for fc in range(FC):
            K.dma(w2s[:, fc, :], w2v[:, fc, :], W=[("w2", fc)], eng="pool")
        xsv = x_src.rearrange("(dc p) t -> p dc t", p=128)
        xdv = x_dst.rearrange("(dc p) t -> p dc t", p=128)
        gs, shift, gate = mv["gs"], mv["mod"], mv["gate"]
        for tt in range(NT):
            tsl = slice(tt * TT, (tt + 1) * TT)
            K.dma(xt[:], xsv[:, :, tsl], R=[("xd", sname, tt)], W=[("xt", dc) for dc in range(DC)])
            K.act(sq[:], xt[:], AF.Square, R=[("xt", dc) for dc in range(DC)], W=["sq"])
            for dc in range(DC):
                K.mm(pst[:], consts["ones_bf"][:], sq[:, dc, :], start=(dc == 0), stop=(dc == DC - 1),
                     R=["sq"], W=["pst"])
            K.act(rs[:], pst[:], AF.Sqrt, R=["pst", "epsc"], W=["rs"], bias=consts["eps"][:], scale=1.0 / D)
            K.recip(rs[:], rs[:], R=["rs"], W=["rs"])
            for dc in range(DC):
                k2 = dc % 2
                K.stt(sa[k2][:], xt[:, dc, :], gs[:, s, dc:dc + 1], rs[:], ALU.mult, ALU.mult,
                      R=[("xt", dc), "rs", "gs"], W=[("sa", k2)])
                K.act(hb[:, dc, :], sa[k2][:], AF.Identity, R=[("sa", k2), "mod"], W=[("hb", dc)],
                      bias=shift[:, s * 24 + dc:s * 24 + dc + 1], scale=1.0)
            for fc in range(FC):
                k2 = fc % 2
                for half, pp in ((0, pa), (1, pb)):
                    for kc in range(DC):
                        K.mm(pp[k2][:], w13s[:, kc, half * DFF + fc * 128:half * DFF + (fc + 1) * 128],
                             hb[:, kc, :], start=(kc == 0), stop=(kc == DC - 1),
                             R=[("w13", kc), ("hb", kc)], W=[("pab", half, k2)])
                K.act(sa[k2][:], pa[k2][:], AF.Silu, R=[("pab", 0, k2)], W=[("sa", k2)])
                K.tt(gb[:, fc, :], sa[k2][:], pb[k2][:], ALU.mult, R=[("sa", k2), ("pab", 1, k2)], W=[("gb", fc)])
            for dc in range(DC):
                k2 = dc % 2
                for fc in range(FC):
                    K.mm(po[k2][:], w2s[:, fc, dc * 128:(dc + 1) * 128], gb[:, fc, :],
                         start=(fc == 0), stop=(fc == FC - 1), R=[("w2", fc), ("gb", fc)], W=[("po", k2)])
                K.stt(xt[:, dc, :], po[k2][:], gate[:, s, dc:dc + 1], xt[:, dc, :], ALU.mult, ALU.add,
                      R=[("po", k2), "gate", ("xt", dc)], W=[("xt", dc)])
            K.dma(xdv[:, :, tsl], xt[:], R=[("xt", dc) for dc in range(DC)], W=[("xd", dname, tt)])
    P.barrier()


OFF = {}
_o = 0
for _n, _w in (("a_q", 256), ("a_k_cmp", 64), ("a_v_cmp", 64), ("a_k_slc", 64), ("a_v_slc", 64), ("a_k_win", 64),
               ("a_v_win", 64), ("a_gate", 12), ("b_b", 256), ("b_c", 256), ("b_x", 256), ("c_q", 256), ("c_k", 256),
               ("c_v", 256), ("c_beta", 4), ("c_alpha", 4), ("c_z", 256), ("d_q", 256), ("d_k", 256), ("d_v", 256),
               ("d_i", 4), ("d_f", 4), ("d_o", 256)):
    OFF[_n] = _o
    _o += _w
assert _o == D_IN
FM = ([(OFF["a_q"] + 64 * h, 64) for h in range(4)] + [(OFF["a_k_cmp"], 128), (OFF["a_k_slc"], 64), (OFF["a_k_win"], 64)]
      + [(OFF["b_b"] + 128 * i, 128) for i in range(2)] + [(OFF["b_c"] + 128 * i, 128) for i in range(2)]
      + [(OFF["b_x"] + 128 * i, 128) for i in range(2)] + [(OFF["c_q"] + 64 * i, 64) for i in range(12)]
      + [(OFF["d_q"] + 64 * i, 64) for i in range(4)] + [(OFF["d_k"] + 64 * i, 64) for i in range(4)])
G_AQ, G_KVC, G_KSLC, G_KWIN, G_BB, G_BC, G_BX, G_CQKV, G_DQ, G_DK = 0, 4, 5, 6, 7, 9, 11, 13, 25, 29
NFM = len(FM)
CM_ID, CM_TRIU, CM_SU, CM_SL, CM_LI, CM_N = 0, 128, 192, 256, 320, 384


def bc(ap, shape):
    return ap.to_broadcast(list(shape))


def mixer_phase(K, lay, x_src, sname, x_dst, dname, Wd, mv, T, TT, consts, yscr, mode, enable="ABCD"):
    nc = K.nc
    P = K.P
    NT = T // TT
    NCH = TT // 64
    tg = "m%d%d" % (lay, mode)
    if mode == 1:
        enable = "".join(c for c in enable if c in "BCD")
        WOFF, WN = OFF["b_b"], D_IN - OFF["b_b"]
    else:
        enable = "".join(c for c in enable if c in "A")
        WOFF, WN = 0, OFF["b_b"]
    SUM = cm_su = None
    cm = consts["cmask"]
    SUM = cm[0:64, CM_SU:CM_SU + 64]
    TRIU = cm[0:64, CM_TRIU:CM_TRIU + 64]
    SLM = cm[0:64, CM_SL:CM_SL + 64]
    IDENT = cm[:, CM_ID:CM_ID + 128]
    onesf = consts["ones_f"]
    with ExitStack() as es:
        sb = lambda name, shape, d=F32: _sb(es, nc, name + "_" + tg, shape, d)
        wins = sb("win", [128, DC, WN], BF16)
        wouts = sb("wout", [128, DC, D if mode == 2 else 2], BF16)
        bfm = sb("bfm", [128, NFM], F32)
        bfm8 = sb("bfm8", [128, 4], F32)
        bfmv = sb("bfmv", [64, 1], F32)
        K.dma(bfmv[:], Wd["b_fm"][64:128, G_KVC:G_KVC + 1], W=["a_bfmv"], allow_slow_non_contiguous=True)
        xt = sb("xt", [128, DC, TT])
        sq = sb("sq", [128, DC, TT], BF16)
        hb = sb("hb", [128, DC, TT], BF16)
        yt = sb("yt", [128, DC, TT], BF16)
        rs = sb("rs", [128, TT])
        sa = [sb("sa%d" % i, [128, TT]) for i in range(2)]
        scw = sb("scw", [128, 2, 3])
        mixg = sb("mixg", [128, 2, 2])
        DC_ = DC
        PJ = [_ps(es, nc, "pj0_" + tg, [128, 512])]
        if mode == 1:
            PST = PJ[0]
            PSTK = ("pj", 0)
            PD = [_ps(es, nc, "pd%d_" % i + tg, [128, 512]) for i in range(3)]
            PC = [_ps(es, nc, "pc%d_" % i + tg, [128, 512]) for i in range(4)]
            PJL = [PJ[0]] + PD + PC
            PJK = [("pj", 0)] + [("pd", i) for i in range(3)] + [("pc", i) for i in range(4)]
        else:
            PST = _ps(es, nc, "pst_" + tg, [128, 512])
            PSTK = "pst"
            PJL = [PJ[0]]
            PJK = [("pj", 0)]
            POW = _ps(es, nc, "pow_" + tg, [128, 512])
            PW = [_ps(es, nc, "pw0_" + tg, [128, 512])]
            PS2 = [_ps(es, nc, "ps2%d_" % i + tg, [128, 512]) for i in range(2)]
            POA = _ps(es, nc, "poa_" + tg, [128, 512])
            PMX = _ps(es, nc, "pmx_" + tg, [128, 512])
            POB = PST

        winv = Wd["w_in"].rearrange("(kc p) n -> p kc n", p=128)
        for kc in range(DC):
            K.dma(wins[:, kc, :], winv[:, kc, WOFF:WOFF + WN], W=[("win", kc)], eng="pool")
        woutv = Wd["w_out"].rearrange("(kc p) n -> p kc n", p=128)
        if mode == 2:
            for kc in range(DC):
                K.dma(wouts[:, kc, :], woutv[:, kc, :], W=[("wout", kc)], eng="pool")
        K.dma(bfm[:], Wd["b_fm"], W=["bfm"])
        K.dma(scw[:], Wd["sc_conv_wT"], W=["scw"])
        K.dma(mixg[:], Wd["mixgT"], W=["mixg"])
        K.ts(bfm8[:], bfm[:, G_DK:G_DK + 4], 0.125, None, ALU.mult, R=["bfm"], W=["bfm8"])

        xsv = x_src.rearrange("(dc p) t -> p dc t", p=128)
        xdv = x_dst.rearrange("(dc p) t -> p dc t", p=128)
        gs, shift, gate = mv["gs"], mv["mod"], mv["gate"]
        s = 1
        pjn = [0]

        def proj_fm(g, dst, dkeys, scale=1.0, bias=None, func=AF.Identity):
            c0, ncol = FM[g]
            pj = PJL[pjn[0] % len(PJL)]
            kk = PJK[pjn[0] % len(PJL)]
            pjn[0] += 1
            for kc in range(DC):
                K.mm(pj[0:ncol, 0:TT], wins[:, kc, c0 - WOFF:c0 - WOFF + ncol], hb[:, kc, :], start=(kc == 0), stop=(kc == DC - 1),
                     R=[("win", kc), ("hb", kc)], W=[kk])
            b = bias if bias is not None else bfm[0:ncol, g:g + 1]
            K.act(dst, pj[0:ncol, 0:TT], func, R=[kk, "bfm", "bfm8"], W=dkeys, bias=b, scale=scale)

        if "B" in enable:
            ub = [sb("ub%d" % ch, [128, 2 + TT]) for ch in range(2)]
            bbt = sb("bbt", [128, TT])
            cct = sb("cct", [128, TT])
            cvt = sb("cvt", [128, TT])
            ybt = [sb("ybt%d" % ch, [128, TT]) for ch in range(2)]
            for ch in range(2):
                K.memset(ub[ch][:, 0:2], 0.0, W=[("ub", ch)])
        if "D" in enable:
            QmT = sb("QmT", [64, 4, TT], BF16)
            KmT = sb("KmT", [64, 4, TT], BF16)
            Smb = sb("Smb", [64, 4, 65], BF16)
            K.memset(Smb[:], 0.0, W=["Smb"])
            Sm = sb("Sm", [64, 4, 65])
            K.memset(Sm[:], 0.0, W=["Sm"])
            btm_d = sb("btm_d", [64, 776])
            fbb = sb("fbb", [64, 4])
            gnd = sb("gnd", [64, 64])
            K.dma(btm_d[:], Wd["b_in"][OFF["d_k"]:OFF["d_k"] + 776].partition_broadcast(64), W=["btm_d"])
            K.dma(fbb[:], Wd["mlstm_f_bias"].partition_broadcast(64), W=["fbb"])
            K.dma(gnd[:], Wd["mlstm_norm_g"].partition_broadcast(64), W=["gnd"])
            K.tt(btm_d[:, 516:520], btm_d[:, 516:520], fbb[:], ALU.add, R=["btm_d", "fbb"], W=["btm_d"])
            DT = {}
            for nm_, *shp in (("ktok", [64, 4, 64]), ("vext", [64, 4, 65], BF16), ("ifo", [64, 264]), ("lf", [64, 4]),
                             ("gtot", [64, 4]), ("bsb", [64, 4]), ("ddd", [64, 4]), ("edec", [64, 4]), ("eb", [64, 4]),
                             ("slg", [64, 4, 64]), ("et", [64, 4, 64]), ("pt", [64, 4, 64], BF16), ("nd", [64, 4, 65]),
                             ("den", [64, 4]), ("hm", [64, 4, 64]), ("hsq", [64, 4, 64]), ("ss", [64, 4]),
                             ("so", [64, 4, 64]), ("kd", [64, 4, 64], BF16), ("stmp", [64, 4, 65])):
                DT[nm_] = [sb("d_%s%d" % (nm_, z), *shp) if False else sb("d_%s%d" % (nm_, z), shp[0], shp[1] if len(shp) > 1 else F32)
                           for z in range(2)]
            for z in range(2):
                K.memset(DT["vext"][z][:], 1.0, W=["d_vext%d" % z])

        def mlstm_chunk(c):
            cs = slice(c * 64, (c + 1) * 64)
            z = c % 2
            t_ = {k_: v_[z] for k_, v_ in DT.items()}
            kk_ = lambda n_: "d_%s%d" % (n_, z)
            ktok, vext, ifo, lf, gtot, bsb, ddd, edec, eb = (t_[x] for x in ("ktok", "vext", "ifo", "lf", "gtot", "bsb", "ddd", "edec", "eb"))
            slg, et, pt, nd, den, hm, hsq, ss, so, kd, stmp = (t_[x] for x in ("slg", "et", "pt", "nd", "den", "hm", "hsq", "ss", "so", "kd", "stmp"))
            DA, DB, DC = PD
            kA, kB, kC = ("pd", 0), ("pd", 1), ("pd", 2)
            for kc in range(DC_):
                K.mm(DC[0:64, 0:512], hb[:, kc, cs], wins[:, kc, OFF["d_k"] - WOFF:OFF["d_k"] - WOFF + 512],
                     start=(kc == 0), stop=(kc == DC_ - 1), R=[("win", kc), ("hb", kc)], W=[kC])
            for kc in range(DC_):
                K.mm(DB[0:64, 0:264], hb[:, kc, cs], wins[:, kc, OFF["d_i"] - WOFF:OFF["d_i"] - WOFF + 264],
                     start=(kc == 0), stop=(kc == DC_ - 1), R=[("win", kc), ("hb", kc)], W=[kB])
            K.tt(ktok[:].rearrange("p a b -> p (a b)"), DC[0:64, 0:256], btm_d[:, 0:256], ALU.add,
                 R=[kC, "btm_d"], W=[kk_("ktok")])
            K.tt(vext[:, :, 0:64], v4(DC[0:64, 256:512]), v4(btm_d[:, 256:512]), ALU.add, R=[kC, "btm_d"], W=[kk_("vext")])
            K.tt(ifo[:], DB[0:64, 0:264], btm_d[:, 512:776], ALU.add, R=[kB, "btm_d"], W=[kk_("ifo")])
            K.act(lf[:], ifo[:, 4:8], AF.Exp, R=[kk_("ifo")], W=[kk_("lf")], scale=-1.0)
            K.act(lf[:], lf[:], AF.Ln, R=[kk_("lf")], W=[kk_("lf")], bias=consts["one"][0:64, :], scale=1.0)
            K.ts(lf[:], lf[:], -1.0, None, ALU.mult, R=[kk_("lf")], W=[kk_("lf")])
            K.mm(DB[0:64, 264:268], TRIU, lf[:], R=[kk_("lf"), "cmask"], W=[kB])
            K.mm(DB[0:64, 268:272], onesf[0:64, 0:64], lf[:], R=[kk_("lf"), "ones_f"], W=[kB])
            K.act(gtot[:], DB[0:64, 268:272], AF.Exp, R=[kB], W=[kk_("gtot")])
            K.copy(bsb[:], DB[0:64, 264:268], R=[kB], W=[kk_("bsb")])
            K.tt(ddd[:], DB[0:64, 268:272], bsb[:], ALU.subtract, R=[kB, kk_("bsb")], W=[kk_("ddd")])
            K.tt(ddd[:], ddd[:], ifo[:, 0:4], ALU.add, R=[kk_("ddd"), kk_("ifo")], W=[kk_("ddd")])
            K.act(edec[:], ddd[:], AF.Exp, R=[kk_("ddd")], W=[kk_("edec")])
            K.act(eb[:], bsb[:], AF.Exp, R=[kk_("bsb")], W=[kk_("eb")])
            K.tt(slg[:], bc(TRIU.unsqueeze(1), [64, 4, 64]), bc(lf[:, :].unsqueeze(2), [64, 4, 64]), ALU.mult,
                 R=[kk_("lf"), "cmask"], W=[kk_("slg")])
            K.mm(DC[0:64, 0:256], SLM, slg[:].rearrange("p a b -> p (a b)"), R=[kk_("slg"), "cmask"], W=[kC])
            for h in range(4):
                K.mm(DC[0:64, 256 + h * 64:256 + (h + 1) * 64], KmT[:, h, cs], QmT[:, h, cs], R=["QmT", "KmT"], W=[kC])
            K.tt(et[:], v4(DC[0:64, 0:256]), bc(ifo[:, 0:4].unsqueeze(2), [64, 4, 64]), ALU.add, R=[kC, kk_("ifo")],
                 W=[kk_("et")])
            K.act(et[:], et[:], AF.Exp, R=[kk_("et")], W=[kk_("et")])
            K.tt(et[:], et[:], bc(TRIU.unsqueeze(1), [64, 4, 64]), ALU.mult, R=[kk_("et"), "cmask"], W=[kk_("et")])
            K.tt(pt[:], et[:], v4(DC[0:64, 256:512]), ALU.mult, R=[kk_("et"), kC], W=[kk_("pt")])
            for h in range(4):
                K.mm(DA[0:64, h * 65:(h + 1) * 65], QmT[:, h, cs], Smb[:, h, :], R=["QmT", "Smb"], W=[kA])
            qs = DA[0:64, 0:260].rearrange("p (a b) -> p a b", a=4)
            K.tt(nd[:], qs, bc(eb[:, :].unsqueeze(2), [64, 4, 65]), ALU.mult, R=[kA, kk_("eb")], W=[kk_("nd")])
            for h in range(4):
                K.mm(DA[0:64, h * 65:(h + 1) * 65], pt[:, h, :], vext[:, h, :], R=[kk_("pt"), kk_("vext")], W=[kA])
            K.tt(nd[:], nd[:], DA[0:64, 0:260].rearrange("p (a b) -> p a b", a=4), ALU.add, R=[kk_("nd"), kA], W=[kk_("nd")])
            K.stt(kd[:], ktok[:], 0.125, bc(edec[:, :].unsqueeze(2), [64, 4, 64]), ALU.mult, ALU.mult,
                  R=[kk_("ktok"), kk_("edec")], W=[kk_("kd")])
            for h in range(4):
                K.mm(DA[0:64, h * 65:(h + 1) * 65], kd[:, h, :], vext[:, h, :], R=[kk_("kd"), kk_("vext")], W=[kA])
            K.tt(stmp[:], Sm[:], bc(gtot[:, :].unsqueeze(2), [64, 4, 65]), ALU.mult, R=["Sm", kk_("gtot")], W=[kk_("stmp")])
            K.tt(Sm[:], stmp[:], DA[0:64, 0:260].rearrange("p (a b) -> p a b", a=4), ALU.add, R=[kk_("stmp"), kA], W=["Sm"])
            K.copy(Smb[:], Sm[:], R=["Sm"], W=["Smb"], eng="act")
            K.act(den[:], nd[:, :, 64], AF.Abs, R=[kk_("nd")], W=[kk_("den")])
            K.ts(den[:], den[:], 1.0, None, ALU.max, R=[kk_("den")], W=[kk_("den")])
            K.recip(den[:], den[:], R=[kk_("den")], W=[kk_("den")])
            K.tt(hm[:], nd[:, :, 0:64], bc(den[:, :].unsqueeze(2), [64, 4, 64]), ALU.mult, R=[kk_("nd"), kk_("den")], W=[kk_("hm")])
            K.tt(hsq[:], hm[:], hm[:], ALU.mult, R=[kk_("hm")], W=[kk_("hsq")])
            K.P.add("dve", lambda e: e.tensor_reduce(ss[:], hsq[:], AX.X, ALU.add), R=[kk_("hsq")], W=[kk_("ss")], cost=400.0)
            K.act(ss[:], ss[:], AF.Sqrt, R=[kk_("ss"), "epsc"], W=[kk_("ss")], bias=consts["eps"][0:64, :], scale=1.0 / 64)
            K.recip(ss[:], ss[:], R=[kk_("ss")], W=[kk_("ss")])
            K.act(so[:].rearrange("p a b -> p (a b)"), ifo[:, 8:264], AF.Sigmoid, R=[kk_("ifo")], W=[kk_("so")])
            K.tt(hm[:], hm[:], bc(ss[:, :].unsqueeze(2), [64, 4, 64]), ALU.mult, R=[kk_("hm"), kk_("ss")], W=[kk_("hm")])
            K.tt(hm[:], hm[:], bc(gnd[:, :].unsqueeze(1), [64, 4, 64]), ALU.mult, R=[kk_("hm"), "gnd"], W=[kk_("hm")])
            K.tt(hm[:], hm[:], so[:], ALU.mult, R=[kk_("hm"), kk_("so")], W=[kk_("hm")])
            hmf = hm[:].rearrange("p a b -> p (a b)")
            for pr in range(2):
                K.tr(DA[:, 272 + pr * 64:272 + (pr + 1) * 64], hmf[:, pr * 128:(pr + 1) * 128], IDENT[0:64, 0:64],
                     R=[kk_("hm"), "cmask"], W=[kA])
                K.copy(yt[:, 6 + pr, cs], DA[:, 272 + pr * 64:272 + (pr + 1) * 64], R=[kA], W=[("yt", 6 + pr)], eng="act")

        GDT = F32
        if "C" in enable:
            cin = sb("c_cin", [64, 12, 3 + TT])
            qkv = sb("c_qkv", [64, 12, TT])
            qkb = sb("c_qkb", [64, 12, TT], GDT)
            Sgb = sb("c_Sb", [64, 4, 64], GDT)
            K.memset(Sgb[:], 0.0, W=["c_Sb"])
            identb1 = sb("c_identb", [64, 64], GDT)
            K.copy(identb1[:], IDENT[0:64, 0:64], R=["cmask"], W=["c_identb"])
            gcw = sb("c_gcw", [64, 12, 4])
            Sg = sb("c_S", [64, 4, 64])
            K.memset(Sg[:], 0.0, W=["c_S"])
            K.memset(cin[:, :, 0:3], 0.0, W=[("c_cin", g) for g in range(12)])
            K.dma(gcw[:], Wd["gdn_conv_wT"], W=["c_gcw"])
            btm_c = sb("c_btm", [64, 264])
            dtb = sb("c_dtb", [64, 4])
            nea = sb("c_nea", [64, 4])
            gng = sb("c_gng", [64, 64])
            K.dma(btm_c[:], Wd["b_in"][OFF["c_beta"]:OFF["c_beta"] + 264].partition_broadcast(64), W=["c_btm"])
            K.dma(dtb[:], Wd["gdn_dt_bias"].partition_broadcast(64), W=["c_dtb"])
            K.dma(nea[:], Wd["gdn_A_log"].partition_broadcast(64), W=["c_nea"])
            K.dma(gng[:], Wd["gdn_norm_g"].partition_broadcast(64), W=["c_gng"])
            K.tt(btm_c[:, 4:8], btm_c[:, 4:8], dtb[:], ALU.add, R=["c_btm", "c_dtb"], W=["c_btm"])
            K.act(nea[:], nea[:], AF.Exp, R=["c_nea"], W=["c_nea"])
            K.ts(nea[:], nea[:], -1.0, None, ALU.mult, R=["c_nea"], W=["c_nea"])
            csq = sb("c_sq", [64, TT])
            crs = sb("c_rs", [64, TT])
            CT = {}
            for nm_, *shp in (("baz", [64, 264]), ("beta", [64, 4]), ("gg", [64, 4]), ("gcs", [64, 4]), ("egc", [64, 4]),
                             ("bgc", [64, 4]), ("edl", [64, 4]), ("gto", [64, 4]), ("ktk", [64, 4, 64]), ("vb", [64, 4, 64], GDT),
                             ("kbe", [64, 4, 64], GDT), ("kdc", [64, 4, 64], GDT), ("ug", [64, 4, 64]), ("slgc", [64, 4, 64]),
                             ("dgb", [64, 4, 64]), ("seg", [64, 4, 64]), ("segT", [64, 4, 64]), ("t1", [64, 4, 64]),
                             ("t2", [64, 4, 64]), ("NN", [64, 2, 4, 64], GDT), ("NN2", [64, 2, 4, 64], GDT), ("XX", [64, 4, 64]), ("XB", [64, 4, 64], GDT),
                             ("ptc", [64, 4, 64], GDT), ("uu", [64, 4, 64]), ("wt", [64, 4, 64], GDT), ("vn", [64, 4, 64]), ("vnb", [64, 4, 64], GDT),
                             ("oo", [64, 4, 64]), ("osq", [64, 4, 64]), ("oss", [64, 4]), ("sz", [64, 4, 64]),
                             ("stg", [64, 4, 64])):
                CT[nm_] = [sb("c_%s%d" % (nm_, z), shp[0], shp[1] if len(shp) > 1 else F32) for z in range(2)]

        def v4(ap):
            return ap.rearrange("p (a b) -> p a b", a=4)

        def gdn_tile():
            for g in range(12):
                proj_fm(G_CQKV + g, cin[:, g, 3:3 + TT], [("c_cin", g)])
                K.ts(qkv[:, g, :], cin[:, g, 3:3 + TT], gcw[:, g, 3:4], None, ALU.mult, R=[("c_cin", g), "c_gcw"],
                     W=[("c_qkv", g)])
                for k in range(3):
                    K.stt(qkv[:, g, :], cin[:, g, k:k + TT], gcw[:, g, k:k + 1], qkv[:, g, :], ALU.mult, ALU.add,
                          R=[("c_cin", g), "c_gcw", ("c_qkv", g)], W=[("c_qkv", g)])
                K.copy(cin[:, g, 0:3], cin[:, g, TT:TT + 3], R=[("c_cin", g)], W=[("c_cin", g)])
                K.act(qkv[:, g, :], qkv[:, g, :], AF.Silu, R=[("c_qkv", g)], W=[("c_qkv", g)])
                if g < 8:
                    K.tt(csq[:], qkv[:, g, :], qkv[:, g, :], ALU.mult, R=[("c_qkv", g)], W=["c_sq"])
                    K.mm(PST[0:64, 0:TT], onesf[0:64, 0:64], csq[:], R=["c_sq", "ones_f"], W=[PSTK])
                    K.act(crs[:], PST[0:64, 0:TT], AF.Sqrt, R=[PSTK, "epsc"], W=["c_rs"], bias=consts["eps"][0:64, :],
                          scale=1.0)
                    K.recip(crs[:], crs[:], R=["c_rs"], W=["c_rs"])
                    K.stt(qkb[:, g, :], qkv[:, g, :], 0.125 if g < 4 else 1.0, crs[:], ALU.mult, ALU.mult,
                          R=[("c_qkv", g), "c_rs"], W=[("c_qkb", g)])
                else:
                    K.copy(qkb[:, g, :], qkv[:, g, :], R=[("c_qkv", g)], W=[("c_qkb", g)])

        def gdn_chunk(c):
            cs = slice(c * 64, (c + 1) * 64)
            z = c % 2
            t_ = {k_: v_[z] for k_, v_ in CT.items()}
            kk_ = lambda n_: "c_%s%d" % (n_, z)
            baz, beta, gg, gcs, egc, bgc, edl, gto = (t_[x] for x in ("baz", "beta", "gg", "gcs", "egc", "bgc", "edl", "gto"))
            ktk, vb, kbe, kdc, ug, slgc, dgb, seg, segT = (t_[x] for x in ("ktk", "vb", "kbe", "kdc", "ug", "slgc", "dgb", "seg", "segT"))
            t1, t2, NN, NN2, XX, ptc, uu, wt, vn, oo, osq, oss, sz, stg = (t_[x] for x in ("t1", "t2", "NN", "NN2", "XX", "ptc", "uu", "wt", "vn", "oo", "osq", "oss", "sz", "stg"))
            XB, vnb = t_["XB"], t_["vnb"]
            CA, CB, CC, CD = PC
            kA, kB, kC, kD = ("pc", 0), ("pc", 1), ("pc", 2), ("pc", 3)
            QK = [("c_qkb", g) for g in range(12)]
            for kc in range(DC_):
                K.mm(CA[0:64, 0:264], hb[:, kc, cs], wins[:, kc, OFF["c_beta"] - WOFF:OFF["c_beta"] - WOFF + 264],
                     start=(kc == 0), stop=(kc == DC_ - 1), R=[("win", kc), ("hb", kc)], W=[kA])
            K.tt(baz[:], CA[0:64, 0:264], btm_c[:], ALU.add, R=[kA, "c_btm"], W=[kk_("baz")])
            K.act(beta[:], baz[:, 0:4], AF.Sigmoid, R=[kk_("baz")], W=[kk_("beta")])
            K.act(gg[:], baz[:, 4:8], AF.Exp, R=[kk_("baz")], W=[kk_("gg")])
            K.act(gg[:], gg[:], AF.Ln, R=[kk_("gg")], W=[kk_("gg")], bias=consts["one"][0:64, :], scale=1.0)
            K.tt(gg[:], gg[:], nea[:], ALU.mult, R=[kk_("gg"), "c_nea"], W=[kk_("gg")])
            K.mm(CA[0:64, 264:268], TRIU, gg[:], R=[kk_("gg"), "cmask"], W=[kA])
            K.mm(CA[0:64, 268:272], onesf[0:64, 0:64], gg[:], R=[kk_("gg"), "ones_f"], W=[kA])
            K.copy(gcs[:], CA[0:64, 264:268], R=[kA], W=[kk_("gcs")])
            K.act(egc[:], CA[0:64, 264:268], AF.Exp, R=[kA], W=[kk_("egc")])
            K.act(gto[:], CA[0:64, 268:272], AF.Exp, R=[kA], W=[kk_("gto")])
            K.tt(edl[:], CA[0:64, 268:272], gcs[:], ALU.subtract, R=[kA, kk_("gcs")], W=[kk_("edl")])
            K.act(edl[:], edl[:], AF.Exp, R=[kk_("edl")], W=[kk_("edl")])
            K.tt(bgc[:], beta[:], egc[:], ALU.mult, R=[kk_("beta"), kk_("egc")], W=[kk_("bgc")])
            for h in range(4):
                K.mm(CB[0:64, h * 64:(h + 1) * 64], qkb[:, 4 + h, cs], identb1[:], R=QK + ["c_identb"], W=[kB])
                K.mm(CB[0:64, 256 + h * 64:256 + (h + 1) * 64], qkb[:, 8 + h, cs], identb1[:], R=QK + ["c_identb"], W=[kB])
            K.copy(ktk[:], v4(CB[0:64, 0:256]), R=[kB], W=[kk_("ktk")], eng="act")
            K.tt(vb[:], v4(CB[0:64, 256:512]), bc(beta[:, :].unsqueeze(2), [64, 4, 64]), ALU.mult, R=[kB, kk_("beta")], W=[kk_("vb")])
            K.tt(kbe[:], ktk[:], bc(bgc[:, :].unsqueeze(2), [64, 4, 64]), ALU.mult, R=[kk_("ktk"), kk_("bgc")], W=[kk_("kbe")])
            K.tt(kdc[:], ktk[:], bc(edl[:, :].unsqueeze(2), [64, 4, 64]), ALU.mult, R=[kk_("ktk"), kk_("edl")], W=[kk_("kdc")])
            K.tt(ug[:], bc(TRIU.unsqueeze(1), [64, 4, 64]), bc(gg[:, :].unsqueeze(2), [64, 4, 64]), ALU.mult,
                 R=[kk_("gg"), "cmask"], W=[kk_("ug")])
            K.tt(slgc[:], bc(SLM.unsqueeze(1), [64, 4, 64]), bc(gg[:, :].unsqueeze(2), [64, 4, 64]), ALU.mult,
                 R=[kk_("gg"), "cmask"], W=[kk_("slgc")])
            K.mm(CC[0:64, 0:256], TRIU, slgc[:].rearrange("p a b -> p (a b)"), R=[kk_("slgc"), "cmask"], W=[kC])
            K.mm(CC[0:64, 256:512], SLM, ug[:].rearrange("p a b -> p (a b)"), R=[kk_("ug"), "cmask"], W=[kC])
            K.tt(dgb[:], bc(IDENT[0:64, 0:64].unsqueeze(1), [64, 4, 64]), bc(beta[:, :].unsqueeze(2), [64, 4, 64]), ALU.mult,
                 R=[kk_("beta"), "cmask"], W=[kk_("dgb")])
            K.act(seg[:], v4(CC[0:64, 0:256]), AF.Exp, R=[kC], W=[kk_("seg")])
            K.act(segT[:], v4(CC[0:64, 256:512]), AF.Exp, R=[kC], W=[kk_("segT")])
            for h in range(4):
                K.mm(CB[0:64, h * 64:(h + 1) * 64], qkb[:, 4 + h, cs], qkb[:, 4 + h, cs], R=QK, W=[kB])
                K.mm(CB[0:64, 256 + h * 64:256 + (h + 1) * 64], qkb[:, 4 + h, cs], qkb[:, h, cs], R=QK, W=[kB])
            K.mm(CC[0:64, 0:256], onesf[0:64, 0:64], dgb[:].rearrange("p a b -> p (a b)"), R=[kk_("dgb"), "ones_f"], W=[kC])
            K.tt(t1[:], seg[:], bc(SLM.unsqueeze(1), [64, 4, 64]), ALU.mult, R=[kk_("seg"), "cmask"], W=[kk_("t1")])
            K.tt(t1[:], t1[:], v4(CB[0:64, 0:256]), ALU.mult, R=[kk_("t1"), kB], W=[kk_("t1")])
            K.tt(NN[:, 0], t1[:], bc(beta[:, :].unsqueeze(2), [64, 4, 64]), ALU.mult, R=[kk_("t1"), kk_("beta")], W=[kk_("NN")])
            K.tt(t2[:], segT[:], bc(SUM.unsqueeze(1), [64, 4, 64]), ALU.mult, R=[kk_("segT"), "cmask"], W=[kk_("t2")])
            K.tt(t2[:], t2[:], v4(CB[0:64, 0:256]), ALU.mult, R=[kk_("t2"), kB], W=[kk_("t2")])
            K.tt(NN[:, 1], t2[:], v4(CC[0:64, 0:256]), ALU.mult, R=[kk_("t2"), kC], W=[kk_("NN")])
            K.tt(ptc[:], segT[:], bc(TRIU.unsqueeze(1), [64, 4, 64]), ALU.mult, R=[kk_("segT"), "cmask"], W=[kk_("ptc")])
            K.tt(ptc[:], ptc[:], v4(CB[0:64, 256:512]), ALU.mult, R=[kk_("ptc"), kB], W=[kk_("ptc")])
            K.tt(XX[:], bc(IDENT[0:64, 0:64].unsqueeze(1), [64, 4, 64]), NN[:, 1], ALU.subtract, R=[kk_("NN"), "cmask"],
                 W=[kk_("XX")])
            K.copy(XB[:], XX[:], R=[kk_("XX")], W=[kk_("XB")], eng="act")
            cur, nxt, ck, nk = NN, NN2, kk_("NN"), kk_("NN2")
            for lvl in range(5):
                last = (lvl == 4)
                for h in range(4):
                    K.mm(CC[0:64, h * 64:(h + 1) * 64], cur[:, 1, h, :], cur[:, 0, h, :], R=[ck], W=[kC])
                    if not last:
                        K.mm(CC[0:64, 256 + h * 64:256 + (h + 1) * 64], cur[:, 0, h, :], cur[:, 1, h, :], R=[ck], W=[kC])
                if last:
                    K.copy(nxt[:, 0], v4(CC[0:64, 0:256]), R=[kC], W=[nk], eng="act")
                else:
                    K.copy(nxt[:].rearrange("p t a b -> p (t a b)"), CC[0:64, 0:512], R=[kC], W=[nk], eng="act")
                for h in range(4):
                    K.mm(CB[0:64, h * 64:(h + 1) * 64], nxt[:, 0, h, :], XB[:, h, :], R=[nk, kk_("XB")], W=[kB])
                K.tt(XX[:], XX[:], v4(CB[0:64, 0:256]), ALU.add, R=[kk_("XX"), kB], W=[kk_("XX")])
                K.copy(XB[:], XX[:], R=[kk_("XX")], W=[kk_("XB")], eng="act")
                cur, nxt, ck, nk = nxt, cur, nk, ck
            for h in range(4):
                K.mm(CC[0:64, h * 64:(h + 1) * 64], XB[:, h, :], vb[:, h, :], R=[kk_("XB"), kk_("vb")], W=[kC])
                K.mm(CC[0:64, 256 + h * 64:256 + (h + 1) * 64], kbe[:, h, :], XB[:, h, :], R=[kk_("XB"), kk_("kbe")], W=[kC])
            K.copy(uu[:], v4(CC[0:64, 0:256]), R=[kC], W=[kk_("uu")], eng="act")
            K.copy(wt[:], v4(CC[0:64, 256:512]), R=[kC], W=[kk_("wt")])
            for h in range(4):
                K.mm(CD[0:64, h * 64:(h + 1) * 64], wt[:, h, :], Sgb[:, h, :], R=[kk_("wt"), "c_Sb"], W=[kD])
                K.mm(CD[0:64, 256 + h * 64:256 + (h + 1) * 64], qkb[:, h, cs], Sgb[:, h, :], R=QK + ["c_Sb"], W=[kD])
            K.tt(vnb[:], uu[:], v4(CD[0:64, 0:256]), ALU.subtract, R=[kk_("uu"), kD], W=[kk_("vnb")])
            K.tt(oo[:], v4(CD[0:64, 256:512]), bc(egc[:, :].unsqueeze(2), [64, 4, 64]), ALU.mult, R=[kD, kk_("egc")], W=[kk_("oo")])
            for h in range(4):
                K.mm(CD[0:64, h * 64:(h + 1) * 64], kdc[:, h, :], vnb[:, h, :], R=[kk_("kdc"), kk_("vnb")], W=[kD])
            K.tt(stg[:], Sg[:], bc(gto[:, :].unsqueeze(2), [64, 4, 64]), ALU.mult, R=["c_S", kk_("gto")], W=[kk_("stg")])
            K.tt(Sg[:], stg[:], v4(CD[0:64, 0:256]), ALU.add, R=[kk_("stg"), kD], W=["c_S"])
            K.copy(Sgb[:], Sg[:], R=["c_S"], W=["c_Sb"], eng="act")
            for h in range(4):
                K.mm(CD[0:64, 256 + h * 64:256 + (h + 1) * 64], ptc[:, h, :], vnb[:, h, :], R=[kk_("ptc"), kk_("vnb")], W=[kD])
            K.tt(oo[:], oo[:], v4(CD[0:64, 256:512]), ALU.add, R=[kk_("oo"), kD], W=[kk_("oo")])
            K.tt(osq[:], oo[:], oo[:], ALU.mult, R=[kk_("oo")], W=[kk_("osq")])
            K.P.add("dve", lambda e: e.tensor_reduce(oss[:], osq[:], AX.X, ALU.add), R=[kk_("osq")], W=[kk_("oss")], cost=400.0)
            K.act(oss[:], oss[:], AF.Sqrt, R=[kk_("oss"), "epsc"], W=[kk_("oss")], bias=consts["eps"][0:64, :], scale=1.0 / 64)
            K.recip(oss[:], oss[:], R=[kk_("oss")], W=[kk_("oss")])
            K.act(sz[:].rearrange("p a b -> p (a b)"), baz[:, 8:264], AF.Silu, R=[kk_("baz")], W=[kk_("sz")])
            K.tt(oo[:], oo[:], bc(oss[:, :].unsqueeze(2), [64, 4, 64]), ALU.mult, R=[kk_("oo"), kk_("oss")], W=[kk_("oo")])
            K.tt(oo[:], oo[:], bc(gng[:, :].unsqueeze(1), [64, 4, 64]), ALU.mult, R=[kk_("oo"), "c_gng"], W=[kk_("oo")])
            K.tt(oo[:], oo[:], sz[:], ALU.mult, R=[kk_("oo"), kk_("sz")], W=[kk_("oo")])
            oof = oo[:].rearrange("p a b -> p (a b)")
            for pr in range(2):
                K.tr(CD[:, pr * 64:(pr + 1) * 64], oof[:, pr * 128:(pr + 1) * 128], IDENT[0:64, 0:64],
                     R=[kk_("oo"), "cmask"], W=[kD])
                K.copy(yt[:, 4 + pr, cs], CD[:, pr * 64:(pr + 1) * 64], R=[kD], W=[("yt", 4 + pr)], eng="act")

        if "A" in enable:
            NB = T // 64
            NCB = T // 16
            NM = (NCB + 127) // 128
            NQ = T // 128
            w1k = sb("a_w1k", [64, 32, 256], BF16)
            w1v = sb("a_w1v", [64, 32, 256], BF16)
            w2k = sb("a_w2k", [128, 2, 64], BF16)
            w2v = sb("a_w2v", [128, 2, 64], BF16)
            K.dma(w1k[:], Wd["cmp_k_w1"].rearrange("(s d) n -> d s n", d=64), W=["a_w1k"], eng="pool")
            K.dma(w1v[:], Wd["cmp_v_w1"].rearrange("(s d) n -> d s n", d=64), W=["a_w1v"], eng="pool")
            K.dma(w2k[:], Wd["cmp_k_w2"].rearrange("(c p) n -> p c n", p=128), W=["a_w2k"], eng="pool")
            K.dma(w2v[:], Wd["cmp_v_w2"].rearrange("(c p) n -> p c n", p=128), W=["a_w2v"], eng="pool")
            posT = sb("a_posT", [64, 32], BF16)
            K.dma(posT[:], Wd["cmp_posT"], W=["a_posT"], eng="pool")
            hbias = sb("a_hbias", [128, 4])
            expc = sb("a_expc", [64, T], BF16)
            K.dma(expc[0:NB, :], Wd["expc"], W=["a_expc"], eng="pool")
            keepc = sb("a_keepc", [128, 2, 2 * NB])
            K.dma(keepc[:], Wd["keepadd"], W=["a_keepc"])
            identb = sb("a_identb", [128, 128], BF16)
            K.copy(identb[:], IDENT, R=["cmask"], W=["a_identb"])
            biasT = sb("a_bias", [128, 19, 512])
            for q in range(19):
                K.dma(biasT[:, q, :], Wd["bias_scr"][q], W=[("a_bias", q)])
            bw4 = sb("a_bw4", [128, 128])
            K.dma(bw4[:], Wd["bw4"], W=["a_bw4"])
            qng = sb("a_qng", [64, 2])
            K.dma(qng[:], Wd["qkng"], W=["a_qng"])
            K.ts(qng[:, 0:1], qng[:, 0:1], 0.125, None, ALU.mult, R=["a_qng"], W=["a_qng"])
            tabrow = sb("a_tabrow", [65, 4])
            K.dma(tabrow[64:65, :], Wd["t5_table"][31:32, :], W=["a_tabrow"])
            mg0 = sb("a_mg0", [128, 256])
            K.dma(mg0[:], Wd["mix_norm_g0"].partition_broadcast(128), W=["a_mg0"])
            btm_a = sb("a_btm", [128, 204])
            K.dma(btm_a[:], Wd["b_in"][OFF["a_v_slc"]:OFF["a_v_slc"] + 204].partition_broadcast(128), W=["a_btm"])
            ovl = sb("a_ovl", [128, NM, 64])
            K.dma(ovl[:], Wd["ovl"], W=["a_ovl"])
            kcmpT = sb("a_kcmpT", [64, T], BF16)
            vcmpT = sb("a_vcmpT", [64, T], BF16)
            KsT = sb("a_KsT", [128, T], BF16)
            KwT = sb("a_KwT", [128, T], BF16)
            kcT = sb("a_kcT", [128, NM * 128])
            vcT = sb("a_vcT", [64, NM * 128])
            vcx = sb("a_vcx", [128, NM, 65])
            Vs = sb("a_Vs", [128, NQ, 65], BF16)
            Vw = sb("a_Vw", [128, NQ, 65], BF16)
            NQT = TT // 128
            QaT = sb("a_QaT", [128, NQT, 4, 128], BF16)
            QaF = sb("a_QaF", [128, NQT, 4, 128])
            gsb = sb("a_gsb", [128, TT // 128, 12])
            K.memset(KsT[64:128, :], 0.0, W=["a_KsT"])
            K.memset(KwT[64:128, :], 0.0, W=["a_KwT"])
            K.memset(KsT[64:65, :], 1.0, W=["a_KsT"])
            K.memset(KwT[64:65, :], 1.0, W=["a_KwT"])
            K.memset(kcT[:, :], 0.0, W=["a_kcT"])
            K.memset(kcT[64:65, :], 1.0, W=["a_kcT"])
            K.memset(QaT[64:128], 0.0, W=["a_QaT"])
            K.memset(QaF[64:128], 0.0, W=["a_QaF"])
            K.memset(vcT[:], 0.0, W=["a_vcT"])
            K.memset(vcx[:], 1.0, W=["a_vcx"])
            K.memset(Vs[:], 1.0, W=["a_Vs"])
            K.memset(Vw[:], 1.0, W=["a_Vw"])
            for q_ in range(NQT):
                K.copy(QaT[64:65, q_], bc(tabrow[64:65, :].unsqueeze(2), [1, 4, 128]), R=["a_tabrow"], W=["a_QaT"])
                K.copy(QaF[64:65, q_], bc(tabrow[64:65, :].unsqueeze(2), [1, 4, 128]), R=["a_tabrow"], W=["a_QaF"])
            for kv, w1 in enumerate((w1k, w1v)):
                for hc in range(2):
                    for s_ in range(32):
                        K.mm(PW[0][:, (kv * 2 + hc) * 2:(kv * 2 + hc) * 2 + 1], w1[:, s_, hc * 128:(hc + 1) * 128],
                             posT[:, s_:s_ + 1], start=(s_ == 0), stop=(s_ == 31), R=["a_w1k", "a_w1v", "a_posT"],
                             W=[("pw", 0)])
            K.copy(hbias[:], PW[0][:, 0:8].rearrange("p (a b) -> p a b", b=2)[:, :, 0], R=[("pw", 0)], W=["a_hbias"])
            a_raw = sb("a_raw", [64, TT])
            a_sq = sb("a_sq", [64, TT])
            a_rs = sb("a_rs", [64, TT])
            hact = sb("a_hact", [128, 2, 2, 32], BF16)
            cst = sb("a_cst", [64, 32])
            csq2 = sb("a_csq2", [64, 32])
            crs2 = sb("a_crs2", [64, 32])
            vtm = sb("a_vtm", [128, 204])
            Eb = [sb("a_E%d" % i, [128, 4, 128]) for i in range(2)]
            Tb = [sb("a_T%d" % i, [128, 4, 128]) for i in range(2)]
            Pb = [sb("a_P%d" % i, [128, 4, 128], BF16) for i in range(2)]
            Pc = [sb("a_Pc%d" % i, [128, 4, 128]) for i in range(2)]
            scr = sb("a_scr", [128, 64])
            sc2 = sb("a_sc2", [128, 64])
            v8 = sb("a_v8", [128, 8])
            mskb = sb("a_mskb", [128, 64], BF16)
            mT = sb("a_mT", [64, 128], BF16)
            rden = sb("a_rden", [128, 4])
            coef = sb("a_coef", [128, 4])
            ya = sb("a_ya", [128, 4, 64])
            ytmp = sb("a_ytmp", [128, 4, 64])
            rdenw = sb("a_rdenw", [128, 4])
            coefw = sb("a_coefw", [128, 4])
            yaw = sb("a_yaw", [128, 4, 64])
            yss = sb("a_yss", [128, 1])

        def nsa_tile(tt):
            t0 = tt * TT
            tsl = slice(t0, t0 + TT)
            c0 = OFF["a_k_cmp"]
            for nm_, col, dst in (("k", OFF["a_k_cmp"], kcmpT), ("v", OFF["a_v_cmp"], vcmpT)):
                pj = PJ[pjn[0] % len(PJ)]
                kk = ("pj", pjn[0] % len(PJ))
                pjn[0] += 1
                for kc in range(DC):
                    K.mm(pj[0:64, 0:TT], wins[:, kc, col:col + 64], hb[:, kc, :], start=(kc == 0), stop=(kc == DC - 1),
                         R=[("win", kc), ("hb", kc)], W=[kk])
                g_ = G_KVC
                bcol = bfm[0:64, G_KVC:G_KVC + 1] if nm_ == "k" else bfmv[:, 0:1]
                K.act(dst[:, tsl], pj[0:64, 0:TT], AF.Identity, R=[kk, "bfm", "a_bfmv"], W=["a_" + nm_ + "cmpT"], bias=bcol,
                      scale=1.0)

            def normed(g, dst, dkey, gcol, split=False):
                proj_fm(g, a_raw[:], ["a_raw"])
                K.tt(a_sq[:], a_raw[:], a_raw[:], ALU.mult, R=["a_raw"], W=["a_sq"])
                K.mm(PST[0:64, 0:TT], onesf[0:64, 0:64], a_sq[:], R=["a_sq", "ones_f"], W=[PSTK])
                K.act(a_rs[:], PST[0:64, 0:TT], AF.Sqrt, R=[PSTK, "epsc"], W=["a_rs"], bias=consts["eps"][0:64, :],
                      scale=1.0 / 64)
                K.recip(a_rs[:], a_rs[:], R=["a_rs"], W=["a_rs"])
                for d_, dk_ in zip(dst, dkey):
                    if split:
                        K.stt(d_, a_raw[:].rearrange("p (a b) -> p a b", b=128), gcol,
                              a_rs[:].rearrange("p (a b) -> p a b", b=128), ALU.mult, ALU.mult,
                              R=["a_raw", "a_rs", "a_qng"], W=[dk_])
                    else:
                        K.stt(d_, a_raw[:], gcol, a_rs[:], ALU.mult, ALU.mult, R=["a_raw", "a_rs", "a_qng"], W=[dk_])

            normed(G_KSLC, [KsT[0:64, tsl]], ["a_KsT"], qng[:, 1:2])
            normed(G_KWIN, [KwT[0:64, tsl]], ["a_KwT"], qng[:, 1:2])
            for h in range(4):
                normed(G_AQ + h, [QaT[0:64, :, h, :], QaF[0:64, :, h, :]], ["a_QaT", "a_QaF"], qng[:, 0:1], split=True)
            for q in range(TT // 128):
                qg = t0 // 128 + q
                for kc in range(DC):
                    K.mm(PW[0][:, 300:504], hb[:, kc, q * 128:(q + 1) * 128], wins[:, kc, OFF["a_v_slc"]:OFF["a_v_slc"] + 204],
                         start=(kc == 0), stop=(kc == DC - 1), R=[("win", kc), ("hb", kc)], W=[("pw", 0)])
                K.tt(vtm[:], PW[0][:, 300:504], btm_a[:], ALU.add, R=[("pw", 0), "a_btm"], W=["a_vtm"])
                K.copy(Vs[:, qg, 0:64], vtm[:, 0:64], R=["a_vtm"], W=["a_Vs"])
                K.copy(Vw[:, qg, 0:64], vtm[:, 128:192], R=["a_vtm"], W=["a_Vw"])
                K.act(gsb[:, q, :], vtm[:, 192:204], AF.Sigmoid, R=["a_vtm"], W=["a_gsb"])
            nb0 = 0 if t0 == 0 else t0 // 16 - 1
            nb1 = (t0 + TT - 32) // 16 + 1
            nn = nb1 - nb0
            for kv, (w1, src, w2) in enumerate(((w1k, kcmpT, w2k), (w1v, vcmpT, w2v))):
                for hc in range(2):
                    for s_ in range(32):
                        K.mm(PW[0][:, 0:nn], w1[:, s_, hc * 128:(hc + 1) * 128],
                             src[:, 16 * nb0 + s_:16 * nb0 + s_ + 16 * (nn - 1) + 1:16], start=(s_ == 0), stop=(s_ == 31),
                             R=["a_w1k", "a_w1v", "a_kcmpT", "a_vcmpT"], W=[("pw", 0)])
                    K.act(hact[:, kv, hc, 0:nn], PW[0][:, 0:nn], AF.Silu, R=[("pw", 0), "a_hbias"], W=["a_hact"],
                          bias=hbias[:, kv * 2 + hc:kv * 2 + hc + 1], scale=1.0)
                for hc in range(2):
                    K.mm(PW[0][0:64, 64:64 + nn], w2[:, hc, :], hact[:, kv, hc, 0:nn], start=(hc == 0), stop=(hc == 1),
                         R=["a_w2k", "a_w2v", "a_hact"], W=[("pw", 0)])
                if kv == 0:
                    K.copy(cst[:, 0:nn], PW[0][0:64, 64:64 + nn], R=[("pw", 0)], W=["a_cst"])
                    K.tt(csq2[:, 0:nn], cst[:, 0:nn], cst[:, 0:nn], ALU.mult, R=["a_cst"], W=["a_csq2"])
                    K.mm(PW[0][0:64, 128:128 + nn], onesf[0:64, 0:64], csq2[:, 0:nn], R=["a_csq2", "ones_f"], W=[("pw", 0)])
                    K.act(crs2[:, 0:nn], PW[0][0:64, 128:128 + nn], AF.Sqrt, R=[("pw", 0), "epsc"], W=["a_crs2"],
                          bias=consts["eps"][0:64, :], scale=1.0 / 64)
                    K.recip(crs2[:, 0:nn], crs2[:, 0:nn], R=["a_crs2"], W=["a_crs2"])
                    K.stt(kcT[0:64, nb0:nb0 + nn], cst[:, 0:nn], qng[:, 1:2], crs2[:, 0:nn], ALU.mult, ALU.mult,
                          R=["a_cst", "a_crs2", "a_qng"], W=["a_kcT"])
                else:
                    K.copy(vcT[:, nb0:nb0 + nn], PW[0][0:64, 64:64 + nn], R=[("pw", 0)], W=["a_vcT"])
            for m in range(NM):
                K.tr(PW[0][:, 160 + m * 64:160 + (m + 1) * 64], vcT[:, m * 128:(m + 1) * 128], IDENT[0:64, 0:64],
                     R=["a_vcT", "cmask"], W=[("pw", 0)])
                K.copy(vcx[:, m, 0:64], PW[0][:, 160 + m * 64:160 + (m + 1) * 64], R=[("pw", 0)], W=["a_vcx"])
            for q in range(TT // 128):
                nsa_qtile(t0 // 128 + q, q)

        def nsa_qtile(i, q):
            qs = slice(q * 128, (q + 1) * 128)
            qT = QaT[:, q]
            qF = QaF[:, q]
            ek = [0]

            def scores(lhsT, rhs, bias_ap, dst, dkey, rkeys):
                k2 = ek[0] % 2
                ek[0] += 1
                ps = PS2[k2]
                K.mm(ps[:, 0:512], lhsT, rhs, R=rkeys, W=[("ps", k2)])
                if bias_ap is not None:
                    K.tt(Tb[k2][:], v4(ps[:, 0:512]), bias_ap, ALU.add, R=[("ps", k2), "a_bw4"] + [("a_bias", x) for x in range(19)],
                         W=[("a_T", k2)])
                    K.act(dst, Tb[k2][:], AF.Exp, R=[("a_T", k2)], W=[dkey])
                else:
                    K.act(dst, v4(ps[:, 0:512]), AF.Exp, R=[("ps", k2)], W=[dkey])

            jl = [j for j in range(i - 4, i + 1) if j >= 0]
            for ji, j in enumerate(jl):
                k2 = j % 2
                if j == i:
                    b_ap = v4(biasT[:, 17, :])
                elif j == i - 1:
                    b_ap = v4(biasT[:, 18, :])
                elif j == i - 4:
                    b_ap = bc(bw4[:, :].unsqueeze(1), [128, 4, 128])
                else:
                    b_ap = None
                scores(KwT[:, j * 128:(j + 1) * 128], qT.rearrange("p a b -> p (a b)"), b_ap, Pb[k2][:], ("a_P", k2),
                       ["a_KwT", "a_QaT"])
                for h in range(4):
                    K.mm(POW[:, h * 65:(h + 1) * 65], Pb[k2][:, h, :], Vw[:, j, :], start=(ji == 0 and h == 0),
                         stop=(j == i and h == 3), R=[("a_P", k2), "a_Vw"], W=["pow"], skip_group_check=True)
            ow = POW[:, 0:260].rearrange("p (a b) -> p a b", a=4)
            K.ts(rdenw[:], ow[:, :, 64], 1e-30, None, ALU.max, R=["pow"], W=["a_rdenw"])
            K.recip(rdenw[:], rdenw[:], R=["a_rdenw"], W=["a_rdenw"])
            K.tt(coefw[:], rdenw[:], gsb[:, q, 2:12:3], ALU.mult, R=["a_rdenw", "a_gsb"], W=["a_coefw"])
            K.tt(yaw[:], ow[:, :, 0:64], bc(coefw[:, :].unsqueeze(2), [128, 4, 64]), ALU.mult, R=["pow", "a_coefw"], W=["a_yaw"])
            first = True
            mlist = [m for m in range(NM) if i - 16 * m >= 0]
            for mi, m in enumerate(mlist):
                ip = i - 16 * m
                k2 = mi % 2
                b_ap = v4(biasT[:, ip, :]) if ip <= 16 else None
                scores(kcT[:, m * 128:(m + 1) * 128], qF.rearrange("p a b -> p (a b)"), b_ap, Pc[k2][:], ("a_Pc", k2),
                       ["a_kcT", "a_QaF"])
                for h in range(4):
                    K.mm(POA[:, h * 65:(h + 1) * 65], Pc[k2][:, h, :], vcx[:, m, :], start=(first and h == 0),
                         stop=(mi == len(mlist) - 1 and h == 3), R=[("a_Pc", k2), "a_vcx"], W=["poa"], skip_group_check=True)
                for h in range(4):
                    K.mm(POB[:, h * 64:(h + 1) * 64], Pc[k2][:, h, :], ovl[:, m, :], start=(first and h == 0),
                         stop=(mi == len(mlist) - 1 and h == 3), R=[("a_Pc", k2), "a_ovl"], W=[PSTK], skip_group_check=True)
                first = False
            oa = POA[:, 0:260].rearrange("p (a b) -> p a b", a=4)
            K.ts(rden[:], oa[:, :, 64], 1e-30, None, ALU.max, R=["poa"], W=["a_rden"])
            K.recip(rden[:], rden[:], R=["a_rden"], W=["a_rden"])
            K.tt(coef[:], rden[:], gsb[:, q, 0:12:3], ALU.mult, R=["a_rden", "a_gsb"], W=["a_coef"])
            K.tt(ya[:], oa[:, :, 0:64], bc(coef[:, :].unsqueeze(2), [128, 4, 64]), ALU.mult, R=["poa", "a_coef"], W=["a_ya"])
            for h in range(4):
                if h == 0:
                    K.ts(scr[:, 0:NB], POB[:, 0:NB], rden[:, 0:1], None, ALU.mult, R=[PSTK, "a_rden"], W=["a_scr"])
                else:
                    K.stt(scr[:, 0:NB], POB[:, h * 64:h * 64 + NB], rden[:, h:h + 1], scr[:, 0:NB], ALU.mult, ALU.add,
                          R=[PSTK, "a_rden", "a_scr"], W=["a_scr"])
            K.tt(scr[:, 0:NB], scr[:, 0:NB], keepc[:, 0, NB - 2 * i:2 * NB - 2 * i], ALU.mult, R=["a_scr", "a_keepc"], W=["a_scr"])
            K.tt(scr[:, 0:NB], scr[:, 0:NB], keepc[:, 1, NB - 2 * i:2 * NB - 2 * i], ALU.add, R=["a_scr", "a_keepc"], W=["a_scr"])
            K.memset(scr[:, 0:1], 1e6, W=["a_scr"])
            K.P.add("dve", lambda e: e.max(v8[:], scr[:, 0:NB]), R=["a_scr"], W=["a_v8"])
            K.P.add("dve", lambda e: e.match_replace(sc2[:, 0:NB], v8[:], scr[:, 0:NB], -3e6), R=["a_scr", "a_v8"], W=["a_sc2"])
            K.P.add("dve", lambda e: e.max(v8[:], sc2[:, 0:NB]), R=["a_sc2"], W=["a_v8"])
            K.ts(mskb[:, 0:NB], scr[:, 0:NB], v8[:, 7:8], None, ALU.is_ge, R=["a_scr", "a_v8"], W=["a_mskb"])
            K.mm(PMX[0:NB, 128:256], mskb[:, 0:NB], identb[:], R=["a_mskb", "a_identb"], W=["pmx"])
            K.copy(mT[0:NB, :], PMX[0:NB, 128:256], R=["pmx"], W=["a_mT"])
            for j in range(i + 1):
                k2 = j % 2
                if j == i:
                    b_ap = v4(biasT[:, 17, :])
                elif j == i - 1:
                    b_ap = v4(biasT[:, 18, :])
                else:
                    b_ap = None
                scores(KsT[:, j * 128:(j + 1) * 128], qT.rearrange("p a b -> p (a b)"), b_ap, Eb[k2][:], ("a_E", k2),
                       ["a_KsT", "a_QaT"])
                K.mm(PMX[:, 0:128], expc[0:NB, j * 128:(j + 1) * 128], mT[0:NB, :], R=["a_expc", "a_mT"], W=["pmx"])
                K.tt(Pb[k2][:], Eb[k2][:], bc(PMX[:, 0:128].unsqueeze(1), [128, 4, 128]), ALU.mult, R=[("a_E", k2), "pmx"],
                     W=[("a_P", k2)])
                for h in range(4):
                    K.mm(POA[:, h * 65:(h + 1) * 65], Pb[k2][:, h, :], Vs[:, j, :], start=(j == 0 and h == 0),
                         stop=(j == i and h == 3), R=[("a_P", k2), "a_Vs"], W=["poa"], skip_group_check=True)
            K.ts(rden[:], oa[:, :, 64], 1e-30, None, ALU.max, R=["poa"], W=["a_rden"])
            K.recip(rden[:], rden[:], R=["a_rden"], W=["a_rden"])
            K.tt(coef[:], rden[:], gsb[:, q, 1:12:3], ALU.mult, R=["a_rden", "a_gsb"], W=["a_coef"])
            K.tt(ytmp[:], oa[:, :, 0:64], bc(coef[:, :].unsqueeze(2), [128, 4, 64]), ALU.mult, R=["poa", "a_coef"], W=["a_ytmp"])
            K.tt(ya[:], ya[:], ytmp[:], ALU.add, R=["a_ya", "a_ytmp"], W=["a_ya"])
            K.tt(ya[:], ya[:], yaw[:], ALU.add, R=["a_ya", "a_yaw"], W=["a_ya"])
            yaf = ya[:].rearrange("p a b -> p (a b)")
            K.tt(ytmp[:], ya[:], ya[:], ALU.mult, R=["a_ya"], W=["a_ytmp"])
            K.P.add("dve", lambda e: e.tensor_reduce(yss[:], ytmp[:].rearrange("p a b -> p (a b)"), AX.X, ALU.add),
                    R=["a_ytmp"], W=["a_yss"])
            K.act(yss[:], yss[:], AF.Sqrt, R=["a_yss", "epsc"], W=["a_yss"], bias=consts["eps"][:], scale=1.0 / 256)
            K.recip(yss[:], yss[:], R=["a_yss"], W=["a_yss"])
            K.stt(yaf, yaf, yss[:, 0:1], mg0[:], ALU.mult, ALU.mult, R=["a_ya", "a_yss", "a_mg0"], W=["a_ya"])
            for pr in range(2):
                K.tr(PMX[:, 256 + pr * 128:256 + (pr + 1) * 128], yaf[:, pr * 128:(pr + 1) * 128], IDENT, R=["a_ya", "cmask"],
                     W=["pmx"])
                K.copy(yt[:, pr, qs], PMX[:, 256 + pr * 128:256 + (pr + 1) * 128], R=["pmx"], W=[("yt", pr)], eng="act")


        for tt in range(NT):
            tsl = slice(tt * TT, (tt + 1) * TT)
            K.dma(xt[:], xsv[:, :, tsl], R=[("xd", sname, tt)], W=[("xt", dc) for dc in range(DC)])
            K.act(sq[:], xt[:], AF.Square, R=[("xt", dc) for dc in range(DC)], W=["sq"])
            for dc in range(DC):
                K.mm(PST[:, 0:TT], consts["ones_bf"][:], sq[:, dc, :], start=(dc == 0), stop=(dc == DC - 1),
                     R=["sq"], W=[PSTK])
            K.act(rs[:], PST[:, 0:TT], AF.Sqrt, R=[PSTK, "epsc"], W=["rs"], bias=consts["eps"][:], scale=1.0 / D)
            K.recip(rs[:], rs[:], R=["rs"], W=["rs"])
            for dc in range(DC):
                k2 = dc % 2
                K.stt(sa[k2][:], xt[:, dc, :], gs[:, s, dc:dc + 1], rs[:], ALU.mult, ALU.mult,
                      R=[("xt", dc), "rs", "gs"], W=[("sa", k2)])
                K.act(hb[:, dc, :], sa[k2][:], AF.Identity, R=[("sa", k2), "mod"], W=[("hb", dc)],
                      bias=shift[:, s * 24 + dc:s * 24 + dc + 1], scale=1.0)
            if mode == 1:
                for r in range(2, DC):
                    if not (("B" in enable and r in (2, 3)) or ("C" in enable and r in (4, 5)) or ("D" in enable and r in (6, 7))):
                        K.memset(yt[:, r, :], 0.0, W=[("yt", r)], eng="pool")
            else:
                K.dma(yt[:, 2:DC, :], yscr.rearrange("(dc p) t -> p dc t", p=128)[:, 2:DC, tsl],
                      R=[("yscr", tt * TT // 256 + q) for q in range(TT // 256)], W=[("yt", r) for r in range(2, DC)])
                if "A" not in enable:
                    for r in range(2):
                        K.memset(yt[:, r, :], 0.0, W=[("yt", r)], eng="pool")
            if "B" in enable:
                for ch in range(2):
                    proj_fm(G_BB + ch, bbt[:], ["bbt"])
                    proj_fm(G_BC + ch, cct[:], ["cct"])
                    proj_fm(G_BX + ch, cvt[:], ["cvt"])
                    K.tt(ub[ch][:, 2:2 + TT], cct[:], cvt[:], ALU.mult, R=["cct", "cvt"], W=[("ub", ch)])
                    K.ts(cvt[:], ub[ch][:, 2:2 + TT], scw[:, ch, 2:3], None, ALU.mult, R=[("ub", ch), "scw"], W=["cvt"])
                    K.stt(cvt[:], ub[ch][:, 1:1 + TT], scw[:, ch, 1:2], cvt[:], ALU.mult, ALU.add,
                          R=[("ub", ch), "scw", "cvt"], W=["cvt"])
                    K.stt(cvt[:], ub[ch][:, 0:TT], scw[:, ch, 0:1], cvt[:], ALU.mult, ALU.add,
                          R=[("ub", ch), "scw", "cvt"], W=["cvt"])
                    K.tt(ybt[ch][:], bbt[:], cvt[:], ALU.mult, R=["bbt", "cvt"], W=[("ybt", ch)])
                    K.copy(ub[ch][:, 0:2], ub[ch][:, TT:TT + 2], R=[("ub", ch)], W=[("ub", ch)])
                    K.act(sq[:, ch, :], ybt[ch][:], AF.Square, R=[("ybt", ch)], W=["sq"])
                for ch in range(2):
                    K.mm(PST[:, 0:TT], consts["ones_bf"][:], sq[:, ch, :], start=(ch == 0), stop=(ch == 1),
                         R=["sq"], W=[PSTK])
                K.act(sa[0][:], PST[:, 0:TT], AF.Sqrt, R=[PSTK, "epsc"], W=[("sa", 0)], bias=consts["eps"][:],
                      scale=1.0 / 256)
                K.recip(sa[0][:], sa[0][:], R=[("sa", 0)], W=[("sa", 0)])
                for ch in range(2):
                    K.stt(yt[:, 2 + ch, :], ybt[ch][:], mixg[:, 1, ch:ch + 1], sa[0][:], ALU.mult, ALU.mult,
                          R=[("ybt", ch), "mixg", ("sa", 0)], W=[("yt", 2 + ch)])
            if "D" in enable:
                for h in range(4):
                    proj_fm(G_DQ + h, QmT[:, h, :], ["QmT"])
                    proj_fm(G_DK + h, KmT[:, h, :], ["KmT"], scale=0.125, bias=bfm8[0:64, h:h + 1])
            if "C" in enable:
                gdn_tile()
            for c in range(NCH):
                if "D" in enable:
                    mlstm_chunk(c)
                if "C" in enable:
                    gdn_chunk(c)
            if mode == 1:
                K.dma(yscr.rearrange("(dc p) t -> p dc t", p=128)[:, 2:DC, tsl], yt[:, 2:DC, :],
                      R=[("yt", r) for r in range(2, DC)], W=[("yscr", tt * TT // 256 + q) for q in range(TT // 256)])
                continue
            if "A" in enable:
                nsa_tile(tt)
                K.dma(yscr.rearrange("(dc p) t -> p dc t", p=128)[:, 0:2, tsl], yt[:, 0:2, :],
                      R=[("yt", r) for r in range(2)], W=[("yscrA", tt)])
            for dc in range(DC):
                pj = PJ[pjn[0] % len(PJ)]
                kk = ("pj", pjn[0] % len(PJ))
                pjn[0] += 1
                for fc in range(DC):
                    K.mm(pj[:, 0:TT], wouts[:, fc, dc * 128:(dc + 1) * 128], yt[:, fc, :], start=(fc == 0),
                         stop=(fc == DC - 1), R=[("wout", fc), ("yt", fc)], W=[kk])
                K.stt(xt[:, dc, :], pj[:, 0:TT], gate[:, s, dc:dc + 1], xt[:, dc, :], ALU.mult, ALU.add,
                      R=[kk, "gate", ("xt", dc)], W=[("xt", dc)])
            K.dma(xdv[:, :, tsl], xt[:], R=[("xt", dc) for dc in range(DC)], W=[("xd", dname, tt)])
    P.barrier()


def t5_thresholds():
    def bucket(n):
        if n < 16:
            return n
        nf = np.float32(n)
        v = np.log(nf / np.float32(16)) / np.float32(math.log(128 / 16)) * np.float32(16)
        return min(16 + int(np.float32(v)), 31)
    bs = [bucket(n) for n in range(0, 400)]
    return [min(n for n in range(400) if bs[n] >= b) for b in range(32)]


def bias_build(K, t5_table, dist_d, bias_scr, consts, es_ext=None):
    nc = K.nc
    lo = t5_thresholds()
    with ExitStack() as es_own:
        es = es_ext if es_ext is not None else es_own
        tb = _sb(es, nc, "bb_tb", [128, 32, 4], F32)
        ndl = _sb(es, nc, "bb_ndl", [128, 31, 4], F32)
        dtl = [_sb(es, nc, "bb_dt%d" % i, [128, 128], F32) for i in range(2)]
        acc = [_sb(es, nc, "bb_acc%d" % i, [128, 4, 128], F32) for i in range(2)]
        tmp = [_sb(es, nc, "bb_tmp%d" % i, [128, 4, 128], F32) for i in range(2)]
        K.dma(tb[:].rearrange("p b h -> p (b h)"), t5_table.rearrange("b h -> (b h)").partition_broadcast(128), W=["bb_tb"])
        K.tt(ndl[:], tb[:, 0:31, :], tb[:, 1:32, :], ALU.subtract, R=["bb_tb"], W=["bb_ndl"])
        for q in range(19):
            k2 = q % 2
            K.dma(dtl[k2][:], dist_d[q], W=[("bb_dt", k2)])
            dbc = bc(dtl[k2][:, :].unsqueeze(1), [128, 4, 128])
            for b in range(1, 32):
                dst = acc[k2] if b == 1 else tmp[b % 2]
                dk = ("bb_acc", k2) if b == 1 else ("bb_tmp", b % 2)
                K.stt(dst[:], dbc, float(lo[b]), bc(ndl[:, b - 1, :].unsqueeze(2), [128, 4, 128]), ALU.is_lt, ALU.mult,
                      R=[("bb_dt", k2), "bb_ndl"], W=[dk])
                if b > 1:
                    K.tt(acc[k2][:], acc[k2][:], dst[:], ALU.add, R=[("bb_acc", k2), dk], W=[("bb_acc", k2)])
            K.ts(tmp[0][:], dbc, 0.0, -30000.0, ALU.is_lt, ALU.mult, R=[("bb_dt", k2)], W=[("bb_tmp", 0)])
            K.tt(acc[k2][:], acc[k2][:], tmp[0][:], ALU.add, R=[("bb_acc", k2), ("bb_tmp", 0)], W=[("bb_acc", k2)])
            K.dma(bias_scr[q], acc[k2][:].rearrange("p a b -> p (a b)"), R=[("bb_acc", k2)], W=[("bias_scr", q)])
    if es_ext is None:
        K.P.barrier()


def build(T=4096, TT=512, layers=2, debug_y=False, enable="ABCD"):
    nc = bass.Bass("TRN2", target_bir_lowering=False)
    K = KB(nc)
    P = K.P
    dt = lambda name, shape, kind="ExternalInput", d=F32: nc.dram_tensor(name, list(shape), d, kind=kind).ap()
    xT = dt("xT", [D, T])
    cT = dt("cT", [128, DC])
    ada_w = dt("ada_w", [2, D, 9 * D])
    ada_bT = dt("ada_bT", [2, 128, 72])
    normgT = dt("normgT", [2, 128, 3, DC])
    ffn_w13 = [dt("ffn1_w13", [2, D, 2 * DFF]), dt("ffn2_w13", [2, D, 2 * DFF])]
    ffn_w2 = [dt("ffn1_w2", [2, DFF, D]), dt("ffn2_w2", [2, DFF, D])]
    outT = dt("outT", [D, T], kind="ExternalOutput")
    Win = {
        "w_in": dt("w_in", [2, D, D_IN]), "b_in": dt("b_in", [2, D_IN]), "b_fm": dt("b_fm", [2, 128, NFM]),
        "w_out": dt("w_out", [2, D, D]), "sc_conv_wT": dt("sc_conv_wT", [2, 128, 2, 3]),
        "mixgT": dt("mixgT", [2, 128, 2, 2]), "mlstm_f_bias": dt("mlstm_f_bias", [2, 4]),
        "mlstm_norm_g": dt("mlstm_norm_g", [2, 64]),
        "gdn_conv_wT": dt("gdn_conv_wT", [2, 64, 12, 4]), "gdn_A_log": dt("gdn_A_log", [2, 4]),
        "gdn_dt_bias": dt("gdn_dt_bias", [2, 4]), "gdn_norm_g": dt("gdn_norm_g", [2, 64]),
        "cmp_k_w1": dt("cmp_k_w1", [2, 2048, 256]), "cmp_v_w1": dt("cmp_v_w1", [2, 2048, 256]),
        "cmp_k_w2": dt("cmp_k_w2", [2, 256, 64]), "cmp_v_w2": dt("cmp_v_w2", [2, 256, 64]),
        "cmp_posT": dt("cmp_posT", [2, 64, 32]), "qkng": dt("qkng", [2, 64, 2]), "mix_norm_g0": dt("mix_norm_g0", [2, 256]),
    }
    NB_ = T // 64
    NM_ = (T // 16 + 127) // 128
    Wsh = {
        "t5_table": dt("t5_table", [32, 4]), "expc": dt("expc", [NB_, T]), "keepadd": dt("keepadd", [128, 2, 2 * NB_]),
        "ovl": dt("ovl", [128, NM_, 64]), "bw4": dt("bw4", [128, 128]),
        "bias_scr": dt("bias_scr", [19, 128, 512], kind="Internal"),
    }
    dist_d = dt("dist_tiles", [19, 128, 128])
    cmask_d = dt("cmask", [128, CM_N])
    yscr = dt("ydbg", [D, T], kind="ExternalOutput" if debug_y else "Internal", d=BF16)
    xa = dt("xa_scr", [D, T], kind="Internal")
    xb = dt("xb_scr", [D, T], kind="Internal")

    with ExitStack() as es:
        consts = {
            "ones_bf": _sb(es, nc, "ones_bf", [128, 128], BF16),
            "eps": _sb(es, nc, "epsc", [128, 1], F32),
        }
        K.memset(consts["ones_bf"][:], 1.0, W=["ones_bf"])
        consts["ones_f"] = _sb(es, nc, "ones_f", [128, 128], F32)
        consts["one"] = _sb(es, nc, "onec", [128, 1], F32)
        consts["cmask"] = _sb(es, nc, "cmask_sb", [128, CM_N], F32)
        K.memset(consts["ones_f"][:], 1.0, W=["ones_f"])
        K.memset(consts["one"][:], 1.0, W=["onec"])
        K.dma(consts["cmask"][:], cmask_d, W=["cmask"])
        K.memset(consts["eps"][:], EPS, W=["epsc"])
        condT = _sb(es, nc, "condT", [128, DC, 2], F32)
        ctmp = _sb(es, nc, "ctmp", [128, DC], F32)
        K.memset(condT[:], 0.0, W=["condT"])
        K.dma(ctmp[:], cT, W=["ctmp"])
        K.act(condT[:, :, 0], ctmp[:], AF.Silu, R=["ctmp"], W=["condT"])
        mv = []
        for l in range(2):
            mv.append({
                "mod": _sb(es, nc, "mod%d" % l, [128, 72], F32),
                "gs": _sb(es, nc, "gs%d" % l, [128, 3, DC], F32),
                "gate": _sb(es, nc, "gate%d" % l, [128, 3, DC], F32),
            })
        adab = _sb(es, nc, "adab", [128, 2, 72], F32)
        normg = _sb(es, nc, "normg", [128, 2, 3, DC], F32)
        K.dma(adab[:], ada_bT.rearrange("l p j -> p l j"), W=["adab"])
        K.dma(normg[:], normgT.rearrange("l p s d -> p l s d"), W=["normg"])
        P.barrier()

        with ExitStack() as es0:
            for l in range(layers):
                mod_phase(K, es0, l, ada_w[l], adab[:, l, :], normg[:, l], condT, mv[l])
            if "A" in enable:
                bias_build(K, Wsh["t5_table"], dist_d, Wsh["bias_scr"], consts, es_ext=es0)
        P.barrier()
        cur = xT
        curname = "xT"
        for l in range(layers):
            last = (l == layers - 1)
            ffn_phase(K, l, 0, cur, curname, xa, "xa", ffn_w13[0][l], ffn_w2[0][l], mv[l], 0, T, TT, consts)
            Wd = {k: v[l] for k, v in Win.items()}
            Wd.update(Wsh)
            mixer_phase(K, l, xa, "xa", xb, "xb", Wd, mv[l], T, 256, consts, yscr, 1, enable=enable)
            mixer_phase(K, l, xa, "xa", xb, "xb", Wd, mv[l], T, 256, consts, yscr, 2, enable=enable)
            ffn_phase(K, l, 1, xb, "xb", outT if last else xa, "outT" if last else "xa", ffn_w13[1][l], ffn_w2[1][l], mv[l], 2, T, TT, consts)
            cur = xa
            curname = "xa"
        P.fence("sp", [("xd", "outT", tt) for tt in range(T // TT)])
        P.emit()
    return nc


def _cmask():
    m = np.zeros((128, CM_N), np.float32)
    m[:, CM_ID:CM_ID + 128] = np.eye(128, dtype=np.float32)
    k = np.arange(64)[:, None]
    i = np.arange(64)[None, :]
    m[0:64, CM_TRIU:CM_TRIU + 64] = (k <= i)
    m[0:64, CM_SU:CM_SU + 64] = (k < i)
    m[0:64, CM_SL:CM_SL + 64] = (k > i)
    m[0:64, CM_LI:CM_LI + 64] = (k >= i)
    return m


def prep_shared(inp, T=4096):
    f = lambda a: np.ascontiguousarray(np.asarray(a, dtype=np.float32))
    b_in = f(inp["b_in"])
    b_fm = np.zeros((2, 128, NFM), np.float32)
    for g, (c0, n) in enumerate(FM):
        b_fm[:, 0:n, g] = b_in[:, c0:c0 + n]
    sh = {
        "ada_w": f(inp["ada_w"]),
        "ada_bT": f(np.asarray(inp["ada_b"]).reshape(2, 72, 128).transpose(0, 2, 1)),
        "normgT": f(np.asarray(inp["norm_g"]).reshape(2, 3, 8, 128).transpose(0, 3, 1, 2)),
        "ffn1_w13": f(inp["ffn1_w13"]), "ffn2_w13": f(inp["ffn2_w13"]),
        "ffn1_w2": f(inp["ffn1_w2"]), "ffn2_w2": f(inp["ffn2_w2"]),
        "w_in": f(inp["w_in"]), "b_in": b_in, "b_fm": b_fm, "w_out": f(inp["w_out"]),
        "sc_conv_wT": f(np.asarray(inp["sc_conv_w"]).reshape(2, 3, 2, 128).transpose(0, 3, 2, 1)),
        "mixgT": f(np.asarray(inp["mix_norm_g"]).reshape(2, 2, 2, 128).transpose(0, 3, 1, 2)),
        "mlstm_f_bias": f(inp["mlstm_f_bias"]), "mlstm_norm_g": f(inp["mlstm_norm_g"]),
        "gdn_conv_wT": f(np.asarray(inp["gdn_conv_w"]).reshape(2, 4, 12, 64).transpose(0, 3, 2, 1)),
        "gdn_A_log": f(inp["gdn_A_log"]), "gdn_dt_bias": f(inp["gdn_dt_bias"]), "gdn_norm_g": f(inp["gdn_norm_g"]),
        "cmask": _cmask(),
        "cmp_k_w1": f(inp["cmp_k_w1"]), "cmp_v_w1": f(inp["cmp_v_w1"]), "cmp_k_w2": f(inp["cmp_k_w2"]),
        "cmp_v_w2": f(inp["cmp_v_w2"]), "cmp_posT": f(np.asarray(inp["cmp_pos"]).transpose(0, 2, 1)),
        "qkng": f(np.stack([np.asarray(inp["q_norm_g"]), np.asarray(inp["k_norm_g"])], axis=-1)),
        "mix_norm_g0": f(np.asarray(inp["mix_norm_g"])[:, 0]), "t5_table": f(inp["t5_table"]),
    }
    sh.update(_nsa_consts(T))
    return sh


def _nsa_consts(T):
    NB = T // 64
    NM = (T // 16 + 127) // 128
    c = np.arange(128)[:, None]
    r = np.arange(128)[None, :]
    dist = np.zeros((19, 128, 128), np.float32)
    for ip in range(17):
        dist[ip] = r - 16 * c + 128 * ip - 31
    dist[17] = r - c
    dist[18] = 128 + r - c
    expc = (np.arange(T)[None, :] // 64 == np.arange(NB)[:, None]).astype(np.float32)
    keep = np.ones((128, 2 * NB), np.float32)
    add = np.zeros((128, 2 * NB), np.float32)
    for rr in range(128):
        for x in range(2 * NB):
            rb = x - NB
            if rr < 64:
                forced, invalid = rb in (-1, 0), rb > 0
            else:
                forced, invalid = rb in (0, 1), rb > 1
            if invalid:
                keep[rr, x], add[rr, x] = 0.0, -1e6
            elif forced:
                keep[rr, x], add[rr, x] = 0.0, 1e6
    ovl = np.zeros((128, NM, 64), np.float32)
    for m in range(NM):
        ci = 128 * m + np.arange(128)[:, None]
        bj = np.arange(64)[None, :]
        ovl[:, m, :] = ((ci * 16 < (bj + 1) * 64) & (ci * 16 + 32 > bj * 64) & (bj < NB) & (ci < T // 16 - 1))
    bw4 = np.where(c > r, 0.0, -30000.0).astype(np.float32)
    return {"dist_tiles": dist, "expc": expc, "keepadd": np.ascontiguousarray(np.stack([keep, add], axis=1)),
            "ovl": ovl, "bw4": bw4}


def prep_core(inp, b):
    x = np.asarray(inp["x"], dtype=np.float32)
    c = np.asarray(inp["c"], dtype=np.float32)
    return {"xT": np.ascontiguousarray(x[b].T), "cT": np.ascontiguousarray(c[b].reshape(8, 128).T)}


_NC_CACHE = {}


def kernel(**inputs):
    x = np.asarray(inputs["x"])
    B, T, _ = x.shape
    if T not in _NC_CACHE:
        _NC_CACHE[T] = build(T=T)
    nc = _NC_CACHE[T]
    sh = prep_shared(inputs, T)
    in_maps = []
    for b in range(B):
        m = dict(sh)
        m.update(prep_core(inputs, b))
        in_maps.append(m)
    res = run_bass_kernel_spmd(nc, in_maps, core_ids=list(range(B)))
    out = np.stack([np.asarray(r["outT"]).T for r in res.results], axis=0)
    return np.ascontiguousarray(out.astype(np.float32))
```

```python
import math
import os
GSTOP = float(os.environ.get('GSTOP', '99'))
from contextlib import ExitStack
import numpy as np
import concourse.bass as bass
import concourse.mybir as mybir
from concourse.bass_utils import run_bass_kernel_spmd

F32 = mybir.dt.float32
BF16 = mybir.dt.bfloat16
AF = mybir.ActivationFunctionType
ALU = mybir.AluOpType
AX = mybir.AxisListType

D = 1024
DC = 8
DFF = 2816
FC = 22
D_IN = 3484
EPS = 1e-6

ENGS = ("pe", "act", "dve", "pool", "sp")


class _Op:
    __slots__ = ("eng", "fn", "reads", "writes", "deps", "sig", "tok", "waits", "dma", "snap", "inc", "cost", "odeps", "idx")

    def __init__(self, eng, fn, reads, writes, dma):
        self.eng = eng
        self.fn = fn
        self.reads = reads
        self.writes = writes
        self.dma = dma
        self.deps = ()
        self.sig = False
        self.tok = None
        self.waits = ()
        self.snap = None
        self.inc = 1
        self.cost = 300.0
        self.odeps = ()
        self.idx = 0


class Prog:
    EPOCH = 20000
    NDMA = 12

    def __init__(self, nc):
        self.nc = nc
        self.ops = []
        self.last_w = {}
        self.readers = {}
        self.last_on = {}
        self.dma_ops = []

    EXCL = ("pj", "pw", "ptk", "pab", "po", "pst", "pmod", "ps", "poa", "pob", "pmx", "pd", "pc", "pow")

    def _excl(self, k):
        return (k[0] if isinstance(k, tuple) else k) in self.EXCL

    def add(self, eng, fn, R=(), W=(), dma=False, cost=300.0):
        xr = [k for k in R if self._excl(k)]
        if xr:
            R = [k for k in R if not self._excl(k)]
            W = list(W) + [k for k in xr if k not in W]
        op = _Op(eng, fn, tuple(R), tuple(W), dma)
        i = len(self.ops)
        deps = set()
        for k in op.reads:
            w = self.last_w.get(k)
            if w is not None:
                deps.add(w)
        for k in op.writes:
            w = self.last_w.get(k)
            if w is not None:
                deps.add(w)
            deps.update(self.readers.get(k, ()))
        for k in op.writes:
            self.last_w[k] = i
            self.readers[k] = []
        for k in op.reads:
            if k not in op.writes:
                self.readers.setdefault(k, []).append(i)
        op.deps = deps
        op.cost = cost
        self.ops.append(op)
        self.last_on[eng] = i
        if dma:
            self.dma_ops.append(i)
        return i

    def barrier(self):
        lasts = set(self.last_on.values()) | set(self.dma_ops)
        self.dma_ops = []
        for e in ENGS:
            op = _Op(e, None, (), (), False)
            op.deps = set(lasts)
            self.ops.append(op)
            self.last_on[e] = len(self.ops) - 1
        self.last_w = {}
        self.readers = {}

    def fence(self, eng, keys):
        self.add(eng, None, R=keys)

    def schedule(self):
        import heapq
        ops = self.ops
        n = len(ops)
        for i, op in enumerate(ops):
            op.idx = i
        order = []
        LAT = 250.0
        W = 24
        seg_start = 0
        i = 0
        segs = []
        while i < n:
            if ops[i].fn is None and not ops[i].dma and len(ops[i].reads) == 0 and len(ops[i].writes) == 0:
                j = i
                while j < n and ops[j].fn is None and not ops[j].dma:
                    j += 1
                segs.append((seg_start, i))
                segs.append((i, j))
                seg_start = j
                i = j
            else:
                i += 1
        segs.append((seg_start, n))
        for (a, b) in segs:
            if b <= a:
                continue
            if ops[a].fn is None and not ops[a].dma:
                order.extend(range(a, b))
                continue
            nrem = {}
            users = {}
            for k in range(a, b):
                dl = [d for d in ops[k].deps if d >= a]
                nrem[k] = len(dl)
                for d in dl:
                    users.setdefault(d, []).append(k)
            blev = {}
            for k in range(b - 1, a - 1, -1):
                m_ = 0.0
                for u in users.get(k, ()):
                    if blev[u] > m_:
                        m_ = blev[u]
                blev[k] = ops[k].cost + LAT + m_
            ready = {e: [] for e in ENGS}
            for k in range(a, b):
                if nrem[k] == 0:
                    heapq.heappush(ready[ops[k].eng], k)
            fin = {}
            free = {e: 0.0 for e in ENGS}
            left = b - a
            while left > 0:
                best = None
                for e in ENGS:
                    rl = ready[e]
                    if not rl:
                        continue
                    cands = heapq.nsmallest(W, rl)
                    for k in cands:
                        st = free[e]
                        for d in ops[k].deps:
                            if d >= a:
                                f = fin[d] + LAT
                                if f > st:
                                    st = f
                        key = (int(st / 250.0), -blev[k], k, st)
                        if best is None or key < best[0]:
                            best = (key, e, k)
                (_q, _b, k, st), e, _ = best
                ready[e].remove(k)
                heapq.heapify(ready[e])
                op = ops[k]
                if op.dma:
                    free[e] = st + 60.0
                    fin[k] = st + op.cost
                else:
                    free[e] = st + op.cost
                    fin[k] = st + op.cost
                order.append(k)
                left -= 1
                for u in users.get(k, ()):
                    nrem[u] -= 1
                    if nrem[u] == 0:
                        heapq.heappush(ready[ops[u].eng], u)
        return order

    def emit(self):
        nc = self.nc
        if os.environ.get("NOSCHED", "0") != "1":
            order = self.schedule()
            old = self.ops
            remap = {o: nidx for nidx, o in enumerate(order)}
            newops = [old[o] for o in order]
            for op in newops:
                op.deps = {remap[d] for d in op.deps}
            self.ops = newops
        ops = self.ops
        for i, op in enumerate(ops):
            if op.eng == "pe":
                op.deps = {d for d in op.deps if not (ops[d].eng == "pe" and not ops[d].dma)}
        ndma = 0
        slot_last = {}
        for i, op in enumerate(ops):
            if op.dma:
                slot = ndma % self.NDMA
                if slot in slot_last:
                    op.deps = set(op.deps) | {slot_last[slot]}
                slot_last[slot] = i
                op.tok = ("dma", slot, 16 * (ndma // self.NDMA + 1))
                ndma += 1
        for op in ops:
            for d in op.deps:
                ops[d].sig = True
        cnt = {e: 0 for e in ENGS}
        for op in ops:
            if op.sig and not op.dma:
                if op.fn is None:
                    continue
                cnt[op.eng] += 1
                c = cnt[op.eng]
                op.tok = (op.eng, (c - 1) // self.EPOCH, (c - 1) % self.EPOCH + 1)
        nep = {e: (cnt[e] + self.EPOCH - 1) // self.EPOCH for e in ENGS}
        sems = {}
        for e in ENGS:
            for k in range(max(nep[e], 0)):
                sems[(e, k)] = nc.alloc_semaphore("s_%s_%d" % (e, k))
        for s in range(min(self.NDMA, max(ndma, 1))):
            sems[("dma", s)] = nc.alloc_semaphore("s_dma_%d" % s)
        known = {e: {} for e in ENGS}
        for op in ops:
            kn = known[op.eng]
            waits = []
            stack = list(op.deps)
            seen = set()
            while stack:
                d = stack.pop()
                if d in seen:
                    continue
                seen.add(d)
                dop = ops[d]
                if dop.fn is None and not dop.dma:
                    if dop.eng == op.eng:
                        continue
                    stack.extend(dop.deps)
                    continue
                tk = dop.tok
                key = (tk[0], tk[1])
                if kn.get(key, 0) >= tk[2]:
                    continue
                waits.append((key, tk[2]))
                if dop.snap is not None:
                    for k2, v2 in dop.snap.items():
                        if kn.get(k2, 0) < v2:
                            kn[k2] = v2
                kn[key] = tk[2]
            wm = {}
            for key, v in waits:
                if wm.get(key, 0) < v:
                    wm[key] = v
            op.waits = tuple(wm.items())
            if op.tok is not None:
                op.snap = dict(kn)
        handles = {"pe": nc.tensor, "act": nc.scalar, "dve": nc.vector, "pool": nc.gpsimd, "sp": nc.sync}
        per = {e: [op for op in ops if op.eng == e] for e in ENGS}

        def run(e, eng):
            for op in per[e]:
                for key, v in op.waits:
                    eng.wait_ge(sems[key], v)
                if op.fn is None:
                    continue
                ins = op.fn(eng)
                if op.tok is not None:
                    tk = op.tok
                    ins.then_inc(sems[(tk[0], tk[1])], 16 if op.dma else 1)

        with nc.Block() as block:
            @block.tensor
            def _(eng):
                run("pe", eng)

            @block.scalar
            def _(eng):
                run("act", eng)

            @block.vector
            def _(eng):
                run("dve", eng)

            @block.gpsimd
            def _(eng):
                run("pool", eng)

            @block.sync
            def _(eng):
                run("sp", eng)
        self.stats = dict(n_ops=len(ops), cnt=cnt, ndma=ndma)


class KB:
    def __init__(self, nc):
        self.nc = nc
        self.P = Prog(nc)

    @staticmethod
    def _fs(ap):
        n = 1
        for d in ap.shape[1:]:
            n *= d
        return n

    def mm(self, out, lhsT, rhs, start=True, stop=True, R=(), W=(), **kw):
        passes = 2.0 if rhs.dtype == F32 else 1.0
        c = 40.0 + (self._fs(lhsT) * 0.85 + max(64, self._fs(rhs)) * 0.85) * passes
        return self.P.add("pe", lambda e: e.matmul(out, lhsT, rhs, start=start, stop=stop, **kw), R, W, cost=c)

    def tr(self, out, in_, ident, R=(), W=()):
        return self.P.add("pe", lambda e: e.transpose(out, in_, ident), R, W, cost=120.0)

    def act(self, out, in_, func, R=(), W=(), bias=None, scale=None):
        kw = {}
        if bias is not None:
            kw["bias"] = bias
        if scale is not None:
            kw["scale"] = scale
        return self.P.add("act", lambda e: e.activation(out, in_, func, **kw), R, W, cost=260.0 + 0.75 * self._fs(in_))

    def tt(self, out, in0, in1, op, R=(), W=(), eng="dve"):
        return self.P.add(eng, lambda e: e.tensor_tensor(out, in0, in1, op), R, W, cost=150.0 + 1.45 * self._fs(out))

    def ts(self, out, in0, s1, s2, op0, op1=None, R=(), W=(), eng="dve"):
        if op1 is None:
            return self.P.add(eng, lambda e: e.tensor_scalar(out, in0, s1, None, op0), R, W, cost=150.0 + 1.1 * self._fs(out))
        return self.P.add(eng, lambda e: e.tensor_scalar(out, in0, s1, s2, op0, op1), R, W, cost=150.0 + 1.1 * self._fs(out))

    def stt(self, out, in0, scalar, in1, op0, op1, R=(), W=()):
        return self.P.add("dve", lambda e: e.scalar_tensor_tensor(out, in0, scalar, in1, op0, op1), R, W,
                          cost=150.0 + 1.1 * self._fs(out))

    def copy(self, out, in_, R=(), W=(), eng="dve"):
        if eng == "act":
            return self.P.add("act", lambda e: e.copy(out, in_), R, W, cost=260.0 + 0.75 * self._fs(out))
        return self.P.add(eng, lambda e: e.tensor_copy(out, in_), R, W, cost=150.0 + 0.9 * self._fs(out))

    def recip(self, out, in_, R=(), W=()):
        return self.P.add("dve", lambda e: e.reciprocal(out, in_), R, W, cost=200.0 + 2.6 * self._fs(out))

    def memset(self, ap, val, W=(), eng="dve"):
        return self.P.add(eng, lambda e: e.memset(ap, val), (), W, cost=110.0 + 1.05 * self._fs(ap))

    def dma(self, out, in_, R=(), W=(), eng="sp", **kw):
        nbytes = out.shape[0] * self._fs(out) * (2 if out.dtype == BF16 else 4)
        return self.P.add(eng, lambda e: e.dma_start(out, in_, **kw), R, W, dma=True, cost=2200.0 + nbytes / 120.0)


def _sb(es, nc, name, shape, dt):
    return es.enter_context(nc.sbuf_tensor(name, shape, dt))


def _ps(es, nc, name, shape, dt=F32):
    return es.enter_context(nc.psum_tensor(name, shape, dt))


def mod_phase(K, es_glob, lay, ada_w_l, ada_bT_l, normgT_l, condT, out):
    nc = K.nc
    P = K.P
    with ExitStack() as es_own:
        es = es_glob if es_glob is not None else es_own
        wt = [_sb(es, nc, "adaw%d_%d" % (lay, i), [128, DC, 1024], F32) for i in range(2)]
        pm = _ps(es, nc, "pmod%d" % lay, [128, 72 * 2], F32)
        awv = ada_w_l.rearrange("(kc p) n -> p kc n", p=128)
        for g in range(9):
            b = wt[g % 2]
            for kc in range(DC):
                K.dma(b[:, kc, :], awv[:, kc, g * 1024:(g + 1) * 1024], W=[("adaw", g % 2, kc)])
            for jj in range(8):
                j = g * 8 + jj
                for kc in range(DC):
                    K.mm(pm[:, 2 * j:2 * j + 2], b[:, kc, jj * 128:(jj + 1) * 128], condT[:, kc, :],
                         start=(kc == 0), stop=(kc == DC - 1), R=[("adaw", g % 2, kc), "condT"], W=["pmod"])
        mod = out["mod"]
        pmv = pm[:].rearrange("p (j two) -> p j two", two=2)[:, :, 0]
        K.tt(mod[:], pmv, ada_bT_l, ALU.add, R=["pmod", "adab"], W=["mod"])
        for s in range(3):
            K.stt(out["gs"][:, s, :], mod[:, s * 24 + 8:s * 24 + 16], 1.0, normgT_l[:, s, :], ALU.add, ALU.mult,
                  R=["mod", "normg"], W=["gs"])
            K.ts(out["gate"][:, s, :], mod[:, s * 24 + 16:s * 24 + 24], 0.5 if s != 1 else 1.0, None, ALU.mult,
                 R=["mod"], W=["gate"])
    if es_glob is None:
        P.barrier()


def ffn_phase(K, lay, which, x_src, sname, x_dst, dname, w13, w2, mv, s, T, TT, consts):
    nc = K.nc
    P = K.P
    NT = T // TT
    tg = "f%d%d" % (lay, which)
    with ExitStack() as es:
        w13s = _sb(es, nc, "w13_" + tg, [128, DC, 2 * DFF], BF16)
        w2s = _sb(es, nc, "w2_" + tg, [128, FC, D], BF16)
        xt = _sb(es, nc, "xt_" + tg, [128, DC, TT], F32)
        sq = _sb(es, nc, "sq_" + tg, [128, DC, TT], BF16)
        hb = _sb(es, nc, "hb_" + tg, [128, DC, TT], BF16)
        gb = _sb(es, nc, "gb_" + tg, [128, FC, TT], BF16)
        rs = _sb(es, nc, "rs_" + tg, [128, TT], F32)
        sa = [_sb(es, nc, "sa%d_" % i + tg, [128, TT], F32) for i in range(2)]
        pst = _ps(es, nc, "pst_" + tg, [128, TT])
        pa = [_ps(es, nc, "pa%d_" % i + tg, [128, TT]) for i in range(2)]
        pb = [_ps(es, nc, "pb%d_" % i + tg, [128, TT]) for i in range(2)]
        po = [_ps(es, nc, "po%d_" % i + tg, [128, TT]) for i in range(2)]
        w13v = w13.rearrange("(kc p) f -> p kc f", p=128)
        w2v = w2.rearrange("(fc p) d -> p fc d", p=128)
        for kc in range(DC):
            K.dma(w13s[:, kc, :], w13v[:, kc, :], W=[("w13", kc)], eng="pool")
        for fc in range(FC):
            K.dma(w2s[:, fc, :], w2v[:, fc, :], W=[("w2", fc)], eng="pool")
        xsv = x_src.rearrange("(dc p) t -> p dc t", p=128)
        xdv = x_dst.rearrange("(dc p) t -> p dc t", p=128)
        gs, shift, gate = mv["gs"], mv["mod"], mv["gate"]
        for tt in range(NT):
            tsl = slice(tt * TT, (tt + 1) * TT)
            K.dma(xt[:], xsv[:, :, tsl], R=[("xd", sname, tt)], W=[("xt", dc) for dc in range(DC)])
            K.act(sq[:], xt[:], AF.Square, R=[("xt", dc) for dc in range(DC)], W=["sq"])
            for dc in range(DC):
                K.mm(pst[:], consts["ones_bf"][:], sq[:, dc, :], start=(dc == 0), stop=(dc == DC - 1),
                     R=["sq"], W=["pst"])
            K.act(rs[:], pst[:], AF.Sqrt, R=["pst", "epsc"], W=["rs"], bias=consts["eps"][:], scale=1.0 / D)
            K.recip(rs[:], rs[:], R=["rs"], W=["rs"])
            for dc in range(DC):
                k2 = dc % 2
                K.stt(sa[k2][:], xt[:, dc, :], gs[:, s, dc:dc + 1], rs[:], ALU.mult, ALU.mult,
                      R=[("xt", dc), "rs", "gs"], W=[("sa", k2)])
                K.act(hb[:, dc, :], sa[k2][:], AF.Identity, R=[("sa", k2), "mod"], W=[("hb", dc)],
                      bias=shift[:, s * 24 + dc:s * 24 + dc + 1], scale=1.0)
            for fc in range(FC):
                k2 = fc % 2
                for half, pp in ((0, pa), (1, pb)):
                    for kc in range(DC):
                        K.mm(pp[k2][:], w13s[:, kc, half * DFF + fc * 128:half * DFF + (fc + 1) * 128],
                             hb[:, kc, :], start=(kc == 0), stop=(kc == DC - 1),
                             R=[("w13", kc), ("hb", kc)], W=[("pab", half, k2)])
                K.act(sa[k2][:], pa[k2][:], AF.Silu, R=[("pab", 0, k2)], W=[("sa", k2)])
                K.tt(gb[:, fc, :], sa[k2][:], pb[k2][:], ALU.mult, R=[("sa", k2), ("pab", 1, k2)], W=[("gb", fc)])
            for dc in range(DC):
                k2 = dc % 2
                for fc in range(FC):
                    K.mm(po[k2][:], w2s[:, fc, dc * 128:(dc + 1) * 128], gb[:, fc, :],
                         start=(fc == 0), stop=(fc == FC - 1), R=[("w2", fc), ("gb", fc)], W=[("po", k2)])
                K.stt(xt[:, dc, :], po[k2][:], gate[:, s, dc:dc + 1], xt[:, dc, :], ALU.mult, ALU.add,
                      R=[("po", k2), "gate", ("xt", dc)], W=[("xt", dc)])
            K.dma(xdv[:, :, tsl], xt[:], R=[("xt", dc) for dc in range(DC)], W=[("xd", dname, tt)])
    P.barrier()


OFF = {}
_o = 0
for _n, _w in (("a_q", 256), ("a_k_cmp", 64), ("a_v_cmp", 64), ("a_k_slc", 64), ("a_v_slc", 64), ("a_k_win", 64),
               ("a_v_win", 64), ("a_gate", 12), ("b_b", 256), ("b_c", 256), ("b_x", 256), ("c_q", 256), ("c_k", 256),
               ("c_v", 256), ("c_beta", 4), ("c_alpha", 4), ("c_z", 256), ("d_q", 256), ("d_k", 256), ("d_v", 256),
               ("d_i", 4), ("d_f", 4), ("d_o", 256)):
    OFF[_n] = _o
    _o += _w
assert _o == D_IN
FM = ([(OFF["a_q"] + 64 * h, 64) for h in range(4)] + [(OFF["a_k_cmp"], 128), (OFF["a_k_slc"], 64), (OFF["a_k_win"], 64)]
      + [(OFF["b_b"] + 128 * i, 128) for i in range(2)] + [(OFF["b_c"] + 128 * i, 128) for i in range(2)]
      + [(OFF["b_x"] + 128 * i, 128) for i in range(2)] + [(OFF["c_q"] + 64 * i, 64) for i in range(12)]
      + [(OFF["d_q"] + 64 * i, 64) for i in range(4)] + [(OFF["d_k"] + 64 * i, 64) for i in range(4)])
G_AQ, G_KVC, G_KSLC, G_KWIN, G_BB, G_BC, G_BX, G_CQKV, G_DQ, G_DK = 0, 4, 5, 6, 7, 9, 11, 13, 25, 29
NFM = len(FM)
CM_ID, CM_TRIU, CM_SU, CM_SL, CM_LI, CM_N = 0, 128, 192, 256, 320, 384


def bc(ap, shape):
    return ap.to_broadcast(list(shape))


def mixer_phase(K, lay, x_src, sname, x_dst, dname, Wd, mv, T, TT, consts, yscr, mode, enable="ABCD"):
    nc = K.nc
    P = K.P
    NT = T // TT
    NCH = TT // 64
    tg = "m%d%d" % (lay, mode)
    if mode == 1:
        enable = "".join(c for c in enable if c in "BCD")
        WOFF, WN = OFF["b_b"], D_IN - OFF["b_b"]
    else:
        enable = "".join(c for c in enable if c in "A")
        WOFF, WN = 0, OFF["b_b"]
    SUM = cm_su = None
    cm = consts["cmask"]
    SUM = cm[0:64, CM_SU:CM_SU + 64]
    TRIU = cm[0:64, CM_TRIU:CM_TRIU + 64]
    SLM = cm[0:64, CM_SL:CM_SL + 64]
    IDENT = cm[:, CM_ID:CM_ID + 128]
    onesf = consts["ones_f"]
    with ExitStack() as es:
        sb = lambda name, shape, d=F32: _sb(es, nc, name + "_" + tg, shape, d)
        wins = sb("win", [128, DC, WN], BF16)
        wouts = sb("wout", [128, DC, D if mode == 2 else 2], BF16)
        bfm = sb("bfm", [128, NFM], F32)
        bfm8 = sb("bfm8", [128, 4], F32)
        bfmv = sb("bfmv", [64, 1], F32)
        K.dma(bfmv[:], Wd["b_fm"][64:128, G_KVC:G_KVC + 1], W=["a_bfmv"], allow_slow_non_contiguous=True)
        xt = sb("xt", [128, DC, TT])
        sq = sb("sq", [128, DC, TT], BF16)
        hb = sb("hb", [128, DC, TT], BF16)
        yt = sb("yt", [128, DC, TT], BF16)
        rs = sb("rs", [128, TT])
        sa = [sb("sa%d" % i, [128, TT]) for i in range(2)]
        scw = sb("scw", [128, 2, 3])
        mixg = sb("mixg", [128, 2, 2])
        DC_ = DC
        PJ = [_ps(es, nc, "pj0_" + tg, [128, 512])]
        if mode == 1:
            PST = PJ[0]
            PSTK = ("pj", 0)
            PD = [_ps(es, nc, "pd%d_" % i + tg, [128, 512]) for i in range(3)]
            PC = [_ps(es, nc, "pc%d_" % i + tg, [128, 512]) for i in range(4)]
            PJL = [PJ[0]] + PD + PC
            PJK = [("pj", 0)] + [("pd", i) for i in range(3)] + [("pc", i) for i in range(4)]
        else:
            PST = _ps(es, nc, "pst_" + tg, [128, 512])
            PSTK = "pst"
            PJL = [PJ[0]]
            PJK = [("pj", 0)]
            POW = _ps(es, nc, "pow_" + tg, [128, 512])
            PW = [_ps(es, nc, "pw0_" + tg, [128, 512])]
            PS2 = [_ps(es, nc, "ps2%d_" % i + tg, [128, 512]) for i in range(2)]
            POA = _ps(es, nc, "poa_" + tg, [128, 512])
            PMX = _ps(es, nc, "pmx_" + tg, [128, 512])
            POB = PST

        winv = Wd["w_in"].rearrange("(kc p) n -> p kc n", p=128)
        for kc in range(DC):
            K.dma(wins[:, kc, :], winv[:, kc, WOFF:WOFF + WN], W=[("win", kc)], eng="pool")
        woutv = Wd["w_out"].rearrange("(kc p) n -> p kc n", p=128)
        if mode == 2:
            for kc in range(DC):
                K.dma(wouts[:, kc, :], woutv[:, kc, :], W=[("wout", kc)], eng="pool")
        K.dma(bfm[:], Wd["b_fm"], W=["bfm"])
        K.dma(scw[:], Wd["sc_conv_wT"], W=["scw"])
        K.dma(mixg[:], Wd["mixgT"], W=["mixg"])
        K.ts(bfm8[:], bfm[:, G_DK:G_DK + 4], 0.125, None, ALU.mult, R=["bfm"], W=["bfm8"])

        xsv = x_src.rearrange("(dc p) t -> p dc t", p=128)
        xdv = x_dst.rearrange("(dc p) t -> p dc t", p=128)
        gs, shift, gate = mv["gs"], mv["mod"], mv["gate"]
        s = 1
        pjn = [0]

        def proj_fm(g, dst, dkeys, scale=1.0, bias=None, func=AF.Identity):
            c0, ncol = FM[g]
            pj = PJL[pjn[0] % len(PJL)]
            kk = PJK[pjn[0] % len(PJL)]
            pjn[0] += 1
            for kc in range(DC):
                K.mm(pj[0:ncol, 0:TT], wins[:, kc, c0 - WOFF:c0 - WOFF + ncol], hb[:, kc, :], start=(kc == 0), stop=(kc == DC - 1),
                     R=[("win", kc), ("hb", kc)], W=[kk])
            b = bias if bias is not None else bfm[0:ncol, g:g + 1]
            K.act(dst, pj[0:ncol, 0:TT], func, R=[kk, "bfm", "bfm8"], W=dkeys, bias=b, scale=scale)

        if "B" in enable:
            ub = [sb("ub%d" % ch, [128, 2 + TT]) for ch in range(2)]
            bbt = sb("bbt", [128, TT])
            cct = sb("cct", [128, TT])
            cvt = sb("cvt", [128, TT])
            ybt = [sb("ybt%d" % ch, [128, TT]) for ch in range(2)]
            for ch in range(2):
                K.memset(ub[ch][:, 0:2], 0.0, W=[("ub", ch)])
        if "D" in enable:
            QmT = sb("QmT", [64, 4, TT], BF16)
            KmT = sb("KmT", [64, 4, TT], BF16)
            Smb = sb("Smb", [64, 4, 65], BF16)
            K.memset(Smb[:], 0.0, W=["Smb"])
            Sm = sb("Sm", [64, 4, 65])
            K.memset(Sm[:], 0.0, W=["Sm"])
            btm_d = sb("btm_d", [64, 776])
            fbb = sb("fbb", [64, 4])
            gnd = sb("gnd", [64, 64])
            K.dma(btm_d[:], Wd["b_in"][OFF["d_k"]:OFF["d_k"] + 776].partition_broadcast(64), W=["btm_d"])
            K.dma(fbb[:], Wd["mlstm_f_bias"].partition_broadcast(64), W=["fbb"])
            K.dma(gnd[:], Wd["mlstm_norm_g"].partition_broadcast(64), W=["gnd"])
            K.tt(btm_d[:, 516:520], btm_d[:, 516:520], fbb[:], ALU.add, R=["btm_d", "fbb"], W=["btm_d"])
            DT = {}
            for nm_, *shp in (("ktok", [64, 4, 64]), ("vext", [64, 4, 65], BF16), ("ifo", [64, 264]), ("lf", [64, 4]),
                             ("gtot", [64, 4]), ("bsb", [64, 4]), ("ddd", [64, 4]), ("edec", [64, 4]), ("eb", [64, 4]),
                             ("slg", [64, 4, 64]), ("et", [64, 4, 64]), ("pt", [64, 4, 64], BF16), ("nd", [64, 4, 65]),
                             ("den", [64, 4]), ("hm", [64, 4, 64]), ("hsq", [64, 4, 64]), ("ss", [64, 4]),
                             ("so", [64, 4, 64]), ("kd", [64, 4, 64], BF16), ("stmp", [64, 4, 65])):
                DT[nm_] = [sb("d_%s%d" % (nm_, z), *shp) if False else sb("d_%s%d" % (nm_, z), shp[0], shp[1] if len(shp) > 1 else F32)
                           for z in range(2)]
            for z in range(2):
                K.memset(DT["vext"][z][:], 1.0, W=["d_vext%d" % z])

        def mlstm_chunk(c):
            cs = slice(c * 64, (c + 1) * 64)
            z = c % 2
            t_ = {k_: v_[z] for k_, v_ in DT.items()}
            kk_ = lambda n_: "d_%s%d" % (n_, z)
            ktok, vext, ifo, lf, gtot, bsb, ddd, edec, eb = (t_[x] for x in ("ktok", "vext", "ifo", "lf", "gtot", "bsb", "ddd", "edec", "eb"))
            slg, et, pt, nd, den, hm, hsq, ss, so, kd, stmp = (t_[x] for x in ("slg", "et", "pt", "nd", "den", "hm", "hsq", "ss", "so", "kd", "stmp"))
            DA, DB, DC = PD
            kA, kB, kC = ("pd", 0), ("pd", 1), ("pd", 2)
            for kc in range(DC_):
                K.mm(DC[0:64, 0:512], hb[:, kc, cs], wins[:, kc, OFF["d_k"] - WOFF:OFF["d_k"] - WOFF + 512],
                     start=(kc == 0), stop=(kc == DC_ - 1), R=[("win", kc), ("hb", kc)], W=[kC])
            for kc in range(DC_):
                K.mm(DB[0:64, 0:264], hb[:, kc, cs], wins[:, kc, OFF["d_i"] - WOFF:OFF["d_i"] - WOFF + 264],
                     start=(kc == 0), stop=(kc == DC_ - 1), R=[("win", kc), ("hb", kc)], W=[kB])
            K.tt(ktok[:].rearrange("p a b -> p (a b)"), DC[0:64, 0:256], btm_d[:, 0:256], ALU.add,
                 R=[kC, "btm_d"], W=[kk_("ktok")])
            K.tt(vext[:, :, 0:64], v4(DC[0:64, 256:512]), v4(btm_d[:, 256:512]), ALU.add, R=[kC, "btm_d"], W=[kk_("vext")])
            K.tt(ifo[:], DB[0:64, 0:264], btm_d[:, 512:776], ALU.add, R=[kB, "btm_d"], W=[kk_("ifo")])
            K.act(lf[:], ifo[:, 4:8], AF.Exp, R=[kk_("ifo")], W=[kk_("lf")], scale=-1.0)
            K.act(lf[:], lf[:], AF.Ln, R=[kk_("lf")], W=[kk_("lf")], bias=consts["one"][0:64, :], scale=1.0)
            K.ts(lf[:], lf[:], -1.0, None, ALU.mult, R=[kk_("lf")], W=[kk_("lf")])
            K.mm(DB[0:64, 264:268], TRIU, lf[:], R=[kk_("lf"), "cmask"], W=[kB])
            K.mm(DB[0:64, 268:272], onesf[0:64, 0:64], lf[:], R=[kk_("lf"), "ones_f"], W=[kB])
            K.act(gtot[:], DB[0:64, 268:272], AF.Exp, R=[kB], W=[kk_("gtot")])
            K.copy(bsb[:], DB[0:64, 264:268], R=[kB], W=[kk_("bsb")])
            K.tt(ddd[:], DB[0:64, 268:272], bsb[:], ALU.subtract, R=[kB, kk_("bsb")], W=[kk_("ddd")])
            K.tt(ddd[:], ddd[:], ifo[:, 0:4], ALU.add, R=[kk_("ddd"), kk_("ifo")], W=[kk_("ddd")])
            K.act(edec[:], ddd[:], AF.Exp, R=[kk_("ddd")], W=[kk_("edec")])
            K.act(eb[:], bsb[:], AF.Exp, R=[kk_("bsb")], W=[kk_("eb")])
            K.tt(slg[:], bc(TRIU.unsqueeze(1), [64, 4, 64]), bc(lf[:, :].unsqueeze(2), [64, 4, 64]), ALU.mult,
                 R=[kk_("lf"), "cmask"], W=[kk_("slg")])
            K.mm(DC[0:64, 0:256], SLM, slg[:].rearrange("p a b -> p (a b)"), R=[kk_("slg"), "cmask"], W=[kC])
            for h in range(4):
                K.mm(DC[0:64, 256 + h * 64:256 + (h + 1) * 64], KmT[:, h, cs], QmT[:, h, cs], R=["QmT", "KmT"], W=[kC])
            K.tt(et[:], v4(DC[0:64, 0:256]), bc(ifo[:, 0:4].unsqueeze(2), [64, 4, 64]), ALU.add, R=[kC, kk_("ifo")],
                 W=[kk_("et")])
            K.act(et[:], et[:], AF.Exp, R=[kk_("et")], W=[kk_("et")])
            K.tt(et[:], et[:], bc(TRIU.unsqueeze(1), [64, 4, 64]), ALU.mult, R=[kk_("et"), "cmask"], W=[kk_("et")])
            K.tt(pt[:], et[:], v4(DC[0:64, 256:512]), ALU.mult, R=[kk_("et"), kC], W=[kk_("pt")])
            for h in range(4):
                K.mm(DA[0:64, h * 65:(h + 1) * 65], QmT[:, h, cs], Smb[:, h, :], R=["QmT", "Smb"], W=[kA])
            qs = DA[0:64, 0:260].rearrange("p (a b) -> p a b", a=4)
            K.tt(nd[:], qs, bc(eb[:, :].unsqueeze(2), [64, 4, 65]), ALU.mult, R=[kA, kk_("eb")], W=[kk_("nd")])
            for h in range(4):
                K.mm(DA[0:64, h * 65:(h + 1) * 65], pt[:, h, :], vext[:, h, :], R=[kk_("pt"), kk_("vext")], W=[kA])
            K.tt(nd[:], nd[:], DA[0:64, 0:260].rearrange("p (a b) -> p a b", a=4), ALU.add, R=[kk_("nd"), kA], W=[kk_("nd")])
            K.stt(kd[:], ktok[:], 0.125, bc(edec[:, :].unsqueeze(2), [64, 4, 64]), ALU.mult, ALU.mult,
                  R=[kk_("ktok"), kk_("edec")], W=[kk_("kd")])
            for h in range(4):
                K.mm(DA[0:64, h * 65:(h + 1) * 65], kd[:, h, :], vext[:, h, :], R=[kk_("kd"), kk_("vext")], W=[kA])
            K.tt(stmp[:], Sm[:], bc(gtot[:, :].unsqueeze(2), [64, 4, 65]), ALU.mult, R=["Sm", kk_("gtot")], W=[kk_("stmp")])
            K.tt(Sm[:], stmp[:], DA[0:64, 0:260].rearrange("p (a b) -> p a b", a=4), ALU.add, R=[kk_("stmp"), kA], W=["Sm"])
            K.copy(Smb[:], Sm[:], R=["Sm"], W=["Smb"], eng="act")
            K.act(den[:], nd[:, :, 64], AF.Abs, R=[kk_("nd")], W=[kk_("den")])
            K.ts(den[:], den[:], 1.0, None, ALU.max, R=[kk_("den")], W=[kk_("den")])
            K.recip(den[:], den[:], R=[kk_("den")], W=[kk_("den")])
            K.tt(hm[:], nd[:, :, 0:64], bc(den[:, :].unsqueeze(2), [64, 4, 64]), ALU.mult, R=[kk_("nd"), kk_("den")], W=[kk_("hm")])
            K.tt(hsq[:], hm[:], hm[:], ALU.mult, R=[kk_("hm")], W=[kk_("hsq")])
            K.P.add("dve", lambda e: e.tensor_reduce(ss[:], hsq[:], AX.X, ALU.add), R=[kk_("hsq")], W=[kk_("ss")], cost=400.0)
            K.act(ss[:], ss[:], AF.Sqrt, R=[kk_("ss"), "epsc"], W=[kk_("ss")], bias=consts["eps"][0:64, :], scale=1.0 / 64)
            K.recip(ss[:], ss[:], R=[kk_("ss")], W=[kk_("ss")])
            K.act(so[:].rearrange("p a b -> p (a b)"), ifo[:, 8:264], AF.Sigmoid, R=[kk_("ifo")], W=[kk_("so")])
            K.tt(hm[:], hm[:], bc(ss[:, :].unsqueeze(2), [64, 4, 64]), ALU.mult, R=[kk_("hm"), kk_("ss")], W=[kk_("hm")])
            K.tt(hm[:], hm[:], bc(gnd[:, :].unsqueeze(1), [64, 4, 64]), ALU.mult, R=[kk_("hm"), "gnd"], W=[kk_("hm")])
            K.tt(hm[:], hm[:], so[:], ALU.mult, R=[kk_("hm"), kk_("so")], W=[kk_("hm")])
            hmf = hm[:].rearrange("p a b -> p (a b)")
            for pr in range(2):
                K.tr(DA[:, 272 + pr * 64:272 + (pr + 1) * 64], hmf[:, pr * 128:(pr + 1) * 128], IDENT[0:64, 0:64],
                     R=[kk_("hm"), "cmask"], W=[kA])
                K.copy(yt[:, 6 + pr, cs], DA[:, 272 + pr * 64:272 + (pr + 1) * 64], R=[kA], W=[("yt", 6 + pr)], eng="act")

        GDT = F32
        if "C" in enable:
            cin = sb("c_cin", [64, 12, 3 + TT])
            qkv = sb("c_qkv", [64, 12, TT])
            qkb = sb("c_qkb", [64, 12, TT], GDT)
            Sgb = sb("c_Sb", [64, 4, 64], GDT)
            K.memset(Sgb[:], 0.0, W=["c_Sb"])
            identb1 = sb("c_identb", [64, 64], GDT)
            K.copy(identb1[:], IDENT[0:64, 0:64], R=["cmask"], W=["c_identb"])
            gcw = sb("c_gcw", [64, 12, 4])
            Sg = sb("c_S", [64, 4, 64])
            K.memset(Sg[:], 0.0, W=["c_S"])
            K.memset(cin[:, :, 0:3], 0.0, W=[("c_cin", g) for g in range(12)])
            K.dma(gcw[:], Wd["gdn_conv_wT"], W=["c_gcw"])
            btm_c = sb("c_btm", [64, 264])
            dtb = sb("c_dtb", [64, 4])
            nea = sb("c_nea", [64, 4])
            gng = sb("c_gng", [64, 64])
            K.dma(btm_c[:], Wd["b_in"][OFF["c_beta"]:OFF["c_beta"] + 264].partition_broadcast(64), W=["c_btm"])
            K.dma(dtb[:], Wd["gdn_dt_bias"].partition_broadcast(64), W=["c_dtb"])
            K.dma(nea[:], Wd["gdn_A_log"].partition_broadcast(64), W=["c_nea"])
            K.dma(gng[:], Wd["gdn_norm_g"].partition_broadcast(64), W=["c_gng"])
            K.tt(btm_c[:, 4:8], btm_c[:, 4:8], dtb[:], ALU.add, R=["c_btm", "c_dtb"], W=["c_btm"])
            K.act(nea[:], nea[:], AF.Exp, R=["c_nea"], W=["c_nea"])
            K.ts(nea[:], nea[:], -1.0, None, ALU.mult, R=["c_nea"], W=["c_nea"])
            csqL = [sb("c_sq%d" % i, [64, TT]) for i in range(2)]
            crsL = [sb("c_rs%d" % i, [64, TT]) for i in range(2)]
            CT = {}
            for nm_, *shp in (("baz", [64, 264]), ("beta", [64, 4]), ("gg", [64, 4]), ("gcs", [64, 4]), ("egc", [64, 4]),
                             ("bgc", [64, 4]), ("edl", [64, 4]), ("gto", [64, 4]), ("ktk", [64, 4, 64]), ("vb", [64, 4, 64], GDT),
                             ("kbe", [64, 4, 64], GDT), ("kdc", [64, 4, 64], GDT), ("ug", [64, 4, 64]), ("slgc", [64, 4, 64]),
                             ("dgb", [64, 4, 64]), ("seg", [64, 4, 64]), ("segT", [64, 4, 64]), ("t1", [64, 4, 64]),
                             ("t2", [64, 4, 64]), ("NN", [64, 2, 4, 64], GDT), ("NN2", [64, 2, 4, 64], GDT), ("XX", [64, 4, 64]), ("XB", [64, 4, 64], GDT),
                             ("ptc", [64, 4, 64], GDT), ("uu", [64, 4, 64]), ("wt", [64, 4, 64], GDT), ("vn", [64, 4, 64]), ("vnb", [64, 4, 64], GDT),
                             ("oo", [64, 4, 64]), ("osq", [64, 4, 64]), ("oss", [64, 4]), ("sz", [64, 4, 64]),
                             ("stg", [64, 4, 64])):
                CT[nm_] = [sb("c_%s%d" % (nm_, z), shp[0], shp[1] if len(shp) > 1 else F32) for z in range(2)]

        def v4(ap):
            return ap.rearrange("p (a b) -> p a b", a=4)

        def gdn_tile():
            for g in range(12):
                proj_fm(G_CQKV + g, cin[:, g, 3:3 + TT], [("c_cin", g)])
                K.ts(qkv[:, g, :], cin[:, g, 3:3 + TT], gcw[:, g, 3:4], None, ALU.mult, R=[("c_cin", g), "c_gcw"],
                     W=[("c_qkv", g)])
                for k in range(3):
                    K.stt(qkv[:, g, :], cin[:, g, k:k + TT], gcw[:, g, k:k + 1], qkv[:, g, :], ALU.mult, ALU.add,
                          R=[("c_cin", g), "c_gcw", ("c_qkv", g)], W=[("c_qkv", g)])
                K.copy(cin[:, g, 0:3], cin[:, g, TT:TT + 3], R=[("c_cin", g)], W=[("c_cin", g)])
                K.act(qkv[:, g, :], qkv[:, g, :], AF.Silu, R=[("c_qkv", g)], W=[("c_qkv", g)])
                if g < 8:
                    csq, crs = csqL[g % 2], crsL[g % 2]
                    ksq, krs = "c_sq%d" % (g % 2), "c_rs%d" % (g % 2)
                    pstb = PJL[pjn[0] % len(PJL)]
                    pstk = PJK[pjn[0] % len(PJL)]
                    pjn[0] += 1
                    K.tt(csq[:], qkv[:, g, :], qkv[:, g, :], ALU.mult, R=[("c_qkv", g)], W=[ksq])
                    K.mm(pstb[0:64, 0:TT], onesf[0:64, 0:64], csq[:], R=[ksq, "ones_f"], W=[pstk])
                    K.act(crs[:], pstb[0:64, 0:TT], AF.Sqrt, R=[pstk, "epsc"], W=[krs], bias=consts["eps"][0:64, :],
                          scale=1.0)
                    K.recip(crs[:], crs[:], R=[krs], W=[krs])
                    K.stt(qkb[:, g, :], qkv[:, g, :], 0.125 if g < 4 else 1.0, crs[:], ALU.mult, ALU.mult,
                          R=[("c_qkv", g), krs], W=[("c_qkb", g)])
                else:
                    K.copy(qkb[:, g, :], qkv[:, g, :], R=[("c_qkv", g)], W=[("c_qkb", g)])

        def gdn_chunk(c):
            cs = slice(c * 64, (c + 1) * 64)
            z = c % 2
            t_ = {k_: v_[z] for k_, v_ in CT.items()}
            kk_ = lambda n_: "c_%s%d" % (n_, z)
            baz, beta, gg, gcs, egc, bgc, edl, gto = (t_[x] for x in ("baz", "beta", "gg", "gcs", "egc", "bgc", "edl", "gto"))
            ktk, vb, kbe, kdc, ug, slgc, dgb, seg, segT = (t_[x] for x in ("ktk", "vb", "kbe", "kdc", "ug", "slgc", "dgb", "seg", "segT"))
            t1, t2, NN, NN2, XX, ptc, uu, wt, vn, oo, osq, oss, sz, stg = (t_[x] for x in ("t1", "t2", "NN", "NN2", "XX", "ptc", "uu", "wt", "vn", "oo", "osq", "oss", "sz", "stg"))
            XB, vnb = t_["XB"], t_["vnb"]
            CA, CB, CC, CD = PC
            kA, kB, kC, kD = ("pc", 0), ("pc", 1), ("pc", 2), ("pc", 3)
            QK = [("c_qkb", g) for g in range(12)]
            for kc in range(DC_):
                K.mm(CA[0:64, 0:264], hb[:, kc, cs], wins[:, kc, OFF["c_beta"] - WOFF:OFF["c_beta"] - WOFF + 264],
                     start=(kc == 0), stop=(kc == DC_ - 1), R=[("win", kc), ("hb", kc)], W=[kA])
            K.tt(baz[:], CA[0:64, 0:264], btm_c[:], ALU.add, R=[kA, "c_btm"], W=[kk_("baz")])
            K.act(beta[:], baz[:, 0:4], AF.Sigmoid, R=[kk_("baz")], W=[kk_("beta")])
            K.act(gg[:], baz[:, 4:8], AF.Exp, R=[kk_("baz")], W=[kk_("gg")])
            K.act(gg[:], gg[:], AF.Ln, R=[kk_("gg")], W=[kk_("gg")], bias=consts["one"][0:64, :], scale=1.0)
            K.tt(gg[:], gg[:], nea[:], ALU.mult, R=[kk_("gg"), "c_nea"], W=[kk_("gg")])
            K.mm(CA[0:64, 264:268], TRIU, gg[:], R=[kk_("gg"), "cmask"], W=[kA])
            K.mm(CA[0:64, 268:272], onesf[0:64, 0:64], gg[:], R=[kk_("gg"), "ones_f"], W=[kA])
            K.copy(gcs[:], CA[0:64, 264:268], R=[kA], W=[kk_("gcs")])
            K.act(egc[:], CA[0:64, 264:268], AF.Exp, R=[kA], W=[kk_("egc")])
            K.act(gto[:], CA[0:64, 268:272], AF.Exp, R=[kA], W=[kk_("gto")])
            K.tt(edl[:], CA[0:64, 268:272], gcs[:], ALU.subtract, R=[kA, kk_("gcs")], W=[kk_("edl")])
            K.act(edl[:], edl[:], AF.Exp, R=[kk_("edl")], W=[kk_("edl")])
            K.tt(bgc[:], beta[:], egc[:], ALU.mult, R=[kk_("beta"), kk_("egc")], W=[kk_("bgc")])
            for h in range(4):
                K.mm(CB[0:64, h * 64:(h + 1) * 64], qkb[:, 4 + h, cs], identb1[:], R=QK + ["c_identb"], W=[kB])
                K.mm(CB[0:64, 256 + h * 64:256 + (h + 1) * 64], qkb[:, 8 + h, cs], identb1[:], R=QK + ["c_identb"], W=[kB])
            K.copy(ktk[:], v4(CB[0:64, 0:256]), R=[kB], W=[kk_("ktk")], eng="act")
            K.tt(vb[:], v4(CB[0:64, 256:512]), bc(beta[:, :].unsqueeze(2), [64, 4, 64]), ALU.mult, R=[kB, kk_("beta")], W=[kk_("vb")])
            K.tt(kbe[:], ktk[:], bc(bgc[:, :].unsqueeze(2), [64, 4, 64]), ALU.mult, R=[kk_("ktk"), kk_("bgc")], W=[kk_("kbe")])
            K.tt(kdc[:], ktk[:], bc(edl[:, :].unsqueeze(2), [64, 4, 64]), ALU.mult, R=[kk_("ktk"), kk_("edl")], W=[kk_("kdc")])
            K.tt(ug[:], bc(TRIU.unsqueeze(1), [64, 4, 64]), bc(gg[:, :].unsqueeze(2), [64, 4, 64]), ALU.mult,
                 R=[kk_("gg"), "cmask"], W=[kk_("ug")])
            K.tt(slgc[:], bc(SLM.unsqueeze(1), [64, 4, 64]), bc(gg[:, :].unsqueeze(2), [64, 4, 64]), ALU.mult,
                 R=[kk_("gg"), "cmask"], W=[kk_("slgc")])
            K.mm(CC[0:64, 0:256], TRIU, slgc[:].rearrange("p a b -> p (a b)"), R=[kk_("slgc"), "cmask"], W=[kC])
            K.mm(CC[0:64, 256:512], SLM, ug[:].rearrange("p a b -> p (a b)"), R=[kk_("ug"), "cmask"], W=[kC])
            K.tt(dgb[:], bc(IDENT[0:64, 0:64].unsqueeze(1), [64, 4, 64]), bc(beta[:, :].unsqueeze(2), [64, 4, 64]), ALU.mult,
                 R=[kk_("beta"), "cmask"], W=[kk_("dgb")])
            K.act(seg[:], v4(CC[0:64, 0:256]), AF.Exp, R=[kC], W=[kk_("seg")])
            K.act(segT[:], v4(CC[0:64, 256:512]), AF.Exp, R=[kC], W=[kk_("segT")])
            for h in range(4):
                K.mm(CB[0:64, h * 64:(h + 1) * 64], qkb[:, 4 + h, cs], qkb[:, 4 + h, cs], R=QK, W=[kB])
                K.mm(CB[0:64, 256 + h * 64:256 + (h + 1) * 64], qkb[:, 4 + h, cs], qkb[:, h, cs], R=QK, W=[kB])
            K.mm(CC[0:64, 0:256], onesf[0:64, 0:64], dgb[:].rearrange("p a b -> p (a b)"), R=[kk_("dgb"), "ones_f"], W=[kC])
            K.tt(t1[:], seg[:], bc(SLM.unsqueeze(1), [64, 4, 64]), ALU.mult, R=[kk_("seg"), "cmask"], W=[kk_("t1")])
            K.tt(t1[:], t1[:], v4(CB[0:64, 0:256]), ALU.mult, R=[kk_("t1"), kB], W=[kk_("t1")])
            K.tt(NN[:, 0], t1[:], bc(beta[:, :].unsqueeze(2), [64, 4, 64]), ALU.mult, R=[kk_("t1"), kk_("beta")], W=[kk_("NN")])
            K.tt(t2[:], segT[:], bc(SUM.unsqueeze(1), [64, 4, 64]), ALU.mult, R=[kk_("segT"), "cmask"], W=[kk_("t2")])
            K.tt(t2[:], t2[:], v4(CB[0:64, 0:256]), ALU.mult, R=[kk_("t2"), kB], W=[kk_("t2")])
            K.tt(NN[:, 1], t2[:], v4(CC[0:64, 0:256]), ALU.mult, R=[kk_("t2"), kC], W=[kk_("NN")])
            K.tt(ptc[:], segT[:], bc(TRIU.unsqueeze(1), [64, 4, 64]), ALU.mult, R=[kk_("segT"), "cmask"], W=[kk_("ptc")])
            K.tt(ptc[:], ptc[:], v4(CB[0:64, 256:512]), ALU.mult, R=[kk_("ptc"), kB], W=[kk_("ptc")])
            K.tt(XX[:], bc(IDENT[0:64, 0:64].unsqueeze(1), [64, 4, 64]), NN[:, 1], ALU.subtract, R=[kk_("NN"), "cmask"],
                 W=[kk_("XX")])
            K.copy(XB[:], XX[:], R=[kk_("XX")], W=[kk_("XB")], eng="act")
            cur, nxt, ck, nk = NN, NN2, kk_("NN"), kk_("NN2")
            for lvl in range(5):
                last = (lvl == 4)
                for h in range(4):
                    K.mm(CC[0:64, h * 64:(h + 1) * 64], cur[:, 1, h, :], cur[:, 0, h, :], R=[ck], W=[kC])
                    if not last:
                        K.mm(CC[0:64, 256 + h * 64:256 + (h + 1) * 64], cur[:, 0, h, :], cur[:, 1, h, :], R=[ck], W=[kC])
                if last:
                    K.copy(nxt[:, 0], v4(CC[0:64, 0:256]), R=[kC], W=[nk], eng="act")
                else:
                    K.copy(nxt[:].rearrange("p t a b -> p (t a b)"), CC[0:64, 0:512], R=[kC], W=[nk], eng="act")
                for h in range(4):
                    K.mm(CB[0:64, h * 64:(h + 1) * 64], nxt[:, 0, h, :], XB[:, h, :], R=[nk, kk_("XB")], W=[kB])
                K.tt(XX[:], XX[:], v4(CB[0:64, 0:256]), ALU.add, R=[kk_("XX"), kB], W=[kk_("XX")])
                K.copy(XB[:], XX[:], R=[kk_("XX")], W=[kk_("XB")], eng="act")
                cur, nxt, ck, nk = nxt, cur, nk, ck
            for h in range(4):
                K.mm(CC[0:64, h * 64:(h + 1) * 64], XB[:, h, :], vb[:, h, :], R=[kk_("XB"), kk_("vb")], W=[kC])
                K.mm(CC[0:64, 256 + h * 64:256 + (h + 1) * 64], kbe[:, h, :], XB[:, h, :], R=[kk_("XB"), kk_("kbe")], W=[kC])
            K.copy(uu[:], v4(CC[0:64, 0:256]), R=[kC], W=[kk_("uu")], eng="act")
            K.copy(wt[:], v4(CC[0:64, 256:512]), R=[kC], W=[kk_("wt")])
            for h in range(4):
                K.mm(CD[0:64, h * 64:(h + 1) * 64], wt[:, h, :], Sgb[:, h, :], R=[kk_("wt"), "c_Sb"], W=[kD])
                K.mm(CD[0:64, 256 + h * 64:256 + (h + 1) * 64], qkb[:, h, cs], Sgb[:, h, :], R=QK + ["c_Sb"], W=[kD])
            K.tt(vnb[:], uu[:], v4(CD[0:64, 0:256]), ALU.subtract, R=[kk_("uu"), kD], W=[kk_("vnb")])
            K.tt(oo[:], v4(CD[0:64, 256:512]), bc(egc[:, :].unsqueeze(2), [64, 4, 64]), ALU.mult, R=[kD, kk_("egc")], W=[kk_("oo")])
            for h in range(4):
                K.mm(CD[0:64, h * 64:(h + 1) * 64], kdc[:, h, :], vnb[:, h, :], R=[kk_("kdc"), kk_("vnb")], W=[kD])
            K.tt(stg[:], Sg[:], bc(gto[:, :].unsqueeze(2), [64, 4, 64]), ALU.mult, R=["c_S", kk_("gto")], W=[kk_("stg")])
            K.tt(Sg[:], stg[:], v4(CD[0:64, 0:256]), ALU.add, R=[kk_("stg"), kD], W=["c_S"])
            K.copy(Sgb[:], Sg[:], R=["c_S"], W=["c_Sb"], eng="act")
            for h in range(4):
                K.mm(CD[0:64, 256 + h * 64:256 + (h + 1) * 64], ptc[:, h, :], vnb[:, h, :], R=[kk_("ptc"), kk_("vnb")], W=[kD])
            K.tt(oo[:], oo[:], v4(CD[0:64, 256:512]), ALU.add, R=[kk_("oo"), kD], W=[kk_("oo")])
            K.tt(osq[:], oo[:], oo[:], ALU.mult, R=[kk_("oo")], W=[kk_("osq")])
            K.P.add("dve", lambda e: e.tensor_reduce(oss[:], osq[:], AX.X, ALU.add), R=[kk_("osq")], W=[kk_("oss")], cost=400.0)
            K.act(oss[:], oss[:], AF.Sqrt, R=[kk_("oss"), "epsc"], W=[kk_("oss")], bias=consts["eps"][0:64, :], scale=1.0 / 64)
            K.recip(oss[:], oss[:], R=[kk_("oss")], W=[kk_("oss")])
            K.act(sz[:].rearrange("p a b -> p (a b)"), baz[:, 8:264], AF.Silu, R=[kk_("baz")], W=[kk_("sz")])
            K.tt(oo[:], oo[:], bc(oss[:, :].unsqueeze(2), [64, 4, 64]), ALU.mult, R=[kk_("oo"), kk_("oss")], W=[kk_("oo")])
            K.tt(oo[:], oo[:], bc(gng[:, :].unsqueeze(1), [64, 4, 64]), ALU.mult, R=[kk_("oo"), "c_gng"], W=[kk_("oo")])
            K.tt(oo[:], oo[:], sz[:], ALU.mult, R=[kk_("oo"), kk_("sz")], W=[kk_("oo")])
            oof = oo[:].rearrange("p a b -> p (a b)")
            for pr in range(2):
                K.tr(CD[:, pr * 64:(pr + 1) * 64], oof[:, pr * 128:(pr + 1) * 128], IDENT[0:64, 0:64],
                     R=[kk_("oo"), "cmask"], W=[kD])
                K.copy(yt[:, 4 + pr, cs], CD[:, pr * 64:(pr + 1) * 64], R=[kD], W=[("yt", 4 + pr)], eng="act")

        if "A" in enable:
            NB = T // 64
            NCB = T // 16
            NM = (NCB + 127) // 128
            NQ = T // 128
            w1k = sb("a_w1k", [64, 32, 256], BF16)
            w1v = sb("a_w1v", [64, 32, 256], BF16)
            w2k = sb("a_w2k", [128, 2, 64], BF16)
            w2v = sb("a_w2v", [128, 2, 64], BF16)
            K.dma(w1k[:], Wd["cmp_k_w1"].rearrange("(s d) n -> d s n", d=64), W=["a_w1k"], eng="pool")
            K.dma(w1v[:], Wd["cmp_v_w1"].rearrange("(s d) n -> d s n", d=64), W=["a_w1v"], eng="pool")
            K.dma(w2k[:], Wd["cmp_k_w2"].rearrange("(c p) n -> p c n", p=128), W=["a_w2k"], eng="pool")
            K.dma(w2v[:], Wd["cmp_v_w2"].rearrange("(c p) n -> p c n", p=128), W=["a_w2v"], eng="pool")
            posT = sb("a_posT", [64, 32], BF16)
            K.dma(posT[:], Wd["cmp_posT"], W=["a_posT"], eng="pool")
            hbias = sb("a_hbias", [128, 4])
            expc = sb("a_expc", [64, T], BF16)
            K.dma(expc[0:NB, :], Wd["expc"], W=["a_expc"], eng="pool")
            keepc = sb("a_keepc", [128, 2, 2 * NB])
            K.dma(keepc[:], Wd["keepadd"], W=["a_keepc"])
            identb = sb("a_identb", [128, 128], BF16)
            K.copy(identb[:], IDENT, R=["cmask"], W=["a_identb"])
            biasT = sb("a_bias", [128, 19, 512])
            for q in range(19):
                K.dma(biasT[:, q, :], Wd["bias_scr"][q], W=[("a_bias", q)])
            bw4 = sb("a_bw4", [128, 128])
            K.dma(bw4[:], Wd["bw4"], W=["a_bw4"])
            qng = sb("a_qng", [64, 2])
            K.dma(qng[:], Wd["qkng"], W=["a_qng"])
            K.ts(qng[:, 0:1], qng[:, 0:1], 0.125, None, ALU.mult, R=["a_qng"], W=["a_qng"])
            tabrow = sb("a_tabrow", [65, 4])
            K.dma(tabrow[64:65, :], Wd["t5_table"][31:32, :], W=["a_tabrow"])
            mg0 = sb("a_mg0", [128, 256])
            K.dma(mg0[:], Wd["mix_norm_g0"].partition_broadcast(128), W=["a_mg0"])
            btm_a = sb("a_btm", [128, 204])
            K.dma(btm_a[:], Wd["b_in"][OFF["a_v_slc"]:OFF["a_v_slc"] + 204].partition_broadcast(128), W=["a_btm"])
            ovl = sb("a_ovl", [128, NM, 64])
            K.dma(ovl[:], Wd["ovl"], W=["a_ovl"])
            kcmpT = sb("a_kcmpT", [64, T], BF16)
            vcmpT = sb("a_vcmpT", [64, T], BF16)
            KsT = sb("a_KsT", [128, T], BF16)
            KwT = sb("a_KwT", [128, T], BF16)
            kcT = sb("a_kcT", [128, NM * 128])
            vcT = sb("a_vcT", [64, NM * 128])
            vcx = sb("a_vcx", [128, NM, 65])
            Vs = sb("a_Vs", [128, NQ, 65], BF16)
            Vw = sb("a_Vw", [128, NQ, 65], BF16)
            NQT = TT // 128
            QaT = sb("a_QaT", [128, NQT, 4, 128], BF16)
            QaF = sb("a_QaF", [128, NQT, 4, 128])
            gsb = sb("a_gsb", [128, TT // 128, 12])
            K.memset(KsT[64:128, :], 0.0, W=["a_KsT"])
            K.memset(KwT[64:128, :], 0.0, W=["a_KwT"])
            K.memset(KsT[64:65, :], 1.0, W=["a_KsT"])
            K.memset(KwT[64:65, :], 1.0, W=["a_KwT"])
            K.memset(kcT[:, :], 0.0, W=["a_kcT"])
            K.memset(kcT[64:65, :], 1.0, W=["a_kcT"])
            K.memset(QaT[64:128], 0.0, W=["a_QaT"])
            K.memset(QaF[64:128], 0.0, W=["a_QaF"])
            K.memset(vcT[:], 0.0, W=["a_vcT"])
            K.memset(vcx[:], 1.0, W=["a_vcx"])
            K.memset(Vs[:], 1.0, W=["a_Vs"])
            K.memset(Vw[:], 1.0, W=["a_Vw"])
            for q_ in range(NQT):
                K.copy(QaT[64:65, q_], bc(tabrow[64:65, :].unsqueeze(2), [1, 4, 128]), R=["a_tabrow"], W=["a_QaT"])
                K.copy(QaF[64:65, q_], bc(tabrow[64:65, :].unsqueeze(2), [1, 4, 128]), R=["a_tabrow"], W=["a_QaF"])
            for kv, w1 in enumerate((w1k, w1v)):
                for hc in range(2):
                    for s_ in range(32):
                        K.mm(PW[0][:, (kv * 2 + hc) * 2:(kv * 2 + hc) * 2 + 1], w1[:, s_, hc * 128:(hc + 1) * 128],
                             posT[:, s_:s_ + 1], start=(s_ == 0), stop=(s_ == 31), R=["a_w1k", "a_w1v", "a_posT"],
                             W=[("pw", 0)])
            K.copy(hbias[:], PW[0][:, 0:8].rearrange("p (a b) -> p a b", b=2)[:, :, 0], R=[("pw", 0)], W=["a_hbias"])
            a_raw = sb("a_raw", [64, TT])
            a_sq = sb("a_sq", [64, TT])
            a_rs = sb("a_rs", [64, TT])
            hact = sb("a_hact", [128, 2, 2, 32], BF16)
            cst = sb("a_cst", [64, 32])
            csq2 = sb("a_csq2", [64, 32])
            crs2 = sb("a_crs2", [64, 32])
            vtm = sb("a_vtm", [128, 204])
            Eb = [sb("a_E%d" % i, [128, 4, 128]) for i in range(2)]
            Tb = [sb("a_T%d" % i, [128, 4, 128]) for i in range(2)]
            Pb = [sb("a_P%d" % i, [128, 4, 128], BF16) for i in range(2)]
            Pc = [sb("a_Pc%d" % i, [128, 4, 128]) for i in range(2)]
            scr = sb("a_scr", [128, 64])
            sc2 = sb("a_sc2", [128, 64])
            v8 = sb("a_v8", [128, 8])
            mskb = sb("a_mskb", [128, 64], BF16)
            mT = sb("a_mT", [64, 128], BF16)
            rden = sb("a_rden", [128, 4])
            coef = sb("a_coef", [128, 4])
            ya = sb("a_ya", [128, 4, 64])
            ytmp = sb("a_ytmp", [128, 4, 64])
            rdenw = sb("a_rdenw", [128, 4])
            coefw = sb("a_coefw", [128, 4])
            yaw = sb("a_yaw", [128, 4, 64])
            yss = sb("a_yss", [128, 1])

        def nsa_tile(tt):
            t0 = tt * TT
            tsl = slice(t0, t0 + TT)
            c0 = OFF["a_k_cmp"]
            for nm_, col, dst in (("k", OFF["a_k_cmp"], kcmpT), ("v", OFF["a_v_cmp"], vcmpT)):
                pj = PJ[pjn[0] % len(PJ)]
                kk = ("pj", pjn[0] % len(PJ))
                pjn[0] += 1
                for kc in range(DC):
                    K.mm(pj[0:64, 0:TT], wins[:, kc, col:col + 64], hb[:, kc, :], start=(kc == 0), stop=(kc == DC - 1),
                         R=[("win", kc), ("hb", kc)], W=[kk])
                g_ = G_KVC
                bcol = bfm[0:64, G_KVC:G_KVC + 1] if nm_ == "k" else bfmv[:, 0:1]
                K.act(dst[:, tsl], pj[0:64, 0:TT], AF.Identity, R=[kk, "bfm", "a_bfmv"], W=["a_" + nm_ + "cmpT"], bias=bcol,
                      scale=1.0)

            def normed(g, dst, dkey, gcol, split=False):
                proj_fm(g, a_raw[:], ["a_raw"])
                K.tt(a_sq[:], a_raw[:], a_raw[:], ALU.mult, R=["a_raw"], W=["a_sq"])
                K.mm(PST[0:64, 0:TT], onesf[0:64, 0:64], a_sq[:], R=["a_sq", "ones_f"], W=[PSTK])
                K.act(a_rs[:], PST[0:64, 0:TT], AF.Sqrt, R=[PSTK, "epsc"], W=["a_rs"], bias=consts["eps"][0:64, :],
                      scale=1.0 / 64)
                K.recip(a_rs[:], a_rs[:], R=["a_rs"], W=["a_rs"])
                for d_, dk_ in zip(dst, dkey):
                    if split:
                        K.stt(d_, a_raw[:].rearrange("p (a b) -> p a b", b=128), gcol,
                              a_rs[:].rearrange("p (a b) -> p a b", b=128), ALU.mult, ALU.mult,
                              R=["a_raw", "a_rs", "a_qng"], W=[dk_])
                    else:
                        K.stt(d_, a_raw[:], gcol, a_rs[:], ALU.mult, ALU.mult, R=["a_raw", "a_rs", "a_qng"], W=[dk_])

            normed(G_KSLC, [KsT[0:64, tsl]], ["a_KsT"], qng[:, 1:2])
            normed(G_KWIN, [KwT[0:64, tsl]], ["a_KwT"], qng[:, 1:2])
            for h in range(4):
                normed(G_AQ + h, [QaT[0:64, :, h, :], QaF[0:64, :, h, :]], ["a_QaT", "a_QaF"], qng[:, 0:1], split=True)
            for q in range(TT // 128):
                qg = t0 // 128 + q
                for kc in range(DC):
                    K.mm(PW[0][:, 300:504], hb[:, kc, q * 128:(q + 1) * 128], wins[:, kc, OFF["a_v_slc"]:OFF["a_v_slc"] + 204],
                         start=(kc == 0), stop=(kc == DC - 1), R=[("win", kc), ("hb", kc)], W=[("pw", 0)])
                K.tt(vtm[:], PW[0][:, 300:504], btm_a[:], ALU.add, R=[("pw", 0), "a_btm"], W=["a_vtm"])
                K.copy(Vs[:, qg, 0:64], vtm[:, 0:64], R=["a_vtm"], W=["a_Vs"])
                K.copy(Vw[:, qg, 0:64], vtm[:, 128:192], R=["a_vtm"], W=["a_Vw"])
                K.act(gsb[:, q, :], vtm[:, 192:204], AF.Sigmoid, R=["a_vtm"], W=["a_gsb"])
            nb0 = 0 if t0 == 0 else t0 // 16 - 1
            nb1 = (t0 + TT - 32) // 16 + 1
            nn = nb1 - nb0
            for kv, (w1, src, w2) in enumerate(((w1k, kcmpT, w2k), (w1v, vcmpT, w2v))):
                for hc in range(2):
                    for s_ in range(32):
                        K.mm(PW[0][:, 0:nn], w1[:, s_, hc * 128:(hc + 1) * 128],
                             src[:, 16 * nb0 + s_:16 * nb0 + s_ + 16 * (nn - 1) + 1:16], start=(s_ == 0), stop=(s_ == 31),
                             R=["a_w1k", "a_w1v", "a_kcmpT", "a_vcmpT"], W=[("pw", 0)])
                    K.act(hact[:, kv, hc, 0:nn], PW[0][:, 0:nn], AF.Silu, R=[("pw", 0), "a_hbias"], W=["a_hact"],
                          bias=hbias[:, kv * 2 + hc:kv * 2 + hc + 1], scale=1.0)
                for hc in range(2):
                    K.mm(PW[0][0:64, 64:64 + nn], w2[:, hc, :], hact[:, kv, hc, 0:nn], start=(hc == 0), stop=(hc == 1),
                         R=["a_w2k", "a_w2v", "a_hact"], W=[("pw", 0)])
                if kv == 0:
                    K.copy(cst[:, 0:nn], PW[0][0:64, 64:64 + nn], R=[("pw", 0)], W=["a_cst"])
                    K.tt(csq2[:, 0:nn], cst[:, 0:nn], cst[:, 0:nn], ALU.mult, R=["a_cst"], W=["a_csq2"])
                    K.mm(PW[0][0:64, 128:128 + nn], onesf[0:64, 0:64], csq2[:, 0:nn], R=["a_csq2", "ones_f"], W=[("pw", 0)])
                    K.act(crs2[:, 0:nn], PW[0][0:64, 128:128 + nn], AF.Sqrt, R=[("pw", 0), "epsc"], W=["a_crs2"],
                          bias=consts["eps"][0:64, :], scale=1.0 / 64)
                    K.recip(crs2[:, 0:nn], crs2[:, 0:nn], R=["a_crs2"], W=["a_crs2"])
                    K.stt(kcT[0:64, nb0:nb0 + nn], cst[:, 0:nn], qng[:, 1:2], crs2[:, 0:nn], ALU.mult, ALU.mult,
                          R=["a_cst", "a_crs2", "a_qng"], W=["a_kcT"])
                else:
                    K.copy(vcT[:, nb0:nb0 + nn], PW[0][0:64, 64:64 + nn], R=[("pw", 0)], W=["a_vcT"])
            for m in range(NM):
                K.tr(PW[0][:, 160 + m * 64:160 + (m + 1) * 64], vcT[:, m * 128:(m + 1) * 128], IDENT[0:64, 0:64],
                     R=["a_vcT", "cmask"], W=[("pw", 0)])
                K.copy(vcx[:, m, 0:64], PW[0][:, 160 + m * 64:160 + (m + 1) * 64], R=[("pw", 0)], W=["a_vcx"])
            for q in range(TT // 128):
                nsa_qtile(t0 // 128 + q, q)

        def nsa_qtile(i, q):
            qs = slice(q * 128, (q + 1) * 128)
            qT = QaT[:, q]
            qF = QaF[:, q]
            ek = [0]

            def scores(lhsT, rhs, bias_ap, dst, dkey, rkeys):
                k2 = ek[0] % 2
                ek[0] += 1
                ps = PS2[k2]
                K.mm(ps[:, 0:512], lhsT, rhs, R=rkeys, W=[("ps", k2)])
                if bias_ap is not None:
                    K.tt(Tb[k2][:], v4(ps[:, 0:512]), bias_ap, ALU.add, R=[("ps", k2), "a_bw4"] + [("a_bias", x) for x in range(19)],
                         W=[("a_T", k2)])
                    K.act(dst, Tb[k2][:], AF.Exp, R=[("a_T", k2)], W=[dkey])
                else:
                    K.act(dst, v4(ps[:, 0:512]), AF.Exp, R=[("ps", k2)], W=[dkey])

            jl = [j for j in range(i - 4, i + 1) if j >= 0]
            for ji, j in enumerate(jl):
                k2 = j % 2
                if j == i:
                    b_ap = v4(biasT[:, 17, :])
                elif j == i - 1:
                    b_ap = v4(biasT[:, 18, :])
                elif j == i - 4:
                    b_ap = bc(bw4[:, :].unsqueeze(1), [128, 4, 128])
                else:
                    b_ap = None
                scores(KwT[:, j * 128:(j + 1) * 128], qT.rearrange("p a b -> p (a b)"), b_ap, Pb[k2][:], ("a_P", k2),
                       ["a_KwT", "a_QaT"])
                for h in range(4):
                    K.mm(POW[:, h * 65:(h + 1) * 65], Pb[k2][:, h, :], Vw[:, j, :], start=(ji == 0 and h == 0),
                         stop=(j == i and h == 3), R=[("a_P", k2), "a_Vw"], W=["pow"], skip_group_check=True)
            ow = POW[:, 0:260].rearrange("p (a b) -> p a b", a=4)
            K.ts(rdenw[:], ow[:, :, 64], 1e-30, None, ALU.max, R=["pow"], W=["a_rdenw"])
            K.recip(rdenw[:], rdenw[:], R=["a_rdenw"], W=["a_rdenw"])
            K.tt(coefw[:], rdenw[:], gsb[:, q, 2:12:3], ALU.mult, R=["a_rdenw", "a_gsb"], W=["a_coefw"])
            K.tt(yaw[:], ow[:, :, 0:64], bc(coefw[:, :].unsqueeze(2), [128, 4, 64]), ALU.mult, R=["pow", "a_coefw"], W=["a_yaw"])
            first = True
            mlist = [m for m in range(NM) if i - 16 * m >= 0]
            for mi, m in enumerate(mlist):
                ip = i - 16 * m
                k2 = mi % 2
                b_ap = v4(biasT[:, ip, :]) if ip <= 16 else None
                scores(kcT[:, m * 128:(m + 1) * 128], qF.rearrange("p a b -> p (a b)"), b_ap, Pc[k2][:], ("a_Pc", k2),
                       ["a_kcT", "a_QaF"])
                for h in range(4):
                    K.mm(POA[:, h * 65:(h + 1) * 65], Pc[k2][:, h, :], vcx[:, m, :], start=(first and h == 0),
                         stop=(mi == len(mlist) - 1 and h == 3), R=[("a_Pc", k2), "a_vcx"], W=["poa"], skip_group_check=True)
                for h in range(4):
                    K.mm(POB[:, h * 64:(h + 1) * 64], Pc[k2][:, h, :], ovl[:, m, :], start=(first and h == 0),
                         stop=(mi == len(mlist) - 1 and h == 3), R=[("a_Pc", k2), "a_ovl"], W=[PSTK], skip_group_check=True)
                first = False
            oa = POA[:, 0:260].rearrange("p (a b) -> p a b", a=4)
            K.ts(rden[:], oa[:, :, 64], 1e-30, None, ALU.max, R=["poa"], W=["a_rden"])
            K.recip(rden[:], rden[:], R=["a_rden"], W=["a_rden"])
            K.tt(coef[:], rden[:], gsb[:, q, 0:12:3], ALU.mult, R=["a_rden", "a_gsb"], W=["a_coef"])
            K.tt(ya[:], oa[:, :, 0:64], bc(coef[:, :].unsqueeze(2), [128, 4, 64]), ALU.mult, R=["poa", "a_coef"], W=["a_ya"])
            for h in range(4):
                if h == 0:
                    K.ts(scr[:, 0:NB], POB[:, 0:NB], rden[:, 0:1], None, ALU.mult, R=[PSTK, "a_rden"], W=["a_scr"])
                else:
                    K.stt(scr[:, 0:NB], POB[:, h * 64:h * 64 + NB], rden[:, h:h + 1], scr[:, 0:NB], ALU.mult, ALU.add,
                          R=[PSTK, "a_rden", "a_scr"], W=["a_scr"])
            K.tt(scr[:, 0:NB], scr[:, 0:NB], keepc[:, 0, NB - 2 * i:2 * NB - 2 * i], ALU.mult, R=["a_scr", "a_keepc"], W=["a_scr"])
            K.tt(scr[:, 0:NB], scr[:, 0:NB], keepc[:, 1, NB - 2 * i:2 * NB - 2 * i], ALU.add, R=["a_scr", "a_keepc"], W=["a_scr"])
            K.memset(scr[:, 0:1], 1e6, W=["a_scr"])
            K.P.add("dve", lambda e: e.max(v8[:], scr[:, 0:NB]), R=["a_scr"], W=["a_v8"])
            K.P.add("dve", lambda e: e.match_replace(sc2[:, 0:NB], v8[:], scr[:, 0:NB], -3e6), R=["a_scr", "a_v8"], W=["a_sc2"])
            K.P.add("dve", lambda e: e.max(v8[:], sc2[:, 0:NB]), R=["a_sc2"], W=["a_v8"])
            K.ts(mskb[:, 0:NB], scr[:, 0:NB], v8[:, 7:8], None, ALU.is_ge, R=["a_scr", "a_v8"], W=["a_mskb"])
            K.mm(PMX[0:NB, 128:256], mskb[:, 0:NB], identb[:], R=["a_mskb", "a_identb"], W=["pmx"])
            K.copy(mT[0:NB, :], PMX[0:NB, 128:256], R=["pmx"], W=["a_mT"])
            for j in range(i + 1):
                k2 = j % 2
                if j == i:
                    b_ap = v4(biasT[:, 17, :])
                elif j == i - 1:
                    b_ap = v4(biasT[:, 18, :])
                else:
                    b_ap = None
                scores(KsT[:, j * 128:(j + 1) * 128], qT.rearrange("p a b -> p (a b)"), b_ap, Eb[k2][:], ("a_E", k2),
                       ["a_KsT", "a_QaT"])
                K.mm(PMX[:, 0:128], expc[0:NB, j * 128:(j + 1) * 128], mT[0:NB, :], R=["a_expc", "a_mT"], W=["pmx"])
                K.tt(Pb[k2][:], Eb[k2][:], bc(PMX[:, 0:128].unsqueeze(1), [128, 4, 128]), ALU.mult, R=[("a_E", k2), "pmx"],
                     W=[("a_P", k2)])
                for h in range(4):
                    K.mm(POA[:, h * 65:(h + 1) * 65], Pb[k2][:, h, :], Vs[:, j, :], start=(j == 0 and h == 0),
                         stop=(j == i and h == 3), R=[("a_P", k2), "a_Vs"], W=["poa"], skip_group_check=True)
            K.ts(rden[:], oa[:, :, 64], 1e-30, None, ALU.max, R=["poa"], W=["a_rden"])
            K.recip(rden[:], rden[:], R=["a_rden"], W=["a_rden"])
            K.tt(coef[:], rden[:], gsb[:, q, 1:12:3], ALU.mult, R=["a_rden", "a_gsb"], W=["a_coef"])
            K.tt(ytmp[:], oa[:, :, 0:64], bc(coef[:, :].unsqueeze(2), [128, 4, 64]), ALU.mult, R=["poa", "a_coef"], W=["a_ytmp"])
            K.tt(ya[:], ya[:], ytmp[:], ALU.add, R=["a_ya", "a_ytmp"], W=["a_ya"])
            K.tt(ya[:], ya[:], yaw[:], ALU.add, R=["a_ya", "a_yaw"], W=["a_ya"])
            yaf = ya[:].rearrange("p a b -> p (a b)")
            K.tt(ytmp[:], ya[:], ya[:], ALU.mult, R=["a_ya"], W=["a_ytmp"])
            K.P.add("dve", lambda e: e.tensor_reduce(yss[:], ytmp[:].rearrange("p a b -> p (a b)"), AX.X, ALU.add),
                    R=["a_ytmp"], W=["a_yss"])
            K.act(yss[:], yss[:], AF.Sqrt, R=["a_yss", "epsc"], W=["a_yss"], bias=consts["eps"][:], scale=1.0 / 256)
            K.recip(yss[:], yss[:], R=["a_yss"], W=["a_yss"])
            K.stt(yaf, yaf, yss[:, 0:1], mg0[:], ALU.mult, ALU.mult, R=["a_ya", "a_yss", "a_mg0"], W=["a_ya"])
            for pr in range(2):
                K.tr(PMX[:, 256 + pr * 128:256 + (pr + 1) * 128], yaf[:, pr * 128:(pr + 1) * 128], IDENT, R=["a_ya", "cmask"],
                     W=["pmx"])
                K.copy(yt[:, pr, qs], PMX[:, 256 + pr * 128:256 + (pr + 1) * 128], R=["pmx"], W=[("yt", pr)], eng="act")


        for tt in range(NT):
            tsl = slice(tt * TT, (tt + 1) * TT)
            K.dma(xt[:], xsv[:, :, tsl], R=[("xd", sname, tt)], W=[("xt", dc) for dc in range(DC)])
            K.act(sq[:], xt[:], AF.Square, R=[("xt", dc) for dc in range(DC)], W=["sq"])
            for dc in range(DC):
                K.mm(PST[:, 0:TT], consts["ones_bf"][:], sq[:, dc, :], start=(dc == 0), stop=(dc == DC - 1),
                     R=["sq"], W=[PSTK])
            K.act(rs[:], PST[:, 0:TT], AF.Sqrt, R=[PSTK, "epsc"], W=["rs"], bias=consts["eps"][:], scale=1.0 / D)
            K.recip(rs[:], rs[:], R=["rs"], W=["rs"])
            for dc in range(DC):
                k2 = dc % 2
                K.stt(sa[k2][:], xt[:, dc, :], gs[:, s, dc:dc + 1], rs[:], ALU.mult, ALU.mult,
                      R=[("xt", dc), "rs", "gs"], W=[("sa", k2)])
                K.act(hb[:, dc, :], sa[k2][:], AF.Identity, R=[("sa", k2), "mod"], W=[("hb", dc)],
                      bias=shift[:, s * 24 + dc:s * 24 + dc + 1], scale=1.0)
            if mode == 1:
                for r in range(2, DC):
                    if not (("B" in enable and r in (2, 3)) or ("C" in enable and r in (4, 5)) or ("D" in enable and r in (6, 7))):
                        K.memset(yt[:, r, :], 0.0, W=[("yt", r)], eng="pool")
            else:
                K.dma(yt[:, 2:DC, :], yscr.rearrange("(dc p) t -> p dc t", p=128)[:, 2:DC, tsl],
                      R=[("yscr", tt * TT // 256 + q) for q in range(TT // 256)], W=[("yt", r) for r in range(2, DC)])
                if "A" not in enable:
                    for r in range(2):
                        K.memset(yt[:, r, :], 0.0, W=[("yt", r)], eng="pool")
            if "B" in enable:
                for ch in range(2):
                    proj_fm(G_BB + ch, bbt[:], ["bbt"])
                    proj_fm(G_BC + ch, cct[:], ["cct"])
                    proj_fm(G_BX + ch, cvt[:], ["cvt"])
                    K.tt(ub[ch][:, 2:2 + TT], cct[:], cvt[:], ALU.mult, R=["cct", "cvt"], W=[("ub", ch)])
                    K.ts(cvt[:], ub[ch][:, 2:2 + TT], scw[:, ch, 2:3], None, ALU.mult, R=[("ub", ch), "scw"], W=["cvt"])
                    K.stt(cvt[:], ub[ch][:, 1:1 + TT], scw[:, ch, 1:2], cvt[:], ALU.mult, ALU.add,
                          R=[("ub", ch), "scw", "cvt"], W=["cvt"])
                    K.stt(cvt[:], ub[ch][:, 0:TT], scw[:, ch, 0:1], cvt[:], ALU.mult, ALU.add,
                          R=[("ub", ch), "scw", "cvt"], W=["cvt"])
                    K.tt(ybt[ch][:], bbt[:], cvt[:], ALU.mult, R=["bbt", "cvt"], W=[("ybt", ch)])
                    K.copy(ub[ch][:, 0:2], ub[ch][:, TT:TT + 2], R=[("ub", ch)], W=[("ub", ch)])
                    K.act(sq[:, ch, :], ybt[ch][:], AF.Square, R=[("ybt", ch)], W=["sq"])
                for ch in range(2):
                    K.mm(PST[:, 0:TT], consts["ones_bf"][:], sq[:, ch, :], start=(ch == 0), stop=(ch == 1),
                         R=["sq"], W=[PSTK])
                K.act(sa[0][:], PST[:, 0:TT], AF.Sqrt, R=[PSTK, "epsc"], W=[("sa", 0)], bias=consts["eps"][:],
                      scale=1.0 / 256)
                K.recip(sa[0][:], sa[0][:], R=[("sa", 0)], W=[("sa", 0)])
                for ch in range(2):
                    K.stt(yt[:, 2 + ch, :], ybt[ch][:], mixg[:, 1, ch:ch + 1], sa[0][:], ALU.mult, ALU.mult,
                          R=[("ybt", ch), "mixg", ("sa", 0)], W=[("yt", 2 + ch)])
            if "D" in enable:
                for h in range(4):
                    proj_fm(G_DQ + h, QmT[:, h, :], ["QmT"])
                    proj_fm(G_DK + h, KmT[:, h, :], ["KmT"], scale=0.125, bias=bfm8[0:64, h:h + 1])
            if "C" in enable:
                gdn_tile()
            for c in range(NCH):
                if "D" in enable:
                    mlstm_chunk(c)
                if "C" in enable:
                    gdn_chunk(c)
            if mode == 1:
                K.dma(yscr.rearrange("(dc p) t -> p dc t", p=128)[:, 2:DC, tsl], yt[:, 2:DC, :],
                      R=[("yt", r) for r in range(2, DC)], W=[("yscr", tt * TT // 256 + q) for q in range(TT // 256)])
                continue
            if "A" in enable:
                nsa_tile(tt)
                K.dma(yscr.rearrange("(dc p) t -> p dc t", p=128)[:, 0:2, tsl], yt[:, 0:2, :],
                      R=[("yt", r) for r in range(2)], W=[("yscrA", tt)])
            for dc in range(DC):
                pj = PJ[pjn[0] % len(PJ)]
                kk = ("pj", pjn[0] % len(PJ))
                pjn[0] += 1
                for fc in range(DC):
                    K.mm(pj[:, 0:TT], wouts[:, fc, dc * 128:(dc + 1) * 128], yt[:, fc, :], start=(fc == 0),
                         stop=(fc == DC - 1), R=[("wout", fc), ("yt", fc)], W=[kk])
                K.stt(xt[:, dc, :], pj[:, 0:TT], gate[:, s, dc:dc + 1], xt[:, dc, :], ALU.mult, ALU.add,
                      R=[kk, "gate", ("xt", dc)], W=[("xt", dc)])
            K.dma(xdv[:, :, tsl], xt[:], R=[("xt", dc) for dc in range(DC)], W=[("xd", dname, tt)])
    P.barrier()


def t5_thresholds():
    def bucket(n):
        if n < 16:
            return n
        nf = np.float32(n)
        v = np.log(nf / np.float32(16)) / np.float32(math.log(128 / 16)) * np.float32(16)
        return min(16 + int(np.float32(v)), 31)
    bs = [bucket(n) for n in range(0, 400)]
    return [min(n for n in range(400) if bs[n] >= b) for b in range(32)]


def bias_build(K, t5_table, dist_d, bias_scr, consts, es_ext=None):
    nc = K.nc
    lo = t5_thresholds()
    with ExitStack() as es_own:
        es = es_ext if es_ext is not None else es_own
        tb = _sb(es, nc, "bb_tb", [128, 32, 4], F32)
        ndl = _sb(es, nc, "bb_ndl", [128, 31, 4], F32)
        dtl = [_sb(es, nc, "bb_dt%d" % i, [128, 128], F32) for i in range(2)]
        acc = [_sb(es, nc, "bb_acc%d" % i, [128, 4, 128], F32) for i in range(2)]
        tmp = [_sb(es, nc, "bb_tmp%d" % i, [128, 4, 128], F32) for i in range(2)]
        K.dma(tb[:].rearrange("p b h -> p (b h)"), t5_table.rearrange("b h -> (b h)").partition_broadcast(128), W=["bb_tb"])
        K.tt(ndl[:], tb[:, 0:31, :], tb[:, 1:32, :], ALU.subtract, R=["bb_tb"], W=["bb_ndl"])
        for q in range(19):
            k2 = q % 2
            K.dma(dtl[k2][:], dist_d[q], W=[("bb_dt", k2)])
            dbc = bc(dtl[k2][:, :].unsqueeze(1), [128, 4, 128])
            for b in range(1, 32):
                dst = acc[k2] if b == 1 else tmp[b % 2]
                dk = ("bb_acc", k2) if b == 1 else ("bb_tmp", b % 2)
                K.stt(dst[:], dbc, float(lo[b]), bc(ndl[:, b - 1, :].unsqueeze(2), [128, 4, 128]), ALU.is_lt, ALU.mult,
                      R=[("bb_dt", k2), "bb_ndl"], W=[dk])
                if b > 1:
                    K.tt(acc[k2][:], acc[k2][:], dst[:], ALU.add, R=[("bb_acc", k2), dk], W=[("bb_acc", k2)])
            K.ts(tmp[0][:], dbc, 0.0, -30000.0, ALU.is_lt, ALU.mult, R=[("bb_dt", k2)], W=[("bb_tmp", 0)])
            K.tt(acc[k2][:], acc[k2][:], tmp[0][:], ALU.add, R=[("bb_acc", k2), ("bb_tmp", 0)], W=[("bb_acc", k2)])
            K.dma(bias_scr[q], acc[k2][:].rearrange("p a b -> p (a b)"), R=[("bb_acc", k2)], W=[("bias_scr", q)])
    if es_ext is None:
        K.P.barrier()


def build(T=4096, TT=512, layers=2, debug_y=False, enable="ABCD"):
    nc = bass.Bass("TRN2", target_bir_lowering=False)
    K = KB(nc)
    P = K.P
    dt = lambda name, shape, kind="ExternalInput", d=F32: nc.dram_tensor(name, list(shape), d, kind=kind).ap()
    xT = dt("xT", [D, T])
    cT = dt("cT", [128, DC])
    ada_w = dt("ada_w", [2, D, 9 * D])
    ada_bT = dt("ada_bT", [2, 128, 72])
    normgT = dt("normgT", [2, 128, 3, DC])
    ffn_w13 = [dt("ffn1_w13", [2, D, 2 * DFF]), dt("ffn2_w13", [2, D, 2 * DFF])]
    ffn_w2 = [dt("ffn1_w2", [2, DFF, D]), dt("ffn2_w2", [2, DFF, D])]
    outT = dt("outT", [D, T], kind="ExternalOutput")
    Win = {
        "w_in": dt("w_in", [2, D, D_IN]), "b_in": dt("b_in", [2, D_IN]), "b_fm": dt("b_fm", [2, 128, NFM]),
        "w_out": dt("w_out", [2, D, D]), "sc_conv_wT": dt("sc_conv_wT", [2, 128, 2, 3]),
        "mixgT": dt("mixgT", [2, 128, 2, 2]), "mlstm_f_bias": dt("mlstm_f_bias", [2, 4]),
        "mlstm_norm_g": dt("mlstm_norm_g", [2, 64]),
        "gdn_conv_wT": dt("gdn_conv_wT", [2, 64, 12, 4]), "gdn_A_log": dt("gdn_A_log", [2, 4]),
        "gdn_dt_bias": dt("gdn_dt_bias", [2, 4]), "gdn_norm_g": dt("gdn_norm_g", [2, 64]),
        "cmp_k_w1": dt("cmp_k_w1", [2, 2048, 256]), "cmp_v_w1": dt("cmp_v_w1", [2, 2048, 256]),
        "cmp_k_w2": dt("cmp_k_w2", [2, 256, 64]), "cmp_v_w2": dt("cmp_v_w2", [2, 256, 64]),
        "cmp_posT": dt("cmp_posT", [2, 64, 32]), "qkng": dt("qkng", [2, 64, 2]), "mix_norm_g0": dt("mix_norm_g0", [2, 256]),
    }
    NB_ = T // 64
    NM_ = (T // 16 + 127) // 128
    Wsh = {
        "t5_table": dt("t5_table", [32, 4]), "expc": dt("expc", [NB_, T]), "keepadd": dt("keepadd", [128, 2, 2 * NB_]),
        "ovl": dt("ovl", [128, NM_, 64]), "bw4": dt("bw4", [128, 128]),
        "bias_scr": dt("bias_scr", [19, 128, 512], kind="Internal"),
    }
    dist_d = dt("dist_tiles", [19, 128, 128])
    cmask_d = dt("cmask", [128, CM_N])
    yscr = dt("ydbg", [D, T], kind="ExternalOutput" if debug_y else "Internal", d=BF16)
    xa = dt("xa_scr", [D, T], kind="Internal")
    xb = dt("xb_scr", [D, T], kind="Internal")

    with ExitStack() as es:
        consts = {
            "ones_bf": _sb(es, nc, "ones_bf", [128, 128], BF16),
            "eps": _sb(es, nc, "epsc", [128, 1], F32),
        }
        K.memset(consts["ones_bf"][:], 1.0, W=["ones_bf"])
        consts["ones_f"] = _sb(es, nc, "ones_f", [128, 128], F32)
        consts["one"] = _sb(es, nc, "onec", [128, 1], F32)
        consts["cmask"] = _sb(es, nc, "cmask_sb", [128, CM_N], F32)
        K.memset(consts["ones_f"][:], 1.0, W=["ones_f"])
        K.memset(consts["one"][:], 1.0, W=["onec"])
        K.dma(consts["cmask"][:], cmask_d, W=["cmask"])
        K.memset(consts["eps"][:], EPS, W=["epsc"])
        condT = _sb(es, nc, "condT", [128, DC, 2], F32)
        ctmp = _sb(es, nc, "ctmp", [128, DC], F32)
        K.memset(condT[:], 0.0, W=["condT"])
        K.dma(ctmp[:], cT, W=["ctmp"])
        K.act(condT[:, :, 0], ctmp[:], AF.Silu, R=["ctmp"], W=["condT"])
        mv = []
        for l in range(2):
            mv.append({
                "mod": _sb(es, nc, "mod%d" % l, [128, 72], F32),
                "gs": _sb(es, nc, "gs%d" % l, [128, 3, DC], F32),
                "gate": _sb(es, nc, "gate%d" % l, [128, 3, DC], F32),
            })
        adab = _sb(es, nc, "adab", [128, 2, 72], F32)
        normg = _sb(es, nc, "normg", [128, 2, 3, DC], F32)
        K.dma(adab[:], ada_bT.rearrange("l p j -> p l j"), W=["adab"])
        K.dma(normg[:], normgT.rearrange("l p s d -> p l s d"), W=["normg"])
        P.barrier()

        with ExitStack() as es0:
            for l in range(layers):
                mod_phase(K, es0, l, ada_w[l], adab[:, l, :], normg[:, l], condT, mv[l])
            if "A" in enable:
                bias_build(K, Wsh["t5_table"], dist_d, Wsh["bias_scr"], consts, es_ext=es0)
        P.barrier()
        cur = xT
        curname = "xT"
        for l in range(layers):
            last = (l == layers - 1)
            ffn_phase(K, l, 0, cur, curname, xa, "xa", ffn_w13[0][l], ffn_w2[0][l], mv[l], 0, T, TT, consts)
            Wd = {k: v[l] for k, v in Win.items()}
            Wd.update(Wsh)
            mixer_phase(K, l, xa, "xa", xb, "xb", Wd, mv[l], T, 256, consts, yscr, 1, enable=enable)
            mixer_phase(K, l, xa, "xa", xb, "xb", Wd, mv[l], T, 256, consts, yscr, 2, enable=enable)
            ffn_phase(K, l, 1, xb, "xb", outT if last else xa, "outT" if last else "xa", ffn_w13[1][l], ffn_w2[1][l], mv[l], 2, T, TT, consts)
            cur = xa
            curname = "xa"
        P.fence("sp", [("xd", "outT", tt) for tt in range(T // TT)])
        P.emit()
    return nc


def _cmask():
    m = np.zeros((128, CM_N), np.float32)
    m[:, CM_ID:CM_ID + 128] = np.eye(128, dtype=np.float32)
    k = np.arange(64)[:, None]
    i = np.arange(64)[None, :]
    m[0:64, CM_TRIU:CM_TRIU + 64] = (k <= i)
    m[0:64, CM_SU:CM_SU + 64] = (k < i)
    m[0:64, CM_SL:CM_SL + 64] = (k > i)
    m[0:64, CM_LI:CM_LI + 64] = (k >= i)
    return m


def prep_shared(inp, T=4096):
    f = lambda a: np.ascontiguousarray(np.asarray(a, dtype=np.float32))
    b_in = f(inp["b_in"])
    b_fm = np.zeros((2, 128, NFM), np.float32)
    for g, (c0, n) in enumerate(FM):
        b_fm[:, 0:n, g] = b_in[:, c0:c0 + n]
    sh = {
        "ada_w": f(inp["ada_w"]),
        "ada_bT": f(np.asarray(inp["ada_b"]).reshape(2, 72, 128).transpose(0, 2, 1)),
        "normgT": f(np.asarray(inp["norm_g"]).reshape(2, 3, 8, 128).transpose(0, 3, 1, 2)),
        "ffn1_w13": f(inp["ffn1_w13"]), "ffn2_w13": f(inp["ffn2_w13"]),
        "ffn1_w2": f(inp["ffn1_w2"]), "ffn2_w2": f(inp["ffn2_w2"]),
        "w_in": f(inp["w_in"]), "b_in": b_in, "b_fm": b_fm, "w_out": f(inp["w_out"]),
        "sc_conv_wT": f(np.asarray(inp["sc_conv_w"]).reshape(2, 3, 2, 128).transpose(0, 3, 2, 1)),
        "mixgT": f(np.asarray(inp["mix_norm_g"]).reshape(2, 2, 2, 128).transpose(0, 3, 1, 2)),
        "mlstm_f_bias": f(inp["mlstm_f_bias"]), "mlstm_norm_g": f(inp["mlstm_norm_g"]),
        "gdn_conv_wT": f(np.asarray(inp["gdn_conv_w"]).reshape(2, 4, 12, 64).transpose(0, 3, 2, 1)),
        "gdn_A_log": f(inp["gdn_A_log"]), "gdn_dt_bias": f(inp["gdn_dt_bias"]), "gdn_norm_g": f(inp["gdn_norm_g"]),
        "cmask": _cmask(),
        "cmp_k_w1": f(inp["cmp_k_w1"]), "cmp_v_w1": f(inp["cmp_v_w1"]), "cmp_k_w2": f(inp["cmp_k_w2"]),
        "cmp_v_w2": f(inp["cmp_v_w2"]), "cmp_posT": f(np.asarray(inp["cmp_pos"]).transpose(0, 2, 1)),
        "qkng": f(np.stack([np.asarray(inp["q_norm_g"]), np.asarray(inp["k_norm_g"])], axis=-1)),
        "mix_norm_g0": f(np.asarray(inp["mix_norm_g"])[:, 0]), "t5_table": f(inp["t5_table"]),
    }
    sh.update(_nsa_consts(T))
    return sh


def _nsa_consts(T):
    NB = T // 64
    NM = (T // 16 + 127) // 128
    c = np.arange(128)[:, None]
    r = np.arange(128)[None, :]
    dist = np.zeros((19, 128, 128), np.float32)
    for ip in range(17):
        dist[ip] = r - 16 * c + 128 * ip - 31
    dist[17] = r - c
    dist[18] = 128 + r - c
    expc = (np.arange(T)[None, :] // 64 == np.arange(NB)[:, None]).astype(np.float32)
    keep = np.ones((128, 2 * NB), np.float32)
    add = np.zeros((128, 2 * NB), np.float32)
    for rr in range(128):
        for x in range(2 * NB):
            rb = x - NB
            if rr < 64:
                forced, invalid = rb in (-1, 0), rb > 0
            else:
                forced, invalid = rb in (0, 1), rb > 1
            if invalid:
                keep[rr, x], add[rr, x] = 0.0, -1e6
            elif forced:
                keep[rr, x], add[rr, x] = 0.0, 1e6
    ovl = np.zeros((128, NM, 64), np.float32)
    for m in range(NM):
        ci = 128 * m + np.arange(128)[:, None]
        bj = np.arange(64)[None, :]
        ovl[:, m, :] = ((ci * 16 < (bj + 1) * 64) & (ci * 16 + 32 > bj * 64) & (bj < NB) & (ci < T // 16 - 1))
    bw4 = np.where(c > r, 0.0, -30000.0).astype(np.float32)
    return {"dist_tiles": dist, "expc": expc, "keepadd": np.ascontiguousarray(np.stack([keep, add], axis=1)),
            "ovl": ovl, "bw4": bw4}


def prep_core(inp, b):
    x = np.asarray(inp["x"], dtype=np.float32)
    c = np.asarray(inp["c"], dtype=np.float32)
    return {"xT": np.ascontiguousarray(x[b].T), "cT": np.ascontiguousarray(c[b].reshape(8, 128).T)}


_NC_CACHE = {}


def kernel(**inputs):
    x = np.asarray(inputs["x"])
    B, T, _ = x.shape
    if T not in _NC_CACHE:
        _NC_CACHE[T] = build(T=T)
    nc = _NC_CACHE[T]
    sh = prep_shared(inputs, T)
    in_maps = []
    for b in range(B):
        m = dict(sh)
        m.update(prep_core(inputs, b))
        in_maps.append(m)
    res = run_bass_kernel_spmd(nc, in_maps, core_ids=list(range(B)))
    out = np.stack([np.asarray(r["outT"]).T for r in res.results], axis=0)
    return np.ascontiguousarray(out.astype(np.float32))
```

```python
import math
import os
GSTOP = float(os.environ.get('GSTOP', '99'))
from contextlib import ExitStack
import numpy as np
import concourse.bass as bass
import concourse.mybir as mybir
from concourse.bass_utils import run_bass_kernel_spmd

F32 = mybir.dt.float32
BF16 = mybir.dt.bfloat16
AF = mybir.ActivationFunctionType
ALU = mybir.AluOpType
AX = mybir.AxisListType

D = 1024
DC = 8
DFF = 2816
FC = 22
D_IN = 3484
EPS = 1e-6

ENGS = ("pe", "act", "dve", "pool", "sp")


class _Op:
    __slots__ = ("eng", "fn", "reads", "writes", "deps", "sig", "tok", "waits", "dma", "snap", "inc", "cost", "odeps", "idx")

    def __init__(self, eng, fn, reads, writes, dma):
        self.eng = eng
        self.fn = fn
        self.reads = reads
        self.writes = writes
        self.dma = dma
        self.deps = ()
        self.sig = False
        self.tok = None
        self.waits = ()
        self.snap = None
        self.inc = 1
        self.cost = 300.0
        self.odeps = ()
        self.idx = 0


class Prog:
    EPOCH = 20000
    NDMA = 12

    def __init__(self, nc):
        self.nc = nc
        self.ops = []
        self.last_w = {}
        self.readers = {}
        self.last_on = {}
        self.dma_ops = []

    EXCL = ("pj", "pw", "ptk", "pab", "po", "pst", "pmod", "ps", "poa", "pob", "pmx", "pd", "pc", "pow")

    def _excl(self, k):
        return (k[0] if isinstance(k, tuple) else k) in self.EXCL

    def add(self, eng, fn, R=(), W=(), dma=False, cost=300.0):
        xr = [k for k in R if self._excl(k)]
        if xr:
            R = [k for k in R if not self._excl(k)]
            W = list(W) + [k for k in xr if k not in W]
        op = _Op(eng, fn, tuple(R), tuple(W), dma)
        i = len(self.ops)
        deps = set()
        for k in op.reads:
            w = self.last_w.get(k)
            if w is not None:
                deps.add(w)
        for k in op.writes:
            w = self.last_w.get(k)
            if w is not None:
                deps.add(w)
            deps.update(self.readers.get(k, ()))
        for k in op.writes:
            self.last_w[k] = i
            self.readers[k] = []
        for k in op.reads:
            if k not in op.writes:
                self.readers.setdefault(k, []).append(i)
        op.deps = deps
        op.cost = cost
        self.ops.append(op)
        self.last_on[eng] = i
        if dma:
            self.dma_ops.append(i)
        return i

    def barrier(self):
        lasts = set(self.last_on.values()) | set(self.dma_ops)
        self.dma_ops = []
        for e in ENGS:
            op = _Op(e, None, (), (), False)
            op.deps = set(lasts)
            self.ops.append(op)
            self.last_on[e] = len(self.ops) - 1
        self.last_w = {}
        self.readers = {}

    def fence(self, eng, keys):
        self.add(eng, None, R=keys)

    def schedule(self):
        import heapq
        ops = self.ops
        n = len(ops)
        for i, op in enumerate(ops):
            op.idx = i
        order = []
        LAT = 250.0
        W = 24
        seg_start = 0
        i = 0
        segs = []
        while i < n:
            if ops[i].fn is None and not ops[i].dma and len(ops[i].reads) == 0 and len(ops[i].writes) == 0:
                j = i
                while j < n and ops[j].fn is None and not ops[j].dma:
                    j += 1
                segs.append((seg_start, i))
                segs.append((i, j))
                seg_start = j
                i = j
            else:
                i += 1
        segs.append((seg_start, n))
        for (a, b) in segs:
            if b <= a:
                continue
            if ops[a].fn is None and not ops[a].dma:
                order.extend(range(a, b))
                continue
            nrem = {}
            users = {}
            for k in range(a, b):
                dl = [d for d in ops[k].deps if d >= a]
                nrem[k] = len(dl)
                for d in dl:
                    users.setdefault(d, []).append(k)
            blev = {}
            for k in range(b - 1, a - 1, -1):
                m_ = 0.0
                for u in users.get(k, ()):
                    if blev[u] > m_:
                        m_ = blev[u]
                blev[k] = ops[k].cost + LAT + m_
            ready = {e: [] for e in ENGS}
            for k in range(a, b):
                if nrem[k] == 0:
                    heapq.heappush(ready[ops[k].eng], k)
            fin = {}
            free = {e: 0.0 for e in ENGS}
            left = b - a
            while left > 0:
                best = None
                for e in ENGS:
                    rl = ready[e]
                    if not rl:
                        continue
                    cands = heapq.nsmallest(W, rl)
                    for k in cands:
                        st = free[e]
                        for d in ops[k].deps:
                            if d >= a:
                                f = fin[d] + LAT
                                if f > st:
                                    st = f
                        key = (int(st / 250.0), -blev[k], k, st)
                        if best is None or key < best[0]:
                            best = (key, e, k)
                (_q, _b, k, st), e, _ = best
                ready[e].remove(k)
                heapq.heapify(ready[e])
                op = ops[k]
                if op.dma:
                    free[e] = st + 60.0
                    fin[k] = st + op.cost
                else:
                    free[e] = st + op.cost
                    fin[k] = st + op.cost
                order.append(k)
                left -= 1
                for u in users.get(k, ()):
                    nrem[u] -= 1
                    if nrem[u] == 0:
                        heapq.heappush(ready[ops[u].eng], u)
        return order

    def emit(self):
        nc = self.nc
        if os.environ.get("NOSCHED", "0") != "1":
            order = self.schedule()
            old = self.ops
            remap = {o: nidx for nidx, o in enumerate(order)}
            newops = [old[o] for o in order]
            for op in newops:
                op.deps = {remap[d] for d in op.deps}
            self.ops = newops
        ops = self.ops
        for i, op in enumerate(ops):
            if op.eng == "pe":
                op.deps = {d for d in op.deps if not (ops[d].eng == "pe" and not ops[d].dma)}
        ndma = 0
        slot_last = {}
        for i, op in enumerate(ops):
            if op.dma:
                slot = ndma % self.NDMA
                if slot in slot_last:
                    op.deps = set(op.deps) | {slot_last[slot]}
                slot_last[slot] = i
                op.tok = ("dma", slot, 16 * (ndma // self.NDMA + 1))
                ndma += 1
        for op in ops:
            for d in op.deps:
                ops[d].sig = True
        cnt = {e: 0 for e in ENGS}
        for op in ops:
            if op.sig and not op.dma:
                if op.fn is None:
                    continue
                cnt[op.eng] += 1
                c = cnt[op.eng]
                op.tok = (op.eng, (c - 1) // self.EPOCH, (c - 1) % self.EPOCH + 1)
        nep = {e: (cnt[e] + self.EPOCH - 1) // self.EPOCH for e in ENGS}
        sems = {}
        for e in ENGS:
            for k in range(max(nep[e], 0)):
                sems[(e, k)] = nc.alloc_semaphore("s_%s_%d" % (e, k))
        for s in range(min(self.NDMA, max(ndma, 1))):
            sems[("dma", s)] = nc.alloc_semaphore("s_dma_%d" % s)
        known = {e: {} for e in ENGS}
        for op in ops:
            kn = known[op.eng]
            waits = []
            stack = list(op.deps)
            seen = set()
            while stack:
                d = stack.pop()
                if d in seen:
                    continue
                seen.add(d)
                dop = ops[d]
                if dop.fn is None and not dop.dma:
                    if dop.eng == op.eng:
                        continue
                    stack.extend(dop.deps)
                    continue
                tk = dop.tok
                key = (tk[0], tk[1])
                if kn.get(key, 0) >= tk[2]:
                    continue
                waits.append((key, tk[2]))
                if dop.snap is not None:
                    for k2, v2 in dop.snap.items():
                        if kn.get(k2, 0) < v2:
                            kn[k2] = v2
                kn[key] = tk[2]
            wm = {}
            for key, v in waits:
                if wm.get(key, 0) < v:
                    wm[key] = v
            op.waits = tuple(wm.items())
            if op.tok is not None:
                op.snap = dict(kn)
        handles = {"pe": nc.tensor, "act": nc.scalar, "dve": nc.vector, "pool": nc.gpsimd, "sp": nc.sync}
        per = {e: [op for op in ops if op.eng == e] for e in ENGS}

        def run(e, eng):
            for op in per[e]:
                for key, v in op.waits:
                    eng.wait_ge(sems[key], v)
                if op.fn is None:
                    continue
                ins = op.fn(eng)
                if op.tok is not None:
                    tk = op.tok
                    ins.then_inc(sems[(tk[0], tk[1])], 16 if op.dma else 1)

        with nc.Block() as block:
            @block.tensor
            def _(eng):
                run("pe", eng)

            @block.scalar
            def _(eng):
                run("act", eng)

            @block.vector
            def _(eng):
                run("dve", eng)

            @block.gpsimd
            def _(eng):
                run("pool", eng)

            @block.sync
            def _(eng):
                run("sp", eng)
        self.stats = dict(n_ops=len(ops), cnt=cnt, ndma=ndma)


class KB:
    def __init__(self, nc):
        self.nc = nc
        self.P = Prog(nc)

    @staticmethod
    def _fs(ap):
        n = 1
        for d in ap.shape[1:]:
            n *= d
        return n

    def mm(self, out, lhsT, rhs, start=True, stop=True, R=(), W=(), **kw):
        passes = 2.0 if rhs.dtype == F32 else 1.0
        c = 40.0 + (self._fs(lhsT) * 0.85 + max(64, self._fs(rhs)) * 0.85) * passes
        return self.P.add("pe", lambda e: e.matmul(out, lhsT, rhs, start=start, stop=stop, **kw), R, W, cost=c)

    def tr(self, out, in_, ident, R=(), W=()):
        return self.P.add("pe", lambda e: e.transpose(out, in_, ident), R, W, cost=120.0)

    def act(self, out, in_, func, R=(), W=(), bias=None, scale=None):
        kw = {}
        if bias is not None:
            kw["bias"] = bias
        if scale is not None:
            kw["scale"] = scale
        return self.P.add("act", lambda e: e.activation(out, in_, func, **kw), R, W, cost=260.0 + 0.75 * self._fs(in_))

    def tt(self, out, in0, in1, op, R=(), W=(), eng="dve"):
        return self.P.add(eng, lambda e: e.tensor_tensor(out, in0, in1, op), R, W, cost=150.0 + 1.45 * self._fs(out))

    def ts(self, out, in0, s1, s2, op0, op1=None, R=(), W=(), eng="dve"):
        if op1 is None:
            return self.P.add(eng, lambda e: e.tensor_scalar(out, in0, s1, None, op0), R, W, cost=150.0 + 1.1 * self._fs(out))
        return self.P.add(eng, lambda e: e.tensor_scalar(out, in0, s1, s2, op0, op1), R, W, cost=150.0 + 1.1 * self._fs(out))

    def stt(self, out, in0, scalar, in1, op0, op1, R=(), W=()):
        return self.P.add("dve", lambda e: e.scalar_tensor_tensor(out, in0, scalar, in1, op0, op1), R, W,
                          cost=150.0 + 1.1 * self._fs(out))

    def copy(self, out, in_, R=(), W=(), eng="dve"):
        if eng == "act":
            return self.P.add("act", lambda e: e.copy(out, in_), R, W, cost=260.0 + 0.75 * self._fs(out))
        return self.P.add(eng, lambda e: e.tensor_copy(out, in_), R, W, cost=150.0 + 0.9 * self._fs(out))

    def recip(self, out, in_, R=(), W=()):
        return self.P.add("dve", lambda e: e.reciprocal(out, in_), R, W, cost=200.0 + 2.6 * self._fs(out))

    def memset(self, ap, val, W=(), eng="dve"):
        return self.P.add(eng, lambda e: e.memset(ap, val), (), W, cost=110.0 + 1.05 * self._fs(ap))

    def dma(self, out, in_, R=(), W=(), eng="sp", **kw):
        nbytes = out.shape[0] * self._fs(out) * (2 if out.dtype == BF16 else 4)
        return self.P.add(eng, lambda e: e.dma_start(out, in_, **kw), R, W, dma=True, cost=2200.0 + nbytes / 120.0)


def _sb(es, nc, name, shape, dt):
    return es.enter_context(nc.sbuf_tensor(name, shape, dt))


def _ps(es, nc, name, shape, dt=F32):
    return es.enter_context(nc.psum_tensor(name, shape, dt))


def mod_phase(K, es_glob, lay, ada_w_l, ada_bT_l, normgT_l, condT, out):
    nc = K.nc
    P = K.P
    with ExitStack() as es_own:
        es = es_glob if es_glob is not None else es_own
        wt = [_sb(es, nc, "adaw%d_%d" % (lay, i), [128, DC, 1024], F32) for i in range(2)]
        pm = _ps(es, nc, "pmod%d" % lay, [128, 72 * 2], F32)
        awv = ada_w_l.rearrange("(kc p) n -> p kc n", p=128)
        for g in range(9):
            b = wt[g % 2]
            for kc in range(DC):
                K.dma(b[:, kc, :], awv[:, kc, g * 1024:(g + 1) * 1024], W=[("adaw", g % 2, kc)])
            for jj in range(8):
                j = g * 8 + jj
                for kc in range(DC):
                    K.mm(pm[:, 2 * j:2 * j + 2], b[:, kc, jj * 128:(jj + 1) * 128], condT[:, kc, :],
                         start=(kc == 0), stop=(kc == DC - 1), R=[("adaw", g % 2, kc), "condT"], W=["pmod"])
        mod = out["mod"]
        pmv = pm[:].rearrange("p (j two) -> p j two", two=2)[:, :, 0]
        K.tt(mod[:], pmv, ada_bT_l, ALU.add, R=["pmod", "adab"], W=["mod"])
        for s in range(3):
            K.stt(out["gs"][:, s, :], mod[:, s * 24 + 8:s * 24 + 16], 1.0, normgT_l[:, s, :], ALU.add, ALU.mult,
                  R=["mod", "normg"], W=["gs"])
            K.ts(out["gate"][:, s, :], mod[:, s * 24 + 16:s * 24 + 24], 0.5 if s != 1 else 1.0, None, ALU.mult,
                 R=["mod"], W=["gate"])
    if es_glob is None:
        P.barrier()


def ffn_phase(K, lay, which, x_src, sname, x_dst, dname, w13, w2, mv, s, T, TT, consts):
    nc = K.nc
    P = K.P
    NT = T // TT
    tg = "f%d%d" % (lay, which)
    with ExitStack() as es:
        w13s = _sb(es, nc, "w13_" + tg, [128, DC, 2 * DFF], BF16)
        w2s = _sb(es, nc, "w2_" + tg, [128, FC, D], BF16)
        xt = _sb(es, nc, "xt_" + tg, [128, DC, TT], F32)
        sq = _sb(es, nc, "sq_" + tg, [128, DC, TT], BF16)
        hb = _sb(es, nc, "hb_" + tg, [128, DC, TT], BF16)
        gb = _sb(es, nc, "gb_" + tg, [128, FC, TT], BF16)
        rs = _sb(es, nc, "rs_" + tg, [128, TT], F32)
        sa = [_sb(es, nc, "sa%d_" % i + tg, [128, TT], F32) for i in range(2)]
        pst = _ps(es, nc, "pst_" + tg, [128, TT])
        pa = [_ps(es, nc, "pa%d_" % i + tg, [128, TT]) for i in range(2)]
        pb = [_ps(es, nc, "pb%d_" % i + tg, [128, TT]) for i in range(2)]
        po = [_ps(es, nc, "po%d_" % i + tg, [128, TT]) for i in range(2)]
        w13v = w13.rearrange("(kc p) f -> p kc f", p=128)
        w2v = w2.rearrange("(fc p) d -> p fc d", p=128)
        for kc in range(DC):
            K.dma(w13s[:, kc, :], w13v[:, kc, :], W=[("w13", kc)], eng="pool")
        for fc in range(FC):
            K.dma(w2s[:, fc, :], w2v[:, fc, :], W=[("w2", fc)], eng="pool")
        xsv = x_src.rearrange("(dc p) t -> p dc t", p=128)
        xdv = x_dst.rearrange("(dc p) t -> p dc t", p=128)
        gs, shift, gate = mv["gs"], mv["mod"], mv["gate"]
        for tt in range(NT):
            tsl = slice(tt * TT, (tt + 1) * TT)
            K.dma(xt[:], xsv[:, :, tsl], R=[("xd", sname, tt)], W=[("xt", dc) for dc in range(DC)])
            K.act(sq[:], xt[:], AF.Square, R=[("xt", dc) for dc in range(DC)], W=["sq"])
            for dc in range(DC):
                K.mm(pst[:], consts["ones_bf"][:], sq[:, dc, :], start=(dc == 0), stop=(dc == DC - 1),
                     R=["sq"], W=["pst"])
            K.act(rs[:], pst[:], AF.Sqrt, R=["pst", "epsc"], W=["rs"], bias=consts["eps"][:], scale=1.0 / D)
            K.recip(rs[:], rs[:], R=["rs"], W=["rs"])
            for dc in range(DC):
                k2 = dc % 2
                K.stt(sa[k2][:], xt[:, dc, :], gs[:, s, dc:dc + 1], rs[:], ALU.mult, ALU.mult,
                      R=[("xt", dc), "rs", "gs"], W=[("sa", k2)])
                K.act(hb[:, dc, :], sa[k2][:], AF.Identity, R=[("sa", k2), "mod"], W=[("hb", dc)],
                      bias=shift[:, s * 24 + dc:s * 24 + dc + 1], scale=1.0)
            for fc in range(FC):
                k2 = fc % 2
                for half, pp in ((0, pa), (1, pb)):
                    for kc in range(DC):
                        K.mm(pp[k2][:], w13s[:, kc, half * DFF + fc * 128:half * DFF + (fc + 1) * 128],
                             hb[:, kc, :], start=(kc == 0), stop=(kc == DC - 1),
                             R=[("w13", kc), ("hb", kc)], W=[("pab", half, k2)])
                K.act(sa[k2][:], pa[k2][:], AF.Silu, R=[("pab", 0, k2)], W=[("sa", k2)])
                K.tt(gb[:, fc, :], sa[k2][:], pb[k2][:], ALU.mult, R=[("sa", k2), ("pab", 1, k2)], W=[("gb", fc)])
            for dc in range(DC):
                k2 = dc % 2
                for fc in range(FC):
                    K.mm(po[k2][:], w2s[:, fc, dc * 128:(dc + 1) * 128], gb[:, fc, :],
                         start=(fc == 0), stop=(fc == FC - 1), R=[("w2", fc), ("gb", fc)], W=[("po", k2)])
                K.stt(xt[:, dc, :], po[k2][:], gate[:, s, dc:dc + 1], xt[:, dc, :], ALU.mult, ALU.add,
                      R=[("po", k2), "gate", ("xt", dc)], W=[("xt", dc)])
            K.dma(xdv[:, :, tsl], xt[:], R=[("xt", dc) for dc in range(DC)], W=[("xd", dname, tt)])
    P.barrier()


OFF = {}
_o = 0
for _n, _w in (("a_q", 256), ("a_k_cmp", 64), ("a_v_cmp", 64), ("a_k_slc", 64), ("a_v_slc", 64), ("a_k_win", 64),
               ("a_v_win", 64), ("a_gate", 12), ("b_b", 256), ("b_c", 256), ("b_x", 256), ("c_q", 256), ("c_k", 256),
               ("c_v", 256), ("c_beta", 4), ("c_alpha", 4), ("c_z", 256), ("d_q", 256), ("d_k", 256), ("d_v", 256),
               ("d_i", 4), ("d_f", 4), ("d_o", 256)):
    OFF[_n] = _o
    _o += _w
assert _o == D_IN
FM = ([(OFF["a_q"] + 64 * h, 64) for h in range(4)] + [(OFF["a_k_cmp"], 128), (OFF["a_k_slc"], 64), (OFF["a_k_win"], 64)]
      + [(OFF["b_b"] + 128 * i, 128) for i in range(2)] + [(OFF["b_c"] + 128 * i, 128) for i in range(2)]
      + [(OFF["b_x"] + 128 * i, 128) for i in range(2)] + [(OFF["c_q"] + 64 * i, 64) for i in range(12)]
      + [(OFF["d_q"] + 64 * i, 64) for i in range(4)] + [(OFF["d_k"] + 64 * i, 64) for i in range(4)])
G_AQ, G_KVC, G_KSLC, G_KWIN, G_BB, G_BC, G_BX, G_CQKV, G_DQ, G_DK = 0, 4, 5, 6, 7, 9, 11, 13, 25, 29
NFM = len(FM)
CM_ID, CM_TRIU, CM_SU, CM_SL, CM_LI, CM_N = 0, 128, 192, 256, 320, 384


def bc(ap, shape):
    return ap.to_broadcast(list(shape))


def mixer_phase(K, lay, x_src, sname, x_dst, dname, Wd, mv, T, TT, consts, yscr, mode, enable="ABCD"):
    nc = K.nc
    P = K.P
    NT = T // TT
    NCH = TT // 64
    tg = "m%d%d" % (lay, mode)
    if mode == 1:
        enable = "".join(c for c in enable if c in "BCD")
        WOFF, WN = OFF["b_b"], D_IN - OFF["b_b"]
    else:
        enable = "".join(c for c in enable if c in "A")
        WOFF, WN = 0, OFF["b_b"]
    SUM = cm_su = None
    cm = consts["cmask"]
    SUM = cm[0:64, CM_SU:CM_SU + 64]
    TRIU = cm[0:64, CM_TRIU:CM_TRIU + 64]
    SLM = cm[0:64, CM_SL:CM_SL + 64]
    IDENT = cm[:, CM_ID:CM_ID + 128]
    onesf = consts["ones_f"]
    with ExitStack() as es:
        sb = lambda name, shape, d=F32: _sb(es, nc, name + "_" + tg, shape, d)
        wins = sb("win", [128, DC, WN], BF16)
        wouts = sb("wout", [128, DC, D if mode == 2 else 2], BF16)
        bfm = sb("bfm", [128, NFM], F32)
        bfm8 = sb("bfm8", [128, 4], F32)
        bfmv = sb("bfmv", [64, 1], F32)
        K.dma(bfmv[:], Wd["b_fm"][64:128, G_KVC:G_KVC + 1], W=["a_bfmv"], allow_slow_non_contiguous=True)
        xt = sb("xt", [128, DC, TT])
        hb = sb("hb", [128, DC, TT], BF16)
        sq = sb("sq", [128, DC, TT], BF16) if mode == 1 else hb
        SQK = ["sq"] if mode == 1 else [("hb", dc_) for dc_ in range(DC)]
        yt = sb("yt", [128, DC, TT], BF16)
        rs = sb("rs", [128, TT])
        sa = [sb("sa%d" % i, [128, TT]) for i in range(2)]
        scw = sb("scw", [128, 2, 3])
        mixg = sb("mixg", [128, 2, 2])
        DC_ = DC
        PJ = [_ps(es, nc, "pj0_" + tg, [128, 512])]
        if mode == 1:
            PST = PJ[0]
            PSTK = ("pj", 0)
            PD = [_ps(es, nc, "pd%d_" % i + tg, [128, 512]) for i in range(3)]
            PC = [_ps(es, nc, "pc%d_" % i + tg, [128, 512]) for i in range(4)]
            PJL = [PJ[0]] + PD + PC
            PJK = [("pj", 0)] + [("pd", i) for i in range(3)] + [("pc", i) for i in range(4)]
        else:
            PST = _ps(es, nc, "pst_" + tg, [128, 512])
            PSTK = "pst"
            PJL = [PJ[0]]
            PJK = [("pj", 0)]
            POW = _ps(es, nc, "pow_" + tg, [128, 512])
            PW = [_ps(es, nc, "pw0_" + tg, [128, 512])]
            PS2 = [_ps(es, nc, "ps2%d_" % i + tg, [128, 512]) for i in range(2)]
            POA = _ps(es, nc, "poa_" + tg, [128, 512])
            PMX = _ps(es, nc, "pmx_" + tg, [128, 512])
            POB = PST

        winv = Wd["w_in"].rearrange("(kc p) n -> p kc n", p=128)
        for kc in range(DC):
            K.dma(wins[:, kc, :], winv[:, kc, WOFF:WOFF + WN], W=[("win", kc)], eng="pool")
        woutv = Wd["w_out"].rearrange("(kc p) n -> p kc n", p=128)
        if mode == 2:
            for kc in range(DC):
                K.dma(wouts[:, kc, :], woutv[:, kc, :], W=[("wout", kc)], eng="pool")
        K.dma(bfm[:], Wd["b_fm"], W=["bfm"])
        K.dma(scw[:], Wd["sc_conv_wT"], W=["scw"])
        K.dma(mixg[:], Wd["mixgT"], W=["mixg"])
        K.ts(bfm8[:], bfm[:, G_DK:G_DK + 4], 0.125, None, ALU.mult, R=["bfm"], W=["bfm8"])

        xsv = x_src.rearrange("(dc p) t -> p dc t", p=128)
        xdv = x_dst.rearrange("(dc p) t -> p dc t", p=128)
        gs, shift, gate = mv["gs"], mv["mod"], mv["gate"]
        s = 1
        pjn = [0]

        def proj_fm(g, dst, dkeys, scale=1.0, bias=None, func=AF.Identity):
            c0, ncol = FM[g]
            pj = PJL[pjn[0] % len(PJL)]
            kk = PJK[pjn[0] % len(PJL)]
            pjn[0] += 1
            for kc in range(DC):
                K.mm(pj[0:ncol, 0:TT], wins[:, kc, c0 - WOFF:c0 - WOFF + ncol], hb[:, kc, :], start=(kc == 0), stop=(kc == DC - 1),
                     R=[("win", kc), ("hb", kc)], W=[kk])
            b = bias if bias is not None else bfm[0:ncol, g:g + 1]
            K.act(dst, pj[0:ncol, 0:TT], func, R=[kk, "bfm", "bfm8"], W=dkeys, bias=b, scale=scale)

        if "B" in enable:
            ub = [sb("ub%d" % ch, [128, 2 + TT]) for ch in range(2)]
            bbt = sb("bbt", [128, TT])
            cct = sb("cct", [128, TT])
            cvt = sb("cvt", [128, TT])
            ybt = [sb("ybt%d" % ch, [128, TT]) for ch in range(2)]
            for ch in range(2):
                K.memset(ub[ch][:, 0:2], 0.0, W=[("ub", ch)])
        if "D" in enable:
            QmT = sb("QmT", [64, 4, TT], BF16)
            KmT = sb("KmT", [64, 4, TT], BF16)
            Smb = sb("Smb", [64, 4, 65], BF16)
            K.memset(Smb[:], 0.0, W=["Smb"])
            Sm = sb("Sm", [64, 4, 65])
            K.memset(Sm[:], 0.0, W=["Sm"])
            btm_d = sb("btm_d", [64, 776])
            fbb = sb("fbb", [64, 4])
            gnd = sb("gnd", [64, 64])
            K.dma(btm_d[:], Wd["b_in"][OFF["d_k"]:OFF["d_k"] + 776].partition_broadcast(64), W=["btm_d"])
            K.dma(fbb[:], Wd["mlstm_f_bias"].partition_broadcast(64), W=["fbb"])
            K.dma(gnd[:], Wd["mlstm_norm_g"].partition_broadcast(64), W=["gnd"])
            K.tt(btm_d[:, 516:520], btm_d[:, 516:520], fbb[:], ALU.add, R=["btm_d", "fbb"], W=["btm_d"])
            DT = {}
            for nm_, *shp in (("ktok", [64, 4, 64]), ("vext", [64, 4, 65], BF16), ("ifo", [64, 264]), ("lf", [64, 4]),
                             ("gtot", [64, 4]), ("bsb", [64, 4]), ("ddd", [64, 4]), ("edec", [64, 4]), ("eb", [64, 4]),
                             ("slg", [64, 4, 64]), ("et", [64, 4, 64]), ("pt", [64, 4, 64], BF16), ("nd", [64, 4, 65]),
                             ("den", [64, 4]), ("hm", [64, 4, 64]), ("hsq", [64, 4, 64]), ("ss", [64, 4]),
                             ("so", [64, 4, 64]), ("kd", [64, 4, 64], BF16), ("stmp", [64, 4, 65])):
                DT[nm_] = [sb("d_%s%d" % (nm_, z), *shp) if False else sb("d_%s%d" % (nm_, z), shp[0], shp[1] if len(shp) > 1 else F32)
                           for z in range(2)]
            for z in range(2):
                K.memset(DT["vext"][z][:], 1.0, W=["d_vext%d" % z])

        def mlstm_chunk(c):
            cs = slice(c * 64, (c + 1) * 64)
            z = c % 2
            t_ = {k_: v_[z] for k_, v_ in DT.items()}
            kk_ = lambda n_: "d_%s%d" % (n_, z)
            ktok, vext, ifo, lf, gtot, bsb, ddd, edec, eb = (t_[x] for x in ("ktok", "vext", "ifo", "lf", "gtot", "bsb", "ddd", "edec", "eb"))
            slg, et, pt, nd, den, hm, hsq, ss, so, kd, stmp = (t_[x] for x in ("slg", "et", "pt", "nd", "den", "hm", "hsq", "ss", "so", "kd", "stmp"))
            DA, DB, DC = PD
            kA, kB, kC = ("pd", 0), ("pd", 1), ("pd", 2)
            for kc in range(DC_):
                K.mm(DC[0:64, 0:512], hb[:, kc, cs], wins[:, kc, OFF["d_k"] - WOFF:OFF["d_k"] - WOFF + 512],
                     start=(kc == 0), stop=(kc == DC_ - 1), R=[("win", kc), ("hb", kc)], W=[kC])
            for kc in range(DC_):
                K.mm(DB[0:64, 0:264], hb[:, kc, cs], wins[:, kc, OFF["d_i"] - WOFF:OFF["d_i"] - WOFF + 264],
                     start=(kc == 0), stop=(kc == DC_ - 1), R=[("win", kc), ("hb", kc)], W=[kB])
            K.tt(ktok[:].rearrange("p a b -> p (a b)"), DC[0:64, 0:256], btm_d[:, 0:256], ALU.add,
                 R=[kC, "btm_d"], W=[kk_("ktok")])
            K.tt(vext[:, :, 0:64], v4(DC[0:64, 256:512]), v4(btm_d[:, 256:512]), ALU.add, R=[kC, "btm_d"], W=[kk_("vext")])
            K.tt(ifo[:], DB[0:64, 0:264], btm_d[:, 512:776], ALU.add, R=[kB, "btm_d"], W=[kk_("ifo")])
            K.act(lf[:], ifo[:, 4:8], AF.Exp, R=[kk_("ifo")], W=[kk_("lf")], scale=-1.0)
            K.act(lf[:], lf[:], AF.Ln, R=[kk_("lf")], W=[kk_("lf")], bias=consts["one"][0:64, :], scale=1.0)
            K.ts(lf[:], lf[:], -1.0, None, ALU.mult, R=[kk_("lf")], W=[kk_("lf")])
            K.mm(DB[0:64, 264:268], TRIU, lf[:], R=[kk_("lf"), "cmask"], W=[kB])
            K.mm(DB[0:64, 268:272], onesf[0:64, 0:64], lf[:], R=[kk_("lf"), "ones_f"], W=[kB])
            K.act(gtot[:], DB[0:64, 268:272], AF.Exp, R=[kB], W=[kk_("gtot")])
            K.copy(bsb[:], DB[0:64, 264:268], R=[kB], W=[kk_("bsb")])
            K.tt(ddd[:], DB[0:64, 268:272], bsb[:], ALU.subtract, R=[kB, kk_("bsb")], W=[kk_("ddd")])
            K.tt(ddd[:], ddd[:], ifo[:, 0:4], ALU.add, R=[kk_("ddd"), kk_("ifo")], W=[kk_("ddd")])
            K.act(edec[:], ddd[:], AF.Exp, R=[kk_("ddd")], W=[kk_("edec")])
            K.act(eb[:], bsb[:], AF.Exp, R=[kk_("bsb")], W=[kk_("eb")])
            K.tt(slg[:], bc(TRIU.unsqueeze(1), [64, 4, 64]), bc(lf[:, :].unsqueeze(2), [64, 4, 64]), ALU.mult,
                 R=[kk_("lf"), "cmask"], W=[kk_("slg")])
            K.mm(DC[0:64, 0:256], SLM, slg[:].rearrange("p a b -> p (a b)"), R=[kk_("slg"), "cmask"], W=[kC])
            for h in range(4):
                K.mm(DC[0:64, 256 + h * 64:256 + (h + 1) * 64], KmT[:, h, cs], QmT[:, h, cs], R=["QmT", "KmT"], W=[kC])
            K.tt(et[:], v4(DC[0:64, 0:256]), bc(ifo[:, 0:4].unsqueeze(2), [64, 4, 64]), ALU.add, R=[kC, kk_("ifo")],
                 W=[kk_("et")])
            K.act(et[:], et[:], AF.Exp, R=[kk_("et")], W=[kk_("et")])
            K.tt(et[:], et[:], bc(TRIU.unsqueeze(1), [64, 4, 64]), ALU.mult, R=[kk_("et"), "cmask"], W=[kk_("et")])
            K.tt(pt[:], et[:], v4(DC[0:64, 256:512]), ALU.mult, R=[kk_("et"), kC], W=[kk_("pt")])
            for h in range(4):
                K.mm(DA[0:64, h * 65:(h + 1) * 65], QmT[:, h, cs], Smb[:, h, :], R=["QmT", "Smb"], W=[kA])
            qs = DA[0:64, 0:260].rearrange("p (a b) -> p a b", a=4)
            K.tt(nd[:], qs, bc(eb[:, :].unsqueeze(2), [64, 4, 65]), ALU.mult, R=[kA, kk_("eb")], W=[kk_("nd")])
            for h in range(4):
                K.mm(DA[0:64, h * 65:(h + 1) * 65], pt[:, h, :], vext[:, h, :], R=[kk_("pt"), kk_("vext")], W=[kA])
            K.tt(nd[:], nd[:], DA[0:64, 0:260].rearrange("p (a b) -> p a b", a=4), ALU.add, R=[kk_("nd"), kA], W=[kk_("nd")])
            K.stt(kd[:], ktok[:], 0.125, bc(edec[:, :].unsqueeze(2), [64, 4, 64]), ALU.mult, ALU.mult,
                  R=[kk_("ktok"), kk_("edec")], W=[kk_("kd")])
            for h in range(4):
                K.mm(DA[0:64, h * 65:(h + 1) * 65], kd[:, h, :], vext[:, h, :], R=[kk_("kd"), kk_("vext")], W=[kA])
            K.tt(stmp[:], Sm[:], bc(gtot[:, :].unsqueeze(2), [64, 4, 65]), ALU.mult, R=["Sm", kk_("gtot")], W=[kk_("stmp")])
            K.tt(Sm[:], stmp[:], DA[0:64, 0:260].rearrange("p (a b) -> p a b", a=4), ALU.add, R=[kk_("stmp"), kA], W=["Sm"])
            K.copy(Smb[:], Sm[:], R=["Sm"], W=["Smb"], eng="act")
            K.act(den[:], nd[:, :, 64], AF.Abs, R=[kk_("nd")], W=[kk_("den")])
            K.ts(den[:], den[:], 1.0, None, ALU.max, R=[kk_("den")], W=[kk_("den")])
            K.recip(den[:], den[:], R=[kk_("den")], W=[kk_("den")])
            K.tt(hm[:], nd[:, :, 0:64], bc(den[:, :].unsqueeze(2), [64, 4, 64]), ALU.mult, R=[kk_("nd"), kk_("den")], W=[kk_("hm")])
            K.tt(hsq[:], hm[:], hm[:], ALU.mult, R=[kk_("hm")], W=[kk_("hsq")])
            K.P.add("dve", lambda e: e.tensor_reduce(ss[:], hsq[:], AX.X, ALU.add), R=[kk_("hsq")], W=[kk_("ss")], cost=400.0)
            K.act(ss[:], ss[:], AF.Sqrt, R=[kk_("ss"), "epsc"], W=[kk_("ss")], bias=consts["eps"][0:64, :], scale=1.0 / 64)
            K.recip(ss[:], ss[:], R=[kk_("ss")], W=[kk_("ss")])
            K.act(so[:].rearrange("p a b -> p (a b)"), ifo[:, 8:264], AF.Sigmoid, R=[kk_("ifo")], W=[kk_("so")])
            K.tt(hm[:], hm[:], bc(ss[:, :].unsqueeze(2), [64, 4, 64]), ALU.mult, R=[kk_("hm"), kk_("ss")], W=[kk_("hm")])
            K.tt(hm[:], hm[:], bc(gnd[:, :].unsqueeze(1), [64, 4, 64]), ALU.mult, R=[kk_("hm"), "gnd"], W=[kk_("hm")])
            K.tt(hm[:], hm[:], so[:], ALU.mult, R=[kk_("hm"), kk_("so")], W=[kk_("hm")])
            hmf = hm[:].rearrange("p a b -> p (a b)")
            for pr in range(2):
                K.tr(DA[:, 272 + pr * 64:272 + (pr + 1) * 64], hmf[:, pr * 128:(pr + 1) * 128], IDENT[0:64, 0:64],
                     R=[kk_("hm"), "cmask"], W=[kA])
                K.copy(yt[:, 6 + pr, cs], DA[:, 272 + pr * 64:272 + (pr + 1) * 64], R=[kA], W=[("yt", 6 + pr)], eng="act")

        GDT = F32
        if "C" in enable:
            cin = sb("c_cin", [64, 12, 3 + TT])
            qkv = sb("c_qkv", [64, 12, TT])
            qkb = sb("c_qkb", [64, 12, TT], GDT)
            Sgb = sb("c_Sb", [64, 4, 64], GDT)
            K.memset(Sgb[:], 0.0, W=["c_Sb"])
            identb1 = sb("c_identb", [64, 64], GDT)
            K.copy(identb1[:], IDENT[0:64, 0:64], R=["cmask"], W=["c_identb"])
            gcw = sb("c_gcw", [64, 12, 4])
            Sg = sb("c_S", [64, 4, 64])
            K.memset(Sg[:], 0.0, W=["c_S"])
            K.memset(cin[:, :, 0:3], 0.0, W=[("c_cin", g) for g in range(12)])
            K.dma(gcw[:], Wd["gdn_conv_wT"], W=["c_gcw"])
            btm_c = sb("c_btm", [64, 264])
            dtb = sb("c_dtb", [64, 4])
            nea = sb("c_nea", [64, 4])
            gng = sb("c_gng", [64, 64])
            K.dma(btm_c[:], Wd["b_in"][OFF["c_beta"]:OFF["c_beta"] + 264].partition_broadcast(64), W=["c_btm"])
            K.dma(dtb[:], Wd["gdn_dt_bias"].partition_broadcast(64), W=["c_dtb"])
            K.dma(nea[:], Wd["gdn_A_log"].partition_broadcast(64), W=["c_nea"])
            K.dma(gng[:], Wd["gdn_norm_g"].partition_broadcast(64), W=["c_gng"])
            K.tt(btm_c[:, 4:8], btm_c[:, 4:8], dtb[:], ALU.add, R=["c_btm", "c_dtb"], W=["c_btm"])
            K.act(nea[:], nea[:], AF.Exp, R=["c_nea"], W=["c_nea"])
            K.ts(nea[:], nea[:], -1.0, None, ALU.mult, R=["c_nea"], W=["c_nea"])
            csqL = [sb("c_sq%d" % i, [64, TT]) for i in range(2)]
            crsL = [sb("c_rs%d" % i, [64, TT]) for i in range(2)]
            CT = {}
            for nm_, *shp in (("baz", [64, 264]), ("beta", [64, 4]), ("gg", [64, 4]), ("gcs", [64, 4]), ("egc", [64, 4]),
                             ("bgc", [64, 4]), ("edl", [64, 4]), ("gto", [64, 4]), ("ktk", [64, 4, 64]), ("vb", [64, 4, 64], GDT),
                             ("kbe", [64, 4, 64], GDT), ("kdc", [64, 4, 64], GDT), ("ug", [64, 4, 64]), ("slgc", [64, 4, 64]),
                             ("dgb", [64, 4, 64]), ("seg", [64, 4, 64]), ("segT", [64, 4, 64]), ("t1", [64, 4, 64]),
                             ("t2", [64, 4, 64]), ("NN", [64, 2, 4, 64], GDT), ("NN2", [64, 2, 4, 64], GDT), ("XX", [64, 4, 64]), ("XB", [64, 4, 64], GDT),
                             ("ptc", [64, 4, 64], GDT), ("uu", [64, 4, 64]), ("wt", [64, 4, 64], GDT), ("vn", [64, 4, 64]), ("vnb", [64, 4, 64], GDT),
                             ("oo", [64, 4, 64]), ("osq", [64, 4, 64]), ("oss", [64, 4]), ("sz", [64, 4, 64]),
                             ("stg", [64, 4, 64])):
                CT[nm_] = [sb("c_%s%d" % (nm_, z), shp[0], shp[1] if len(shp) > 1 else F32) for z in range(2)]

        def v4(ap):
            return ap.rearrange("p (a b) -> p a b", a=4)

        def gdn_tile():
            for g in range(12):
                proj_fm(G_CQKV + g, cin[:, g, 3:3 + TT], [("c_cin", g)])
                K.ts(qkv[:, g, :], cin[:, g, 3:3 + TT], gcw[:, g, 3:4], None, ALU.mult, R=[("c_cin", g), "c_gcw"],
                     W=[("c_qkv", g)])
                for k in range(3):
                    K.stt(qkv[:, g, :], cin[:, g, k:k + TT], gcw[:, g, k:k + 1], qkv[:, g, :], ALU.mult, ALU.add,
                          R=[("c_cin", g), "c_gcw", ("c_qkv", g)], W=[("c_qkv", g)])
                K.copy(cin[:, g, 0:3], cin[:, g, TT:TT + 3], R=[("c_cin", g)], W=[("c_cin", g)])
                K.act(qkv[:, g, :], qkv[:, g, :], AF.Silu, R=[("c_qkv", g)], W=[("c_qkv", g)])
                if g < 8:
                    csq, crs = csqL[g % 2], crsL[g % 2]
                    ksq, krs = "c_sq%d" % (g % 2), "c_rs%d" % (g % 2)
                    pstb = PJL[pjn[0] % len(PJL)]
                    pstk = PJK[pjn[0] % len(PJL)]
                    pjn[0] += 1
                    K.tt(csq[:], qkv[:, g, :], qkv[:, g, :], ALU.mult, R=[("c_qkv", g)], W=[ksq])
                    K.mm(pstb[0:64, 0:TT], onesf[0:64, 0:64], csq[:], R=[ksq, "ones_f"], W=[pstk])
                    K.act(crs[:], pstb[0:64, 0:TT], AF.Sqrt, R=[pstk, "epsc"], W=[krs], bias=consts["eps"][0:64, :],
                          scale=1.0)
                    K.recip(crs[:], crs[:], R=[krs], W=[krs])
                    K.stt(qkb[:, g, :], qkv[:, g, :], 0.125 if g < 4 else 1.0, crs[:], ALU.mult, ALU.mult,
                          R=[("c_qkv", g), krs], W=[("c_qkb", g)])
                else:
                    K.copy(qkb[:, g, :], qkv[:, g, :], R=[("c_qkv", g)], W=[("c_qkb", g)])

        def gdn_chunk(c):
            cs = slice(c * 64, (c + 1) * 64)
            z = c % 2
            t_ = {k_: v_[z] for k_, v_ in CT.items()}
            kk_ = lambda n_: "c_%s%d" % (n_, z)
            baz, beta, gg, gcs, egc, bgc, edl, gto = (t_[x] for x in ("baz", "beta", "gg", "gcs", "egc", "bgc", "edl", "gto"))
            ktk, vb, kbe, kdc, ug, slgc, dgb, seg, segT = (t_[x] for x in ("ktk", "vb", "kbe", "kdc", "ug", "slgc", "dgb", "seg", "segT"))
            t1, t2, NN, NN2, XX, ptc, uu, wt, vn, oo, osq, oss, sz, stg = (t_[x] for x in ("t1", "t2", "NN", "NN2", "XX", "ptc", "uu", "wt", "vn", "oo", "osq", "oss", "sz", "stg"))
            XB, vnb = t_["XB"], t_["vnb"]
            CA, CB, CC, CD = PC
            kA, kB, kC, kD = ("pc", 0), ("pc", 1), ("pc", 2), ("pc", 3)
            QK = [("c_qkb", g) for g in range(12)]
            for kc in range(DC_):
                K.mm(CA[0:64, 0:264], hb[:, kc, cs], wins[:, kc, OFF["c_beta"] - WOFF:OFF["c_beta"] - WOFF + 264],
                     start=(kc == 0), stop=(kc == DC_ - 1), R=[("win", kc), ("hb", kc)], W=[kA])
            K.tt(baz[:], CA[0:64, 0:264], btm_c[:], ALU.add, R=[kA, "c_btm"], W=[kk_("baz")])
            K.act(beta[:], baz[:, 0:4], AF.Sigmoid, R=[kk_("baz")], W=[kk_("beta")])
            K.act(gg[:], baz[:, 4:8], AF.Exp, R=[kk_("baz")], W=[kk_("gg")])
            K.act(gg[:], gg[:], AF.Ln, R=[kk_("gg")], W=[kk_("gg")], bias=consts["one"][0:64, :], scale=1.0)
            K.tt(gg[:], gg[:], nea[:], ALU.mult, R=[kk_("gg"), "c_nea"], W=[kk_("gg")])
            K.mm(CA[0:64, 264:268], TRIU, gg[:], R=[kk_("gg"), "cmask"], W=[kA])
            K.mm(CA[0:64, 268:272], onesf[0:64, 0:64], gg[:], R=[kk_("gg"), "ones_f"], W=[kA])
            K.copy(gcs[:], CA[0:64, 264:268], R=[kA], W=[kk_("gcs")])
            K.act(egc[:], CA[0:64, 264:268], AF.Exp, R=[kA], W=[kk_("egc")])
            K.act(gto[:], CA[0:64, 268:272], AF.Exp, R=[kA], W=[kk_("gto")])
            K.tt(edl[:], CA[0:64, 268:272], gcs[:], ALU.subtract, R=[kA, kk_("gcs")], W=[kk_("edl")])
            K.act(edl[:], edl[:], AF.Exp, R=[kk_("edl")], W=[kk_("edl")])
            K.tt(bgc[:], beta[:], egc[:], ALU.mult, R=[kk_("beta"), kk_("egc")], W=[kk_("bgc")])
            for h in range(4):
                K.mm(CB[0:64, h * 64:(h + 1) * 64], qkb[:, 4 + h, cs], identb1[:], R=QK + ["c_identb"], W=[kB])
                K.mm(CB[0:64, 256 + h * 64:256 + (h + 1) * 64], qkb[:, 8 + h, cs], identb1[:], R=QK + ["c_identb"], W=[kB])
            K.copy(ktk[:], v4(CB[0:64, 0:256]), R=[kB], W=[kk_("ktk")], eng="act")
            K.tt(vb[:], v4(CB[0:64, 256:512]), bc(beta[:, :].unsqueeze(2), [64, 4, 64]), ALU.mult, R=[kB, kk_("beta")], W=[kk_("vb")])
            K.tt(kbe[:], ktk[:], bc(bgc[:, :].unsqueeze(2), [64, 4, 64]), ALU.mult, R=[kk_("ktk"), kk_("bgc")], W=[kk_("kbe")])
            K.tt(kdc[:], ktk[:], bc(edl[:, :].unsqueeze(2), [64, 4, 64]), ALU.mult, R=[kk_("ktk"), kk_("edl")], W=[kk_("kdc")])
            K.tt(ug[:], bc(TRIU.unsqueeze(1), [64, 4, 64]), bc(gg[:, :].unsqueeze(2), [64, 4, 64]), ALU.mult,
                 R=[kk_("gg"), "cmask"], W=[kk_("ug")])
            K.tt(slgc[:], bc(SLM.unsqueeze(1), [64, 4, 64]), bc(gg[:, :].unsqueeze(2), [64, 4, 64]), ALU.mult,
                 R=[kk_("gg"), "cmask"], W=[kk_("slgc")])
            K.mm(CC[0:64, 0:256], TRIU, slgc[:].rearrange("p a b -> p (a b)"), R=[kk_("slgc"), "cmask"], W=[kC])
            K.mm(CC[0:64, 256:512], SLM, ug[:].rearrange("p a b -> p (a b)"), R=[kk_("ug"), "cmask"], W=[kC])
            K.tt(dgb[:], bc(IDENT[0:64, 0:64].unsqueeze(1), [64, 4, 64]), bc(beta[:, :].unsqueeze(2), [64, 4, 64]), ALU.mult,
                 R=[kk_("beta"), "cmask"], W=[kk_("dgb")])
            K.act(seg[:], v4(CC[0:64, 0:256]), AF.Exp, R=[kC], W=[kk_("seg")])
            K.act(segT[:], v4(CC[0:64, 256:512]), AF.Exp, R=[kC], W=[kk_("segT")])
            for h in range(4):
                K.mm(CB[0:64, h * 64:(h + 1) * 64], qkb[:, 4 + h, cs], qkb[:, 4 + h, cs], R=QK, W=[kB])
                K.mm(CB[0:64, 256 + h * 64:256 + (h + 1) * 64], qkb[:, 4 + h, cs], qkb[:, h, cs], R=QK, W=[kB])
            K.mm(CC[0:64, 0:256], onesf[0:64, 0:64], dgb[:].rearrange("p a b -> p (a b)"), R=[kk_("dgb"), "ones_f"], W=[kC])
            K.tt(t1[:], seg[:], bc(SLM.unsqueeze(1), [64, 4, 64]), ALU.mult, R=[kk_("seg"), "cmask"], W=[kk_("t1")])
            K.tt(t1[:], t1[:], v4(CB[0:64, 0:256]), ALU.mult, R=[kk_("t1"), kB], W=[kk_("t1")])
            K.tt(NN[:, 0], t1[:], bc(beta[:, :].unsqueeze(2), [64, 4, 64]), ALU.mult, R=[kk_("t1"), kk_("beta")], W=[kk_("NN")])
            K.tt(t2[:], segT[:], bc(SUM.unsqueeze(1), [64, 4, 64]), ALU.mult, R=[kk_("segT"), "cmask"], W=[kk_("t2")])
            K.tt(t2[:], t2[:], v4(CB[0:64, 0:256]), ALU.mult, R=[kk_("t2"), kB], W=[kk_("t2")])
            K.tt(NN[:, 1], t2[:], v4(CC[0:64, 0:256]), ALU.mult, R=[kk_("t2"), kC], W=[kk_("NN")])
            K.tt(ptc[:], segT[:], bc(TRIU.unsqueeze(1), [64, 4, 64]), ALU.mult, R=[kk_("segT"), "cmask"], W=[kk_("ptc")])
            K.tt(ptc[:], ptc[:], v4(CB[0:64, 256:512]), ALU.mult, R=[kk_("ptc"), kB], W=[kk_("ptc")])
            K.tt(XX[:], bc(IDENT[0:64, 0:64].unsqueeze(1), [64, 4, 64]), NN[:, 1], ALU.subtract, R=[kk_("NN"), "cmask"],
                 W=[kk_("XX")])
            K.copy(XB[:], XX[:], R=[kk_("XX")], W=[kk_("XB")], eng="act")
            cur, nxt, ck, nk = NN, NN2, kk_("NN"), kk_("NN2")
            for lvl in range(5):
                last = (lvl == 4)
                for h in range(4):
                    K.mm(CC[0:64, h * 64:(h + 1) * 64], cur[:, 1, h, :], cur[:, 0, h, :], R=[ck], W=[kC])
                    if not last:
                        K.mm(CC[0:64, 256 + h * 64:256 + (h + 1) * 64], cur[:, 0, h, :], cur[:, 1, h, :], R=[ck], W=[kC])
                if last:
                    K.copy(nxt[:, 0], v4(CC[0:64, 0:256]), R=[kC], W=[nk], eng="act")
                else:
                    K.copy(nxt[:].rearrange("p t a b -> p (t a b)"), CC[0:64, 0:512], R=[kC], W=[nk], eng="act")
                for h in range(4):
                    K.mm(CB[0:64, h * 64:(h + 1) * 64], nxt[:, 0, h, :], XB[:, h, :], R=[nk, kk_("XB")], W=[kB])
                K.tt(XX[:], XX[:], v4(CB[0:64, 0:256]), ALU.add, R=[kk_("XX"), kB], W=[kk_("XX")])
                K.copy(XB[:], XX[:], R=[kk_("XX")], W=[kk_("XB")], eng="act")
                cur, nxt, ck, nk = nxt, cur, nk, ck
            for h in range(4):
                K.mm(CC[0:64, h * 64:(h + 1) * 64], XB[:, h, :], vb[:, h, :], R=[kk_("XB"), kk_("vb")], W=[kC])
                K.mm(CC[0:64, 256 + h * 64:256 + (h + 1) * 64], kbe[:, h, :], XB[:, h, :], R=[kk_("XB"), kk_("kbe")], W=[kC])
            K.copy(uu[:], v4(CC[0:64, 0:256]), R=[kC], W=[kk_("uu")], eng="act")
            K.copy(wt[:], v4(CC[0:64, 256:512]), R=[kC], W=[kk_("wt")])
            for h in range(4):
                K.mm(CD[0:64, h * 64:(h + 1) * 64], wt[:, h, :], Sgb[:, h, :], R=[kk_("wt"), "c_Sb"], W=[kD])
                K.mm(CD[0:64, 256 + h * 64:256 + (h + 1) * 64], qkb[:, h, cs], Sgb[:, h, :], R=QK + ["c_Sb"], W=[kD])
            K.tt(vnb[:], uu[:], v4(CD[0:64, 0:256]), ALU.subtract, R=[kk_("uu"), kD], W=[kk_("vnb")])
            K.tt(oo[:], v4(CD[0:64, 256:512]), bc(egc[:, :].unsqueeze(2), [64, 4, 64]), ALU.mult, R=[kD, kk_("egc")], W=[kk_("oo")])
            for h in range(4):
                K.mm(CD[0:64, h * 64:(h + 1) * 64], kdc[:, h, :], vnb[:, h, :], R=[kk_("kdc"), kk_("vnb")], W=[kD])
            K.tt(stg[:], Sg[:], bc(gto[:, :].unsqueeze(2), [64, 4, 64]), ALU.mult, R=["c_S", kk_("gto")], W=[kk_("stg")])
            K.tt(Sg[:], stg[:], v4(CD[0:64, 0:256]), ALU.add, R=[kk_("stg"), kD], W=["c_S"])
            K.copy(Sgb[:], Sg[:], R=["c_S"], W=["c_Sb"], eng="act")
            for h in range(4):
                K.mm(CD[0:64, 256 + h * 64:256 + (h + 1) * 64], ptc[:, h, :], vnb[:, h, :], R=[kk_("ptc"), kk_("vnb")], W=[kD])
            K.tt(oo[:], oo[:], v4(CD[0:64, 256:512]), ALU.add, R=[kk_("oo"), kD], W=[kk_("oo")])
            K.tt(osq[:], oo[:], oo[:], ALU.mult, R=[kk_("oo")], W=[kk_("osq")])
            K.P.add("dve", lambda e: e.tensor_reduce(oss[:], osq[:], AX.X, ALU.add), R=[kk_("osq")], W=[kk_("oss")], cost=400.0)
            K.act(oss[:], oss[:], AF.Sqrt, R=[kk_("oss"), "epsc"], W=[kk_("oss")], bias=consts["eps"][0:64, :], scale=1.0 / 64)
            K.recip(oss[:], oss[:], R=[kk_("oss")], W=[kk_("oss")])
            K.act(sz[:].rearrange("p a b -> p (a b)"), baz[:, 8:264], AF.Silu, R=[kk_("baz")], W=[kk_("sz")])
            K.tt(oo[:], oo[:], bc(oss[:, :].unsqueeze(2), [64, 4, 64]), ALU.mult, R=[kk_("oo"), kk_("oss")], W=[kk_("oo")])
            K.tt(oo[:], oo[:], bc(gng[:, :].unsqueeze(1), [64, 4, 64]), ALU.mult, R=[kk_("oo"), "c_gng"], W=[kk_("oo")])
            K.tt(oo[:], oo[:], sz[:], ALU.mult, R=[kk_("oo"), kk_("sz")], W=[kk_("oo")])
            oof = oo[:].rearrange("p a b -> p (a b)")
            for pr in range(2):
                K.tr(CD[:, pr * 64:(pr + 1) * 64], oof[:, pr * 128:(pr + 1) * 128], IDENT[0:64, 0:64],
                     R=[kk_("oo"), "cmask"], W=[kD])
                K.copy(yt[:, 4 + pr, cs], CD[:, pr * 64:(pr + 1) * 64], R=[kD], W=[("yt", 4 + pr)], eng="act")

        if "A" in enable:
            NB = T // 64
            NCB = T // 16
            NM = (NCB + 127) // 128
            NQ = T // 128
            w1k = sb("a_w1k", [64, 32, 256], BF16)
            w1v = sb("a_w1v", [64, 32, 256], BF16)
            w2k = sb("a_w2k", [128, 2, 64], BF16)
            w2v = sb("a_w2v", [128, 2, 64], BF16)
            K.dma(w1k[:], Wd["cmp_k_w1"].rearrange("(s d) n -> d s n", d=64), W=["a_w1k"], eng="pool")
            K.dma(w1v[:], Wd["cmp_v_w1"].rearrange("(s d) n -> d s n", d=64), W=["a_w1v"], eng="pool")
            K.dma(w2k[:], Wd["cmp_k_w2"].rearrange("(c p) n -> p c n", p=128), W=["a_w2k"], eng="pool")
            K.dma(w2v[:], Wd["cmp_v_w2"].rearrange("(c p) n -> p c n", p=128), W=["a_w2v"], eng="pool")
            posT = sb("a_posT", [64, 32], BF16)
            K.dma(posT[:], Wd["cmp_posT"], W=["a_posT"], eng="pool")
            hbias = sb("a_hbias", [128, 4])
            expc = sb("a_expc", [64, T], BF16)
            K.dma(expc[0:NB, :], Wd["expc"], W=["a_expc"], eng="pool")
            keepc = sb("a_keepc", [128, 2, 2 * NB])
            K.dma(keepc[:], Wd["keepadd"], W=["a_keepc"])
            identb = sb("a_identb", [128, 128], BF16)
            K.copy(identb[:], IDENT, R=["cmask"], W=["a_identb"])
            biasT = sb("a_bias", [128, 19, 512])
            for q in range(19):
                K.dma(biasT[:, q, :], Wd["bias_scr"][q], W=[("a_bias", q)])
            bw4 = sb("a_bw4", [128, 128])
            K.dma(bw4[:], Wd["bw4"], W=["a_bw4"])
            qng = sb("a_qng", [64, 2])
            K.dma(qng[:], Wd["qkng"], W=["a_qng"])
            K.ts(qng[:, 0:1], qng[:, 0:1], 0.125, None, ALU.mult, R=["a_qng"], W=["a_qng"])
            tabrow = sb("a_tabrow", [65, 4])
            K.dma(tabrow[64:65, :], Wd["t5_table"][31:32, :], W=["a_tabrow"])
            mg0 = sb("a_mg0", [128, 256])
            K.dma(mg0[:], Wd["mix_norm_g0"].partition_broadcast(128), W=["a_mg0"])
            btm_a = sb("a_btm", [128, 204])
            K.dma(btm_a[:], Wd["b_in"][OFF["a_v_slc"]:OFF["a_v_slc"] + 204].partition_broadcast(128), W=["a_btm"])
            ovl = sb("a_ovl", [128, NM, 64])
            K.dma(ovl[:], Wd["ovl"], W=["a_ovl"])
            kcmpT = sb("a_kcmpT", [64, T], BF16)
            vcmpT = sb("a_vcmpT", [64, T], BF16)
            KsT = sb("a_KsT", [128, T], BF16)
            KwT = sb("a_KwT", [128, T], BF16)
            kcT = sb("a_kcT", [128, NM * 128])
            vcT = sb("a_vcT", [64, NM * 128])
            vcx = sb("a_vcx", [128, NM, 65])
            Vs = sb("a_Vs", [128, NQ, 65], BF16)
            Vw = sb("a_Vw", [128, NQ, 65], BF16)
            NQT = TT // 128
            QaT = sb("a_QaT", [128, NQT, 4, 128], BF16)
            QaF = sb("a_QaF", [128, NQT, 4, 128])
            gsb = sb("a_gsb", [128, TT // 128, 12])
            K.memset(KsT[64:128, :], 0.0, W=["a_KsT"])
            K.memset(KwT[64:128, :], 0.0, W=["a_KwT"])
            K.memset(KsT[64:65, :], 1.0, W=["a_KsT"])
            K.memset(KwT[64:65, :], 1.0, W=["a_KwT"])
            K.memset(kcT[:, :], 0.0, W=["a_kcT"])
            K.memset(kcT[64:65, :], 1.0, W=["a_kcT"])
            K.memset(QaT[64:128], 0.0, W=["a_QaT"])
            K.memset(QaF[64:128], 0.0, W=["a_QaF"])
            K.memset(vcT[:], 0.0, W=["a_vcT"])
            K.memset(vcx[:], 1.0, W=["a_vcx"])
            K.memset(Vs[:], 1.0, W=["a_Vs"])
            K.memset(Vw[:], 1.0, W=["a_Vw"])
            for q_ in range(NQT):
                K.copy(QaT[64:65, q_], bc(tabrow[64:65, :].unsqueeze(2), [1, 4, 128]), R=["a_tabrow"], W=["a_QaT"])
                K.copy(QaF[64:65, q_], bc(tabrow[64:65, :].unsqueeze(2), [1, 4, 128]), R=["a_tabrow"], W=["a_QaF"])
            for kv, w1 in enumerate((w1k, w1v)):
                for hc in range(2):
                    for s_ in range(32):
                        K.mm(PW[0][:, (kv * 2 + hc) * 2:(kv * 2 + hc) * 2 + 1], w1[:, s_, hc * 128:(hc + 1) * 128],
                             posT[:, s_:s_ + 1], start=(s_ == 0), stop=(s_ == 31), R=["a_w1k", "a_w1v", "a_posT"],
                             W=[("pw", 0)])
            K.copy(hbias[:], PW[0][:, 0:8].rearrange("p (a b) -> p a b", b=2)[:, :, 0], R=[("pw", 0)], W=["a_hbias"])
            a_rawL = [sb("a_raw%d" % i, [64, TT]) for i in range(2)]
            a_sqL = [sb("a_sq%d" % i, [64, TT]) for i in range(2)]
            a_rsL = [sb("a_rs%d" % i, [64, TT]) for i in range(2)]
            nrm_n = [0]
            hact = sb("a_hact", [128, 2, 2, 32], BF16)
            cst = sb("a_cst", [64, 32])
            csq2 = sb("a_csq2", [64, 32])
            crs2 = sb("a_crs2", [64, 32])
            vtm = sb("a_vtm", [128, 204])
            Eb = [sb("a_E%d" % i, [128, 4, 128]) for i in range(2)]
            Tb = [sb("a_T%d" % i, [128, 4, 128]) for i in range(2)]
            Pb = [sb("a_P%d" % i, [128, 4, 128], BF16) for i in range(2)]
            Pc = [sb("a_Pc%d" % i, [128, 4, 128]) for i in range(2)]
            scr = sb("a_scr", [128, 64])
            sc2 = sb("a_sc2", [128, 64])
            v8 = sb("a_v8", [128, 8])
            mskb = sb("a_mskb", [128, 64], BF16)
            mT = sb("a_mT", [64, 128], BF16)
            rden = sb("a_rden", [128, 4])
            coef = sb("a_coef", [128, 4])
            ya = sb("a_ya", [128, 4, 64])
            ytmp = sb("a_ytmp", [128, 4, 64])
            rdenw = sb("a_rdenw", [128, 4])
            coefw = sb("a_coefw", [128, 4])
            yaw = sb("a_yaw", [128, 4, 64])
            yss = sb("a_yss", [128, 1])

        def nsa_tile(tt):
            t0 = tt * TT
            tsl = slice(t0, t0 + TT)
            c0 = OFF["a_k_cmp"]
            for nm_, col, dst in (("k", OFF["a_k_cmp"], kcmpT), ("v", OFF["a_v_cmp"], vcmpT)):
                pj = PJ[pjn[0] % len(PJ)]
                kk = ("pj", pjn[0] % len(PJ))
                pjn[0] += 1
                for kc in range(DC):
                    K.mm(pj[0:64, 0:TT], wins[:, kc, col:col + 64], hb[:, kc, :], start=(kc == 0), stop=(kc == DC - 1),
                         R=[("win", kc), ("hb", kc)], W=[kk])
                g_ = G_KVC
                bcol = bfm[0:64, G_KVC:G_KVC + 1] if nm_ == "k" else bfmv[:, 0:1]
                K.act(dst[:, tsl], pj[0:64, 0:TT], AF.Identity, R=[kk, "bfm", "a_bfmv"], W=["a_" + nm_ + "cmpT"], bias=bcol,
                      scale=1.0)

            def normed(g, dst, dkey, gcol, split=False):
                z_ = nrm_n[0] % 2
                nrm_n[0] += 1
                a_raw, a_sq, a_rs = a_rawL[z_], a_sqL[z_], a_rsL[z_]
                kraw, ksq_, krs_ = "a_raw%d" % z_, "a_sq%d" % z_, "a_rs%d" % z_
                proj_fm(g, a_raw[:], [kraw])
                K.tt(a_sq[:], a_raw[:], a_raw[:], ALU.mult, R=[kraw], W=[ksq_])
                K.mm(PST[0:64, 0:TT], onesf[0:64, 0:64], a_sq[:], R=[ksq_, "ones_f"], W=[PSTK])
                K.act(a_rs[:], PST[0:64, 0:TT], AF.Sqrt, R=[PSTK, "epsc"], W=[krs_], bias=consts["eps"][0:64, :],
                      scale=1.0 / 64)
                K.recip(a_rs[:], a_rs[:], R=[krs_], W=[krs_])
                for d_, dk_ in zip(dst, dkey):
                    if split:
                        K.stt(d_, a_raw[:].rearrange("p (a b) -> p a b", b=128), gcol,
                              a_rs[:].rearrange("p (a b) -> p a b", b=128), ALU.mult, ALU.mult,
                              R=[kraw, krs_, "a_qng"], W=[dk_])
                    else:
                        K.stt(d_, a_raw[:], gcol, a_rs[:], ALU.mult, ALU.mult, R=[kraw, krs_, "a_qng"], W=[dk_])

            normed(G_KSLC, [KsT[0:64, tsl]], ["a_KsT"], qng[:, 1:2])
            normed(G_KWIN, [KwT[0:64, tsl]], ["a_KwT"], qng[:, 1:2])
            for h in range(4):
                normed(G_AQ + h, [QaT[0:64, :, h, :], QaF[0:64, :, h, :]], ["a_QaT", "a_QaF"], qng[:, 0:1], split=True)
            for q in range(TT // 128):
                qg = t0 // 128 + q
                for kc in range(DC):
                    K.mm(PW[0][:, 300:504], hb[:, kc, q * 128:(q + 1) * 128], wins[:, kc, OFF["a_v_slc"]:OFF["a_v_slc"] + 204],
                         start=(kc == 0), stop=(kc == DC - 1), R=[("win", kc), ("hb", kc)], W=[("pw", 0)])
                K.tt(vtm[:], PW[0][:, 300:504], btm_a[:], ALU.add, R=[("pw", 0), "a_btm"], W=["a_vtm"])
                K.copy(Vs[:, qg, 0:64], vtm[:, 0:64], R=["a_vtm"], W=["a_Vs"])
                K.copy(Vw[:, qg, 0:64], vtm[:, 128:192], R=["a_vtm"], W=["a_Vw"])
                K.act(gsb[:, q, :], vtm[:, 192:204], AF.Sigmoid, R=["a_vtm"], W=["a_gsb"])
            nb0 = 0 if t0 == 0 else t0 // 16 - 1
            nb1 = (t0 + TT - 32) // 16 + 1
            nn = nb1 - nb0
            for kv, (w1, src, w2) in enumerate(((w1k, kcmpT, w2k), (w1v, vcmpT, w2v))):
                for hc in range(2):
                    for s_ in range(32):
                        K.mm(PW[0][:, 0:nn], w1[:, s_, hc * 128:(hc + 1) * 128],
                             src[:, 16 * nb0 + s_:16 * nb0 + s_ + 16 * (nn - 1) + 1:16], start=(s_ == 0), stop=(s_ == 31),
                             R=["a_w1k", "a_w1v", "a_kcmpT", "a_vcmpT"], W=[("pw", 0)])
                    K.act(hact[:, kv, hc, 0:nn], PW[0][:, 0:nn], AF.Silu, R=[("pw", 0), "a_hbias"], W=["a_hact"],
                          bias=hbias[:, kv * 2 + hc:kv * 2 + hc + 1], scale=1.0)
                for hc in range(2):
                    K.mm(PW[0][0:64, 64:64 + nn], w2[:, hc, :], hact[:, kv, hc, 0:nn], start=(hc == 0), stop=(hc == 1),
                         R=["a_w2k", "a_w2v", "a_hact"], W=[("pw", 0)])
                if kv == 0:
                    K.copy(cst[:, 0:nn], PW[0][0:64, 64:64 + nn], R=[("pw", 0)], W=["a_cst"])
                    K.tt(csq2[:, 0:nn], cst[:, 0:nn], cst[:, 0:nn], ALU.mult, R=["a_cst"], W=["a_csq2"])
                    K.mm(PW[0][0:64, 128:128 + nn], onesf[0:64, 0:64], csq2[:, 0:nn], R=["a_csq2", "ones_f"], W=[("pw", 0)])
                    K.act(crs2[:, 0:nn], PW[0][0:64, 128:128 + nn], AF.Sqrt, R=[("pw", 0), "epsc"], W=["a_crs2"],
                          bias=consts["eps"][0:64, :], scale=1.0 / 64)
                    K.recip(crs2[:, 0:nn], crs2[:, 0:nn], R=["a_crs2"], W=["a_crs2"])
                    K.stt(kcT[0:64, nb0:nb0 + nn], cst[:, 0:nn], qng[:, 1:2], crs2[:, 0:nn], ALU.mult, ALU.mult,
                          R=["a_cst", "a_crs2", "a_qng"], W=["a_kcT"])
                else:
                    K.copy(vcT[:, nb0:nb0 + nn], PW[0][0:64, 64:64 + nn], R=[("pw", 0)], W=["a_vcT"])
            for m in range(NM):
                K.tr(PW[0][:, 160 + m * 64:160 + (m + 1) * 64], vcT[:, m * 128:(m + 1) * 128], IDENT[0:64, 0:64],
                     R=["a_vcT", "cmask"], W=[("pw", 0)])
                K.copy(vcx[:, m, 0:64], PW[0][:, 160 + m * 64:160 + (m + 1) * 64], R=[("pw", 0)], W=["a_vcx"])
            for q in range(TT // 128):
                nsa_qtile(t0 // 128 + q, q)

        def nsa_qtile(i, q):
            qs = slice(q * 128, (q + 1) * 128)
            qT = QaT[:, q]
            qF = QaF[:, q]
            ek = [0]

            def scores(lhsT, rhs, bias_ap, dst, dkey, rkeys):
                k2 = ek[0] % 2
                ek[0] += 1
                ps = PS2[k2]
                K.mm(ps[:, 0:512], lhsT, rhs, R=rkeys, W=[("ps", k2)])
                if bias_ap is not None:
                    K.tt(Tb[k2][:], v4(ps[:, 0:512]), bias_ap, ALU.add, R=[("ps", k2), "a_bw4"] + [("a_bias", x) for x in range(19)],
                         W=[("a_T", k2)])
                    K.act(dst, Tb[k2][:], AF.Exp, R=[("a_T", k2)], W=[dkey])
                else:
                    K.act(dst, v4(ps[:, 0:512]), AF.Exp, R=[("ps", k2)], W=[dkey])

            jl = [j for j in range(i - 4, i + 1) if j >= 0]
            for ji, j in enumerate(jl):
                k2 = j % 2
                if j == i:
                    b_ap = v4(biasT[:, 17, :])
                elif j == i - 1:
                    b_ap = v4(biasT[:, 18, :])
                elif j == i - 4:
                    b_ap = bc(bw4[:, :].unsqueeze(1), [128, 4, 128])
                else:
                    b_ap = None
                scores(KwT[:, j * 128:(j + 1) * 128], qT.rearrange("p a b -> p (a b)"), b_ap, Pb[k2][:], ("a_P", k2),
                       ["a_KwT", "a_QaT"])
                for h in range(4):
                    K.mm(POW[:, h * 65:(h + 1) * 65], Pb[k2][:, h, :], Vw[:, j, :], start=(ji == 0 and h == 0),
                         stop=(j == i and h == 3), R=[("a_P", k2), "a_Vw"], W=["pow"], skip_group_check=True)
            ow = POW[:, 0:260].rearrange("p (a b) -> p a b", a=4)
            K.ts(rdenw[:], ow[:, :, 64], 1e-30, None, ALU.max, R=["pow"], W=["a_rdenw"])
            K.recip(rdenw[:], rdenw[:], R=["a_rdenw"], W=["a_rdenw"])
            K.tt(coefw[:], rdenw[:], gsb[:, q, 2:12:3], ALU.mult, R=["a_rdenw", "a_gsb"], W=["a_coefw"])
            K.tt(yaw[:], ow[:, :, 0:64], bc(coefw[:, :].unsqueeze(2), [128, 4, 64]), ALU.mult, R=["pow", "a_coefw"], W=["a_yaw"])
            first = True
            mlist = [m for m in range(NM) if i - 16 * m >= 0]
            for mi, m in enumerate(mlist):
                ip = i - 16 * m
                k2 = mi % 2
                b_ap = v4(biasT[:, ip, :]) if ip <= 16 else None
                scores(kcT[:, m * 128:(m + 1) * 128], qF.rearrange("p a b -> p (a b)"), b_ap, Pc[k2][:], ("a_Pc", k2),
                       ["a_kcT", "a_QaF"])
                for h in range(4):
                    K.mm(POA[:, h * 65:(h + 1) * 65], Pc[k2][:, h, :], vcx[:, m, :], start=(first and h == 0),
                         stop=(mi == len(mlist) - 1 and h == 3), R=[("a_Pc", k2), "a_vcx"], W=["poa"], skip_group_check=True)
                for h in range(4):
                    K.mm(POB[:, h * 64:(h + 1) * 64], Pc[k2][:, h, :], ovl[:, m, :], start=(first and h == 0),
                         stop=(mi == len(mlist) - 1 and h == 3), R=[("a_Pc", k2), "a_ovl"], W=[PSTK], skip_group_check=True)
                first = False
            oa = POA[:, 0:260].rearrange("p (a b) -> p a b", a=4)
            K.ts(rden[:], oa[:, :, 64], 1e-30, None, ALU.max, R=["poa"], W=["a_rden"])
            K.recip(rden[:], rden[:], R=["a_rden"], W=["a_rden"])
            K.tt(coef[:], rden[:], gsb[:, q, 0:12:3], ALU.mult, R=["a_rden", "a_gsb"], W=["a_coef"])
            K.tt(ya[:], oa[:, :, 0:64], bc(coef[:, :].unsqueeze(2), [128, 4, 64]), ALU.mult, R=["poa", "a_coef"], W=["a_ya"])
            for h in range(4):
                if h == 0:
                    K.ts(scr[:, 0:NB], POB[:, 0:NB], rden[:, 0:1], None, ALU.mult, R=[PSTK, "a_rden"], W=["a_scr"])
                else:
                    K.stt(scr[:, 0:NB], POB[:, h * 64:h * 64 + NB], rden[:, h:h + 1], scr[:, 0:NB], ALU.mult, ALU.add,
                          R=[PSTK, "a_rden", "a_scr"], W=["a_scr"])
            K.tt(scr[:, 0:NB], scr[:, 0:NB], keepc[:, 0, NB - 2 * i:2 * NB - 2 * i], ALU.mult, R=["a_scr", "a_keepc"], W=["a_scr"])
            K.tt(scr[:, 0:NB], scr[:, 0:NB], keepc[:, 1, NB - 2 * i:2 * NB - 2 * i], ALU.add, R=["a_scr", "a_keepc"], W=["a_scr"])
            K.memset(scr[:, 0:1], 1e6, W=["a_scr"])
            K.P.add("dve", lambda e: e.max(v8[:], scr[:, 0:NB]), R=["a_scr"], W=["a_v8"])
            K.P.add("dve", lambda e: e.match_replace(sc2[:, 0:NB], v8[:], scr[:, 0:NB], -3e6), R=["a_scr", "a_v8"], W=["a_sc2"])
            K.P.add("dve", lambda e: e.max(v8[:], sc2[:, 0:NB]), R=["a_sc2"], W=["a_v8"])
            K.ts(mskb[:, 0:NB], scr[:, 0:NB], v8[:, 7:8], None, ALU.is_ge, R=["a_scr", "a_v8"], W=["a_mskb"])
            K.mm(PMX[0:NB, 128:256], mskb[:, 0:NB], identb[:], R=["a_mskb", "a_identb"], W=["pmx"])
            K.copy(mT[0:NB, :], PMX[0:NB, 128:256], R=["pmx"], W=["a_mT"])
            for j in range(i + 1):
                k2 = j % 2
                if j == i:
                    b_ap = v4(biasT[:, 17, :])
                elif j == i - 1:
                    b_ap = v4(biasT[:, 18, :])
                else:
                    b_ap = None
                scores(KsT[:, j * 128:(j + 1) * 128], qT.rearrange("p a b -> p (a b)"), b_ap, Eb[k2][:], ("a_E", k2),
                       ["a_KsT", "a_QaT"])
                K.mm(PMX[:, 0:128], expc[0:NB, j * 128:(j + 1) * 128], mT[0:NB, :], R=["a_expc", "a_mT"], W=["pmx"])
                K.tt(Pb[k2][:], Eb[k2][:], bc(PMX[:, 0:128].unsqueeze(1), [128, 4, 128]), ALU.mult, R=[("a_E", k2), "pmx"],
                     W=[("a_P", k2)])
                for h in range(4):
                    K.mm(POA[:, h * 65:(h + 1) * 65], Pb[k2][:, h, :], Vs[:, j, :], start=(j == 0 and h == 0),
                         stop=(j == i and h == 3), R=[("a_P", k2), "a_Vs"], W=["poa"], skip_group_check=True)
            K.ts(rden[:], oa[:, :, 64], 1e-30, None, ALU.max, R=["poa"], W=["a_rden"])
            K.recip(rden[:], rden[:], R=["a_rden"], W=["a_rden"])
            K.tt(coef[:], rden[:], gsb[:, q, 1:12:3], ALU.mult, R=["a_rden", "a_gsb"], W=["a_coef"])
            K.tt(ytmp[:], oa[:, :, 0:64], bc(coef[:, :].unsqueeze(2), [128, 4, 64]), ALU.mult, R=["poa", "a_coef"], W=["a_ytmp"])
            K.tt(ya[:], ya[:], ytmp[:], ALU.add, R=["a_ya", "a_ytmp"], W=["a_ya"])
            K.tt(ya[:], ya[:], yaw[:], ALU.add, R=["a_ya", "a_yaw"], W=["a_ya"])
            yaf = ya[:].rearrange("p a b -> p (a b)")
            K.tt(ytmp[:], ya[:], ya[:], ALU.mult, R=["a_ya"], W=["a_ytmp"])
            K.P.add("dve", lambda e: e.tensor_reduce(yss[:], ytmp[:].rearrange("p a b -> p (a b)"), AX.X, ALU.add),
                    R=["a_ytmp"], W=["a_yss"])
            K.act(yss[:], yss[:], AF.Sqrt, R=["a_yss", "epsc"], W=["a_yss"], bias=consts["eps"][:], scale=1.0 / 256)
            K.recip(yss[:], yss[:], R=["a_yss"], W=["a_yss"])
            K.stt(yaf, yaf, yss[:, 0:1], mg0[:], ALU.mult, ALU.mult, R=["a_ya", "a_yss", "a_mg0"], W=["a_ya"])
            for pr in range(2):
                K.tr(PMX[:, 256 + pr * 128:256 + (pr + 1) * 128], yaf[:, pr * 128:(pr + 1) * 128], IDENT, R=["a_ya", "cmask"],
                     W=["pmx"])
                K.copy(yt[:, pr, qs], PMX[:, 256 + pr * 128:256 + (pr + 1) * 128], R=["pmx"], W=[("yt", pr)], eng="act")


        for tt in range(NT):
            tsl = slice(tt * TT, (tt + 1) * TT)
            K.dma(xt[:], xsv[:, :, tsl], R=[("xd", sname, tt)], W=[("xt", dc) for dc in range(DC)])
            K.act(sq[:], xt[:], AF.Square, R=[("xt", dc) for dc in range(DC)], W=SQK)
            for dc in range(DC):
                K.mm(PST[:, 0:TT], consts["ones_bf"][:], sq[:, dc, :], start=(dc == 0), stop=(dc == DC - 1),
                     R=SQK, W=[PSTK])
            K.act(rs[:], PST[:, 0:TT], AF.Sqrt, R=[PSTK, "epsc"], W=["rs"], bias=consts["eps"][:], scale=1.0 / D)
            K.recip(rs[:], rs[:], R=["rs"], W=["rs"])
            for dc in range(DC):
                k2 = dc % 2
                K.stt(sa[k2][:], xt[:, dc, :], gs[:, s, dc:dc + 1], rs[:], ALU.mult, ALU.mult,
                      R=[("xt", dc), "rs", "gs"], W=[("sa", k2)])
                K.act(hb[:, dc, :], sa[k2][:], AF.Identity, R=[("sa", k2), "mod"], W=[("hb", dc)],
                      bias=shift[:, s * 24 + dc:s * 24 + dc + 1], scale=1.0)
            if mode == 1:
                for r in range(2, DC):
                    if not (("B" in enable and r in (2, 3)) or ("C" in enable and r in (4, 5)) or ("D" in enable and r in (6, 7))):
                        K.memset(yt[:, r, :], 0.0, W=[("yt", r)], eng="pool")
            else:
                K.dma(yt[:, 2:DC, :], yscr.rearrange("(dc p) t -> p dc t", p=128)[:, 2:DC, tsl],
                      R=[("yscr", tt * TT // 256 + q) for q in range(TT // 256)], W=[("yt", r) for r in range(2, DC)])
                if "A" not in enable:
                    for r in range(2):
                        K.memset(yt[:, r, :], 0.0, W=[("yt", r)], eng="pool")
            if "B" in enable:
                for ch in range(2):
                    proj_fm(G_BB + ch, bbt[:], ["bbt"])
                    proj_fm(G_BC + ch, cct[:], ["cct"])
                    proj_fm(G_BX + ch, cvt[:], ["cvt"])
                    K.tt(ub[ch][:, 2:2 + TT], cct[:], cvt[:], ALU.mult, R=["cct", "cvt"], W=[("ub", ch)])
                    K.ts(cvt[:], ub[ch][:, 2:2 + TT], scw[:, ch, 2:3], None, ALU.mult, R=[("ub", ch), "scw"], W=["cvt"])
                    K.stt(cvt[:], ub[ch][:, 1:1 + TT], scw[:, ch, 1:2], cvt[:], ALU.mult, ALU.add,
                          R=[("ub", ch), "scw", "cvt"], W=["cvt"])
                    K.stt(cvt[:], ub[ch][:, 0:TT], scw[:, ch, 0:1], cvt[:], ALU.mult, ALU.add,
                          R=[("ub", ch), "scw", "cvt"], W=["cvt"])
                    K.tt(ybt[ch][:], bbt[:], cvt[:], ALU.mult, R=["bbt", "cvt"], W=[("ybt", ch)])
                    K.copy(ub[ch][:, 0:2], ub[ch][:, TT:TT + 2], R=[("ub", ch)], W=[("ub", ch)])
                    K.act(sq[:, ch, :], ybt[ch][:], AF.Square, R=[("ybt", ch)], W=["sq"])
                for ch in range(2):
                    K.mm(PST[:, 0:TT], consts["ones_bf"][:], sq[:, ch, :], start=(ch == 0), stop=(ch == 1),
                         R=["sq"], W=[PSTK])
                K.act(sa[0][:], PST[:, 0:TT], AF.Sqrt, R=[PSTK, "epsc"], W=[("sa", 0)], bias=consts["eps"][:],
                      scale=1.0 / 256)
                K.recip(sa[0][:], sa[0][:], R=[("sa", 0)], W=[("sa", 0)])
                for ch in range(2):
                    K.stt(yt[:, 2 + ch, :], ybt[ch][:], mixg[:, 1, ch:ch + 1], sa[0][:], ALU.mult, ALU.mult,
                          R=[("ybt", ch), "mixg", ("sa", 0)], W=[("yt", 2 + ch)])
            if "D" in enable:
                for h in range(4):
                    proj_fm(G_DQ + h, QmT[:, h, :], ["QmT"])
                    proj_fm(G_DK + h, KmT[:, h, :], ["KmT"], scale=0.125, bias=bfm8[0:64, h:h + 1])
            if "C" in enable:
                gdn_tile()
            for c in range(NCH):
                if "D" in enable:
                    mlstm_chunk(c)
                if "C" in enable:
                    gdn_chunk(c)
            if mode == 1:
                K.dma(yscr.rearrange("(dc p) t -> p dc t", p=128)[:, 2:DC, tsl], yt[:, 2:DC, :],
                      R=[("yt", r) for r in range(2, DC)], W=[("yscr", tt * TT // 256 + q) for q in range(TT // 256)])
                continue
            if "A" in enable:
                nsa_tile(tt)
                K.dma(yscr.rearrange("(dc p) t -> p dc t", p=128)[:, 0:2, tsl], yt[:, 0:2, :],
                      R=[("yt", r) for r in range(2)], W=[("yscrA", tt)])
            for dc in range(DC):
                pj = PJ[pjn[0] % len(PJ)]
                kk = ("pj", pjn[0] % len(PJ))
                pjn[0] += 1
                for fc in range(DC):
                    K.mm(pj[:, 0:TT], wouts[:, fc, dc * 128:(dc + 1) * 128], yt[:, fc, :], start=(fc == 0),
                         stop=(fc == DC - 1), R=[("wout", fc), ("yt", fc)], W=[kk])
                K.stt(xt[:, dc, :], pj[:, 0:TT], gate[:, s, dc:dc + 1], xt[:, dc, :], ALU.mult, ALU.add,
                      R=[kk, "gate", ("xt", dc)], W=[("xt", dc)])
            K.dma(xdv[:, :, tsl], xt[:], R=[("xt", dc) for dc in range(DC)], W=[("xd", dname, tt)])
    P.barrier()


def t5_thresholds():
    def bucket(n):
        if n < 16:
            return n
        nf = np.float32(n)
        v = np.log(nf / np.float32(16)) / np.float32(math.log(128 / 16)) * np.float32(16)
        return min(16 + int(np.float32(v)), 31)
    bs = [bucket(n) for n in range(0, 400)]
    return [min(n for n in range(400) if bs[n] >= b) for b in range(32)]


def bias_build(K, t5_table, dist_d, bias_scr, consts, es_ext=None):
    nc = K.nc
    lo = t5_thresholds()
    with ExitStack() as es_own:
        es = es_ext if es_ext is not None else es_own
        tb = _sb(es, nc, "bb_tb", [128, 32, 4], F32)
        ndl = _sb(es, nc, "bb_ndl", [128, 31, 4], F32)
        dtl = [_sb(es, nc, "bb_dt%d" % i, [128, 128], F32) for i in range(2)]
        acc = [_sb(es, nc, "bb_acc%d" % i, [128, 4, 128], F32) for i in range(2)]
        tmp = [_sb(es, nc, "bb_tmp%d" % i, [128, 4, 128], F32) for i in range(2)]
        K.dma(tb[:].rearrange("p b h -> p (b h)"), t5_table.rearrange("b h -> (b h)").partition_broadcast(128), W=["bb_tb"])
        K.tt(ndl[:], tb[:, 0:31, :], tb[:, 1:32, :], ALU.subtract, R=["bb_tb"], W=["bb_ndl"])
        for q in range(19):
            k2 = q % 2
            K.dma(dtl[k2][:], dist_d[q], W=[("bb_dt", k2)])
            dbc = bc(dtl[k2][:, :].unsqueeze(1), [128, 4, 128])
            for b in range(1, 32):
                dst = acc[k2] if b == 1 else tmp[b % 2]
                dk = ("bb_acc", k2) if b == 1 else ("bb_tmp", b % 2)
                K.stt(dst[:], dbc, float(lo[b]), bc(ndl[:, b - 1, :].unsqueeze(2), [128, 4, 128]), ALU.is_lt, ALU.mult,
                      R=[("bb_dt", k2), "bb_ndl"], W=[dk])
                if b > 1:
                    K.tt(acc[k2][:], acc[k2][:], dst[:], ALU.add, R=[("bb_acc", k2), dk], W=[("bb_acc", k2)])
            K.ts(tmp[0][:], dbc, 0.0, -30000.0, ALU.is_lt, ALU.mult, R=[("bb_dt", k2)], W=[("bb_tmp", 0)])
            K.tt(acc[k2][:], acc[k2][:], tmp[0][:], ALU.add, R=[("bb_acc", k2), ("bb_tmp", 0)], W=[("bb_acc", k2)])
            K.dma(bias_scr[q], acc[k2][:].rearrange("p a b -> p (a b)"), R=[("bb_acc", k2)], W=[("bias_scr", q)])
    if es_ext is None:
        K.P.barrier()


def build(T=4096, TT=512, layers=2, debug_y=False, enable="ABCD"):
    nc = bass.Bass("TRN2", target_bir_lowering=False)
    K = KB(nc)
    P = K.P
    dt = lambda name, shape, kind="ExternalInput", d=F32: nc.dram_tensor(name, list(shape), d, kind=kind).ap()
    xT = dt("xT", [D, T])
    cT = dt("cT", [128, DC])
    ada_w = dt("ada_w", [2, D, 9 * D])
    ada_bT = dt("ada_bT", [2, 128, 72])
    normgT = dt("normgT", [2, 128, 3, DC])
    ffn_w13 = [dt("ffn1_w13", [2, D, 2 * DFF]), dt("ffn2_w13", [2, D, 2 * DFF])]
    ffn_w2 = [dt("ffn1_w2", [2, DFF, D]), dt("ffn2_w2", [2, DFF, D])]
    outT = dt("outT", [D, T], kind="ExternalOutput")
    Win = {
        "w_in": dt("w_in", [2, D, D_IN]), "b_in": dt("b_in", [2, D_IN]), "b_fm": dt("b_fm", [2, 128, NFM]),
        "w_out": dt("w_out", [2, D, D]), "sc_conv_wT": dt("sc_conv_wT", [2, 128, 2, 3]),
        "mixgT": dt("mixgT", [2, 128, 2, 2]), "mlstm_f_bias": dt("mlstm_f_bias", [2, 4]),
        "mlstm_norm_g": dt("mlstm_norm_g", [2, 64]),
        "gdn_conv_wT": dt("gdn_conv_wT", [2, 64, 12, 4]), "gdn_A_log": dt("gdn_A_log", [2, 4]),
        "gdn_dt_bias": dt("gdn_dt_bias", [2, 4]), "gdn_norm_g": dt("gdn_norm_g", [2, 64]),
        "cmp_k_w1": dt("cmp_k_w1", [2, 2048, 256]), "cmp_v_w1": dt("cmp_v_w1", [2, 2048, 256]),
        "cmp_k_w2": dt("cmp_k_w2", [2, 256, 64]), "cmp_v_w2": dt("cmp_v_w2", [2, 256, 64]),
        "cmp_posT": dt("cmp_posT", [2, 64, 32]), "qkng": dt("qkng", [2, 64, 2]), "mix_norm_g0": dt("mix_norm_g0", [2, 256]),
    }
    NB_ = T // 64
    NM_ = (T // 16 + 127) // 128
    Wsh = {
        "t5_table": dt("t5_table", [32, 4]), "expc": dt("expc", [NB_, T]), "keepadd": dt("keepadd", [128, 2, 2 * NB_]),
        "ovl": dt("ovl", [128, NM_, 64]), "bw4": dt("bw4", [128, 128]),
        "bias_scr": dt("bias_scr", [19, 128, 512], kind="Internal"),
    }
    dist_d = dt("dist_tiles", [19, 128, 128])
    cmask_d = dt("cmask", [128, CM_N])
    yscr = dt("ydbg", [D, T], kind="ExternalOutput" if debug_y else "Internal", d=BF16)
    xa = dt("xa_scr", [D, T], kind="Internal")
    xb = dt("xb_scr", [D, T], kind="Internal")

    with ExitStack() as es:
        consts = {
            "ones_bf": _sb(es, nc, "ones_bf", [128, 128], BF16),
            "eps": _sb(es, nc, "epsc", [128, 1], F32),
        }
        K.memset(consts["ones_bf"][:], 1.0, W=["ones_bf"])
        consts["ones_f"] = _sb(es, nc, "ones_f", [128, 128], F32)
        consts["one"] = _sb(es, nc, "onec", [128, 1], F32)
        consts["cmask"] = _sb(es, nc, "cmask_sb", [128, CM_N], F32)
        K.memset(consts["ones_f"][:], 1.0, W=["ones_f"])
        K.memset(consts["one"][:], 1.0, W=["onec"])
        K.dma(consts["cmask"][:], cmask_d, W=["cmask"])
        K.memset(consts["eps"][:], EPS, W=["epsc"])
        condT = _sb(es, nc, "condT", [128, DC, 2], F32)
        ctmp = _sb(es, nc, "ctmp", [128, DC], F32)
        K.memset(condT[:], 0.0, W=["condT"])
        K.dma(ctmp[:], cT, W=["ctmp"])
        K.act(condT[:, :, 0], ctmp[:], AF.Silu, R=["ctmp"], W=["condT"])
        mv = []
        for l in range(2):
            mv.append({
                "mod": _sb(es, nc, "mod%d" % l, [128, 72], F32),
                "gs": _sb(es, nc, "gs%d" % l, [128, 3, DC], F32),
                "gate": _sb(es, nc, "gate%d" % l, [128, 3, DC], F32),
            })
        adab = _sb(es, nc, "adab", [128, 2, 72], F32)
        normg = _sb(es, nc, "normg", [128, 2, 3, DC], F32)
        K.dma(adab[:], ada_bT.rearrange("l p j -> p l j"), W=["adab"])
        K.dma(normg[:], normgT.rearrange("l p s d -> p l s d"), W=["normg"])
        P.barrier()

        with ExitStack() as es0:
            for l in range(layers):
                mod_phase(K, es0, l, ada_w[l], adab[:, l, :], normg[:, l], condT, mv[l])
            if "A" in enable:
                bias_build(K, Wsh["t5_table"], dist_d, Wsh["bias_scr"], consts, es_ext=es0)
        P.barrier()
        cur = xT
        curname = "xT"
        for l in range(layers):
            last = (l == layers - 1)
            ffn_phase(K, l, 0, cur, curname, xa, "xa", ffn_w13[0][l], ffn_w2[0][l], mv[l], 0, T, TT, consts)
            Wd = {k: v[l] for k, v in Win.items()}
            Wd.update(Wsh)
            mixer_phase(K, l, xa, "xa", xb, "xb", Wd, mv[l], T, 256, consts, yscr, 1, enable=enable)
            mixer_phase(K, l, xa, "xa", xb, "xb", Wd, mv[l], T, 256, consts, yscr, 2, enable=enable)
            ffn_phase(K, l, 1, xb, "xb", outT if last else xa, "outT" if last else "xa", ffn_w13[1][l], ffn_w2[1][l], mv[l], 2, T, TT, consts)
            cur = xa
            curname = "xa"
        P.fence("sp", [("xd", "outT", tt) for tt in range(T // TT)])
        P.emit()
    return nc


def _cmask():
    m = np.zeros((128, CM_N), np.float32)
    m[:, CM_ID:CM_ID + 128] = np.eye(128, dtype=np.float32)
    k = np.arange(64)[:, None]
    i = np.arange(64)[None, :]
    m[0:64, CM_TRIU:CM_TRIU + 64] = (k <= i)
    m[0:64, CM_SU:CM_SU + 64] = (k < i)
    m[0:64, CM_SL:CM_SL + 64] = (k > i)
    m[0:64, CM_LI:CM_LI + 64] = (k >= i)
    return m


def prep_shared(inp, T=4096):
    f = lambda a: np.ascontiguousarray(np.asarray(a, dtype=np.float32))
    b_in = f(inp["b_in"])
    b_fm = np.zeros((2, 128, NFM), np.float32)
    for g, (c0, n) in enumerate(FM):
        b_fm[:, 0:n, g] = b_in[:, c0:c0 + n]
    sh = {
        "ada_w": f(inp["ada_w"]),
        "ada_bT": f(np.asarray(inp["ada_b"]).reshape(2, 72, 128).transpose(0, 2, 1)),
        "normgT": f(np.asarray(inp["norm_g"]).reshape(2, 3, 8, 128).transpose(0, 3, 1, 2)),
        "ffn1_w13": f(inp["ffn1_w13"]), "ffn2_w13": f(inp["ffn2_w13"]),
        "ffn1_w2": f(inp["ffn1_w2"]), "ffn2_w2": f(inp["ffn2_w2"]),
        "w_in": f(inp["w_in"]), "b_in": b_in, "b_fm": b_fm, "w_out": f(inp["w_out"]),
        "sc_conv_wT": f(np.asarray(inp["sc_conv_w"]).reshape(2, 3, 2, 128).transpose(0, 3, 2, 1)),
        "mixgT": f(np.asarray(inp["mix_norm_g"]).reshape(2, 2, 2, 128).transpose(0, 3, 1, 2)),
        "mlstm_f_bias": f(inp["mlstm_f_bias"]), "mlstm_norm_g": f(inp["mlstm_norm_g"]),
        "gdn_conv_wT": f(np.asarray(inp["gdn_conv_w"]).reshape(2, 4, 12, 64).transpose(0, 3, 2, 1)),
        "gdn_A_log": f(inp["gdn_A_log"]), "gdn_dt_bias": f(inp["gdn_dt_bias"]), "gdn_norm_g": f(inp["gdn_norm_g"]),
        "cmask": _cmask(),
        "cmp_k_w1": f(inp["cmp_k_w1"]), "cmp_v_w1": f(inp["cmp_v_w1"]), "cmp_k_w2": f(inp["cmp_k_w2"]),
        "cmp_v_w2": f(inp["cmp_v_w2"]), "cmp_posT": f(np.asarray(inp["cmp_pos"]).transpose(0, 2, 1)),
        "qkng": f(np.stack([np.asarray(inp["q_norm_g"]), np.asarray(inp["k_norm_g"])], axis=-1)),
        "mix_norm_g0": f(np.asarray(inp["mix_norm_g"])[:, 0]), "t5_table": f(inp["t5_table"]),
    }
    sh.update(_nsa_consts(T))
    return sh


def _nsa_consts(T):
    NB = T // 64
    NM = (T // 16 + 127) // 128
    c = np.arange(128)[:, None]
    r = np.arange(128)[None, :]
    dist = np.zeros((19, 128, 128), np.float32)
    for ip in range(17):
        dist[ip] = r - 16 * c + 128 * ip - 31
    dist[17] = r - c
    dist[18] = 128 + r - c
    expc = (np.arange(T)[None, :] // 64 == np.arange(NB)[:, None]).astype(np.float32)
    keep = np.ones((128, 2 * NB), np.float32)
    add = np.zeros((128, 2 * NB), np.float32)
    for rr in range(128):
        for x in range(2 * NB):
            rb = x - NB
            if rr < 64:
                forced, invalid = rb in (-1, 0), rb > 0
            else:
                forced, invalid = rb in (0, 1), rb > 1
            if invalid:
                keep[rr, x], add[rr, x] = 0.0, -1e6
            elif forced:
                keep[rr, x], add[rr, x] = 0.0, 1e6
    ovl = np.zeros((128, NM, 64), np.float32)
    for m in range(NM):
        ci = 128 * m + np.arange(128)[:, None]
        bj = np.arange(64)[None, :]
        ovl[:, m, :] = ((ci * 16 < (bj + 1) * 64) & (ci * 16 + 32 > bj * 64) & (bj < NB) & (ci < T // 16 - 1))
    bw4 = np.where(c > r, 0.0, -30000.0).astype(np.float32)
    return {"dist_tiles": dist, "expc": expc, "keepadd": np.ascontiguousarray(np.stack([keep, add], axis=1)),
            "ovl": ovl, "bw4": bw4}


def prep_core(inp, b):
    x = np.asarray(inp["x"], dtype=np.float32)
    c = np.asarray(inp["c"], dtype=np.float32)
    return {"xT": np.ascontiguousarray(x[b].T), "cT": np.ascontiguousarray(c[b].reshape(8, 128).T)}


_NC_CACHE = {}


def kernel(**inputs):
    x = np.asarray(inputs["x"])
    B, T, _ = x.shape
    if T not in _NC_CACHE:
        _NC_CACHE[T] = build(T=T)
    nc = _NC_CACHE[T]
    sh = prep_shared(inputs, T)
    in_maps = []
    for b in range(B):
        m = dict(sh)
        m.update(prep_core(inputs, b))
        in_maps.append(m)
    res = run_bass_kernel_spmd(nc, in_maps, core_ids=list(range(B)))
    out = np.stack([np.asarray(r["outT"]).T for r in res.results], axis=0)
    return np.ascontiguousarray(out.astype(np.float32))
```
